# Optimizing a Trainium2 kernel written in Bass

```python
import math
import jax
import jax.numpy as jnp
from jax import lax
import numpy as np


D_MODEL = 1024
BATCH = 8
SEQ = 4096
DEPTH = 2

GRID_W = 64
CTX_LEN = 256
N_Q_HEADS = 8
N_KV_HEADS = 2
GQA_GROUP = N_Q_HEADS // N_KV_HEADS
HEAD_DIM = D_MODEL // N_Q_HEADS
N_FREQ = HEAD_DIM // 4
ROPE_THETA = 10000.0
Q_BLOCK = 128
CONV_DIM = D_MODEL
CONV_K = 31
RWKV_HEAD = 64
RWKV_HEADS = D_MODEL // RWKV_HEAD
RWKV_DIM = RWKV_HEADS * RWKV_HEAD
DECAY_LORA = 64
ICLR_LORA = 64
GATE_LORA = 128
SHIFT_K = 3
DECAY_SCALE = math.exp(-0.5)
D_FF = 256 * ((8 * D_MODEL // 3 + 255) // 256)
FFN_CONV_K = 3
N_BRANCH = 3
N_MOD = 6
EPS = 1e-6
LN_EPS = 1e-5
GN_EPS = RWKV_HEAD * 1e-5
IN_WIDTHS = (N_Q_HEADS * HEAD_DIM, N_KV_HEADS * HEAD_DIM, N_KV_HEADS * HEAD_DIM, 2 * CONV_DIM, 3 * RWKV_DIM,
             DECAY_LORA, DECAY_LORA, ICLR_LORA, ICLR_LORA, GATE_LORA, N_BRANCH * D_MODEL)
IN_SPLITS = tuple(int(s) for s in np.cumsum(IN_WIDTHS)[:-1])
N_IN = int(sum(IN_WIDTHS))

kernel_name = 'hybrid_gqa_conformer_rwkv7_dit_block'


def rms_norm(x, g, eps=EPS):
    xf = x.astype(jnp.float32)
    y = xf * lax.rsqrt(jnp.mean(xf * xf, axis=-1, keepdims=True) + eps)
    return (y * g.astype(jnp.float32)).astype(x.dtype)


def layer_norm(x, g, b, eps=LN_EPS):
    xf = x.astype(jnp.float32)
    mu = jnp.mean(xf, axis=-1, keepdims=True)
    var = jnp.mean(jnp.square(xf - mu), axis=-1, keepdims=True)
    y = (xf - mu) * lax.rsqrt(var + eps)
    return (y * g.astype(jnp.float32) + b.astype(jnp.float32)).astype(x.dtype)


def dwconv(x, w):
    return lax.conv_general_dilated(x, w[:, None, :].astype(x.dtype), window_strides=(1,), padding='SAME',
                                    dimension_numbers=('NWC', 'WIO', 'NWC'), feature_group_count=x.shape[-1])


def axial_rope_tables(rows):
    row = jnp.repeat(jnp.arange(rows, dtype=jnp.float32), GRID_W)
    col = jnp.tile(jnp.arange(GRID_W, dtype=jnp.float32), rows)
    inv_freq = ROPE_THETA ** (-jnp.arange(N_FREQ, dtype=jnp.float32) / N_FREQ)
    ang_r = row[:, None] * inv_freq
    ang_c = col[:, None] * inv_freq
    return (jnp.cos(ang_r), jnp.sin(ang_r), jnp.cos(ang_c), jnp.sin(ang_c))


def _rotate(x, cos, sin):
    x1, x2 = jnp.split(x, 2, axis=-1)
    cos = cos[None, :, None, :].astype(x.dtype)
    sin = sin[None, :, None, :].astype(x.dtype)
    return jnp.concatenate([x1 * cos - x2 * sin, x2 * cos + x1 * sin], axis=-1)


def apply_axial_rope(x, rope):
    cos_r, sin_r, cos_c, sin_c = rope
    xr, xc = jnp.split(x, 2, axis=-1)
    return jnp.concatenate([_rotate(xr, cos_r, sin_r), _rotate(xc, cos_c, sin_c)], axis=-1)


def gqa_attend(q, k, v):
    s = jnp.einsum('bqkgd,bskd->bkgqs', q, k).astype(jnp.float32) * (HEAD_DIM ** -0.5)
    p = jax.nn.softmax(s, axis=-1).astype(v.dtype)
    return jnp.einsum('bkgqs,bskd->bqkgd', p, v)


def attention_branch(qc, kc, vc, qx, kx, vx, lp, rope, need_ctx):
    B, S, _ = qx.shape
    L = kc.shape[1]
    heads = lambda t, n: t.reshape(t.shape[0], t.shape[1], n, HEAD_DIM)
    kc = rms_norm(heads(kc, N_KV_HEADS), lp['k_norm'])
    vc = heads(vc, N_KV_HEADS)
    qx = apply_axial_rope(rms_norm(heads(qx, N_Q_HEADS), lp['q_norm']), rope)
    kx = apply_axial_rope(rms_norm(heads(kx, N_KV_HEADS), lp['k_norm']), rope)
    k_all = jnp.concatenate([kc, kx], axis=1)
    v_all = jnp.concatenate([vc, heads(vx, N_KV_HEADS)], axis=1)
    nb = S // Q_BLOCK
    q_blocks = qx.reshape(B, nb, Q_BLOCK, N_KV_HEADS, GQA_GROUP, HEAD_DIM).swapaxes(0, 1)
    out_x = lax.map(lambda qb: gqa_attend(qb, k_all, v_all), q_blocks)
    out_x = out_x.swapaxes(0, 1).reshape(B, S, N_Q_HEADS * HEAD_DIM)
    if not need_ctx:
        return None, out_x
    qc = rms_norm(heads(qc, N_Q_HEADS), lp['q_norm']).reshape(B, L, N_KV_HEADS, GQA_GROUP, HEAD_DIM)
    out_c = gqa_attend(qc, kc, vc).reshape(B, L, N_Q_HEADS * HEAD_DIM)
    return out_c, out_x


def conformer_conv(u, lp):
    a, b = jnp.split(u, 2, axis=-1)
    y = dwconv(a * jax.nn.sigmoid(b), lp['conv_w']) + lp['conv_b']
    return jax.nn.silu(layer_norm(y, lp['conv_ln_g'], lp['conv_ln_b']))


def l2_normalize(x):
    xf = x.astype(jnp.float32)
    return (xf * lax.rsqrt(jnp.maximum(jnp.sum(xf * xf, axis=-1, keepdims=True), 1e-12))).astype(x.dtype)


def rwkv_prepare(rkv, lw, la, lp):
    B, T, _ = rkv.shape
    hd = lambda t: t.reshape(B, T, RWKV_HEADS, RWKV_HEAD)
    r, k, v = jnp.split(dwconv(rkv, lp['shift_w']), 3, axis=-1)
    kap = l2_normalize(hd(k * lp['k_k']))
    dirs = []
    for d in range(2):
        w = jnp.exp(-DECAY_SCALE * jax.nn.sigmoid(lp['decay_w0'][d] + jnp.tanh(lw[d]) @ lp['decay_up'][d]))
        a = jax.nn.sigmoid(lp['iclr_a0'][d] + la[d] @ lp['iclr_up'][d])
        k_d = k * (1.0 + (a - 1.0) * lp['k_a'])
        dirs.append((hd(w), hd(k_d), hd(a) * kap))
    return hd(r), hd(v), kap, dirs


def wkv_scan(r, w, k, v, kap, b, s0, reverse, emit):
    xs = tuple(jnp.moveaxis(t, 1, 0) for t in (r, w, k, v, kap, b))

    def step(S, inp):
        r_t, w_t, k_t, v_t, kap_t, b_t = inp
        sa = jnp.einsum('bhvk,bhk->bhv', S, kap_t)
        S = S * w_t[:, :, None, :] - sa[..., :, None] * b_t[..., None, :] + v_t[..., :, None] * k_t[..., None, :]
        y = jnp.einsum('bhvk,bhk->bhv', S, r_t) if emit else None
        return S, y

    S, ys = lax.scan(step, s0, xs, reverse=reverse)
    if not emit:
        return None, S
    return jnp.moveaxis(ys, 0, 1).astype(r.dtype), S


def rwkv_output(y, r, v, dirs, lg, lp):
    B, T = y.shape[0], y.shape[1]
    yf = y.astype(jnp.float32)
    mu = jnp.mean(yf, axis=-1, keepdims=True)
    var = jnp.mean(jnp.square(yf - mu), axis=-1, keepdims=True)
    yn = ((yf - mu) * lax.rsqrt(var + GN_EPS)).reshape(B, T, RWKV_DIM)
    yn = (yn * lp['wkv_gn_g'] + lp['wkv_gn_b']).astype(y.dtype)
    bonus = (jnp.sum(r * dirs[0][1] * lp['r_k'], axis=-1, keepdims=True)
             + jnp.sum(r * dirs[1][1] * lp['r_k'], axis=-1, keepdims=True)) * v
    g = jax.nn.sigmoid(lg) @ lp['gate_up']
    return (yn + bonus.reshape(B, T, RWKV_DIM)) * g


def rwkv_branch(prep_c, prep_x, lg_c, lg_x, lp, need_ctx):
    r_c, v_c, kap_c, dirs_c = prep_c
    r_x, v_x, kap_x, dirs_x = prep_x
    B = r_x.shape[0]
    s0 = jnp.zeros((B, RWKV_HEADS, RWKV_HEAD, RWKV_HEAD), jnp.float32)
    ys_c, ys_x = [], []
    for d, rev in enumerate((False, True)):
        w_c, k_c, b_c = dirs_c[d]
        y_c, s_ctx = wkv_scan(r_c, w_c, k_c, v_c, kap_c, b_c, s0, rev, need_ctx)
        w_x, k_x, b_x = dirs_x[d]
        y_x, _ = wkv_scan(r_x, w_x, k_x, v_x, kap_x, b_x, s_ctx, rev, True)
        ys_c.append(y_c)
        ys_x.append(y_x)
    out_x = rwkv_output(ys_x[0] + ys_x[1], r_x, v_x, dirs_x, lg_x, lp)
    if not need_ctx:
        return None, out_x
    out_c = rwkv_output(ys_c[0] + ys_c[1], r_c, v_c, dirs_c, lg_c, lp)
    return out_c, out_x


def gated_merge(gates, att, conv, rwkv, lp):
    ga, gc, gr = jnp.split(jax.nn.sigmoid(gates), N_BRANCH, axis=-1)
    m = ga * (att @ lp['w_attn_o']) + gc * (conv @ lp['w_conv_o']) + gr * (rwkv @ lp['w_rwkv_o'])
    return m @ lp['w_out']


def conv_ffn(h, lp):
    z = dwconv(h @ lp['w_ffn_up'], lp['ffn_conv_w'])
    gate, val = jnp.split(z, 2, axis=-1)
    return (jax.nn.silu(gate) * val) @ lp['w_ffn_down']


def modulated(t, g, shift, scale):
    return rms_norm(t, g) * (1.0 + scale) + shift


def trunk_layer(x, xc, c, c_ctx, rope, lp, last):
    mod_x = jnp.split((jax.nn.silu(c) @ lp['w_mod'] + lp['b_mod'])[:, None, :], N_MOD, axis=-1)
    mod_c = jnp.split((jax.nn.silu(c_ctx) @ lp['w_mod'] + lp['b_mod'])[None, None, :], N_MOD, axis=-1)
    need_ctx = not last
    hx = modulated(x, lp['g_pre_mix'], mod_x[0], mod_x[1])
    hc = modulated(xc, lp['g_pre_mix'], mod_c[0], mod_c[1])
    qx, kx, vx, glux, rkvx, wfx, wbx, afx, abx, lgx, gatex = jnp.split(hx @ lp['w_in'], IN_SPLITS, axis=-1)
    qc, kc, vc, gluc, rkvc, wfc, wbc, afc, abc, lgc, gatec = jnp.split(hc @ lp['w_in'], IN_SPLITS, axis=-1)
    att_c, att_x = attention_branch(qc, kc, vc, qx, kx, vx, lp, rope, need_ctx)
    conv_x = conformer_conv(glux, lp)
    prep_x = rwkv_prepare(rkvx, (wfx, wbx), (afx, abx), lp)
    prep_c = rwkv_prepare(rkvc, (wfc, wbc), (afc, abc), lp)
    rw_c, rw_x = rwkv_branch(prep_c, prep_x, lgc, lgx, lp, need_ctx)
    x = x + mod_x[2] * rms_norm(gated_merge(gatex, att_x, conv_x, rw_x, lp), lp['g_post_mix'])
    hx = modulated(x, lp['g_pre_ffn'], mod_x[3], mod_x[4])
    x = x + mod_x[5] * rms_norm(conv_ffn(hx, lp), lp['g_post_ffn'])
    if last:
        return x, None
    conv_c = conformer_conv(gluc, lp)
    xc = xc + mod_c[2] * rms_norm(gated_merge(gatec, att_c, conv_c, rw_c, lp), lp['g_post_mix'])
    hc = modulated(xc, lp['g_pre_ffn'], mod_c[3], mod_c[4])
    xc = xc + mod_c[5] * rms_norm(conv_ffn(hc, lp), lp['g_post_ffn'])
    return x, xc


def setup_inputs(seed: int = 0) -> dict:
    key = jax.random.key(seed)
    ks = iter(jax.random.split(key, 48))

    def nrm(shape, scale):
        return scale * jax.random.normal(next(ks), shape, jnp.float32)

    def gain(shape):
        return 1.0 + nrm(shape, 0.02)

    D = D_MODEL
    Ld = DEPTH
    sd = D ** -0.5
    return {
        'x': nrm((BATCH, SEQ, D), 1.0),
        'c': nrm((BATCH, D), 1.0),
        'ctx': nrm((BATCH, CTX_LEN, D), 1.0),
        'c_ctx': nrm((D,), 1.0),
        'w_mod': nrm((Ld, D, N_MOD * D), sd),
        'b_mod': nrm((Ld, N_MOD * D), 0.01),
        'g_pre_mix': gain((Ld, D)),
        'g_post_mix': gain((Ld, D)),
        'g_pre_ffn': gain((Ld, D)),
        'g_post_ffn': gain((Ld, D)),
        'w_in': nrm((Ld, D, N_IN), sd),
        'q_norm': gain((Ld, HEAD_DIM)),
        'k_norm': gain((Ld, HEAD_DIM)),
        'w_attn_o': nrm((Ld, N_Q_HEADS * HEAD_DIM, D), (N_Q_HEADS * HEAD_DIM) ** -0.5),
        'conv_w': nrm((Ld, CONV_K, CONV_DIM), CONV_K ** -0.5),
        'conv_b': nrm((Ld, CONV_DIM), 0.01),
        'conv_ln_g': gain((Ld, CONV_DIM)),
        'conv_ln_b': nrm((Ld, CONV_DIM), 0.01),
        'w_conv_o': nrm((Ld, CONV_DIM, D), CONV_DIM ** -0.5),
        'shift_w': nrm((Ld, SHIFT_K, 3 * RWKV_DIM), 0.1) + jnp.array([0.0, 1.0, 0.0], jnp.float32)[None, :, None],
        'decay_w0': nrm((Ld, 2, RWKV_DIM), 0.5),
        'decay_up': nrm((Ld, 2, DECAY_LORA, RWKV_DIM), 0.1),
        'iclr_a0': nrm((Ld, 2, RWKV_DIM), 0.5),
        'iclr_up': nrm((Ld, 2, ICLR_LORA, RWKV_DIM), 0.1),
        'gate_up': nrm((Ld, GATE_LORA, RWKV_DIM), GATE_LORA ** -0.5),
        'k_k': 1.0 + nrm((Ld, RWKV_DIM), 0.1),
        'k_a': 1.0 + nrm((Ld, RWKV_DIM), 0.1),
        'r_k': nrm((Ld, RWKV_HEADS, RWKV_HEAD), 0.1),
        'wkv_gn_g': gain((Ld, RWKV_DIM)),
        'wkv_gn_b': nrm((Ld, RWKV_DIM), 0.01),
        'w_rwkv_o': nrm((Ld, RWKV_DIM, D), RWKV_DIM ** -0.5),
        'w_out': nrm((Ld, D, D), sd),
        'w_ffn_up': nrm((Ld, D, 2 * D_FF), sd),
        'ffn_conv_w': nrm((Ld, FFN_CONV_K, 2 * D_FF), FFN_CONV_K ** -0.5),
        'w_ffn_down': nrm((Ld, D_FF, D), D_FF ** -0.5),
    }


def reference(x, c, ctx, c_ctx, w_mod, b_mod, g_pre_mix, g_post_mix, g_pre_ffn, g_post_ffn, w_in, q_norm, k_norm,
              w_attn_o, conv_w, conv_b, conv_ln_g, conv_ln_b, w_conv_o, shift_w, decay_w0, decay_up, iclr_a0,
              iclr_up, gate_up, k_k, k_a, r_k, wkv_gn_g, wkv_gn_b, w_rwkv_o, w_out, w_ffn_up, ffn_conv_w, w_ffn_down):
    rows = x.shape[1] // GRID_W
    rope = axial_rope_tables(rows)
    xc = ctx
    for l in range(DEPTH):
        lp = {
            'w_mod': w_mod[l], 'b_mod': b_mod[l],
            'g_pre_mix': g_pre_mix[l], 'g_post_mix': g_post_mix[l],
            'g_pre_ffn': g_pre_ffn[l], 'g_post_ffn': g_post_ffn[l],
            'w_in': w_in[l], 'q_norm': q_norm[l], 'k_norm': k_norm[l], 'w_attn_o': w_attn_o[l],
            'conv_w': conv_w[l], 'conv_b': conv_b[l], 'conv_ln_g': conv_ln_g[l], 'conv_ln_b': conv_ln_b[l],
            'w_conv_o': w_conv_o[l], 'shift_w': shift_w[l], 'decay_w0': decay_w0[l], 'decay_up': decay_up[l],
            'iclr_a0': iclr_a0[l], 'iclr_up': iclr_up[l], 'gate_up': gate_up[l], 'k_k': k_k[l], 'k_a': k_a[l],
            'r_k': r_k[l], 'wkv_gn_g': wkv_gn_g[l], 'wkv_gn_b': wkv_gn_b[l], 'w_rwkv_o': w_rwkv_o[l],
            'w_out': w_out[l], 'w_ffn_up': w_ffn_up[l], 'ffn_conv_w': ffn_conv_w[l], 'w_ffn_down': w_ffn_down[l],
        }
        x, xc = trunk_layer(x, xc, c, c_ctx, rope, lp, l == DEPTH - 1)
    return x
```

```python
import math
import contextlib
import numpy as np
import concourse.bass as bass
import concourse.mybir as mybir
from concourse.bass_utils import run_bass_kernel_spmd

F32 = mybir.dt.float32
BF16 = mybir.dt.bfloat16
AF = mybir.ActivationFunctionType
ALU = mybir.AluOpType

D = 1024
SEQ = 4096
CTX = 256
T = SEQ + CTX
DEPTH = 2
NIN = 10112
DFF = 2816
DECAY_SCALE = math.exp(-0.5)
EPS = 1e-6
LN_EPS = 1e-5
GN_EPS = 64 * 1e-5
CH = 64
NCHUNK = T // CH

VCOL = {}
_o = 0
for _n, _w in (("b_mod", 48), ("g_pre_mix", 8), ("g_post_mix", 8), ("g_pre_ffn", 8), ("g_post_ffn", 8),
               ("q_norm", 1), ("k_norm", 1), ("conv_w", 248), ("conv_b", 8), ("conv_ln_g", 8), ("conv_ln_b", 8),
               ("shift_w", 72), ("iclr_a0", 16), ("k_k", 8), ("k_a", 8), ("r_k", 8), ("gn_g", 8), ("gn_b", 8),
               ("ffn_conv_w", 132)):
    VCOL[_n] = _o
    _o += _w
NV = _o
C_ONES, C_BLK, C_ID, C_ROT, C_TRIF, C_TRIB = 0, 128, 256, 384, 512, 768
NCONST = 1024
M_LOS, M_UPS, M_LOI, M_UPI, M_ID = 0, 256, 512, 768, 1024


class Buf:
    __slots__ = ("w", "r")

    def __init__(self):
        self.w = None
        self.r = {}


class Eng:
    def __init__(self, name, e, sem):
        self.name = name
        self.e = e
        self.sem = sem
        self.count = 0
        self.seen = {}


class FW:
    SAME_ENG_DIST = 2

    def __init__(self, nc, es, n_dma_sems=24):
        self.nc = nc
        self.sems = {}

        def mk(name, e):
            s = es.enter_context(nc.semaphore("sem_" + name))
            self.sems[name] = s
            return Eng(name, e, s)
        self.pe = mk("pe", nc.tensor)
        self.act = mk("act", nc.scalar)
        self.dve = mk("dve", nc.vector)
        self.pool = mk("pool", nc.gpsimd)
        self.sp = mk("sp", nc.sync)
        self.engs = [self.pe, self.act, self.dve, self.pool, self.sp]
        self.dma_sems = []
        for i in range(n_dma_sems):
            nm = "dq%d" % i
            self.sems[nm] = es.enter_context(nc.semaphore("sem_" + nm))
            self.dma_sems.append([nm, 0])
        self.dma_rr = 0
        self.bufs = {}
        self.n_ins = 0

    def buf(self, *key):
        b = self.bufs.get(key)
        if b is None:
            b = Buf()
            self.bufs[key] = b
        return b

    def _need(self, eng, tok, raw):
        if tok is None:
            return
        sk, v = tok
        if sk == eng.name:
            if not (raw and (eng.count - v) < self.SAME_ENG_DIST):
                return
        if eng.seen.get(sk, 0) >= v:
            return
        eng.e.wait_ge(self.sems[sk], v)
        eng.seen[sk] = v

    def _deps(self, eng, reads, writes):
        for b in reads:
            self._need(eng, b.w, True)
        for b in writes:
            self._need(eng, b.w, False)
            for sk, v in b.r.items():
                self._need(eng, (sk, v), False)

    def _mark(self, tok, reads, writes):
        for b in reads:
            if b.r.get(tok[0], 0) < tok[1]:
                b.r[tok[0]] = tok[1]
        for b in writes:
            b.w = tok
            b.r = {}

    def op(self, eng, fn, reads=(), writes=(), inc=True):
        self._deps(eng, reads, writes)
        ins = fn(eng.e)
        self.n_ins += 1
        if inc:
            eng.count += 1
            ins.then_inc(eng.sem, 1)
            tok = (eng.name, eng.count)
        else:
            tok = (eng.name, eng.count + 1)
        self._mark(tok, reads, writes)
        return ins

    def dma(self, eng, out, in_, reads=(), writes=()):
        slot = self.dma_sems[self.dma_rr]
        self.dma_rr = (self.dma_rr + 1) % len(self.dma_sems)
        nm, cnt = slot
        self._deps(eng, reads, writes)
        self._need(eng, (nm, cnt), False)
        ins = eng.e.dma_start(out=out, in_=in_)
        slot[1] = cnt + 16
        ins.then_inc(self.sems[nm], 16)
        self.n_ins += 1
        self._mark((nm, cnt + 16), reads, writes)
        return ins

    def barrier(self):
        for e in self.engs:
            for o in self.engs:
                if o is not e and o.count > 0:
                    self._need(e, (o.name, o.count), False)
            for nm, cnt in self.dma_sems:
                if cnt > 0:
                    self._need(e, (nm, cnt), False)


class Prog:
    def __init__(self, debug=None, nlayers=DEPTH, stop_after=None):
        self.debug = debug or []
        self.nlayers = nlayers
        self.stop_after = stop_after
        self.nc = bass.Bass("TRN2", target_bir_lowering=False)
        self.uid = 0

    def din(self, name, shape, dt=F32):
        return self.nc.dram_tensor(name, list(shape), dt, kind="ExternalInput").ap()

    def dscr(self, name, shape, dt=F32):
        kind = "ExternalOutput" if name in self.debug else "Internal"
        return self.nc.dram_tensor(name, list(shape), dt, kind=kind).ap()

    def sbuf(self, name, shape, dt):
        self.uid += 1
        return self.nc.sbuf_tensor("%s_u%d" % (name, self.uid), shape, dt)

    def fm(self, ap):
        return ap.rearrange("(c p) t -> p c t", p=128)

    def build(self):
        nc = self.nc
        L = self.nlayers
        I = {}
        I["xs0"] = self.din("xs0", [D, T])
        I["cc"] = self.din("cc", [128, 16])
        I["vecs"] = self.din("vecs", [DEPTH, 128, NV])
        I["w0row"] = self.din("w0row", [DEPTH, 1, 2048])
        I["constf"] = self.din("constf", [128, NCONST])
        I["maskf"] = self.din("maskf", [128, 5 * 256])
        I["cos"] = self.din("cos", [128, SEQ])
        I["sin"] = self.din("sin", [128, SEQ])
        I["w_mod"] = self.din("w_mod", [DEPTH, D, 6 * D])
        I["w_in"] = self.din("w_in", [DEPTH, D, NIN])
        for n in ("w_attn_o", "w_conv_o", "w_rwkv_o", "w_out"):
            I[n] = self.din(n, [DEPTH, D, D])
        I["decay_up"] = self.din("decay_up", [DEPTH, 128, D])
        I["iclr_up"] = self.din("iclr_up", [DEPTH, 128, D])
        I["gate_up"] = self.din("gate_up", [DEPTH, 128, D])
        I["w_ffn_up"] = self.din("w_ffn_up", [DEPTH, D, 2 * DFF])
        I["w_ffn_down"] = self.din("w_ffn_down", [DEPTH, DFF, D])
        self.I = I
        out = nc.dram_tensor("out", [D, SEQ], F32, kind="ExternalOutput").ap()
        S = {}
        S["proj"] = self.dscr("proj", [NIN, T])
        S["att"] = self.dscr("att", [D, T], BF16)
        S["cnv"] = self.dscr("cnv", [D, T], BF16)
        S["rwo"] = self.dscr("rwo", [D, T], BF16)
        S["ffa"] = self.dscr("ffa", [DFF, T], BF16)
        for n in ("rw_r", "rw_v", "rw_k01", "yf", "yb"):
            S[n] = self.dscr(n, [D, T])
        for d in range(2):
            for n in ("At", "Bt", "Kt", "Rt"):
                S["%s%d" % (n, d)] = self.dscr("%s%d" % (n, d), [D, T])
            S["gC%d" % d] = self.dscr("gC%d" % d, [D, NCHUNK])
        S["xsm0"] = self.dscr("xsm0", [D, T])
        S["xs1"] = self.dscr("xs1", [D, T])
        S["xsm1"] = self.dscr("xsm1", [D, T])
        self.S = S

        with contextlib.ExitStack() as es:
            self.es = es
            f = FW(nc, es)
            self.f = f
            self.constf = es.enter_context(self.sbuf("constf", [128, NCONST], F32))
            self.maskf = es.enter_context(self.sbuf("maskf", [128, 5 * 256], F32))
            self.vecs = es.enter_context(self.sbuf("vecs", [128, DEPTH, NV], F32))
            self.modS = es.enter_context(self.sbuf("modS", [128, 48, 2], F32))
            self.dsc = es.enter_context(self.sbuf("dsc", [128, 64, 2], F32))
            self.onesb = es.enter_context(self.sbuf("onesb", [128, 128], BF16))
            self.ps = [es.enter_context(nc.psum_tensor("ps%d" % i, [128, 512], F32)) for i in range(8)]
            self.psb = [f.buf("ps", i) for i in range(8)]
            B = f.buf
            f.dma(f.sp, self.constf[:], I["constf"][:, :], writes=[B("constf")])
            f.dma(f.sp, self.maskf[:], I["maskf"][:, :], writes=[B("maskf")])
            for l in range(DEPTH):
                f.dma(f.sp, self.vecs[:, l, :], I["vecs"][l, :, :], writes=[B("vecs")])
            f.op(f.dve, lambda e: e.tensor_copy(out=self.onesb[:], in_=self.constf[:, C_ONES:C_ONES + 128]),
                 reads=[B("constf")], writes=[B("onesb")])
            f.barrier()
            xs_in = I["xs0"]
            for l in range(L):
                last = (l == DEPTH - 1)
                xsm = S["xsm%d" % l]
                xs_out = out if last else S["xs1"]
                self.phase_mod(l)
                if self.stop_after == ("mod", l): break
                self.phase_inproj(l, xs_in, last)
                if self.stop_after == ("inproj", l): break
                self.phase_attn(l, last)
                if self.stop_after == ("attn", l): break
                self.phase_conv(l, last)
                if self.stop_after == ("conv", l): break
                self.phase_rwkv_prep(l)
                if self.stop_after == ("rwprep", l): break
                self.phase_rwkv_scan(l, last)
                if self.stop_after == ("rwscan", l): break
                self.phase_rwkv_out(l, last)
                if self.stop_after == ("rwout", l): break
                self.phase_merge(l, xs_in, xsm, last)
                if self.stop_after == ("merge", l): break
                self.phase_ffn_up(l, xsm, last)
                if self.stop_after == ("ffnup", l): break
                self.phase_ffn_down(l, xsm, xs_out, last)
                xs_in = xs_out
            f.barrier()
        return nc

    def V(self, l, name, c=0, n=1):
        o = VCOL[name] + c
        return self.vecs[:, l, o:o + n]

    def seqs(self, last):
        return [(CTX, SEQ)] if last else [(0, CTX), (CTX, SEQ)]

    def tiles(self, seqs, n=512):
        r = []
        for s0, sl in seqs:
            t = s0
            while t < s0 + sl:
                m = min(n, s0 + sl - t)
                r.append((t, m, s0, sl))
                t += m
        return r

    def rstd_from_ps(self, ps_ap, psbuf, out_ap, outbuf, scale, eps):
        f = self.f
        f.op(f.act, lambda e: e.activation(out=out_ap, in_=ps_ap, func=AF.Ln, bias=float(eps), scale=float(scale)),
             reads=[psbuf], writes=[outbuf])
        f.op(f.act, lambda e: e.activation(out=out_ap, in_=out_ap, func=AF.Exp, scale=-0.5),
             reads=[outbuf], writes=[outbuf])

    def phase_mod(self, l):
        nc, f, es0 = self.nc, self.f, self.es
        B = f.buf
        I = self.I
        with contextlib.ExitStack() as es:
            cT = es.enter_context(self.sbuf("cT", [128, 8, 2], F32))
            wt = [es.enter_context(self.sbuf("wmod%d" % i, [128, 8, 1024], F32)) for i in range(2)]
            f.dma(f.sp, cT[:], I["cc"].rearrange("p (k j) -> p k j", j=2), writes=[B("cT")])
            f.op(f.act, lambda e: e.activation(out=cT[:], in_=cT[:], func=AF.Silu), reads=[B("cT")], writes=[B("cT")])
            wv = I["w_mod"][l].rearrange("(k p) n -> p k n", p=128)
            ps = self.ps[0]
            for g in range(6):
                w = wt[g % 2]
                for k in range(8):
                    f.dma(f.sp, w[:, k, :], wv[:, k, g * 1024:(g + 1) * 1024], writes=[B("wmod", g % 2, k)])
                for oc in range(8):
                    col = (g * 8 + oc) * 2
                    for k in range(8):
                        f.op(f.pe, lambda e: e.matmul(ps[:, col:col + 2], lhsT=w[:, k, oc * 128:(oc + 1) * 128],
                                                      rhs=cT[:, k, :], start=(k == 0), stop=(k == 7)),
                             reads=[B("wmod", g % 2, k), B("cT")], writes=[self.psb[0]], inc=(k == 7))
            psv = ps[:, 0:96].rearrange("p (c j) -> p c j", j=2)
            for j in range(2):
                f.op(f.dve, lambda e: e.tensor_tensor(out=self.modS[:, :, j], in0=psv[:, :, j],
                                                      in1=self.V(l, "b_mod", 0, 48), op=ALU.add),
                     reads=[self.psb[0], B("vecs")], writes=[B("modS")])
            for j in range(2):
                def m(i):
                    return self.modS[:, i * 8:(i + 1) * 8, j]
                rd = [B("modS"), B("vecs")]
                wr = [B("dsc")]
                f.op(f.dve, lambda e: e.scalar_tensor_tensor(out=self.dsc[:, 0:8, j], in0=m(1), scalar=1.0,
                                                             in1=self.V(l, "g_pre_mix", 0, 8), op0=ALU.add, op1=ALU.mult),
                     reads=rd, writes=wr)
                f.op(f.dve, lambda e: e.tensor_copy(out=self.dsc[:, 8:16, j], in_=m(0)), reads=rd, writes=wr)
                f.op(f.dve, lambda e: e.tensor_tensor(out=self.dsc[:, 16:24, j], in0=m(2),
                                                      in1=self.V(l, "g_post_mix", 0, 8), op=ALU.mult), reads=rd, writes=wr)
                f.op(f.dve, lambda e: e.scalar_tensor_tensor(out=self.dsc[:, 24:32, j], in0=m(4), scalar=1.0,
                                                             in1=self.V(l, "g_pre_ffn", 0, 8), op0=ALU.add, op1=ALU.mult),
                     reads=rd, writes=wr)
                f.op(f.dve, lambda e: e.tensor_copy(out=self.dsc[:, 32:40, j], in_=m(3)), reads=rd, writes=wr)
                f.op(f.dve, lambda e: e.tensor_tensor(out=self.dsc[:, 40:48, j], in0=m(5),
                                                      in1=self.V(l, "g_post_ffn", 0, 8), op=ALU.mult), reads=rd, writes=wr)
            f.barrier()

    def prenorm(self, es, src, seqs, base, hT, hcol_of):
        nc, f = self.nc, self.f
        B = f.buf
        xt = [es.enter_context(self.sbuf("pn_x%d" % i, [128, 8, 512], F32)) for i in range(2)]
        sq = es.enter_context(self.sbuf("pn_sq", [128, 8, 512], F32))
        rs = es.enter_context(self.sbuf("pn_rs", [128, 512], F32))
        srcv = self.fm(src)
        for i, (t0, n, s0, sl) in enumerate(self.tiles(seqs)):
            j = 1 if s0 == 0 else 0
            x = xt[i % 2]
            xb = B("pn_x", i % 2)
            f.dma(f.sp, x[:, :, 0:n], srcv[:, :, t0:t0 + n], writes=[xb])
            f.op(f.act, lambda e: e.activation(out=sq[:, :, 0:n], in_=x[:, :, 0:n], func=AF.Square),
                 reads=[xb], writes=[B("pn_sq")])
            ps, pb = self.ps[7], self.psb[7]
            for k in range(8):
                f.op(f.pe, lambda e: e.matmul(ps[:, 0:n], lhsT=self.constf[:, C_ONES:C_ONES + 128], rhs=sq[:, k, 0:n],
                                              start=(k == 0), stop=(k == 7)),
                     reads=[B("pn_sq"), B("constf")], writes=[pb], inc=(k == 7))
            self.rstd_from_ps(ps[:, 0:n], pb, rs[:, 0:n], B("pn_rs"), 1.0 / D, EPS)
            c0 = hcol_of(t0)
            for k in range(8):
                f.op(f.dve, lambda e: e.scalar_tensor_tensor(out=x[:, k, 0:n], in0=x[:, k, 0:n],
                                                             scalar=self.dsc[:, base + k, j:j + 1], in1=rs[:, 0:n],
                                                             op0=ALU.mult, op1=ALU.mult),
                     reads=[xb, B("pn_rs"), B("dsc")], writes=[xb])
                f.op(f.act, lambda e: e.activation(out=hT[:, k, c0:c0 + n], in_=x[:, k, 0:n], func=AF.Identity,
                                                   bias=self.dsc[:, base + 8 + k, j:j + 1], scale=1.0),
                     reads=[xb, B("dsc")], writes=[B("hT")])

    def phase_inproj(self, l, xs_in, last):
        nc, f = self.nc, self.f
        B = f.buf
        I, S = self.I, self.S
        seqs = [(0, CTX), (CTX, SEQ)]
        with contextlib.ExitStack() as es:
            hT = es.enter_context(self.sbuf("hT", [128, 8, T], BF16))
            with contextlib.ExitStack() as es2:
                self.prenorm(es2, xs_in, seqs, 0, hT, lambda t: t)
                f.barrier()
            wt = [es.enter_context(self.sbuf("win%d" % i, [128, 8, 1024], BF16)) for i in range(2)]
            st = [es.enter_context(self.sbuf("ipst%d" % i, [128, 512], F32)) for i in range(4)]
            wv = I["w_in"][l].rearrange("(k p) n -> p k n", p=128)
            pv = self.fm(S["proj"])
            ngrp = (NIN + 1023) // 1024
            cnt = 0
            for g in range(ngrp):
                ncol = min(1024, NIN - g * 1024)
                w = wt[g % 2]
                for k in range(8):
                    f.dma(f.pool, w[:, k, 0:ncol], wv[:, k, g * 1024:g * 1024 + ncol], writes=[B("win", g % 2, k)])
                for (t0, n, s0, sl) in self.tiles(seqs):
                    for oc in range(ncol // 128):
                        pi = cnt % 6
                        ps, pb = self.ps[pi], self.psb[pi]
                        for k in range(8):
                            f.op(f.pe, lambda e: e.matmul(ps[:, 0:n], lhsT=w[:, k, oc * 128:(oc + 1) * 128],
                                                          rhs=hT[:, k, t0:t0 + n], start=(k == 0), stop=(k == 7)),
                                 reads=[B("win", g % 2, k), B("hT")], writes=[pb], inc=(k == 7))
                        s = st[cnt % 4]
                        sb = B("ipst", cnt % 4)
                        if cnt % 2 == 0:
                            f.op(f.act, lambda e: e.copy(out=s[:, 0:n], in_=ps[:, 0:n]), reads=[pb], writes=[sb])
                        else:
                            f.op(f.dve, lambda e: e.tensor_copy(out=s[:, 0:n], in_=ps[:, 0:n]), reads=[pb], writes=[sb])
                        f.dma(f.sp, pv[:, g * 8 + oc, t0:t0 + n], s[:, 0:n], reads=[sb])
                        cnt += 1
            f.barrier()


    def phase_attn(self, l, last):
        nc, f = self.nc, self.f
        B = f.buf
        I, S = self.I, self.S
        pv = self.fm(S["proj"])
        av = self.fm(S["att"])
        cf = self.constf
        with contextlib.ExitStack() as es:
            sb = lambda n, s, d: es.enter_context(self.sbuf(n, s, d))
            kT = sb("kT", [128, 2, T], BF16)
            Vt = sb("Vt", [128, T // 128, 2, 128], BF16)
            cos = sb("cos", [128, SEQ], F32)
            sin = sb("sin", [128, SEQ], F32)
            qg = sb("qg", [128, 1], F32)
            raw = [sb("a_raw%d" % i, [128, 512], F32) for i in range(2)]
            sq = sb("a_sq", [128, 512], F32)
            rs = sb("a_rs", [128, 512], F32)
            kn = sb("a_kn", [128, 512], F32)
            t1 = sb("a_t1", [128, 512], F32)
            t2 = sb("a_t2", [128, 512], F32)
            qT = [sb("a_qT%d" % i, [128, 512], BF16) for i in range(2)]
            pT = [sb("a_pT%d" % i, [128, 512], BF16) for i in range(3)]
            rinv = sb("a_rinv", [128, 512], F32)
            racc = [[sb("a_racc%d%d" % (i, j), [128, 512], F32) for j in range(2)] for i in range(2)]
            ost = [sb("a_ost%d" % i, [128, 512], BF16) for i in range(2)]
            f.dma(f.sp, cos[:], I["cos"][:, :], writes=[B("cos")])
            f.dma(f.sp, sin[:], I["sin"][:, :], writes=[B("sin")])
            f.op(f.dve, lambda e: e.tensor_scalar(out=qg[:], in0=self.V(l, "q_norm"), scalar1=float(128 ** -0.5),
                                                  scalar2=None, op0=ALU.mult), reads=[B("vecs")], writes=[B("qg")])
            self._nr = 0

            def normrope(chunk, t0, n, gain, is_x, out_ap, outbuf):
                i = self._nr
                self._nr += 1
                r = raw[i % 2]
                rb = B("a_raw", i % 2)
                f.dma(f.sp, r[:, 0:n], pv[:, chunk, t0:t0 + n], writes=[rb])
                f.op(f.act, lambda e: e.activation(out=sq[:, 0:n], in_=r[:, 0:n], func=AF.Square),
                     reads=[rb], writes=[B("a_sq")])
                f.op(f.pe, lambda e: e.matmul(self.ps[7][:, 0:n], lhsT=cf[:, C_ONES:C_ONES + 128], rhs=sq[:, 0:n],
                                              start=True, stop=True), reads=[B("a_sq"), B("constf")], writes=[self.psb[7]])
                self.rstd_from_ps(self.ps[7][:, 0:n], self.psb[7], rs[:, 0:n], B("a_rs"), 1.0 / 128, EPS)
                f.op(f.dve, lambda e: e.scalar_tensor_tensor(out=kn[:, 0:n], in0=r[:, 0:n], scalar=gain, in1=rs[:, 0:n],
                                                             op0=ALU.mult, op1=ALU.mult),
                     reads=[rb, B("a_rs"), B("vecs"), B("qg")], writes=[B("a_kn")])
                if is_x:
                    p0 = t0 - CTX
                    f.op(f.pe, lambda e: e.matmul(self.ps[6][:, 0:n], lhsT=cf[:, C_ROT:C_ROT + 128], rhs=kn[:, 0:n],
                                                  start=True, stop=True), reads=[B("a_kn"), B("constf")], writes=[self.psb[6]])
                    f.op(f.pool, lambda e: e.tensor_tensor(out=t1[:, 0:n], in0=kn[:, 0:n], in1=cos[:, p0:p0 + n], op=ALU.mult),
                         reads=[B("a_kn"), B("cos")], writes=[B("a_t1")])
                    f.op(f.dve, lambda e: e.tensor_tensor(out=t2[:, 0:n], in0=self.ps[6][:, 0:n], in1=sin[:, p0:p0 + n], op=ALU.mult),
                         reads=[self.psb[6], B("sin")], writes=[B("a_t2")])
                    f.op(f.pool, lambda e: e.tensor_tensor(out=out_ap, in0=t1[:, 0:n], in1=t2[:, 0:n], op=ALU.add),
                         reads=[B("a_t1"), B("a_t2")], writes=[outbuf])
                else:
                    f.op(f.pool, lambda e: e.tensor_copy(out=out_ap, in_=kn[:, 0:n]), reads=[B("a_kn")], writes=[outbuf])

            allseq = [(0, CTX), (CTX, SEQ)]
            for kvh in range(2):
                for (t0, n, s0, sl) in self.tiles(allseq):
                    normrope(8 + kvh, t0, n, self.V(l, "k_norm"), s0 != 0, kT[:, kvh, t0:t0 + n], B("kT"))
            i = 0
            for kvh in range(2):
                for (t0, n, s0, sl) in self.tiles(allseq):
                    r = raw[i % 2]
                    rb = B("a_raw", i % 2)
                    i += 1
                    f.dma(f.sp, r[:, 0:n], pv[:, 10 + kvh, t0:t0 + n], writes=[rb])
                    nb = n // 128
                    for j in range(nb):
                        f.op(f.pe, lambda e: e.transpose(self.ps[7][:, j * 128:(j + 1) * 128], r[:, j * 128:(j + 1) * 128],
                                                         cf[:, C_ID:C_ID + 128]),
                             reads=[rb, B("constf")], writes=[self.psb[7]], inc=(j == nb - 1))
                    b0 = t0 // 128
                    f.op(f.dve, lambda e: e.tensor_copy(out=Vt[:, b0:b0 + nb, kvh, :],
                                                        in_=self.ps[7][:, 0:nb * 128].rearrange("p (b d) -> p b d", d=128)),
                         reads=[self.psb[7]], writes=[B("Vt")])
            qi = 0
            for h in range(8):
                kvh = h // 4
                for (t0, n, s0, sl) in self.tiles(self.seqs(last)):
                    is_x = s0 != 0
                    q = qT[qi % 2]
                    qb = B("a_qT", qi % 2)
                    normrope(h, t0, n, qg[:, 0:1], is_x, q[:, 0:n], qb)
                    nblk = (T // 128) if is_x else (CTX // 128)
                    po, pob = self.ps[2 + qi % 2], self.psb[2 + qi % 2]
                    pr, prb = self.ps[4 + qi % 2], self.psb[4 + qi % 2]

                    def smm(jb):
                        f.op(f.pe, lambda e: e.matmul(self.ps[jb % 2][:, 0:n], lhsT=kT[:, kvh, jb * 128:(jb + 1) * 128],
                                                      rhs=q[:, 0:n], start=True, stop=True),
                             reads=[B("kT"), qb], writes=[self.psb[jb % 2]])
                    smm(0)
                    for jb in range(nblk):
                        p = pT[jb % 3]
                        pb = B("a_pT", jb % 3)
                        f.op(f.act, lambda e: e.activation(out=p[:, 0:n], in_=self.ps[jb % 2][:, 0:n], func=AF.Exp),
                             reads=[self.psb[jb % 2]], writes=[pb])
                        if jb + 1 < nblk:
                            smm(jb + 1)
                        lastb = (jb == nblk - 1)
                        f.op(f.pe, lambda e: e.matmul(po[:, 0:n], lhsT=Vt[:, jb, kvh, :], rhs=p[:, 0:n],
                                                      start=(jb == 0), stop=lastb),
                             reads=[B("Vt"), pb], writes=[pob], inc=lastb)
                        ae = f.pool if jb % 2 == 0 else f.dve
                        ac = racc[qi % 2][jb % 2]
                        acb = B("a_racc", qi % 2, jb % 2)
                        if jb < 2:
                            f.op(ae, lambda e: e.tensor_copy(out=ac[:, 0:n], in_=p[:, 0:n]), reads=[pb], writes=[acb])
                        else:
                            f.op(ae, lambda e: e.tensor_tensor(out=ac[:, 0:n], in0=ac[:, 0:n], in1=p[:, 0:n], op=ALU.add),
                                 reads=[pb, acb], writes=[acb])
                    for j2 in range(2):
                        f.op(f.pe, lambda e: e.matmul(pr[:, 0:n], lhsT=cf[:, C_ONES:C_ONES + 128], rhs=racc[qi % 2][j2][:, 0:n],
                                                      start=(j2 == 0), stop=(j2 == 1)),
                             reads=[B("constf"), B("a_racc", qi % 2, j2)], writes=[prb], inc=(j2 == 1))
                    f.op(f.dve, lambda e: e.reciprocal(out=rinv[:, 0:n], in_=pr[:, 0:n]), reads=[prb], writes=[B("a_rinv")])
                    o = ost[qi % 2]
                    ob = B("a_ost", qi % 2)
                    f.op(f.dve, lambda e: e.tensor_tensor(out=o[:, 0:n], in0=po[:, 0:n], in1=rinv[:, 0:n], op=ALU.mult),
                         reads=[pob, B("a_rinv")], writes=[ob])
                    f.dma(f.sp, av[:, h, t0:t0 + n], o[:, 0:n], reads=[ob])
                    qi += 1
            f.barrier()

    def phase_conv(self, l, last):
        nc, f = self.nc, self.f
        B = f.buf
        S = self.S
        pv = self.fm(S["proj"])
        cv = self.fm(S["cnv"])
        cf = self.constf
        with contextlib.ExitStack() as es:
            sb = lambda n, s, d: es.enter_context(self.sbuf(n, s, d))
            at = [sb("c_a%d" % i, [128, 544], F32) for i in range(2)]
            bt = [sb("c_b%d" % i, [128, 544], F32) for i in range(2)]
            y = sb("c_y", [128, 8, 512], F32)
            sq = [sb("c_sq%d" % i, [128, 512], F32) for i in range(2)]
            mean = sb("c_mean", [128, 512], F32)
            msq = sb("c_msq", [128, 512], F32)
            rstd = sb("c_rstd", [128, 512], F32)
            tt = [sb("c_t%d" % i, [128, 512], F32) for i in range(2)]
            ost = [sb("c_o%d" % i, [128, 512], BF16) for i in range(2)]
            it = 0
            for (t0, n, s0, sl) in self.tiles(self.seqs(last)):
                lo = max(t0 - 15, s0)
                hi = min(t0 + n + 15, s0 + sl)
                off = lo - (t0 - 15)
                edge = (lo != t0 - 15) or (hi != t0 + n + 15)
                for c in range(8):
                    a = at[it % 2]
                    b = bt[it % 2]
                    ab = B("c_a", it % 2)
                    bb = B("c_b", it % 2)
                    it += 1
                    if edge:
                        f.op(f.pool, lambda e: e.memset(a[:, 0:n + 30], 0.0), writes=[ab])
                        f.op(f.pool, lambda e: e.memset(b[:, 0:n + 30], 0.0), writes=[bb])
                    f.dma(f.sp, a[:, off:off + hi - lo], pv[:, 12 + c, lo:hi], writes=[ab])
                    f.dma(f.sp, b[:, off:off + hi - lo], pv[:, 20 + c, lo:hi], writes=[bb])
                    f.op(f.act, lambda e: e.activation(out=b[:, 0:n + 30], in_=b[:, 0:n + 30], func=AF.Sigmoid),
                         reads=[bb], writes=[bb])
                    f.op(f.pool, lambda e: e.tensor_tensor(out=a[:, 0:n + 30], in0=a[:, 0:n + 30], in1=b[:, 0:n + 30], op=ALU.mult),
                         reads=[ab, bb], writes=[ab])
                    yb = B("c_y", c)
                    w = lambda j: self.V(l, "conv_w", c * 31 + j)
                    f.op(f.dve, lambda e: e.tensor_scalar(out=y[:, c, 0:n], in0=a[:, 0:n], scalar1=w(0),
                                                          scalar2=self.V(l, "conv_b", c), op0=ALU.mult, op1=ALU.add),
                         reads=[ab, B("vecs")], writes=[yb])
                    for j in range(1, 31):
                        f.op(f.dve, lambda e: e.scalar_tensor_tensor(out=y[:, c, 0:n], in0=a[:, j:j + n], scalar=w(j),
                                                                     in1=y[:, c, 0:n], op0=ALU.mult, op1=ALU.add),
                             reads=[ab, yb], writes=[yb])
                    s = sq[c % 2]
                    sqb = B("c_sq", c % 2)
                    f.op(f.act, lambda e: e.activation(out=s[:, 0:n], in_=y[:, c, 0:n], func=AF.Square), reads=[yb], writes=[sqb])
                    f.op(f.pe, lambda e: e.matmul(self.ps[0][:, 0:n], lhsT=cf[:, C_ONES:C_ONES + 128], rhs=y[:, c, 0:n],
                                                  start=(c == 0), stop=(c == 7)), reads=[yb, B("constf")], writes=[self.psb[0]], inc=False)
                    f.op(f.pe, lambda e: e.matmul(self.ps[1][:, 0:n], lhsT=cf[:, C_ONES:C_ONES + 128], rhs=s[:, 0:n],
                                                  start=(c == 0), stop=(c == 7)), reads=[sqb, B("constf")], writes=[self.psb[1]])
                f.op(f.act, lambda e: e.activation(out=mean[:, 0:n], in_=self.ps[0][:, 0:n], func=AF.Copy, scale=1.0 / D),
                     reads=[self.psb[0]], writes=[B("c_mean")])
                f.op(f.dve, lambda e: e.tensor_tensor(out=msq[:, 0:n], in0=mean[:, 0:n], in1=mean[:, 0:n], op=ALU.mult),
                     reads=[B("c_mean")], writes=[B("c_msq")])
                f.op(f.dve, lambda e: e.scalar_tensor_tensor(out=rstd[:, 0:n], in0=self.ps[1][:, 0:n], scalar=1.0 / D,
                                                             in1=msq[:, 0:n], op0=ALU.mult, op1=ALU.subtract),
                     reads=[self.psb[1], B("c_msq")], writes=[B("c_rstd")])
                self.rstd_from_ps(rstd[:, 0:n], B("c_rstd"), rstd[:, 0:n], B("c_rstd"), 1.0, LN_EPS)
                for c in range(8):
                    t = tt[c % 2]
                    tb = B("c_t", c % 2)
                    f.op(f.dve, lambda e: e.tensor_tensor(out=t[:, 0:n], in0=y[:, c, 0:n], in1=mean[:, 0:n], op=ALU.subtract),
                         reads=[B("c_y", c), B("c_mean")], writes=[tb])
                    f.op(f.pool, lambda e: e.tensor_tensor(out=t[:, 0:n], in0=t[:, 0:n], in1=rstd[:, 0:n], op=ALU.mult),
                         reads=[tb, B("c_rstd")], writes=[tb])
                    o = ost[c % 2]
                    ob = B("c_o", c % 2)
                    f.op(f.act, lambda e: e.activation(out=o[:, 0:n], in_=t[:, 0:n], func=AF.Silu,
                                                       bias=self.V(l, "conv_ln_b", c), scale=self.V(l, "conv_ln_g", c)),
                         reads=[tb, B("vecs")], writes=[ob])
                    f.dma(f.sp, cv[:, c, t0:t0 + n], o[:, 0:n], reads=[ob])
            f.barrier()

    def post_residual(self, es, mo, mob, xt, xtb, base, j, dstv, c0, n, sq, rs):
        f = self.f
        B = f.buf
        cf = self.constf
        for k in range(8):
            s = sq[k % 2]
            sqb = B("pr_sq", k % 2)
            f.op(f.act, lambda e: e.activation(out=s[:, 0:n], in_=mo[:, k, 0:n], func=AF.Square), reads=[mob], writes=[sqb])
            f.op(f.pe, lambda e: e.matmul(self.ps[7][:, 0:n], lhsT=cf[:, C_ONES:C_ONES + 128], rhs=s[:, 0:n],
                                          start=(k == 0), stop=(k == 7)), reads=[sqb, B("constf")], writes=[self.psb[7]])
        self.rstd_from_ps(self.ps[7][:, 0:n], self.psb[7], rs[:, 0:n], B("pr_rs"), 1.0 / D, EPS)
        for k in range(8):
            f.op(f.pool, lambda e: e.tensor_tensor(out=mo[:, k, 0:n], in0=mo[:, k, 0:n], in1=rs[:, 0:n], op=ALU.mult),
                 reads=[mob, B("pr_rs")], writes=[mob])
            f.op(f.dve, lambda e: e.scalar_tensor_tensor(out=xt[:, k, 0:n], in0=mo[:, k, 0:n],
                                                         scalar=self.dsc[:, base + k, j:j + 1], in1=xt[:, k, 0:n],
                                                         op0=ALU.mult, op1=ALU.add),
                 reads=[mob, xtb, B("dsc")], writes=[xtb])
        f.dma(f.sp, dstv[:, :, c0:c0 + n], xt[:, :, 0:n], reads=[xtb])

    def phase_merge(self, l, xs_in, xsm, last):
        nc, f = self.nc, self.f
        B = f.buf
        I, S = self.I, self.S
        pv = self.fm(S["proj"])
        with contextlib.ExitStack() as es:
            sb = lambda n, s, d: es.enter_context(self.sbuf(n, s, d))
            W = [sb("m_w%d" % i, [128, 8, 1024], BF16) for i in range(4)]
            for i, nm in enumerate(("w_attn_o", "w_conv_o", "w_rwkv_o", "w_out")):
                wv = I[nm][l].rearrange("(k p) n -> p k n", p=128)
                for k in range(8):
                    f.dma(f.pool, W[i][:, k, :], wv[:, k, :], writes=[B("m_w", i, k)])
            br = [sb("m_br%d" % i, [128, 8, 512], BF16) for i in range(3)]
            gt = [sb("m_g%d" % i, [128, 512], F32) for i in range(6)]
            tt = [sb("m_t%d" % i, [128, 512], F32) for i in range(6)]
            mT = sb("m_mT", [128, 8, 512], BF16)
            mo = sb("m_mo", [128, 8, 512], F32)
            xt = sb("m_xt", [128, 8, 512], F32)
            sq = [sb("m_sq%d" % i, [128, 512], F32) for i in range(2)]
            rs = sb("m_rs", [128, 512], F32)
            srcs = [self.fm(S["att"]), self.fm(S["cnv"]), self.fm(S["rwo"])]
            xv = self.fm(xs_in)
            dv = self.fm(xsm)
            for (t0, n, s0, sl) in self.tiles(self.seqs(last)):
                j = 1 if s0 == 0 else 0
                for b in range(3):
                    f.dma(f.sp, br[b][:, :, 0:n], srcs[b][:, :, t0:t0 + n], writes=[B("m_br", b)])
                f.dma(f.sp, xt[:, :, 0:n], xv[:, :, t0:t0 + n], writes=[B("m_xt")])
                for oc in range(8):
                    par = oc % 2
                    for b in range(3):
                        pi = b + 3 * par
                        for k in range(8):
                            f.op(f.pe, lambda e: e.matmul(self.ps[pi][:, 0:n], lhsT=W[b][:, k, oc * 128:(oc + 1) * 128],
                                                          rhs=br[b][:, k, 0:n], start=(k == 0), stop=(k == 7)),
                                 reads=[B("m_w", b, k), B("m_br", b)], writes=[self.psb[pi]], inc=(k == 7))
                    for b in range(3):
                        pi = b + 3 * par
                        g = gt[pi]
                        gb = B("m_g", pi)
                        f.dma(f.sp, g[:, 0:n], pv[:, 55 + 8 * b + oc, t0:t0 + n], writes=[gb])
                        f.op(f.act, lambda e: e.activation(out=g[:, 0:n], in_=g[:, 0:n], func=AF.Sigmoid), reads=[gb], writes=[gb])
                        f.op(f.dve, lambda e: e.tensor_tensor(out=tt[pi][:, 0:n], in0=self.ps[pi][:, 0:n], in1=g[:, 0:n], op=ALU.mult),
                             reads=[self.psb[pi], gb], writes=[B("m_t", pi)])
                    p0 = 3 * par
                    f.op(f.pool, lambda e: e.tensor_tensor(out=tt[p0][:, 0:n], in0=tt[p0][:, 0:n], in1=tt[p0 + 1][:, 0:n], op=ALU.add),
                         reads=[B("m_t", p0), B("m_t", p0 + 1)], writes=[B("m_t", p0)])
                    f.op(f.pool, lambda e: e.tensor_tensor(out=mT[:, oc, 0:n], in0=tt[p0][:, 0:n], in1=tt[p0 + 2][:, 0:n], op=ALU.add),
                         reads=[B("m_t", p0), B("m_t", p0 + 2)], writes=[B("m_mT")])
                for oc in range(8):
                    pi = 6
                    for k in range(8):
                        f.op(f.pe, lambda e: e.matmul(self.ps[pi][:, 0:n], lhsT=W[3][:, k, oc * 128:(oc + 1) * 128],
                                                      rhs=mT[:, k, 0:n], start=(k == 0), stop=(k == 7)),
                             reads=[B("m_w", 3, k), B("m_mT")], writes=[self.psb[pi]], inc=(k == 7))
                    f.op(f.act, lambda e: e.copy(out=mo[:, oc, 0:n], in_=self.ps[pi][:, 0:n]), reads=[self.psb[pi]], writes=[B("m_mo")])
                self.post_residual(es, mo, B("m_mo"), xt, B("m_xt"), 16, j, dv, t0, n, sq, rs)
            f.barrier()

    def phase_ffn_up(self, l, xsm, last):
        nc, f = self.nc, self.f
        B = f.buf
        I, S = self.I, self.S
        fv = self.fm(S["ffa"])
        TP = T + 4
        hcol = lambda t: (t + 1) if t < CTX else (t + 3)
        with contextlib.ExitStack() as es:
            sb = lambda n, s, d: es.enter_context(self.sbuf(n, s, d))
            hT = sb("hT", [128, 8, TP], BF16)
            for c in (0, CTX + 1, CTX + 2, TP - 1):
                f.op(f.pool, lambda e: e.memset(hT[:, :, c:c + 1], 0.0), writes=[B("hT")])
            with contextlib.ExitStack() as es2:
                self.prenorm(es2, xsm, self.seqs(last), 24, hT, hcol)
                f.barrier()
            GS = 4
            wt = [sb("fu_w%d" % i, [128, 8, 2, GS * 128], BF16) for i in range(2)]
            cg = [sb("fu_cg%d" % i, [128, 512], F32) for i in range(2)]
            cv = [sb("fu_cv%d" % i, [128, 512], F32) for i in range(2)]
            ao = [sb("fu_a%d" % i, [128, 512], BF16) for i in range(2)]
            wv = I["w_ffn_up"][l].rearrange("(k p) n -> p k n", p=128)
            it = 0
            for gi, j0 in enumerate(range(0, 22, GS)):
                gs = min(GS, 22 - j0)
                w = wt[gi % 2]
                for k in range(8):
                    f.dma(f.pool, w[:, k, 0, 0:gs * 128], wv[:, k, j0 * 128:(j0 + gs) * 128], writes=[B("fu_w", gi % 2, k, 0)])
                    f.dma(f.pool, w[:, k, 1, 0:gs * 128], wv[:, k, DFF + j0 * 128:DFF + (j0 + gs) * 128],
                          writes=[B("fu_w", gi % 2, k, 1)])
                for (t0, n, s0, sl) in self.tiles(self.seqs(last), 510):
                    c0 = hcol(t0)
                    for jj in range(gs):
                        jc = j0 + jj
                        par = it % 3
                        for hv in range(2):
                            pi = 2 * par + hv
                            for k in range(8):
                                f.op(f.pe, lambda e: e.matmul(self.ps[pi][:, 0:n + 2], lhsT=w[:, k, hv, jj * 128:(jj + 1) * 128],
                                                              rhs=hT[:, k, c0 - 1:c0 + n + 1], start=(k == 0), stop=(k == 7)),
                                     reads=[B("fu_w", gi % 2, k, hv), B("hT")], writes=[self.psb[pi]], inc=(k == 7))
                        res = []
                        for hv, dst, nm in ((0, cg[it % 2], "fu_cg"), (1, cv[it % 2], "fu_cv")):
                            pi = 2 * par + hv
                            ch = jc + 22 * hv
                            wc = lambda q: self.V(l, "ffn_conv_w", ch * 3 + q)
                            db = B(nm, it % 2)
                            f.op(f.act, lambda e: e.activation(out=dst[:, 0:n], in_=self.ps[pi][:, 0:n], func=AF.Copy, scale=wc(0)),
                                 reads=[self.psb[pi], B("vecs")], writes=[db])
                            for q in (1, 2):
                                f.op(f.dve, lambda e: e.scalar_tensor_tensor(out=dst[:, 0:n], in0=self.ps[pi][:, q:q + n], scalar=wc(q),
                                                                             in1=dst[:, 0:n], op0=ALU.mult, op1=ALU.add),
                                     reads=[self.psb[pi], db, B("vecs")], writes=[db])
                        g_, v_ = cg[it % 2], cv[it % 2]
                        f.op(f.act, lambda e: e.activation(out=g_[:, 0:n], in_=g_[:, 0:n], func=AF.Silu),
                             reads=[B("fu_cg", it % 2)], writes=[B("fu_cg", it % 2)])
                        a = ao[it % 2]
                        f.op(f.pool, lambda e: e.tensor_tensor(out=a[:, 0:n], in0=g_[:, 0:n], in1=v_[:, 0:n], op=ALU.mult),
                             reads=[B("fu_cg", it % 2), B("fu_cv", it % 2)], writes=[B("fu_a", it % 2)])
                        f.dma(f.sp, fv[:, jc, t0:t0 + n], a[:, 0:n], reads=[B("fu_a", it % 2)])
                        it += 1
            f.barrier()

    def phase_ffn_down(self, l, xsm, xs_out, last):
        nc, f = self.nc, self.f
        B = f.buf
        I, S = self.I, self.S
        fv = self.fm(S["ffa"])
        with contextlib.ExitStack() as es:
            sb = lambda n, s, d: es.enter_context(self.sbuf(n, s, d))
            W = sb("fd_w", [128, 22, 1024], BF16)
            wv = I["w_ffn_down"][l].rearrange("(k p) n -> p k n", p=128)
            for k in range(22):
                f.dma(f.pool, W[:, k, :], wv[:, k, :], writes=[B("fd_w", k)])
            at = [sb("fd_a%d" % i, [128, 22, 512], BF16) for i in range(2)]
            mo = sb("fd_mo", [128, 8, 512], F32)
            xt = sb("fd_xt", [128, 8, 512], F32)
            sq = [sb("fd_sq%d" % i, [128, 512], F32) for i in range(2)]
            rs = sb("fd_rs", [128, 512], F32)
            xv = self.fm(xsm)
            dv = self.fm(xs_out)
            for it, (t0, n, s0, sl) in enumerate(self.tiles(self.seqs(last))):
                j = 1 if s0 == 0 else 0
                a = at[it % 2]
                ab = B("fd_a", it % 2)
                f.dma(f.sp, a[:, :, 0:n], fv[:, :, t0:t0 + n], writes=[ab])
                f.dma(f.sp, xt[:, :, 0:n], xv[:, :, t0:t0 + n], writes=[B("fd_xt")])
                for oc in range(8):
                    pi = oc % 4
                    for k in range(22):
                        f.op(f.pe, lambda e: e.matmul(self.ps[pi][:, 0:n], lhsT=W[:, k, oc * 128:(oc + 1) * 128],
                                                      rhs=a[:, k, 0:n], start=(k == 0), stop=(k == 21)),
                             reads=[B("fd_w", k), ab], writes=[self.psb[pi]], inc=(k == 21))
                    f.op(f.act, lambda e: e.copy(out=mo[:, oc, 0:n], in_=self.ps[pi][:, 0:n]), reads=[self.psb[pi]], writes=[B("fd_mo")])
                c0 = (t0 - CTX) if last else t0
                self.post_residual(es, mo, B("fd_mo"), xt, B("fd_xt"), 40, j, dv, c0, n, sq, rs)
            f.barrier()

    def phase_rwkv_prep(self, l):
        nc, f = self.nc, self.f
        B = f.buf
        I, S = self.I, self.S
        pv = self.fm(S["proj"])
        cf = self.constf
        NT = 256
        with contextlib.ExitStack() as es:
            sb = lambda n, s, d: es.enter_context(self.sbuf(n, s, d))
            dup = sb("rp_dup", [128, D], F32)
            iup = sb("rp_iup", [128, D], F32)
            w0r = sb("rp_w0r", [1, 2048], F32)
            omka = sb("rp_omka", [128, 8], F32)
            f.dma(f.sp, dup[:], I["decay_up"][l, :, :], writes=[B("rp_dup")])
            f.dma(f.sp, iup[:], I["iclr_up"][l, :, :], writes=[B("rp_iup")])
            f.dma(f.sp, w0r[:], I["w0row"][l, :, :], writes=[B("rp_w0r")])
            f.op(f.dve, lambda e: e.tensor_scalar(out=omka[:], in0=self.V(l, "k_a", 0, 8), scalar1=-1.0, scalar2=1.0,
                                                  op0=ALU.mult, op1=ALU.add), reads=[B("vecs")], writes=[B("rp_omka")])
            raw = [sb("rp_raw%d" % i, [128, NT + 2], F32) for i in range(3)]
            rkv = [sb("rp_%s" % nm, [128, 8, NT], F32) for nm in ("r", "k", "v")]
            kk = sb("rp_kk", [128, 8, NT], F32)
            sq = [sb("rp_sq%d" % i, [128, NT], F32) for i in range(2)]
            nrm = [sb("rp_nrm%d" % i, [128, NT], F32) for i in range(2)]
            kap = sb("rp_kap", [128, 8, NT], F32)
            lw = sb("rp_lw", [128, NT], F32)
            la = sb("rp_la", [128, NT], F32)
            sg = [sb("rp_sg%d" % i, [128, D], F32) for i in range(2)]
            Ein = sb("rp_Ein", [128, 8, NT], F32)
            Eex = sb("rp_Eex", [128, 8, NT], F32)
            Eng_ = sb("rp_Eneg", [128, 8, NT], F32)
            ag = sb("rp_a", [128, 8, NT], F32)
            kd = sb("rp_kd", [128, 8, NT], F32)
            bd = sb("rp_bd", [128, 8, NT], F32)
            k01 = sb("rp_k01", [128, 8, NT], F32)
            outs = [sb("rp_out%d" % i, [128, 8, NT], F32) for i in range(4)]
            gct = sb("rp_gct", [128, 8, 4], F32)
            ir = 0
            for (t0, n, s0, sl) in self.tiles([(0, CTX), (CTX, SEQ)], NT):
                for c in range(24):
                    r = raw[ir % 3]
                    rb = B("rp_raw", ir % 3)
                    ir += 1
                    lo = max(t0 - 1, s0)
                    hi = min(t0 + n + 1, s0 + sl)
                    off = lo - (t0 - 1)
                    if lo != t0 - 1:
                        f.op(f.pool, lambda e: e.memset(r[:, 0:1], 0.0), writes=[rb])
                    if hi != t0 + n + 1:
                        f.op(f.pool, lambda e: e.memset(r[:, n + 1:n + 2], 0.0), writes=[rb])
                    f.dma(f.sp, r[:, off:off + hi - lo], pv[:, 28 + c, lo:hi], writes=[rb])
                    dst = rkv[c // 8]
                    db = B("rp_rkv", c // 8)
                    cc_ = c % 8
                    wc = lambda q: self.V(l, "shift_w", c * 3 + q)
                    f.op(f.act, lambda e: e.activation(out=dst[:, cc_, 0:n], in_=r[:, 0:n], func=AF.Copy, scale=wc(0)),
                         reads=[rb, B("vecs")], writes=[db])
                    for q in (1, 2):
                        f.op(f.dve, lambda e: e.scalar_tensor_tensor(out=dst[:, cc_, 0:n], in0=r[:, q:q + n], scalar=wc(q),
                                                                     in1=dst[:, cc_, 0:n], op0=ALU.mult, op1=ALU.add),
                             reads=[rb, db, B("vecs")], writes=[db])
                R_, K_, V_ = rkv
                f.dma(f.sp, self.fm(S["rw_r"])[:, :, t0:t0 + n], R_[:, :, 0:n], reads=[B("rp_rkv", 0)])
                f.dma(f.sp, self.fm(S["rw_v"])[:, :, t0:t0 + n], V_[:, :, 0:n], reads=[B("rp_rkv", 2)])
                for c in range(8):
                    f.op(f.pool, lambda e: e.tensor_scalar(out=kk[:, c, 0:n], in0=K_[:, c, 0:n], scalar1=self.V(l, "k_k", c),
                                                           scalar2=None, op0=ALU.mult),
                         reads=[B("rp_rkv", 1), B("vecs")], writes=[B("rp_kk", c)])
                    s = sq[c % 2]
                    sqb = B("rp_sq", c % 2)
                    f.op(f.act, lambda e: e.activation(out=s[:, 0:n], in_=kk[:, c, 0:n], func=AF.Square),
                         reads=[B("rp_kk", c)], writes=[sqb])
                    pi = 6 + c % 2
                    f.op(f.pe, lambda e: e.matmul(self.ps[pi][:, 0:n], lhsT=cf[:, C_BLK:C_BLK + 128], rhs=s[:, 0:n],
                                                  start=True, stop=True), reads=[sqb, B("constf")], writes=[self.psb[pi]])
                    nr = nrm[c % 2]
                    nb = B("rp_nrm", c % 2)
                    f.op(f.dve, lambda e: e.tensor_scalar(out=nr[:, 0:n], in0=self.ps[pi][:, 0:n], scalar1=1e-12, scalar2=None,
                                                          op0=ALU.max), reads=[self.psb[pi]], writes=[nb])
                    self.rstd_from_ps(nr[:, 0:n], nb, nr[:, 0:n], nb, 1.0, 0.0)
                    f.op(f.dve, lambda e: e.tensor_tensor(out=kap[:, c, 0:n], in0=kk[:, c, 0:n], in1=nr[:, 0:n], op=ALU.mult),
                         reads=[B("rp_kk", c), nb], writes=[B("rp_kap")])
                f.dma(f.sp, lw[:, 0:n], pv[:, 52, t0:t0 + n], writes=[B("rp_lw")])
                f.dma(f.sp, la[:, 0:n], pv[:, 53, t0:t0 + n], writes=[B("rp_la")])
                f.op(f.act, lambda e: e.activation(out=lw[:, 0:n], in_=lw[:, 0:n], func=AF.Tanh), reads=[B("rp_lw")], writes=[B("rp_lw")])
                for d in range(2):
                    pr = slice(64 * d, 64 * d + 64)
                    tri = C_TRIF if d == 0 else C_TRIB
                    for jb in range(n // 128):
                        s_ = sg[jb % 2]
                        sgb = B("rp_sg", jb % 2)
                        for fh in range(2):
                            pi = fh
                            f.op(f.pe, lambda e: e.matmul(self.ps[pi][:, 0:512], lhsT=lw[pr, jb * 128:(jb + 1) * 128],
                                                          rhs=dup[pr, fh * 512:(fh + 1) * 512], start=True, stop=False),
                                 reads=[B("rp_lw"), B("rp_dup")], writes=[self.psb[pi]], inc=False)
                            f.op(f.pe, lambda e: e.matmul(self.ps[pi][:, 0:512], lhsT=cf[0:1, C_ONES:C_ONES + 128],
                                                          rhs=w0r[0:1, d * 1024 + fh * 512:d * 1024 + (fh + 1) * 512],
                                                          start=False, stop=True),
                                 reads=[B("rp_w0r"), B("constf")], writes=[self.psb[pi]])
                            f.op(f.act, lambda e: e.activation(out=s_[:, fh * 512:(fh + 1) * 512], in_=self.ps[pi][:, 0:512],
                                                               func=AF.Sigmoid), reads=[self.psb[pi]], writes=[sgb])
                        for c2 in range(4):
                            pi = 2 + c2
                            for h2 in range(2):
                                c = 2 * c2 + h2
                                f.op(f.pe, lambda e: e.matmul(self.ps[pi][:, h2 * 256:(h2 + 1) * 256], lhsT=s_[:, c * 128:(c + 1) * 128],
                                                              rhs=cf[:, tri:tri + 256], start=True, stop=True),
                                     reads=[sgb, B("constf")], writes=[self.psb[pi]], inc=(h2 == 1))
                            pv4 = self.ps[pi][:, 0:512].rearrange("p (c i t) -> p c i t", c=2, i=2)
                            cs = slice(2 * c2, 2 * c2 + 2)
                            ts = slice(jb * 128, (jb + 1) * 128)
                            f.op(f.act, lambda e: e.activation(out=Ein[:, cs, ts], in_=pv4[:, :, 0, :], func=AF.Exp, scale=-DECAY_SCALE),
                                 reads=[self.psb[pi]], writes=[B("rp_Ein")])
                            f.op(f.act, lambda e: e.activation(out=Eex[:, cs, ts], in_=pv4[:, :, 1, :], func=AF.Exp, scale=-DECAY_SCALE),
                                 reads=[self.psb[pi]], writes=[B("rp_Eex")])
                            f.op(f.act, lambda e: e.activation(out=Eng_[:, cs, ts], in_=pv4[:, :, 0, :], func=AF.Exp, scale=DECAY_SCALE),
                                 reads=[self.psb[pi]], writes=[B("rp_Eneg")])
                    for c in range(8):
                        pi = 6 + c % 2
                        f.op(f.pe, lambda e: e.matmul(self.ps[pi][:, 0:n], lhsT=iup[pr, c * 128:(c + 1) * 128], rhs=la[pr, 0:n],
                                                      start=True, stop=True), reads=[B("rp_iup"), B("rp_la")], writes=[self.psb[pi]])
                        f.op(f.act, lambda e: e.activation(out=ag[:, c, 0:n], in_=self.ps[pi][:, 0:n], func=AF.Sigmoid,
                                                           bias=self.V(l, "iclr_a0", d * 8 + c), scale=1.0),
                             reads=[self.psb[pi], B("vecs")], writes=[B("rp_a")])
                        f.op(f.dve, lambda e: e.tensor_scalar(out=kd[:, c, 0:n], in0=ag[:, c, 0:n], scalar1=self.V(l, "k_a", c),
                                                              scalar2=omka[:, c:c + 1], op0=ALU.mult, op1=ALU.add),
                             reads=[B("rp_a"), B("vecs"), B("rp_omka")], writes=[B("rp_kd")])
                    f.op(f.pool, lambda e: e.tensor_tensor(out=kd[:, :, 0:n], in0=kd[:, :, 0:n], in1=K_[:, :, 0:n], op=ALU.mult),
                         reads=[B("rp_kd"), B("rp_rkv", 1)], writes=[B("rp_kd")])
                    f.op(f.pool, lambda e: e.tensor_tensor(out=bd[:, :, 0:n], in0=ag[:, :, 0:n], in1=kap[:, :, 0:n], op=ALU.mult),
                         reads=[B("rp_a"), B("rp_kap")], writes=[B("rp_bd")])
                    if d == 0:
                        f.op(f.pool, lambda e: e.tensor_copy(out=k01[:, :, 0:n], in_=kd[:, :, 0:n]), reads=[B("rp_kd")], writes=[B("rp_k01")])
                    else:
                        f.op(f.pool, lambda e: e.tensor_tensor(out=k01[:, :, 0:n], in0=k01[:, :, 0:n], in1=kd[:, :, 0:n], op=ALU.add),
                             reads=[B("rp_kd"), B("rp_k01")], writes=[B("rp_k01")])
                    o_at, o_bt, o_kt, o_rt = outs
                    f.op(f.dve, lambda e: e.scalar_tensor_tensor(out=o_at[:, :, 0:n], in0=kap[:, :, 0:n], scalar=-1.0, in1=Eex[:, :, 0:n],
                                                                 op0=ALU.mult, op1=ALU.mult),
                         reads=[B("rp_kap"), B("rp_Eex")], writes=[B("rp_out", 0)])
                    f.op(f.pool, lambda e: e.tensor_tensor(out=o_bt[:, :, 0:n], in0=bd[:, :, 0:n], in1=Eng_[:, :, 0:n], op=ALU.mult),
                         reads=[B("rp_bd"), B("rp_Eneg")], writes=[B("rp_out", 1)])
                    f.op(f.dve, lambda e: e.tensor_tensor(out=o_kt[:, :, 0:n], in0=kd[:, :, 0:n], in1=Eng_[:, :, 0:n], op=ALU.mult),
                         reads=[B("rp_kd"), B("rp_Eneg")], writes=[B("rp_out", 2)])
                    f.op(f.pool, lambda e: e.tensor_tensor(out=o_rt[:, :, 0:n], in0=R_[:, :, 0:n], in1=Ein[:, :, 0:n], op=ALU.mult),
                         reads=[B("rp_rkv", 0), B("rp_Ein")], writes=[B("rp_out", 3)])
                    for i_, nm in enumerate(("At", "Bt", "Kt", "Rt")):
                        f.dma(f.sp, self.fm(S["%s%d" % (nm, d)])[:, :, t0:t0 + n], outs[i_][:, :, 0:n], reads=[B("rp_out", i_)])
                    col0 = 63 if d == 0 else 0
                    nch = n // 64
                    f.op(f.act, lambda e: e.copy(out=gct[:, :, 0:nch], in_=Ein[:, :, col0:n:64]), reads=[B("rp_Ein")], writes=[B("rp_gct")])
                    f.dma(f.sp, self.fm(S["gC%d" % d])[:, :, t0 // 64:t0 // 64 + nch], gct[:, :, 0:nch], reads=[B("rp_gct")])
                f.dma(f.sp, self.fm(S["rw_k01"])[:, :, t0:t0 + n], k01[:, :, 0:n], reads=[B("rp_k01")])
            f.barrier()

    def phase_rwkv_scan(self, l, last):
        nc, f = self.nc, self.f
        B = f.buf
        S = self.S
        cf = self.constf
        mk = self.maskf
        with contextlib.ExitStack() as es:
            sb = lambda n, s, d: es.enter_context(self.sbuf(n, s, d))
            ST = [sb("sc_ST%d" % d, [128, 8, 64], F32) for d in range(2)]
            gC = [sb("sc_gC%d" % d, [128, 8, NCHUNK], F32) for d in range(2)]
            names = ("At", "Bt", "Kt", "Rt", "V")
            inp = [[[sb("sc_%s%d_%d" % (nm, d, i), [128, 8, 128], F32) for nm in names] for i in range(2)] for d in range(2)]
            def mk64(nm, k=1):
                return [[[sb("sc_%s%d%d_%d" % (nm, d, h, i), [128, 256], F32) for i in range(k)] for h in range(2)] for d in range(2)]
            Xb = mk64("X", 2)
            XTb = mk64("XT", 2)
            Pb = mk64("P", 6)
            Lb = mk64("L", 3)
            Tk = mk64("Tk", 3)
            Wb = mk64("W", 2)
            Yo = mk64("Yo", 1)
            for d in range(2):
                f.op(f.pool, lambda e: e.memset(ST[d][:], 0.0), writes=[B("ST", d, 0), B("ST", d, 1)])
                f.dma(f.sp, gC[d][:], self.fm(S["gC%d" % d])[:, :, :], writes=[B("sc_gC", d)])
            order = [list(range(NCHUNK)), [3, 2, 1, 0] + list(range(NCHUNK - 1, 3, -1))]
            srcs = [[self.fm(S["%s%d" % (nm, d)]) for nm in ("At", "Bt", "Kt", "Rt")] + [self.fm(S["rw_v"])] for d in range(2)]
            yv = [self.fm(S["yf"]), self.fm(S["yb"])]
            self._psr = 0
            self._cp = 0
            cur_tile = [None, None]
            nload = [0, 0]

            def nps():
                i = self._psr % 8
                self._psr += 1
                return self.ps[i], self.psb[i]

            def evac(out_ap, in_ap, rd, wr):
                self._cp += 1
                if self._cp % 3 == 0:
                    f.op(f.dve, lambda e: e.tensor_copy(out=out_ap, in_=in_ap), reads=rd, writes=wr)
                else:
                    f.op(f.act, lambda e: e.copy(out=out_ap, in_=in_ap), reads=rd, writes=wr)

            def group(d, ch, half):
                tl, cc = ch // 2, ch % 2
                cs = slice(64 * cc, 64 * cc + 64)
                if cur_tile[d] != tl:
                    cur_tile[d] = tl
                    nload[d] += 1
                    bi = nload[d] % 2
                    for i_, nm in enumerate(names):
                        f.dma(f.sp, inp[d][bi][i_][:], srcs[d][i_][:, :, tl * 128:(tl + 1) * 128], writes=[B("sc_in", d, bi, i_)])
                bi = nload[d] % 2
                A_, Bm, Km, R_, Vv = inp[d][bi]
                bA, bB, bK, bR, bV = [B("sc_in", d, bi, i_) for i_ in range(5)]
                heads = [(4 * half + hpi, hh) for hpi in range(4) for hh in range(2)]
                fm_ = lambda T_, hp, hh: T_[64 * hh:64 * hh + 64, hp, cs]
                O = lambda T_, g8: T_[64 * (g8 % 2):64 * (g8 % 2) + 64, (g8 // 2) * 64:(g8 // 2) * 64 + 64]
                stb = B("ST", d, half)
                cst = B("constf")
                mkb = B("maskf")
                if d == 0:
                    mX, mXT, mL = M_LOS, M_UPS, M_UPI
                else:
                    mX, mXT, mL = M_UPS, M_LOS, M_LOI
                Vt_, Bt_, Kt_ = Tk[d][half]
                dh = (d, half)
                for j_, (src, sbuf_, dst) in enumerate(((Vv, bV, Vt_), (Bm, bB, Bt_), (Km, bK, Kt_))):
                    ps, pb = nps()
                    for g8, (hp, hh) in enumerate(heads):
                        f.op(f.pe, lambda e: e.matmul(O(ps, g8), lhsT=fm_(src, hp, hh),
                                                      rhs=cf[64 * hh:64 * hh + 64, C_ID + 64 * hh:C_ID + 64 * hh + 64], start=True, stop=True),
                             reads=[sbuf_, cst], writes=[pb], inc=(g8 == 7))
                    evac(dst[:, :], ps[:, 0:256], [pb], [B("sc_Tk", dh, j_)])
                    yield
                bVt, bBt, bKt = [B("sc_Tk", dh, j_) for j_ in range(3)]

                def mm8(out_rows, fn_l, fn_r, rd):
                    ps, pb = nps()
                    for g8, (hp, hh) in enumerate(heads):
                        f.op(f.pe, lambda e: e.matmul(O(ps, g8), lhsT=fn_l(g8, hp, hh), rhs=fn_r(g8, hp, hh), start=True, stop=True),
                             reads=rd, writes=[pb], inc=(g8 == 7))
                    return ps, pb

                def masked(ps, pb, dst, db, mcol):
                    f.op(f.dve, lambda e: e.tensor_tensor(out=dst[:, :], in0=ps[:, 0:256], in1=mk[:, mcol:mcol + 256], op=ALU.mult),
                         reads=[pb, mkb], writes=[db])
                X = Xb[d][half]
                XT = XTb[d][half]
                P = Pb[d][half]
                bX = [B("sc_X", dh, i_) for i_ in range(2)]
                bXT = [B("sc_XT", dh, i_) for i_ in range(2)]
                bP = [B("sc_P", dh, i_) for i_ in range(6)]
                bL = [B("sc_L", dh, i_) for i_ in range(3)]
                LakT, LrbT, LrkT = Lb[d][half]
                ps, pb = mm8(0, lambda g, hp, hh: fm_(A_, hp, hh), lambda g, hp, hh: fm_(Bm, hp, hh), [bA, bB])
                masked(ps, pb, X[0], bX[0], mX)
                yield
                ps, pb = mm8(0, lambda g, hp, hh: fm_(Bm, hp, hh), lambda g, hp, hh: fm_(A_, hp, hh), [bA, bB])
                masked(ps, pb, XT[0], bXT[0], mXT)
                f.op(f.pool, lambda e: e.tensor_tensor(out=P[0][:, :], in0=XT[0][:, :], in1=mk[:, M_ID:M_ID + 256], op=ALU.add),
                     reads=[bXT[0], mkb], writes=[bP[0]])
                yield
                ps, pb = mm8(0, lambda g, hp, hh: fm_(Km, hp, hh), lambda g, hp, hh: fm_(A_, hp, hh), [bA, bK])
                masked(ps, pb, LakT, bL[0], mXT)
                yield
                ps, pb = mm8(0, lambda g, hp, hh: fm_(Bm, hp, hh), lambda g, hp, hh: fm_(R_, hp, hh), [bR, bB])
                masked(ps, pb, LrbT, bL[1], mL)
                yield
                ps, pb = mm8(0, lambda g, hp, hh: fm_(Km, hp, hh), lambda g, hp, hh: fm_(R_, hp, hh), [bR, bK])
                masked(ps, pb, LrkT, bL[2], mL)
                yield
                for i_ in range(1, 6):
                    p_, c_ = (i_ - 1) % 2, i_ % 2
                    Xp, XTp = X[p_], XT[p_]
                    if i_ <= 4:
                        ps, pb = mm8(0, lambda g, hp, hh: O(XTp, g), lambda g, hp, hh: O(Xp, g), [bX[p_], bXT[p_]])
                        evac(X[c_][:, :], ps[:, 0:256], [pb], [bX[c_]])
                        yield
                    ps, pb = mm8(0, lambda g, hp, hh: O(Xp, g), lambda g, hp, hh: O(XTp, g), [bX[p_], bXT[p_]])
                    evac(XT[c_][:, :], ps[:, 0:256], [pb], [bXT[c_]])
                    f.op(f.pool, lambda e: e.tensor_tensor(out=P[i_][:, :], in0=XT[c_][:, :], in1=mk[:, M_ID:M_ID + 256], op=ALU.add),
                         reads=[bXT[c_], mkb], writes=[bP[i_]])
                    yield
                W = Wb[d][half]
                bW = [B("sc_W", dh, i_) for i_ in range(2)]
                ps, pb = nps()
                for g8, (hp, hh) in enumerate(heads):
                    f.op(f.pe, lambda e: e.matmul(O(ps, g8), lhsT=fm_(A_, hp, hh), rhs=ST[d][64 * hh:64 * hh + 64, hp, :],
                                                  start=True, stop=False), reads=[bA, stb], writes=[pb], inc=False)
                    f.op(f.pe, lambda e: e.matmul(O(ps, g8), lhsT=O(LakT, g8), rhs=O(Vt_, g8),
                                                  start=False, stop=True), reads=[bL[0], bVt], writes=[pb], inc=(g8 == 7))
                evac(W[0][:, :], ps[:, 0:256], [pb], [bW[0]])
                yield
                wi = 0
                for i_ in range(5, -1, -1):
                    Wc = W[wi]
                    ps, pb = mm8(0, lambda g, hp, hh: O(P[i_], g), lambda g, hp, hh: O(Wc, g), [bP[i_], bW[wi]])
                    wi ^= 1
                    evac(W[wi][:, :], ps[:, 0:256], [pb], [bW[wi]])
                    yield
                U = W[wi]
                bU = bW[wi]
                ps, pb = nps()
                for g8, (hp, hh) in enumerate(heads):
                    f.op(f.pe, lambda e: e.matmul(O(ps, g8), lhsT=ST[d][64 * hh:64 * hh + 64, hp, :], rhs=fm_(R_, hp, hh),
                                                  start=True, stop=False), reads=[bR, stb], writes=[pb], inc=False)
                    f.op(f.pe, lambda e: e.matmul(O(ps, g8), lhsT=O(U, g8), rhs=O(LrbT, g8),
                                                  start=False, stop=False), reads=[bU, bL[1]], writes=[pb], inc=False)
                    f.op(f.pe, lambda e: e.matmul(O(ps, g8), lhsT=O(Vt_, g8), rhs=O(LrkT, g8),
                                                  start=False, stop=True), reads=[bVt, bL[2]], writes=[pb], inc=(g8 == 7))
                yo = Yo[d][half][0]
                evac(yo[:, :], ps[:, 0:256], [pb], [B("sc_Yo", dh)])
                f.dma(f.sp, yv[d][:, 4 * half:4 * half + 4, ch * 64:(ch + 1) * 64], yo[:, :].rearrange("p (g t) -> p g t", t=64),
                      reads=[B("sc_Yo", dh)])
                yield
                ps, pb = nps()
                for g8, (hp, hh) in enumerate(heads):
                    o_ = O(ps, g8)
                    f.op(f.pe, lambda e: e.matmul(o_, lhsT=O(Bt_, g8), rhs=O(U, g8), start=True, stop=False),
                         reads=[bBt, bU], writes=[pb], inc=False)
                    f.op(f.pe, lambda e: e.matmul(o_, lhsT=O(Kt_, g8), rhs=O(Vt_, g8), start=False, stop=False),
                         reads=[bKt, bVt], writes=[pb], inc=False)
                    f.op(f.pe, lambda e: e.matmul(o_, lhsT=cf[64 * hh:64 * hh + 64, C_ID + 64 * hh:C_ID + 64 * hh + 64],
                                                  rhs=ST[d][64 * hh:64 * hh + 64, hp, :], start=False, stop=True),
                         reads=[cst, stb], writes=[pb], inc=(g8 == 7))
                for hpi in range(4):
                    hp = 4 * half + hpi
                    f.op(f.act, lambda e: e.activation(out=ST[d][:, hp, :], in_=ps[:, hpi * 64:(hpi + 1) * 64], func=AF.Copy,
                                                       scale=gC[d][:, hp, ch:ch + 1]),
                         reads=[pb, B("sc_gC", d)], writes=[stb])

            for s_ in range(getattr(self, "scan_steps", NCHUNK)):
                gens = [group(d, order[d][s_], half) for half in range(2) for d in range(2)]
                while gens:
                    alive = []
                    for g_ in gens:
                        try:
                            next(g_)
                            alive.append(g_)
                        except StopIteration:
                            pass
                    gens = alive
            f.barrier()

    def phase_rwkv_out(self, l, last):
        nc, f = self.nc, self.f
        B = f.buf
        I, S = self.I, self.S
        pv = self.fm(S["proj"])
        cf = self.constf
        with contextlib.ExitStack() as es:
            sb = lambda n, s, d: es.enter_context(self.sbuf(n, s, d))
            gup = sb("ro_gup", [128, D], F32)
            f.dma(f.sp, gup[:], I["gate_up"][l, :, :], writes=[B("ro_gup")])
            lg = sb("ro_lg", [128, 512], F32)
            tl = {}
            for nm in ("yf", "yb", "r", "k01", "v"):
                tl[nm] = [sb("ro_%s%d" % (nm, i), [128, 512], F32) for i in range(2)]
            sq = [sb("ro_sq%d" % i, [128, 512], F32) for i in range(2)]
            mean = [sb("ro_mean%d" % i, [128, 512], F32) for i in range(2)]
            var = [sb("ro_var%d" % i, [128, 512], F32) for i in range(2)]
            tt = [sb("ro_t%d" % i, [128, 512], F32) for i in range(2)]
            bon = [sb("ro_bon%d" % i, [128, 512], F32) for i in range(2)]
            ost = [sb("ro_o%d" % i, [128, 512], BF16) for i in range(2)]
            srcv = {"yf": self.fm(S["yf"]), "yb": self.fm(S["yb"]), "r": self.fm(S["rw_r"]), "k01": self.fm(S["rw_k01"]),
                    "v": self.fm(S["rw_v"])}
            ov = self.fm(S["rwo"])
            it = 0
            for (t0, n, s0, sl) in self.tiles(self.seqs(last)):
                f.dma(f.sp, lg[:, 0:n], pv[:, 54, t0:t0 + n], writes=[B("ro_lg")])
                f.op(f.act, lambda e: e.activation(out=lg[:, 0:n], in_=lg[:, 0:n], func=AF.Sigmoid), reads=[B("ro_lg")], writes=[B("ro_lg")])
                for c in range(8):
                    p = it % 2
                    it += 1
                    bb = {nm: B("ro_" + nm, p) for nm in tl}
                    for nm in tl:
                        f.dma(f.sp, tl[nm][p][:, 0:n], srcv[nm][:, c, t0:t0 + n], writes=[bb[nm]])
                    y = tl["yf"][p]
                    f.op(f.pool, lambda e: e.tensor_tensor(out=y[:, 0:n], in0=y[:, 0:n], in1=tl["yb"][p][:, 0:n], op=ALU.add),
                         reads=[bb["yf"], bb["yb"]], writes=[bb["yf"]])
                    f.op(f.act, lambda e: e.activation(out=sq[p][:, 0:n], in_=y[:, 0:n], func=AF.Square), reads=[bb["yf"]], writes=[B("ro_sq", p)])
                    r_ = tl["r"][p]
                    f.op(f.dve, lambda e: e.scalar_tensor_tensor(out=r_[:, 0:n], in0=r_[:, 0:n], scalar=self.V(l, "r_k", c),
                                                                 in1=tl["k01"][p][:, 0:n], op0=ALU.mult, op1=ALU.mult),
                         reads=[bb["r"], bb["k01"], B("vecs")], writes=[bb["r"]])
                    q = 4 * p
                    blk = cf[:, C_BLK:C_BLK + 128]
                    f.op(f.pe, lambda e: e.matmul(self.ps[q][:, 0:n], lhsT=blk, rhs=y[:, 0:n], start=True, stop=True),
                         reads=[bb["yf"], B("constf")], writes=[self.psb[q]])
                    f.op(f.pe, lambda e: e.matmul(self.ps[q + 1][:, 0:n], lhsT=blk, rhs=sq[p][:, 0:n], start=True, stop=True),
                         reads=[B("ro_sq", p), B("constf")], writes=[self.psb[q + 1]])
                    f.op(f.pe, lambda e: e.matmul(self.ps[q + 2][:, 0:n], lhsT=blk, rhs=r_[:, 0:n], start=True, stop=True),
                         reads=[bb["r"], B("constf")], writes=[self.psb[q + 2]])
                    f.op(f.pe, lambda e: e.matmul(self.ps[q + 3][:, 0:n], lhsT=gup[:, c * 128:(c + 1) * 128], rhs=lg[:, 0:n], start=True, stop=True),
                         reads=[B("ro_lg"), B("ro_gup")], writes=[self.psb[q + 3]])
                    m_, v_, t_ = mean[p], var[p], tt[p]
                    f.op(f.act, lambda e: e.activation(out=m_[:, 0:n], in_=self.ps[q][:, 0:n], func=AF.Copy, scale=1.0 / 64),
                         reads=[self.psb[q]], writes=[B("ro_mean", p)])
                    f.op(f.dve, lambda e: e.tensor_tensor(out=v_[:, 0:n], in0=m_[:, 0:n], in1=m_[:, 0:n], op=ALU.mult),
                         reads=[B("ro_mean", p)], writes=[B("ro_var", p)])
                    f.op(f.dve, lambda e: e.scalar_tensor_tensor(out=v_[:, 0:n], in0=self.ps[q + 1][:, 0:n], scalar=1.0 / 64, in1=v_[:, 0:n],
                                                                 op0=ALU.mult, op1=ALU.subtract),
                         reads=[self.psb[q + 1], B("ro_var", p)], writes=[B("ro_var", p)])
                    self.rstd_from_ps(v_[:, 0:n], B("ro_var", p), v_[:, 0:n], B("ro_var", p), 1.0, GN_EPS)
                    f.op(f.dve, lambda e: e.tensor_tensor(out=t_[:, 0:n], in0=y[:, 0:n], in1=m_[:, 0:n], op=ALU.subtract),
                         reads=[bb["yf"], B("ro_mean", p)], writes=[B("ro_t", p)])
                    f.op(f.pool, lambda e: e.tensor_tensor(out=t_[:, 0:n], in0=t_[:, 0:n], in1=v_[:, 0:n], op=ALU.mult),
                         reads=[B("ro_t", p), B("ro_var", p)], writes=[B("ro_t", p)])
                    f.op(f.act, lambda e: e.activation(out=t_[:, 0:n], in_=t_[:, 0:n], func=AF.Identity,
                                                       bias=self.V(l, "gn_b", c), scale=self.V(l, "gn_g", c)),
                         reads=[B("ro_t", p), B("vecs")], writes=[B("ro_t", p)])
                    f.op(f.dve, lambda e: e.tensor_tensor(out=bon[p][:, 0:n], in0=self.ps[q + 2][:, 0:n], in1=tl["v"][p][:, 0:n], op=ALU.mult),
                         reads=[self.psb[q + 2], bb["v"]], writes=[B("ro_bon", p)])
                    f.op(f.pool, lambda e: e.tensor_tensor(out=t_[:, 0:n], in0=t_[:, 0:n], in1=bon[p][:, 0:n], op=ALU.add),
                         reads=[B("ro_t", p), B("ro_bon", p)], writes=[B("ro_t", p)])
                    f.op(f.dve, lambda e: e.tensor_tensor(out=ost[p][:, 0:n], in0=self.ps[q + 3][:, 0:n], in1=t_[:, 0:n], op=ALU.mult),
                         reads=[self.psb[q + 3], B("ro_t", p)], writes=[B("ro_o", p)])
                    f.dma(f.sp, ov[:, c, t0:t0 + n], ost[p][:, 0:n], reads=[B("ro_o", p)])
            f.barrier()


def _fm(v):
    v = np.asarray(v, np.float32).reshape(-1, 128)
    return np.ascontiguousarray(v.T)


def _consts():
    c = np.zeros((128, NCONST), np.float32)
    c[:, C_ONES:C_ONES + 128] = 1.0
    for h in range(2):
        c[64 * h:64 * h + 64, C_BLK + 64 * h:C_BLK + 64 * h + 64] = 1.0
    c[:, C_ID:C_ID + 128] = np.eye(128, dtype=np.float32)
    P = np.zeros((128, 128), np.float32)
    for m in range(128):
        if m % 64 < 32:
            P[m, m + 32] = -1.0
        else:
            P[m, m - 32] = 1.0
    c[:, C_ROT:C_ROT + 128] = P.T
    s = np.arange(128)[:, None]
    t = np.arange(128)[None, :]
    same = (s // 64) == (t // 64)
    c[:, C_TRIF:C_TRIF + 128] = (same & (s <= t))
    c[:, C_TRIF + 128:C_TRIF + 256] = (same & (s < t))
    c[:, C_TRIB:C_TRIB + 128] = (same & (s >= t))
    c[:, C_TRIB + 128:C_TRIB + 256] = (same & (s > t))
    r = np.arange(64)[:, None]
    q = np.arange(64)[None, :]
    m = np.zeros((128, 5 * 256), np.float32)
    m[:, M_LOS:M_LOS + 256] = np.tile((q < r).astype(np.float32), (2, 4))
    m[:, M_UPS:M_UPS + 256] = np.tile((q > r).astype(np.float32), (2, 4))
    m[:, M_LOI:M_LOI + 256] = np.tile((q <= r).astype(np.float32), (2, 4))
    m[:, M_UPI:M_UPI + 256] = np.tile((q >= r).astype(np.float32), (2, 4))
    m[:, M_ID:M_ID + 256] = np.tile(np.eye(64, dtype=np.float32), (2, 4))
    tt = np.arange(SEQ)
    row = (tt // 64).astype(np.float32)
    col = (tt % 64).astype(np.float32)
    inv = (10000.0 ** (-np.arange(32, dtype=np.float32) / 32)).astype(np.float32)
    cos = np.zeros((128, SEQ), np.float32)
    sin = np.zeros((128, SEQ), np.float32)
    for p in range(128):
        pos = row if p < 64 else col
        ang = (pos * inv[p % 32]).astype(np.float32)
        cos[p] = np.cos(ang)
        sin[p] = np.sin(ang)
    return c, m, cos, sin


def prep_inputs(inp):
    g = lambda k: np.asarray(inp[k], np.float32)
    vecs = np.zeros((DEPTH, 128, NV), np.float32)
    for l in range(DEPTH):
        def put(name, arr):
            a = np.asarray(arr, np.float32)
            vecs[l, :, VCOL[name]:VCOL[name] + a.shape[1]] = a
        put("b_mod", _fm(g("b_mod")[l]))
        for n in ("g_pre_mix", "g_post_mix", "g_pre_ffn", "g_post_ffn", "q_norm", "k_norm", "conv_b",
                  "k_k", "k_a"):
            put(n, _fm(g(n)[l]))
        put("conv_ln_g", _fm(g("conv_ln_g")[l]))
        put("conv_ln_b", _fm(g("conv_ln_b")[l]))
        put("gn_g", _fm(g("wkv_gn_g")[l]))
        put("gn_b", _fm(g("wkv_gn_b")[l]))
        put("r_k", _fm(g("r_k")[l].reshape(-1)))
        cw = g("conv_w")[l].reshape(31, 8, 128).transpose(2, 1, 0).reshape(128, 248)
        put("conv_w", cw)
        sw = g("shift_w")[l].reshape(3, 24, 128).transpose(2, 1, 0).reshape(128, 72)
        put("shift_w", sw)
        fw_ = g("ffn_conv_w")[l].reshape(3, 44, 128).transpose(2, 1, 0).reshape(128, 132)
        put("ffn_conv_w", fw_)
        a0 = g("iclr_a0")[l].reshape(2, 8, 128).transpose(2, 0, 1).reshape(128, 16)
        put("iclr_a0", a0)
    constf, maskf, cos, sin = _consts()
    shared = {
        "vecs": vecs,
        "w0row": np.ascontiguousarray(g("decay_w0").reshape(DEPTH, 1, 2048)),
        "constf": constf, "maskf": maskf, "cos": cos, "sin": sin,
        "w_mod": g("w_mod"), "w_in": g("w_in"),
        "w_attn_o": g("w_attn_o"), "w_conv_o": g("w_conv_o"), "w_rwkv_o": g("w_rwkv_o"), "w_out": g("w_out"),
        "decay_up": np.ascontiguousarray(g("decay_up").reshape(DEPTH, 128, D)),
        "iclr_up": np.ascontiguousarray(g("iclr_up").reshape(DEPTH, 128, D)),
        "gate_up": g("gate_up"),
        "w_ffn_up": g("w_ffn_up"), "w_ffn_down": g("w_ffn_down"),
    }
    x = g("x")
    ctx = g("ctx")
    c = g("c")
    c_ctx = g("c_ctx")
    per_core = []
    for b in range(x.shape[0]):
        xs0 = np.ascontiguousarray(np.concatenate([ctx[b].T, x[b].T], axis=1))
        cc = np.stack([c[b], c_ctx], axis=-1).reshape(8, 128, 2).transpose(1, 0, 2).reshape(128, 16)
        m = dict(shared)
        m["xs0"] = xs0
        m["cc"] = np.ascontiguousarray(cc)
        per_core.append(m)
    return per_core


def kernel(**inputs):
    per_core = prep_inputs(inputs)
    nc = Prog().build()
    res = run_bass_kernel_spmd(nc, per_core, core_ids=list(range(8)))
    outs = [np.ascontiguousarray(np.asarray(r["out"], np.float32).T) for r in res.results]
    return np.stack(outs, axis=0)
```

```python
import math
import contextlib
import numpy as np
import concourse.bass as bass
import concourse.mybir as mybir
from concourse.bass_utils import run_bass_kernel_spmd

F32 = mybir.dt.float32
BF16 = mybir.dt.bfloat16
AF = mybir.ActivationFunctionType
ALU = mybir.AluOpType

D = 1024
SEQ = 4096
CTX = 256
T = SEQ + CTX
DEPTH = 2
NIN = 10112
DFF = 2816
DECAY_SCALE = math.exp(-0.5)
EPS = 1e-6
LN_EPS = 1e-5
GN_EPS = 64 * 1e-5
CH = 64
NCHUNK = T // CH

VCOL = {}
_o = 0
for _n, _w in (("b_mod", 48), ("g_pre_mix", 8), ("g_post_mix", 8), ("g_pre_ffn", 8), ("g_post_ffn", 8),
               ("q_norm", 1), ("k_norm", 1), ("conv_w", 248), ("conv_b", 8), ("conv_ln_g", 8), ("conv_ln_b", 8),
               ("shift_w", 72), ("iclr_a0", 16), ("k_k", 8), ("k_a", 8), ("r_k", 8), ("gn_g", 8), ("gn_b", 8),
               ("ffn_conv_w", 132)):
    VCOL[_n] = _o
    _o += _w
NV = _o
C_ONES, C_BLK, C_ID, C_ROT, C_TRIF, C_TRIB = 0, 128, 256, 384, 512, 768
NCONST = 1024
M_LOS, M_UPS, M_LOI, M_UPI, M_ID = 0, 256, 512, 768, 1024


class Buf:
    __slots__ = ("w", "r")

    def __init__(self):
        self.w = None
        self.r = {}


class Eng:
    def __init__(self, name, e, sem):
        self.name = name
        self.e = e
        self.sem = sem
        self.count = 0
        self.seen = {}


class FW:
    SAME_ENG_DIST = 2

    def __init__(self, nc, es, n_dma_sems=24):
        self.nc = nc
        self.sems = {}

        def mk(name, e):
            s = es.enter_context(nc.semaphore("sem_" + name))
            self.sems[name] = s
            return Eng(name, e, s)
        self.pe = mk("pe", nc.tensor)
        self.act = mk("act", nc.scalar)
        self.dve = mk("dve", nc.vector)
        self.pool = mk("pool", nc.gpsimd)
        self.sp = mk("sp", nc.sync)
        self.engs = [self.pe, self.act, self.dve, self.pool, self.sp]
        self.dma_sems = []
        for i in range(n_dma_sems):
            nm = "dq%d" % i
            self.sems[nm] = es.enter_context(nc.semaphore("sem_" + nm))
            self.dma_sems.append([nm, 0])
        self.dma_rr = 0
        self.bufs = {}
        self.n_ins = 0

    def buf(self, *key):
        b = self.bufs.get(key)
        if b is None:
            b = Buf()
            self.bufs[key] = b
        return b

    def _need(self, eng, tok, raw):
        if tok is None:
            return
        sk, v = tok
        if sk == eng.name:
            if not (raw and (eng.count - v) < self.SAME_ENG_DIST):
                return
        if eng.seen.get(sk, 0) >= v:
            return
        eng.e.wait_ge(self.sems[sk], v)
        eng.seen[sk] = v

    def _deps(self, eng, reads, writes):
        for b in reads:
            self._need(eng, b.w, True)
        for b in writes:
            self._need(eng, b.w, False)
            for sk, v in b.r.items():
                self._need(eng, (sk, v), False)

    def _mark(self, tok, reads, writes):
        for b in reads:
            if b.r.get(tok[0], 0) < tok[1]:
                b.r[tok[0]] = tok[1]
        for b in writes:
            b.w = tok
            b.r = {}

    def op(self, eng, fn, reads=(), writes=(), inc=True):
        self._deps(eng, reads, writes)
        ins = fn(eng.e)
        self.n_ins += 1
        if inc:
            eng.count += 1
            ins.then_inc(eng.sem, 1)
            tok = (eng.name, eng.count)
        else:
            tok = (eng.name, eng.count + 1)
        self._mark(tok, reads, writes)
        return ins

    def dma(self, eng, out, in_, reads=(), writes=()):
        slot = self.dma_sems[self.dma_rr]
        self.dma_rr = (self.dma_rr + 1) % len(self.dma_sems)
        nm, cnt = slot
        self._deps(eng, reads, writes)
        self._need(eng, (nm, cnt), False)
        ins = eng.e.dma_start(out=out, in_=in_)
        slot[1] = cnt + 16
        ins.then_inc(self.sems[nm], 16)
        self.n_ins += 1
        self._mark((nm, cnt + 16), reads, writes)
        return ins

    def barrier(self):
        for e in self.engs:
            for o in self.engs:
                if o is not e and o.count > 0:
                    self._need(e, (o.name, o.count), False)
            for nm, cnt in self.dma_sems:
                if cnt > 0:
                    self._need(e, (nm, cnt), False)


class Prog:
    def __init__(self, debug=None, nlayers=DEPTH, stop_after=None):
        self.debug = debug or []
        self.nlayers = nlayers
        self.stop_after = stop_after
        self.nc = bass.Bass("TRN2", target_bir_lowering=False)
        self.uid = 0

    def din(self, name, shape, dt=F32):
        return self.nc.dram_tensor(name, list(shape), dt, kind="ExternalInput").ap()

    def dscr(self, name, shape, dt=F32):
        kind = "ExternalOutput" if name in self.debug else "Internal"
        return self.nc.dram_tensor(name, list(shape), dt, kind=kind).ap()

    def sbuf(self, name, shape, dt):
        self.uid += 1
        return self.nc.sbuf_tensor("%s_u%d" % (name, self.uid), shape, dt)

    def fm(self, ap):
        return ap.rearrange("(c p) t -> p c t", p=128)

    def build(self):
        nc = self.nc
        L = self.nlayers
        I = {}
        I["xs0"] = self.din("xs0", [D, T])
        I["cc"] = self.din("cc", [128, 16])
        I["vecs"] = self.din("vecs", [DEPTH, 128, NV])
        I["w0row"] = self.din("w0row", [DEPTH, 1, 2048])
        I["constf"] = self.din("constf", [128, NCONST])
        I["maskf"] = self.din("maskf", [128, 5 * 256])
        I["cos"] = self.din("cos", [128, SEQ])
        I["sin"] = self.din("sin", [128, SEQ])
        I["w_mod"] = self.din("w_mod", [DEPTH, D, 6 * D])
        I["w_in"] = self.din("w_in", [DEPTH, D, NIN])
        for n in ("w_attn_o", "w_conv_o", "w_rwkv_o", "w_out"):
            I[n] = self.din(n, [DEPTH, D, D])
        I["decay_up"] = self.din("decay_up", [DEPTH, 128, D])
        I["iclr_up"] = self.din("iclr_up", [DEPTH, 128, D])
        I["gate_up"] = self.din("gate_up", [DEPTH, 128, D])
        I["w_ffn_up"] = self.din("w_ffn_up", [DEPTH, D, 2 * DFF])
        I["w_ffn_down"] = self.din("w_ffn_down", [DEPTH, DFF, D])
        self.I = I
        out = nc.dram_tensor("out", [D, SEQ], F32, kind="ExternalOutput").ap()
        S = {}
        S["proj"] = self.dscr("proj", [NIN, T])
        S["att"] = self.dscr("att", [D, T], BF16)
        S["cnv"] = self.dscr("cnv", [D, T], BF16)
        S["rwo"] = self.dscr("rwo", [D, T], BF16)
        S["ffa"] = self.dscr("ffa", [DFF, T], BF16)
        for n in ("rw_r", "rw_v", "rw_k01", "yf", "yb"):
            S[n] = self.dscr(n, [D, T])
        for d in range(2):
            for n in ("At", "Bt", "Kt", "Rt"):
                S["%s%d" % (n, d)] = self.dscr("%s%d" % (n, d), [D, T])
            S["gC%d" % d] = self.dscr("gC%d" % d, [D, NCHUNK])
        S["xsm0"] = self.dscr("xsm0", [D, T])
        S["xs1"] = self.dscr("xs1", [D, T])
        S["xsm1"] = self.dscr("xsm1", [D, T])
        self.S = S

        with contextlib.ExitStack() as es:
            self.es = es
            f = FW(nc, es)
            self.f = f
            self.constf = es.enter_context(self.sbuf("constf", [128, NCONST], F32))
            self.maskf = es.enter_context(self.sbuf("maskf", [128, 5 * 256], F32))
            self.vecs = es.enter_context(self.sbuf("vecs", [128, DEPTH, NV], F32))
            self.modS = es.enter_context(self.sbuf("modS", [128, 48, 2], F32))
            self.dsc = es.enter_context(self.sbuf("dsc", [128, 64, 2], F32))
            self.onesb = es.enter_context(self.sbuf("onesb", [128, 128], BF16))
            self.ps = [es.enter_context(nc.psum_tensor("ps%d" % i, [128, 512], F32)) for i in range(8)]
            self.psb = [f.buf("ps", i) for i in range(8)]
            B = f.buf
            f.dma(f.sp, self.constf[:], I["constf"][:, :], writes=[B("constf")])
            f.dma(f.sp, self.maskf[:], I["maskf"][:, :], writes=[B("maskf")])
            for l in range(DEPTH):
                f.dma(f.sp, self.vecs[:, l, :], I["vecs"][l, :, :], writes=[B("vecs")])
            f.op(f.dve, lambda e: e.tensor_copy(out=self.onesb[:], in_=self.constf[:, C_ONES:C_ONES + 128]),
                 reads=[B("constf")], writes=[B("onesb")])
            f.barrier()
            xs_in = I["xs0"]
            for l in range(L):
                last = (l == DEPTH - 1)
                xsm = S["xsm%d" % l]
                xs_out = out if last else S["xs1"]
                self.phase_mod(l)
                if self.stop_after == ("mod", l): break
                self.phase_inproj(l, xs_in, last)
                if self.stop_after == ("inproj", l): break
                self.phase_attn(l, last)
                if self.stop_after == ("attn", l): break
                self.phase_conv(l, last)
                if self.stop_after == ("conv", l): break
                self.phase_rwkv_prep(l)
                if self.stop_after == ("rwprep", l): break
                self.phase_rwkv_scan(l, last)
                if self.stop_after == ("rwscan", l): break
                self.phase_rwkv_out(l, last)
                if self.stop_after == ("rwout", l): break
                self.phase_merge(l, xs_in, xsm, last)
                if self.stop_after == ("merge", l): break
                self.phase_ffn_up(l, xsm, last)
                if self.stop_after == ("ffnup", l): break
                self.phase_ffn_down(l, xsm, xs_out, last)
                xs_in = xs_out
            f.barrier()
        return nc

    def V(self, l, name, c=0, n=1):
        o = VCOL[name] + c
        return self.vecs[:, l, o:o + n]

    def seqs(self, last):
        return [(CTX, SEQ)] if last else [(0, CTX), (CTX, SEQ)]

    def tiles(self, seqs, n=512):
        r = []
        for s0, sl in seqs:
            t = s0
            while t < s0 + sl:
                m = min(n, s0 + sl - t)
                r.append((t, m, s0, sl))
                t += m
        return r

    def rstd_from_ps(self, ps_ap, psbuf, out_ap, outbuf, scale, eps):
        f = self.f
        f.op(f.act, lambda e: e.activation(out=out_ap, in_=ps_ap, func=AF.Ln, bias=float(eps), scale=float(scale)),
             reads=[psbuf], writes=[outbuf])
        f.op(f.act, lambda e: e.activation(out=out_ap, in_=out_ap, func=AF.Exp, scale=-0.5),
             reads=[outbuf], writes=[outbuf])

    def phase_mod(self, l):
        nc, f, es0 = self.nc, self.f, self.es
        B = f.buf
        I = self.I
        with contextlib.ExitStack() as es:
            cT = es.enter_context(self.sbuf("cT", [128, 8, 2], F32))
            wt = [es.enter_context(self.sbuf("wmod%d" % i, [128, 8, 1024], F32)) for i in range(2)]
            f.dma(f.sp, cT[:], I["cc"].rearrange("p (k j) -> p k j", j=2), writes=[B("cT")])
            f.op(f.act, lambda e: e.activation(out=cT[:], in_=cT[:], func=AF.Silu), reads=[B("cT")], writes=[B("cT")])
            wv = I["w_mod"][l].rearrange("(k p) n -> p k n", p=128)
            ps = self.ps[0]
            for g in range(6):
                w = wt[g % 2]
                for k in range(8):
                    f.dma(f.sp, w[:, k, :], wv[:, k, g * 1024:(g + 1) * 1024], writes=[B("wmod", g % 2, k)])
                for oc in range(8):
                    col = (g * 8 + oc) * 2
                    for k in range(8):
                        f.op(f.pe, lambda e: e.matmul(ps[:, col:col + 2], lhsT=w[:, k, oc * 128:(oc + 1) * 128],
                                                      rhs=cT[:, k, :], start=(k == 0), stop=(k == 7)),
                             reads=[B("wmod", g % 2, k), B("cT")], writes=[self.psb[0]], inc=(k == 7))
            psv = ps[:, 0:96].rearrange("p (c j) -> p c j", j=2)
            for j in range(2):
                f.op(f.dve, lambda e: e.tensor_tensor(out=self.modS[:, :, j], in0=psv[:, :, j],
                                                      in1=self.V(l, "b_mod", 0, 48), op=ALU.add),
                     reads=[self.psb[0], B("vecs")], writes=[B("modS")])
            for j in range(2):
                def m(i):
                    return self.modS[:, i * 8:(i + 1) * 8, j]
                rd = [B("modS"), B("vecs")]
                wr = [B("dsc")]
                f.op(f.dve, lambda e: e.scalar_tensor_tensor(out=self.dsc[:, 0:8, j], in0=m(1), scalar=1.0,
                                                             in1=self.V(l, "g_pre_mix", 0, 8), op0=ALU.add, op1=ALU.mult),
                     reads=rd, writes=wr)
                f.op(f.dve, lambda e: e.tensor_copy(out=self.dsc[:, 8:16, j], in_=m(0)), reads=rd, writes=wr)
                f.op(f.dve, lambda e: e.tensor_tensor(out=self.dsc[:, 16:24, j], in0=m(2),
                                                      in1=self.V(l, "g_post_mix", 0, 8), op=ALU.mult), reads=rd, writes=wr)
                f.op(f.dve, lambda e: e.scalar_tensor_tensor(out=self.dsc[:, 24:32, j], in0=m(4), scalar=1.0,
                                                             in1=self.V(l, "g_pre_ffn", 0, 8), op0=ALU.add, op1=ALU.mult),
                     reads=rd, writes=wr)
                f.op(f.dve, lambda e: e.tensor_copy(out=self.dsc[:, 32:40, j], in_=m(3)), reads=rd, writes=wr)
                f.op(f.dve, lambda e: e.tensor_tensor(out=self.dsc[:, 40:48, j], in0=m(5),
                                                      in1=self.V(l, "g_post_ffn", 0, 8), op=ALU.mult), reads=rd, writes=wr)
            f.barrier()

    def prenorm(self, es, src, seqs, base, hT, hcol_of):
        nc, f = self.nc, self.f
        B = f.buf
        xt = [es.enter_context(self.sbuf("pn_x%d" % i, [128, 8, 512], F32)) for i in range(2)]
        sq = es.enter_context(self.sbuf("pn_sq", [128, 8, 512], F32))
        rs = es.enter_context(self.sbuf("pn_rs", [128, 512], F32))
        srcv = self.fm(src)
        for i, (t0, n, s0, sl) in enumerate(self.tiles(seqs)):
            j = 1 if s0 == 0 else 0
            x = xt[i % 2]
            xb = B("pn_x", i % 2)
            f.dma(f.sp, x[:, :, 0:n], srcv[:, :, t0:t0 + n], writes=[xb])
            f.op(f.act, lambda e: e.activation(out=sq[:, :, 0:n], in_=x[:, :, 0:n], func=AF.Square),
                 reads=[xb], writes=[B("pn_sq")])
            ps, pb = self.ps[7], self.psb[7]
            for k in range(8):
                f.op(f.pe, lambda e: e.matmul(ps[:, 0:n], lhsT=self.constf[:, C_ONES:C_ONES + 128], rhs=sq[:, k, 0:n],
                                              start=(k == 0), stop=(k == 7)),
                     reads=[B("pn_sq"), B("constf")], writes=[pb], inc=(k == 7))
            self.rstd_from_ps(ps[:, 0:n], pb, rs[:, 0:n], B("pn_rs"), 1.0 / D, EPS)
            c0 = hcol_of(t0)
            for k in range(8):
                f.op(f.dve, lambda e: e.scalar_tensor_tensor(out=x[:, k, 0:n], in0=x[:, k, 0:n],
                                                             scalar=self.dsc[:, base + k, j:j + 1], in1=rs[:, 0:n],
                                                             op0=ALU.mult, op1=ALU.mult),
                     reads=[xb, B("pn_rs"), B("dsc")], writes=[xb])
                f.op(f.act, lambda e: e.activation(out=hT[:, k, c0:c0 + n], in_=x[:, k, 0:n], func=AF.Identity,
                                                   bias=self.dsc[:, base + 8 + k, j:j + 1], scale=1.0),
                     reads=[xb, B("dsc")], writes=[B("hT")])

    def phase_inproj(self, l, xs_in, last):
        nc, f = self.nc, self.f
        B = f.buf
        I, S = self.I, self.S
        seqs = [(0, CTX), (CTX, SEQ)]
        with contextlib.ExitStack() as es:
            hT = es.enter_context(self.sbuf("hT", [128, 8, T], BF16))
            with contextlib.ExitStack() as es2:
                self.prenorm(es2, xs_in, seqs, 0, hT, lambda t: t)
                f.barrier()
            wt = [es.enter_context(self.sbuf("win%d" % i, [128, 8, 1024], BF16)) for i in range(2)]
            st = [es.enter_context(self.sbuf("ipst%d" % i, [128, 512], F32)) for i in range(4)]
            wv = I["w_in"][l].rearrange("(k p) n -> p k n", p=128)
            pv = self.fm(S["proj"])
            ngrp = (NIN + 1023) // 1024
            cnt = 0
            for g in range(ngrp):
                ncol = min(1024, NIN - g * 1024)
                w = wt[g % 2]
                for k in range(8):
                    f.dma(f.pool, w[:, k, 0:ncol], wv[:, k, g * 1024:g * 1024 + ncol], writes=[B("win", g % 2, k)])
                for (t0, n, s0, sl) in self.tiles(seqs):
                    for oc in range(ncol // 128):
                        pi = cnt % 6
                        ps, pb = self.ps[pi], self.psb[pi]
                        for k in range(8):
                            f.op(f.pe, lambda e: e.matmul(ps[:, 0:n], lhsT=w[:, k, oc * 128:(oc + 1) * 128],
                                                          rhs=hT[:, k, t0:t0 + n], start=(k == 0), stop=(k == 7)),
                                 reads=[B("win", g % 2, k), B("hT")], writes=[pb], inc=(k == 7))
                        s = st[cnt % 4]
                        sb = B("ipst", cnt % 4)
                        if cnt % 2 == 0:
                            f.op(f.act, lambda e: e.copy(out=s[:, 0:n], in_=ps[:, 0:n]), reads=[pb], writes=[sb])
                        else:
                            f.op(f.dve, lambda e: e.tensor_copy(out=s[:, 0:n], in_=ps[:, 0:n]), reads=[pb], writes=[sb])
                        f.dma(f.sp, pv[:, g * 8 + oc, t0:t0 + n], s[:, 0:n], reads=[sb])
                        cnt += 1
            f.barrier()


    def phase_attn(self, l, last):
        nc, f = self.nc, self.f
        B = f.buf
        I, S = self.I, self.S
        pv = self.fm(S["proj"])
        av = self.fm(S["att"])
        cf = self.constf
        with contextlib.ExitStack() as es:
            sb = lambda n, s, d: es.enter_context(self.sbuf(n, s, d))
            kT = sb("kT", [128, 2, T], BF16)
            Vt = sb("Vt", [128, T // 128, 2, 128], BF16)
            cos = sb("cos", [128, SEQ], F32)
            sin = sb("sin", [128, SEQ], F32)
            qg = sb("qg", [128, 1], F32)
            raw = [sb("a_raw%d" % i, [128, 512], F32) for i in range(2)]
            sq = sb("a_sq", [128, 512], F32)
            rs = sb("a_rs", [128, 512], F32)
            kn = sb("a_kn", [128, 512], F32)
            t1 = sb("a_t1", [128, 512], F32)
            t2 = sb("a_t2", [128, 512], F32)
            qT = [sb("a_qT%d" % i, [128, 512], BF16) for i in range(2)]
            pT = [sb("a_pT%d" % i, [128, 512], BF16) for i in range(4)]
            rinv = sb("a_rinv", [128, 512], F32)
            ost = [sb("a_ost%d" % i, [128, 512], BF16) for i in range(2)]
            f.dma(f.sp, cos[:], I["cos"][:, :], writes=[B("cos")])
            f.dma(f.sp, sin[:], I["sin"][:, :], writes=[B("sin")])
            f.op(f.dve, lambda e: e.tensor_scalar(out=qg[:], in0=self.V(l, "q_norm"), scalar1=float(128 ** -0.5),
                                                  scalar2=None, op0=ALU.mult), reads=[B("vecs")], writes=[B("qg")])
            self._nr = 0

            def normrope(chunk, t0, n, gain, is_x, out_ap, outbuf):
                i = self._nr
                self._nr += 1
                r = raw[i % 2]
                rb = B("a_raw", i % 2)
                f.dma(f.sp, r[:, 0:n], pv[:, chunk, t0:t0 + n], writes=[rb])
                f.op(f.act, lambda e: e.activation(out=sq[:, 0:n], in_=r[:, 0:n], func=AF.Square),
                     reads=[rb], writes=[B("a_sq")])
                f.op(f.pe, lambda e: e.matmul(self.ps[7][:, 0:n], lhsT=cf[:, C_ONES:C_ONES + 128], rhs=sq[:, 0:n],
                                              start=True, stop=True), reads=[B("a_sq"), B("constf")], writes=[self.psb[7]])
                self.rstd_from_ps(self.ps[7][:, 0:n], self.psb[7], rs[:, 0:n], B("a_rs"), 1.0 / 128, EPS)
                f.op(f.dve, lambda e: e.scalar_tensor_tensor(out=kn[:, 0:n], in0=r[:, 0:n], scalar=gain, in1=rs[:, 0:n],
                                                             op0=ALU.mult, op1=ALU.mult),
                     reads=[rb, B("a_rs"), B("vecs"), B("qg")], writes=[B("a_kn")])
                if is_x:
                    p0 = t0 - CTX
                    f.op(f.pe, lambda e: e.matmul(self.ps[7][:, 0:n], lhsT=cf[:, C_ROT:C_ROT + 128], rhs=kn[:, 0:n],
                                                  start=True, stop=True), reads=[B("a_kn"), B("constf")], writes=[self.psb[7]])
                    f.op(f.pool, lambda e: e.tensor_tensor(out=t1[:, 0:n], in0=kn[:, 0:n], in1=cos[:, p0:p0 + n], op=ALU.mult),
                         reads=[B("a_kn"), B("cos")], writes=[B("a_t1")])
                    f.op(f.dve, lambda e: e.tensor_tensor(out=t2[:, 0:n], in0=self.ps[7][:, 0:n], in1=sin[:, p0:p0 + n], op=ALU.mult),
                         reads=[self.psb[7], B("sin")], writes=[B("a_t2")])
                    f.op(f.pool, lambda e: e.tensor_tensor(out=out_ap, in0=t1[:, 0:n], in1=t2[:, 0:n], op=ALU.add),
                         reads=[B("a_t1"), B("a_t2")], writes=[outbuf])
                else:
                    f.op(f.pool, lambda e: e.tensor_copy(out=out_ap, in_=kn[:, 0:n]), reads=[B("a_kn")], writes=[outbuf])

            allseq = [(0, CTX), (CTX, SEQ)]
            for kvh in range(2):
                for (t0, n, s0, sl) in self.tiles(allseq):
                    normrope(8 + kvh, t0, n, self.V(l, "k_norm"), s0 != 0, kT[:, kvh, t0:t0 + n], B("kT"))
            i = 0
            for kvh in range(2):
                for (t0, n, s0, sl) in self.tiles(allseq):
                    r = raw[i % 2]
                    rb = B("a_raw", i % 2)
                    i += 1
                    f.dma(f.sp, r[:, 0:n], pv[:, 10 + kvh, t0:t0 + n], writes=[rb])
                    nb = n // 128
                    for j in range(nb):
                        f.op(f.pe, lambda e: e.transpose(self.ps[7][:, j * 128:(j + 1) * 128], r[:, j * 128:(j + 1) * 128],
                                                         cf[:, C_ID:C_ID + 128]),
                             reads=[rb, B("constf")], writes=[self.psb[7]], inc=(j == nb - 1))
                    b0 = t0 // 128
                    f.op(f.dve, lambda e: e.tensor_copy(out=Vt[:, b0:b0 + nb, kvh, :],
                                                        in_=self.ps[7][:, 0:nb * 128].rearrange("p (b d) -> p b d", d=128)),
                         reads=[self.psb[7]], writes=[B("Vt")])
            qi = 0
            for h in range(8):
                kvh = h // 4
                for (t0, n, s0, sl) in self.tiles(self.seqs(last)):
                    is_x = s0 != 0
                    q = qT[qi % 2]
                    qb = B("a_qT", qi % 2)
                    normrope(h, t0, n, qg[:, 0:1], is_x, q[:, 0:n], qb)
                    nblk = (T // 128) if is_x else (CTX // 128)
                    po, pob = self.ps[3 + qi % 2], self.psb[3 + qi % 2]
                    pr, prb = self.ps[5 + qi % 2], self.psb[5 + qi % 2]

                    def smm(jb):
                        f.op(f.pe, lambda e: e.matmul(self.ps[jb % 3][:, 0:n], lhsT=kT[:, kvh, jb * 128:(jb + 1) * 128],
                                                      rhs=q[:, 0:n], start=True, stop=True),
                             reads=[B("kT"), qb], writes=[self.psb[jb % 3]])
                    smm(0)
                    if nblk > 1:
                        smm(1)
                    for jb in range(nblk):
                        p = pT[jb % 4]
                        pb = B("a_pT", jb % 4)
                        f.op(f.act, lambda e: e.activation(out=p[:, 0:n], in_=self.ps[jb % 3][:, 0:n], func=AF.Exp),
                             reads=[self.psb[jb % 3]], writes=[pb])
                        if jb + 2 < nblk:
                            smm(jb + 2)
                        lastb = (jb == nblk - 1)
                        f.op(f.pe, lambda e: e.matmul(po[:, 0:n], lhsT=Vt[:, jb, kvh, :], rhs=p[:, 0:n],
                                                      start=(jb == 0), stop=lastb),
                             reads=[B("Vt"), pb], writes=[pob], inc=False)
                        f.op(f.pe, lambda e: e.matmul(pr[:, 0:n], lhsT=self.onesb[:], rhs=p[:, 0:n],
                                                      start=(jb == 0), stop=lastb),
                             reads=[B("onesb"), pb], writes=[prb], inc=lastb)
                    f.op(f.dve, lambda e: e.reciprocal(out=rinv[:, 0:n], in_=pr[:, 0:n]), reads=[prb], writes=[B("a_rinv")])
                    o = ost[qi % 2]
                    ob = B("a_ost", qi % 2)
                    f.op(f.dve, lambda e: e.tensor_tensor(out=o[:, 0:n], in0=po[:, 0:n], in1=rinv[:, 0:n], op=ALU.mult),
                         reads=[pob, B("a_rinv")], writes=[ob])
                    f.dma(f.sp, av[:, h, t0:t0 + n], o[:, 0:n], reads=[ob])
                    qi += 1
            f.barrier()

    def phase_conv(self, l, last):
        nc, f = self.nc, self.f
        B = f.buf
        S = self.S
        pv = self.fm(S["proj"])
        cv = self.fm(S["cnv"])
        cf = self.constf
        with contextlib.ExitStack() as es:
            sb = lambda n, s, d: es.enter_context(self.sbuf(n, s, d))
            at = [sb("c_a%d" % i, [128, 544], F32) for i in range(2)]
            bt = [sb("c_b%d" % i, [128, 544], F32) for i in range(2)]
            y = sb("c_y", [128, 8, 512], F32)
            sq = [sb("c_sq%d" % i, [128, 512], F32) for i in range(2)]
            mean = sb("c_mean", [128, 512], F32)
            msq = sb("c_msq", [128, 512], F32)
            rstd = sb("c_rstd", [128, 512], F32)
            tt = [sb("c_t%d" % i, [128, 512], F32) for i in range(2)]
            ost = [sb("c_o%d" % i, [128, 512], BF16) for i in range(2)]
            gbf = [sb("c_gb%d" % i, [128, 544], BF16) for i in range(2)]
            dg = sb("c_dg", [128, 8, 31, 128], BF16)
            k_ = 0
            for c in range(8):
                for j in range(31):
                    wj = self.V(l, "conv_w", c * 31 + j)
                    e3 = k_ % 3
                    k_ += 1
                    if e3 == 0:
                        f.op(f.pool, lambda e: e.tensor_scalar(out=dg[:, c, j, :], in0=cf[:, C_ID:C_ID + 128], scalar1=wj, scalar2=None,
                                                               op0=ALU.mult), reads=[B("constf"), B("vecs")], writes=[B("c_dg", c, 0)])
                    elif e3 == 1:
                        f.op(f.dve, lambda e: e.tensor_scalar(out=dg[:, c, j, :], in0=cf[:, C_ID:C_ID + 128], scalar1=wj, scalar2=None,
                                                              op0=ALU.mult), reads=[B("constf"), B("vecs")], writes=[B("c_dg", c, 1)])
                    else:
                        f.op(f.act, lambda e: e.activation(out=dg[:, c, j, :], in_=cf[:, C_ID:C_ID + 128], func=AF.Copy, scale=wj),
                             reads=[B("constf"), B("vecs")], writes=[B("c_dg", c, 2)])
            it = 0
            for (t0, n, s0, sl) in self.tiles(self.seqs(last)):
                lo = max(t0 - 15, s0)
                hi = min(t0 + n + 15, s0 + sl)
                off = lo - (t0 - 15)
                edge = (lo != t0 - 15) or (hi != t0 + n + 15)
                for c in range(8):
                    a = at[it % 2]
                    b = bt[it % 2]
                    ab = B("c_a", it % 2)
                    bb = B("c_b", it % 2)
                    it += 1
                    if edge:
                        f.op(f.pool, lambda e: e.memset(a[:, 0:n + 30], 0.0), writes=[ab])
                        f.op(f.pool, lambda e: e.memset(b[:, 0:n + 30], 0.0), writes=[bb])
                    f.dma(f.sp, a[:, off:off + hi - lo], pv[:, 12 + c, lo:hi], writes=[ab])
                    f.dma(f.sp, b[:, off:off + hi - lo], pv[:, 20 + c, lo:hi], writes=[bb])
                    f.op(f.act, lambda e: e.activation(out=b[:, 0:n + 30], in_=b[:, 0:n + 30], func=AF.Sigmoid),
                         reads=[bb], writes=[bb])
                    gb_ = gbf[it % 2]
                    gbb = B("c_gb", it % 2)
                    f.op(f.pool, lambda e: e.tensor_tensor(out=gb_[:, 0:n + 30], in0=a[:, 0:n + 30], in1=b[:, 0:n + 30], op=ALU.mult),
                         reads=[ab, bb], writes=[gbb])
                    yb = B("c_y", c)
                    pi = 2 + it % 4
                    for j in range(31):
                        f.op(f.pe, lambda e: e.matmul(self.ps[pi][:, 0:n], lhsT=dg[:, c, j, :], rhs=gb_[:, j:j + n],
                                                      start=(j == 0), stop=(j == 30)),
                             reads=[gbb, B("c_dg", c, 0), B("c_dg", c, 1), B("c_dg", c, 2)], writes=[self.psb[pi]], inc=(j == 30))
                    f.op(f.act, lambda e: e.activation(out=y[:, c, 0:n], in_=self.ps[pi][:, 0:n], func=AF.Identity,
                                                       bias=self.V(l, "conv_b", c), scale=1.0),
                         reads=[self.psb[pi], B("vecs")], writes=[yb])
                    s = sq[c % 2]
                    sqb = B("c_sq", c % 2)
                    f.op(f.act, lambda e: e.activation(out=s[:, 0:n], in_=y[:, c, 0:n], func=AF.Square), reads=[yb], writes=[sqb])
                    f.op(f.pe, lambda e: e.matmul(self.ps[0][:, 0:n], lhsT=cf[:, C_ONES:C_ONES + 128], rhs=y[:, c, 0:n],
                                                  start=(c == 0), stop=(c == 7)), reads=[yb, B("constf")], writes=[self.psb[0]], inc=False)
                    f.op(f.pe, lambda e: e.matmul(self.ps[1][:, 0:n], lhsT=cf[:, C_ONES:C_ONES + 128], rhs=s[:, 0:n],
                                                  start=(c == 0), stop=(c == 7)), reads=[sqb, B("constf")], writes=[self.psb[1]])
                f.op(f.act, lambda e: e.activation(out=mean[:, 0:n], in_=self.ps[0][:, 0:n], func=AF.Copy, scale=1.0 / D),
                     reads=[self.psb[0]], writes=[B("c_mean")])
                f.op(f.dve, lambda e: e.tensor_tensor(out=msq[:, 0:n], in0=mean[:, 0:n], in1=mean[:, 0:n], op=ALU.mult),
                     reads=[B("c_mean")], writes=[B("c_msq")])
                f.op(f.dve, lambda e: e.scalar_tensor_tensor(out=rstd[:, 0:n], in0=self.ps[1][:, 0:n], scalar=1.0 / D,
                                                             in1=msq[:, 0:n], op0=ALU.mult, op1=ALU.subtract),
                     reads=[self.psb[1], B("c_msq")], writes=[B("c_rstd")])
                self.rstd_from_ps(rstd[:, 0:n], B("c_rstd"), rstd[:, 0:n], B("c_rstd"), 1.0, LN_EPS)
                for c in range(8):
                    t = tt[c % 2]
                    tb = B("c_t", c % 2)
                    f.op(f.dve, lambda e: e.tensor_tensor(out=t[:, 0:n], in0=y[:, c, 0:n], in1=mean[:, 0:n], op=ALU.subtract),
                         reads=[B("c_y", c), B("c_mean")], writes=[tb])
                    f.op(f.pool, lambda e: e.tensor_tensor(out=t[:, 0:n], in0=t[:, 0:n], in1=rstd[:, 0:n], op=ALU.mult),
                         reads=[tb, B("c_rstd")], writes=[tb])
                    o = ost[c % 2]
                    ob = B("c_o", c % 2)
                    f.op(f.act, lambda e: e.activation(out=o[:, 0:n], in_=t[:, 0:n], func=AF.Silu,
                                                       bias=self.V(l, "conv_ln_b", c), scale=self.V(l, "conv_ln_g", c)),
                         reads=[tb, B("vecs")], writes=[ob])
                    f.dma(f.sp, cv[:, c, t0:t0 + n], o[:, 0:n], reads=[ob])
            f.barrier()

    def post_residual(self, es, mo, mob, xt, xtb, base, j, dstv, c0, n, sq, rs):
        f = self.f
        B = f.buf
        cf = self.constf
        for k in range(8):
            s = sq[k % 2]
            sqb = B("pr_sq", k % 2)
            f.op(f.act, lambda e: e.activation(out=s[:, 0:n], in_=mo[:, k, 0:n], func=AF.Square), reads=[mob], writes=[sqb])
            f.op(f.pe, lambda e: e.matmul(self.ps[7][:, 0:n], lhsT=cf[:, C_ONES:C_ONES + 128], rhs=s[:, 0:n],
                                          start=(k == 0), stop=(k == 7)), reads=[sqb, B("constf")], writes=[self.psb[7]])
        self.rstd_from_ps(self.ps[7][:, 0:n], self.psb[7], rs[:, 0:n], B("pr_rs"), 1.0 / D, EPS)
        for k in range(8):
            f.op(f.pool, lambda e: e.tensor_tensor(out=mo[:, k, 0:n], in0=mo[:, k, 0:n], in1=rs[:, 0:n], op=ALU.mult),
                 reads=[mob, B("pr_rs")], writes=[mob])
            f.op(f.dve, lambda e: e.scalar_tensor_tensor(out=xt[:, k, 0:n], in0=mo[:, k, 0:n],
                                                         scalar=self.dsc[:, base + k, j:j + 1], in1=xt[:, k, 0:n],
                                                         op0=ALU.mult, op1=ALU.add),
                 reads=[mob, xtb, B("dsc")], writes=[xtb])
        f.dma(f.sp, dstv[:, :, c0:c0 + n], xt[:, :, 0:n], reads=[xtb])

    def phase_merge(self, l, xs_in, xsm, last):
        nc, f = self.nc, self.f
        B = f.buf
        I, S = self.I, self.S
        pv = self.fm(S["proj"])
        with contextlib.ExitStack() as es:
            sb = lambda n, s, d: es.enter_context(self.sbuf(n, s, d))
            W = [sb("m_w%d" % i, [128, 8, 1024], BF16) for i in range(4)]
            for i, nm in enumerate(("w_attn_o", "w_conv_o", "w_rwkv_o", "w_out")):
                wv = I[nm][l].rearrange("(k p) n -> p k n", p=128)
                for k in range(8):
                    f.dma(f.pool, W[i][:, k, :], wv[:, k, :], writes=[B("m_w", i, k)])
            br = [sb("m_br%d" % i, [128, 8, 512], BF16) for i in range(3)]
            gt = [sb("m_g%d" % i, [128, 512], F32) for i in range(6)]
            tt = [sb("m_t%d" % i, [128, 512], F32) for i in range(6)]
            mT = sb("m_mT", [128, 8, 512], BF16)
            mo = sb("m_mo", [128, 8, 512], F32)
            xt = sb("m_xt", [128, 8, 512], F32)
            sq = [sb("m_sq%d" % i, [128, 512], F32) for i in range(2)]
            rs = sb("m_rs", [128, 512], F32)
            srcs = [self.fm(S["att"]), self.fm(S["cnv"]), self.fm(S["rwo"])]
            xv = self.fm(xs_in)
            dv = self.fm(xsm)
            for (t0, n, s0, sl) in self.tiles(self.seqs(last)):
                j = 1 if s0 == 0 else 0
                for b in range(3):
                    f.dma(f.sp, br[b][:, :, 0:n], srcs[b][:, :, t0:t0 + n], writes=[B("m_br", b)])
                f.dma(f.sp, xt[:, :, 0:n], xv[:, :, t0:t0 + n], writes=[B("m_xt")])
                for oc in range(8):
                    par = oc % 2
                    for b in range(3):
                        pi = b + 3 * par
                        for k in range(8):
                            f.op(f.pe, lambda e: e.matmul(self.ps[pi][:, 0:n], lhsT=W[b][:, k, oc * 128:(oc + 1) * 128],
                                                          rhs=br[b][:, k, 0:n], start=(k == 0), stop=(k == 7)),
                                 reads=[B("m_w", b, k), B("m_br", b)], writes=[self.psb[pi]], inc=(k == 7))
                    for b in range(3):
                        pi = b + 3 * par
                        g = gt[pi]
                        gb = B("m_g", pi)
                        f.dma(f.sp, g[:, 0:n], pv[:, 55 + 8 * b + oc, t0:t0 + n], writes=[gb])
                        f.op(f.act, lambda e: e.activation(out=g[:, 0:n], in_=g[:, 0:n], func=AF.Sigmoid), reads=[gb], writes=[gb])
                        f.op(f.dve, lambda e: e.tensor_tensor(out=tt[pi][:, 0:n], in0=self.ps[pi][:, 0:n], in1=g[:, 0:n], op=ALU.mult),
                             reads=[self.psb[pi], gb], writes=[B("m_t", pi)])
                    p0 = 3 * par
                    f.op(f.pool, lambda e: e.tensor_tensor(out=tt[p0][:, 0:n], in0=tt[p0][:, 0:n], in1=tt[p0 + 1][:, 0:n], op=ALU.add),
                         reads=[B("m_t", p0), B("m_t", p0 + 1)], writes=[B("m_t", p0)])
                    f.op(f.pool, lambda e: e.tensor_tensor(out=mT[:, oc, 0:n], in0=tt[p0][:, 0:n], in1=tt[p0 + 2][:, 0:n], op=ALU.add),
                         reads=[B("m_t", p0), B("m_t", p0 + 2)], writes=[B("m_mT")])
                for oc in range(8):
                    pi = 6
                    for k in range(8):
                        f.op(f.pe, lambda e: e.matmul(self.ps[pi][:, 0:n], lhsT=W[3][:, k, oc * 128:(oc + 1) * 128],
                                                      rhs=mT[:, k, 0:n], start=(k == 0), stop=(k == 7)),
                             reads=[B("m_w", 3, k), B("m_mT")], writes=[self.psb[pi]], inc=(k == 7))
                    f.op(f.act, lambda e: e.copy(out=mo[:, oc, 0:n], in_=self.ps[pi][:, 0:n]), reads=[self.psb[pi]], writes=[B("m_mo")])
                self.post_residual(es, mo, B("m_mo"), xt, B("m_xt"), 16, j, dv, t0, n, sq, rs)
            f.barrier()

    def phase_ffn_up(self, l, xsm, last):
        nc, f = self.nc, self.f
        B = f.buf
        I, S = self.I, self.S
        fv = self.fm(S["ffa"])
        TP = T + 4
        hcol = lambda t: (t + 1) if t < CTX else (t + 3)
        with contextlib.ExitStack() as es:
            sb = lambda n, s, d: es.enter_context(self.sbuf(n, s, d))
            hT = sb("hT", [128, 8, TP], BF16)
            for c in (0, CTX + 1, CTX + 2, TP - 1):
                f.op(f.pool, lambda e: e.memset(hT[:, :, c:c + 1], 0.0), writes=[B("hT")])
            with contextlib.ExitStack() as es2:
                self.prenorm(es2, xsm, self.seqs(last), 24, hT, hcol)
                f.barrier()
            GS = 4
            wt = [sb("fu_w%d" % i, [128, 8, 2, GS * 128], BF16) for i in range(2)]
            cg = [sb("fu_cg%d" % i, [128, 512], F32) for i in range(2)]
            cv = [sb("fu_cv%d" % i, [128, 512], F32) for i in range(2)]
            ao = [sb("fu_a%d" % i, [128, 512], BF16) for i in range(2)]
            wv = I["w_ffn_up"][l].rearrange("(k p) n -> p k n", p=128)
            it = 0
            for gi, j0 in enumerate(range(0, 22, GS)):
                gs = min(GS, 22 - j0)
                w = wt[gi % 2]
                for k in range(8):
                    f.dma(f.pool, w[:, k, 0, 0:gs * 128], wv[:, k, j0 * 128:(j0 + gs) * 128], writes=[B("fu_w", gi % 2, k, 0)])
                    f.dma(f.pool, w[:, k, 1, 0:gs * 128], wv[:, k, DFF + j0 * 128:DFF + (j0 + gs) * 128],
                          writes=[B("fu_w", gi % 2, k, 1)])
                for (t0, n, s0, sl) in self.tiles(self.seqs(last), 510):
                    c0 = hcol(t0)
                    for jj in range(gs):
                        jc = j0 + jj
                        par = it % 3
                        for hv in range(2):
                            pi = 2 * par + hv
                            for k in range(8):
                                f.op(f.pe, lambda e: e.matmul(self.ps[pi][:, 0:n + 2], lhsT=w[:, k, hv, jj * 128:(jj + 1) * 128],
                                                              rhs=hT[:, k, c0 - 1:c0 + n + 1], start=(k == 0), stop=(k == 7)),
                                     reads=[B("fu_w", gi % 2, k, hv), B("hT")], writes=[self.psb[pi]], inc=(k == 7))
                        res = []
                        for hv, dst, nm in ((0, cg[it % 2], "fu_cg"), (1, cv[it % 2], "fu_cv")):
                            pi = 2 * par + hv
                            ch = jc + 22 * hv
                            wc = lambda q: self.V(l, "ffn_conv_w", ch * 3 + q)
                            db = B(nm, it % 2)
                            f.op(f.act, lambda e: e.activation(out=dst[:, 0:n], in_=self.ps[pi][:, 0:n], func=AF.Copy, scale=wc(0)),
                                 reads=[self.psb[pi], B("vecs")], writes=[db])
                            for q in (1, 2):
                                f.op(f.dve, lambda e: e.scalar_tensor_tensor(out=dst[:, 0:n], in0=self.ps[pi][:, q:q + n], scalar=wc(q),
                                                                             in1=dst[:, 0:n], op0=ALU.mult, op1=ALU.add),
                                     reads=[self.psb[pi], db, B("vecs")], writes=[db])
                        g_, v_ = cg[it % 2], cv[it % 2]
                        f.op(f.act, lambda e: e.activation(out=g_[:, 0:n], in_=g_[:, 0:n], func=AF.Silu),
                             reads=[B("fu_cg", it % 2)], writes=[B("fu_cg", it % 2)])
                        a = ao[it % 2]
                        f.op(f.pool, lambda e: e.tensor_tensor(out=a[:, 0:n], in0=g_[:, 0:n], in1=v_[:, 0:n], op=ALU.mult),
                             reads=[B("fu_cg", it % 2), B("fu_cv", it % 2)], writes=[B("fu_a", it % 2)])
                        f.dma(f.sp, fv[:, jc, t0:t0 + n], a[:, 0:n], reads=[B("fu_a", it % 2)])
                        it += 1
            f.barrier()

    def phase_ffn_down(self, l, xsm, xs_out, last):
        nc, f = self.nc, self.f
        B = f.buf
        I, S = self.I, self.S
        fv = self.fm(S["ffa"])
        with contextlib.ExitStack() as es:
            sb = lambda n, s, d: es.enter_context(self.sbuf(n, s, d))
            W = sb("fd_w", [128, 22, 1024], BF16)
            wv = I["w_ffn_down"][l].rearrange("(k p) n -> p k n", p=128)
            for k in range(22):
                f.dma(f.pool, W[:, k, :], wv[:, k, :], writes=[B("fd_w", k)])
            at = [sb("fd_a%d" % i, [128, 22, 512], BF16) for i in range(2)]
            mo = sb("fd_mo", [128, 8, 512], F32)
            xt = sb("fd_xt", [128, 8, 512], F32)
            sq = [sb("fd_sq%d" % i, [128, 512], F32) for i in range(2)]
            rs = sb("fd_rs", [128, 512], F32)
            xv = self.fm(xsm)
            dv = self.fm(xs_out)
            for it, (t0, n, s0, sl) in enumerate(self.tiles(self.seqs(last))):
                j = 1 if s0 == 0 else 0
                a = at[it % 2]
                ab = B("fd_a", it % 2)
                f.dma(f.sp, a[:, :, 0:n], fv[:, :, t0:t0 + n], writes=[ab])
                f.dma(f.sp, xt[:, :, 0:n], xv[:, :, t0:t0 + n], writes=[B("fd_xt")])
                for oc in range(8):
                    pi = oc % 4
                    for k in range(22):
                        f.op(f.pe, lambda e: e.matmul(self.ps[pi][:, 0:n], lhsT=W[:, k, oc * 128:(oc + 1) * 128],
                                                      rhs=a[:, k, 0:n], start=(k == 0), stop=(k == 21)),
                             reads=[B("fd_w", k), ab], writes=[self.psb[pi]], inc=(k == 21))
                    f.op(f.act, lambda e: e.copy(out=mo[:, oc, 0:n], in_=self.ps[pi][:, 0:n]), reads=[self.psb[pi]], writes=[B("fd_mo")])
                c0 = (t0 - CTX) if last else t0
                self.post_residual(es, mo, B("fd_mo"), xt, B("fd_xt"), 40, j, dv, c0, n, sq, rs)
            f.barrier()

    def phase_rwkv_prep(self, l):
        nc, f = self.nc, self.f
        B = f.buf
        I, S = self.I, self.S
        pv = self.fm(S["proj"])
        cf = self.constf
        NT = 256
        with contextlib.ExitStack() as es:
            sb = lambda n, s, d: es.enter_context(self.sbuf(n, s, d))
            dup = sb("rp_dup", [128, D], F32)
            iup = sb("rp_iup", [128, D], F32)
            w0r = sb("rp_w0r", [1, 2048], F32)
            omka = sb("rp_omka", [128, 8], F32)
            f.dma(f.sp, dup[:], I["decay_up"][l, :, :], writes=[B("rp_dup")])
            f.dma(f.sp, iup[:], I["iclr_up"][l, :, :], writes=[B("rp_iup")])
            f.dma(f.sp, w0r[:], I["w0row"][l, :, :], writes=[B("rp_w0r")])
            f.op(f.dve, lambda e: e.tensor_scalar(out=omka[:], in0=self.V(l, "k_a", 0, 8), scalar1=-1.0, scalar2=1.0,
                                                  op0=ALU.mult, op1=ALU.add), reads=[B("vecs")], writes=[B("rp_omka")])
            raw = [sb("rp_raw%d" % i, [128, NT + 2], F32) for i in range(3)]
            rkv = [sb("rp_%s" % nm, [128, 8, NT], F32) for nm in ("r", "k", "v")]
            kk = sb("rp_kk", [128, 8, NT], F32)
            sq = [sb("rp_sq%d" % i, [128, NT], F32) for i in range(2)]
            nrm = [sb("rp_nrm%d" % i, [128, NT], F32) for i in range(2)]
            kap = sb("rp_kap", [128, 8, NT], F32)
            lw = sb("rp_lw", [128, NT], F32)
            la = sb("rp_la", [128, NT], F32)
            sg = [sb("rp_sg%d" % i, [128, D], F32) for i in range(2)]
            Ein = sb("rp_Ein", [128, 8, NT], F32)
            Eex = sb("rp_Eex", [128, 8, NT], F32)
            Eng_ = sb("rp_Eneg", [128, 8, NT], F32)
            ag = sb("rp_a", [128, 8, NT], F32)
            kd = sb("rp_kd", [128, 8, NT], F32)
            bd = sb("rp_bd", [128, 8, NT], F32)
            k01 = sb("rp_k01", [128, 8, NT], F32)
            outs = [sb("rp_out%d" % i, [128, 8, NT], F32) for i in range(4)]
            gct = sb("rp_gct", [128, 8, 4], F32)
            ir = 0
            for (t0, n, s0, sl) in self.tiles([(0, CTX), (CTX, SEQ)], NT):
                for c in range(24):
                    r = raw[ir % 3]
                    rb = B("rp_raw", ir % 3)
                    ir += 1
                    lo = max(t0 - 1, s0)
                    hi = min(t0 + n + 1, s0 + sl)
                    off = lo - (t0 - 1)
                    if lo != t0 - 1:
                        f.op(f.pool, lambda e: e.memset(r[:, 0:1], 0.0), writes=[rb])
                    if hi != t0 + n + 1:
                        f.op(f.pool, lambda e: e.memset(r[:, n + 1:n + 2], 0.0), writes=[rb])
                    f.dma(f.sp, r[:, off:off + hi - lo], pv[:, 28 + c, lo:hi], writes=[rb])
                    dst = rkv[c // 8]
                    db = B("rp_rkv", c // 8)
                    cc_ = c % 8
                    wc = lambda q: self.V(l, "shift_w", c * 3 + q)
                    f.op(f.act, lambda e: e.activation(out=dst[:, cc_, 0:n], in_=r[:, 0:n], func=AF.Copy, scale=wc(0)),
                         reads=[rb, B("vecs")], writes=[db])
                    for q in (1, 2):
                        f.op(f.dve, lambda e: e.scalar_tensor_tensor(out=dst[:, cc_, 0:n], in0=r[:, q:q + n], scalar=wc(q),
                                                                     in1=dst[:, cc_, 0:n], op0=ALU.mult, op1=ALU.add),
                             reads=[rb, db, B("vecs")], writes=[db])
                R_, K_, V_ = rkv
                f.dma(f.sp, self.fm(S["rw_r"])[:, :, t0:t0 + n], R_[:, :, 0:n], reads=[B("rp_rkv", 0)])
                f.dma(f.sp, self.fm(S["rw_v"])[:, :, t0:t0 + n], V_[:, :, 0:n], reads=[B("rp_rkv", 2)])
                for c in range(8):
                    f.op(f.pool, lambda e: e.tensor_scalar(out=kk[:, c, 0:n], in0=K_[:, c, 0:n], scalar1=self.V(l, "k_k", c),
                                                           scalar2=None, op0=ALU.mult),
                         reads=[B("rp_rkv", 1), B("vecs")], writes=[B("rp_kk", c)])
                    s = sq[c % 2]
                    sqb = B("rp_sq", c % 2)
                    f.op(f.act, lambda e: e.activation(out=s[:, 0:n], in_=kk[:, c, 0:n], func=AF.Square),
                         reads=[B("rp_kk", c)], writes=[sqb])
                    pi = 6 + c % 2
                    f.op(f.pe, lambda e: e.matmul(self.ps[pi][:, 0:n], lhsT=cf[:, C_BLK:C_BLK + 128], rhs=s[:, 0:n],
                                                  start=True, stop=True), reads=[sqb, B("constf")], writes=[self.psb[pi]])
                    nr = nrm[c % 2]
                    nb = B("rp_nrm", c % 2)
                    f.op(f.dve, lambda e: e.tensor_scalar(out=nr[:, 0:n], in0=self.ps[pi][:, 0:n], scalar1=1e-12, scalar2=None,
                                                          op0=ALU.max), reads=[self.psb[pi]], writes=[nb])
                    self.rstd_from_ps(nr[:, 0:n], nb, nr[:, 0:n], nb, 1.0, 0.0)
                    f.op(f.dve, lambda e: e.tensor_tensor(out=kap[:, c, 0:n], in0=kk[:, c, 0:n], in1=nr[:, 0:n], op=ALU.mult),
                         reads=[B("rp_kk", c), nb], writes=[B("rp_kap")])
                f.dma(f.sp, lw[:, 0:n], pv[:, 52, t0:t0 + n], writes=[B("rp_lw")])
                f.dma(f.sp, la[:, 0:n], pv[:, 53, t0:t0 + n], writes=[B("rp_la")])
                f.op(f.act, lambda e: e.activation(out=lw[:, 0:n], in_=lw[:, 0:n], func=AF.Tanh), reads=[B("rp_lw")], writes=[B("rp_lw")])
                for d in range(2):
                    pr = slice(64 * d, 64 * d + 64)
                    tri = C_TRIF if d == 0 else C_TRIB
                    for jb in range(n // 128):
                        s_ = sg[jb % 2]
                        sgb = B("rp_sg", jb % 2)
                        for fh in range(2):
                            pi = fh
                            f.op(f.pe, lambda e: e.matmul(self.ps[pi][:, 0:512], lhsT=lw[pr, jb * 128:(jb + 1) * 128],
                                                          rhs=dup[pr, fh * 512:(fh + 1) * 512], start=True, stop=False),
                                 reads=[B("rp_lw"), B("rp_dup")], writes=[self.psb[pi]], inc=False)
                            f.op(f.pe, lambda e: e.matmul(self.ps[pi][:, 0:512], lhsT=cf[0:1, C_ONES:C_ONES + 128],
                                                          rhs=w0r[0:1, d * 1024 + fh * 512:d * 1024 + (fh + 1) * 512],
                                                          start=False, stop=True),
                                 reads=[B("rp_w0r"), B("constf")], writes=[self.psb[pi]])
                            f.op(f.act, lambda e: e.activation(out=s_[:, fh * 512:(fh + 1) * 512], in_=self.ps[pi][:, 0:512],
                                                               func=AF.Sigmoid), reads=[self.psb[pi]], writes=[sgb])
                        for c2 in range(4):
                            pi = 2 + c2
                            for h2 in range(2):
                                c = 2 * c2 + h2
                                f.op(f.pe, lambda e: e.matmul(self.ps[pi][:, h2 * 256:(h2 + 1) * 256], lhsT=s_[:, c * 128:(c + 1) * 128],
                                                              rhs=cf[:, tri:tri + 256], start=True, stop=True),
                                     reads=[sgb, B("constf")], writes=[self.psb[pi]], inc=(h2 == 1))
                            pv4 = self.ps[pi][:, 0:512].rearrange("p (c i t) -> p c i t", c=2, i=2)
                            cs = slice(2 * c2, 2 * c2 + 2)
                            ts = slice(jb * 128, (jb + 1) * 128)
                            f.op(f.act, lambda e: e.activation(out=Ein[:, cs, ts], in_=pv4[:, :, 0, :], func=AF.Exp, scale=-DECAY_SCALE),
                                 reads=[self.psb[pi]], writes=[B("rp_Ein")])
                            f.op(f.act, lambda e: e.activation(out=Eex[:, cs, ts], in_=pv4[:, :, 1, :], func=AF.Exp, scale=-DECAY_SCALE),
                                 reads=[self.psb[pi]], writes=[B("rp_Eex")])
                            f.op(f.act, lambda e: e.activation(out=Eng_[:, cs, ts], in_=pv4[:, :, 0, :], func=AF.Exp, scale=DECAY_SCALE),
                                 reads=[self.psb[pi]], writes=[B("rp_Eneg")])
                    for c in range(8):
                        pi = 6 + c % 2
                        f.op(f.pe, lambda e: e.matmul(self.ps[pi][:, 0:n], lhsT=iup[pr, c * 128:(c + 1) * 128], rhs=la[pr, 0:n],
                                                      start=True, stop=True), reads=[B("rp_iup"), B("rp_la")], writes=[self.psb[pi]])
                        f.op(f.act, lambda e: e.activation(out=ag[:, c, 0:n], in_=self.ps[pi][:, 0:n], func=AF.Sigmoid,
                                                           bias=self.V(l, "iclr_a0", d * 8 + c), scale=1.0),
                             reads=[self.psb[pi], B("vecs")], writes=[B("rp_a")])
                        f.op(f.dve, lambda e: e.tensor_scalar(out=kd[:, c, 0:n], in0=ag[:, c, 0:n], scalar1=self.V(l, "k_a", c),
                                                              scalar2=omka[:, c:c + 1], op0=ALU.mult, op1=ALU.add),
                             reads=[B("rp_a"), B("vecs"), B("rp_omka")], writes=[B("rp_kd")])
                    f.op(f.pool, lambda e: e.tensor_tensor(out=kd[:, :, 0:n], in0=kd[:, :, 0:n], in1=K_[:, :, 0:n], op=ALU.mult),
                         reads=[B("rp_kd"), B("rp_rkv", 1)], writes=[B("rp_kd")])
                    f.op(f.pool, lambda e: e.tensor_tensor(out=bd[:, :, 0:n], in0=ag[:, :, 0:n], in1=kap[:, :, 0:n], op=ALU.mult),
                         reads=[B("rp_a"), B("rp_kap")], writes=[B("rp_bd")])
                    if d == 0:
                        f.op(f.pool, lambda e: e.tensor_copy(out=k01[:, :, 0:n], in_=kd[:, :, 0:n]), reads=[B("rp_kd")], writes=[B("rp_k01")])
                    else:
                        f.op(f.pool, lambda e: e.tensor_tensor(out=k01[:, :, 0:n], in0=k01[:, :, 0:n], in1=kd[:, :, 0:n], op=ALU.add),
                             reads=[B("rp_kd"), B("rp_k01")], writes=[B("rp_k01")])
                    o_at, o_bt, o_kt, o_rt = outs
                    f.op(f.dve, lambda e: e.scalar_tensor_tensor(out=o_at[:, :, 0:n], in0=kap[:, :, 0:n], scalar=-1.0, in1=Eex[:, :, 0:n],
                                                                 op0=ALU.mult, op1=ALU.mult),
                         reads=[B("rp_kap"), B("rp_Eex")], writes=[B("rp_out", 0)])
                    f.op(f.pool, lambda e: e.tensor_tensor(out=o_bt[:, :, 0:n], in0=bd[:, :, 0:n], in1=Eng_[:, :, 0:n], op=ALU.mult),
                         reads=[B("rp_bd"), B("rp_Eneg")], writes=[B("rp_out", 1)])
                    f.op(f.dve, lambda e: e.tensor_tensor(out=o_kt[:, :, 0:n], in0=kd[:, :, 0:n], in1=Eng_[:, :, 0:n], op=ALU.mult),
                         reads=[B("rp_kd"), B("rp_Eneg")], writes=[B("rp_out", 2)])
                    f.op(f.pool, lambda e: e.tensor_tensor(out=o_rt[:, :, 0:n], in0=R_[:, :, 0:n], in1=Ein[:, :, 0:n], op=ALU.mult),
                         reads=[B("rp_rkv", 0), B("rp_Ein")], writes=[B("rp_out", 3)])
                    for i_, nm in enumerate(("At", "Bt", "Kt", "Rt")):
                        f.dma(f.sp, self.fm(S["%s%d" % (nm, d)])[:, :, t0:t0 + n], outs[i_][:, :, 0:n], reads=[B("rp_out", i_)])
                    col0 = 63 if d == 0 else 0
                    nch = n // 64
                    f.op(f.act, lambda e: e.copy(out=gct[:, :, 0:nch], in_=Ein[:, :, col0:n:64]), reads=[B("rp_Ein")], writes=[B("rp_gct")])
                    f.dma(f.sp, self.fm(S["gC%d" % d])[:, :, t0 // 64:t0 // 64 + nch], gct[:, :, 0:nch], reads=[B("rp_gct")])
                f.dma(f.sp, self.fm(S["rw_k01"])[:, :, t0:t0 + n], k01[:, :, 0:n], reads=[B("rp_k01")])
            f.barrier()

    def phase_rwkv_scan(self, l, last):
        nc, f = self.nc, self.f
        B = f.buf
        S = self.S
        cf = self.constf
        mk = self.maskf
        with contextlib.ExitStack() as es:
            sb = lambda n, s, d: es.enter_context(self.sbuf(n, s, d))
            ST = [sb("sc_ST%d" % d, [128, 8, 64], F32) for d in range(2)]
            gC = [sb("sc_gC%d" % d, [128, 8, NCHUNK], F32) for d in range(2)]
            names = ("At", "Bt", "Kt", "Rt", "V")
            inp = [[[sb("sc_%s%d_%d" % (nm, d, i), [128, 8, 128], F32) for nm in names] for i in range(2)] for d in range(2)]
            def mk64(nm, k=1):
                return [[[sb("sc_%s%d%d_%d" % (nm, d, h, i), [128, 256], F32) for i in range(k)] for h in range(2)] for d in range(2)]
            Xb = mk64("X", 2)
            XTb = mk64("XT", 2)
            Pb = mk64("P", 6)
            Lb = mk64("L", 3)
            Tk = mk64("Tk", 3)
            Wb = mk64("W", 2)
            Yo = mk64("Yo", 1)
            for d in range(2):
                f.op(f.pool, lambda e: e.memset(ST[d][:], 0.0), writes=[B("ST", d, 0), B("ST", d, 1)])
                f.dma(f.sp, gC[d][:], self.fm(S["gC%d" % d])[:, :, :], writes=[B("sc_gC", d)])
            order = [list(range(NCHUNK)), [3, 2, 1, 0] + list(range(NCHUNK - 1, 3, -1))]
            srcs = [[self.fm(S["%s%d" % (nm, d)]) for nm in ("At", "Bt", "Kt", "Rt")] + [self.fm(S["rw_v"])] for d in range(2)]
            yv = [self.fm(S["yf"]), self.fm(S["yb"])]
            self._psr = 0
            self._cp = 0
            cur_tile = [None, None]
            nload = [0, 0]

            def nps():
                i = self._psr % 8
                self._psr += 1
                return self.ps[i], self.psb[i]

            def evac(out_ap, in_ap, rd, wr):
                self._cp += 1
                if self._cp % 3 == 0:
                    f.op(f.dve, lambda e: e.tensor_copy(out=out_ap, in_=in_ap), reads=rd, writes=wr)
                else:
                    f.op(f.act, lambda e: e.copy(out=out_ap, in_=in_ap), reads=rd, writes=wr)

            def group(d, ch, half):
                tl, cc = ch // 2, ch % 2
                cs = slice(64 * cc, 64 * cc + 64)
                if cur_tile[d] != tl:
                    cur_tile[d] = tl
                    nload[d] += 1
                    bi = nload[d] % 2
                    for i_, nm in enumerate(names):
                        f.dma(f.sp, inp[d][bi][i_][:], srcs[d][i_][:, :, tl * 128:(tl + 1) * 128], writes=[B("sc_in", d, bi, i_)])
                bi = nload[d] % 2
                A_, Bm, Km, R_, Vv = inp[d][bi]
                bA, bB, bK, bR, bV = [B("sc_in", d, bi, i_) for i_ in range(5)]
                heads = [(4 * half + hpi, hh) for hpi in range(4) for hh in range(2)]
                fm_ = lambda T_, hp, hh: T_[64 * hh:64 * hh + 64, hp, cs]
                O = lambda T_, g8: T_[64 * (g8 % 2):64 * (g8 % 2) + 64, (g8 // 2) * 64:(g8 // 2) * 64 + 64]
                stb = B("ST", d, half)
                cst = B("constf")
                mkb = B("maskf")
                if d == 0:
                    mX, mXT, mL = M_LOS, M_UPS, M_UPI
                else:
                    mX, mXT, mL = M_UPS, M_LOS, M_LOI
                Vt_, Bt_, Kt_ = Tk[d][half]
                dh = (d, half)
                for j_, (src, sbuf_, dst) in enumerate(((Vv, bV, Vt_), (Bm, bB, Bt_), (Km, bK, Kt_))):
                    ps, pb = nps()
                    for g8, (hp, hh) in enumerate(heads):
                        f.op(f.pe, lambda e: e.matmul(O(ps, g8), lhsT=fm_(src, hp, hh),
                                                      rhs=cf[64 * hh:64 * hh + 64, C_ID + 64 * hh:C_ID + 64 * hh + 64], start=True, stop=True),
                             reads=[sbuf_, cst], writes=[pb], inc=(g8 == 7))
                    evac(dst[:, :], ps[:, 0:256], [pb], [B("sc_Tk", dh, j_)])
                    yield
                bVt, bBt, bKt = [B("sc_Tk", dh, j_) for j_ in range(3)]

                def mm8(out_rows, fn_l, fn_r, rd):
                    ps, pb = nps()
                    for g8, (hp, hh) in enumerate(heads):
                        f.op(f.pe, lambda e: e.matmul(O(ps, g8), lhsT=fn_l(g8, hp, hh), rhs=fn_r(g8, hp, hh), start=True, stop=True),
                             reads=rd, writes=[pb], inc=(g8 == 7))
                    return ps, pb

                def masked(ps, pb, dst, db, mcol):
                    f.op(f.dve, lambda e: e.tensor_tensor(out=dst[:, :], in0=ps[:, 0:256], in1=mk[:, mcol:mcol + 256], op=ALU.mult),
                         reads=[pb, mkb], writes=[db])
                X = Xb[d][half]
                XT = XTb[d][half]
                P = Pb[d][half]
                bX = [B("sc_X", dh, i_) for i_ in range(2)]
                bXT = [B("sc_XT", dh, i_) for i_ in range(2)]
                bP = [B("sc_P", dh, i_) for i_ in range(6)]
                bL = [B("sc_L", dh, i_) for i_ in range(3)]
                LakT, LrbT, LrkT = Lb[d][half]
                ps, pb = mm8(0, lambda g, hp, hh: fm_(A_, hp, hh), lambda g, hp, hh: fm_(Bm, hp, hh), [bA, bB])
                masked(ps, pb, X[0], bX[0], mX)
                yield
                ps, pb = mm8(0, lambda g, hp, hh: fm_(Bm, hp, hh), lambda g, hp, hh: fm_(A_, hp, hh), [bA, bB])
                masked(ps, pb, XT[0], bXT[0], mXT)
                f.op(f.pool, lambda e: e.tensor_tensor(out=P[0][:, :], in0=XT[0][:, :], in1=mk[:, M_ID:M_ID + 256], op=ALU.add),
                     reads=[bXT[0], mkb], writes=[bP[0]])
                yield
                ps, pb = mm8(0, lambda g, hp, hh: fm_(Km, hp, hh), lambda g, hp, hh: fm_(A_, hp, hh), [bA, bK])
                masked(ps, pb, LakT, bL[0], mXT)
                yield
                ps, pb = mm8(0, lambda g, hp, hh: fm_(Bm, hp, hh), lambda g, hp, hh: fm_(R_, hp, hh), [bR, bB])
                masked(ps, pb, LrbT, bL[1], mL)
                yield
                ps, pb = mm8(0, lambda g, hp, hh: fm_(Km, hp, hh), lambda g, hp, hh: fm_(R_, hp, hh), [bR, bK])
                masked(ps, pb, LrkT, bL[2], mL)
                yield
                for i_ in range(1, 6):
                    p_, c_ = (i_ - 1) % 2, i_ % 2
                    Xp, XTp = X[p_], XT[p_]
                    if i_ <= 4:
                        ps, pb = mm8(0, lambda g, hp, hh: O(XTp, g), lambda g, hp, hh: O(Xp, g), [bX[p_], bXT[p_]])
                        evac(X[c_][:, :], ps[:, 0:256], [pb], [bX[c_]])
                        yield
                    ps, pb = mm8(0, lambda g, hp, hh: O(Xp, g), lambda g, hp, hh: O(XTp, g), [bX[p_], bXT[p_]])
                    evac(XT[c_][:, :], ps[:, 0:256], [pb], [bXT[c_]])
                    f.op(f.pool, lambda e: e.tensor_tensor(out=P[i_][:, :], in0=XT[c_][:, :], in1=mk[:, M_ID:M_ID + 256], op=ALU.add),
                         reads=[bXT[c_], mkb], writes=[bP[i_]])
                    yield
                W = Wb[d][half]
                bW = [B("sc_W", dh, i_) for i_ in range(2)]
                ps, pb = nps()
                for g8, (hp, hh) in enumerate(heads):
                    f.op(f.pe, lambda e: e.matmul(O(ps, g8), lhsT=fm_(A_, hp, hh), rhs=ST[d][64 * hh:64 * hh + 64, hp, :],
                                                  start=True, stop=False), reads=[bA, stb], writes=[pb], inc=False)
                    f.op(f.pe, lambda e: e.matmul(O(ps, g8), lhsT=O(LakT, g8), rhs=O(Vt_, g8),
                                                  start=False, stop=True), reads=[bL[0], bVt], writes=[pb], inc=(g8 == 7))
                evac(W[0][:, :], ps[:, 0:256], [pb], [bW[0]])
                yield
                wi = 0
                for i_ in range(5, -1, -1):
                    Wc = W[wi]
                    ps, pb = mm8(0, lambda g, hp, hh: O(P[i_], g), lambda g, hp, hh: O(Wc, g), [bP[i_], bW[wi]])
                    wi ^= 1
                    evac(W[wi][:, :], ps[:, 0:256], [pb], [bW[wi]])
                    yield
                U = W[wi]
                bU = bW[wi]
                ps, pb = nps()
                for g8, (hp, hh) in enumerate(heads):
                    f.op(f.pe, lambda e: e.matmul(O(ps, g8), lhsT=ST[d][64 * hh:64 * hh + 64, hp, :], rhs=fm_(R_, hp, hh),
                                                  start=True, stop=False), reads=[bR, stb], writes=[pb], inc=False)
                    f.op(f.pe, lambda e: e.matmul(O(ps, g8), lhsT=O(U, g8), rhs=O(LrbT, g8),
                                                  start=False, stop=False), reads=[bU, bL[1]], writes=[pb], inc=False)
                    f.op(f.pe, lambda e: e.matmul(O(ps, g8), lhsT=O(Vt_, g8), rhs=O(LrkT, g8),
                                                  start=False, stop=True), reads=[bVt, bL[2]], writes=[pb], inc=(g8 == 7))
                yo = Yo[d][half][0]
                evac(yo[:, :], ps[:, 0:256], [pb], [B("sc_Yo", dh)])
                f.dma(f.sp, yv[d][:, 4 * half:4 * half + 4, ch * 64:(ch + 1) * 64], yo[:, :].rearrange("p (g t) -> p g t", t=64),
                      reads=[B("sc_Yo", dh)])
                yield
                ps, pb = nps()
                for g8, (hp, hh) in enumerate(heads):
                    o_ = O(ps, g8)
                    f.op(f.pe, lambda e: e.matmul(o_, lhsT=O(Bt_, g8), rhs=O(U, g8), start=True, stop=False),
                         reads=[bBt, bU], writes=[pb], inc=False)
                    f.op(f.pe, lambda e: e.matmul(o_, lhsT=O(Kt_, g8), rhs=O(Vt_, g8), start=False, stop=False),
                         reads=[bKt, bVt], writes=[pb], inc=False)
                    f.op(f.pe, lambda e: e.matmul(o_, lhsT=cf[64 * hh:64 * hh + 64, C_ID + 64 * hh:C_ID + 64 * hh + 64],
                                                  rhs=ST[d][64 * hh:64 * hh + 64, hp, :], start=False, stop=True),
                         reads=[cst, stb], writes=[pb], inc=(g8 == 7))
                for hpi in range(4):
                    hp = 4 * half + hpi
                    f.op(f.act, lambda e: e.activation(out=ST[d][:, hp, :], in_=ps[:, hpi * 64:(hpi + 1) * 64], func=AF.Copy,
                                                       scale=gC[d][:, hp, ch:ch + 1]),
                         reads=[pb, B("sc_gC", d)], writes=[stb])

            for s_ in range(getattr(self, "scan_steps", NCHUNK)):
                gens = [group(d, order[d][s_], half) for half in range(2) for d in range(2)]
                while gens:
                    alive = []
                    for g_ in gens:
                        try:
                            next(g_)
                            alive.append(g_)
                        except StopIteration:
                            pass
                    gens = alive
            f.barrier()

    def phase_rwkv_out(self, l, last):
        nc, f = self.nc, self.f
        B = f.buf
        I, S = self.I, self.S
        pv = self.fm(S["proj"])
        cf = self.constf
        with contextlib.ExitStack() as es:
            sb = lambda n, s, d: es.enter_context(self.sbuf(n, s, d))
            gup = sb("ro_gup", [128, D], F32)
            f.dma(f.sp, gup[:], I["gate_up"][l, :, :], writes=[B("ro_gup")])
            lg = sb("ro_lg", [128, 512], F32)
            tl = {}
            for nm in ("yf", "yb", "r", "k01", "v"):
                tl[nm] = [sb("ro_%s%d" % (nm, i), [128, 512], F32) for i in range(2)]
            sq = [sb("ro_sq%d" % i, [128, 512], F32) for i in range(2)]
            mean = [sb("ro_mean%d" % i, [128, 512], F32) for i in range(2)]
            var = [sb("ro_var%d" % i, [128, 512], F32) for i in range(2)]
            tt = [sb("ro_t%d" % i, [128, 512], F32) for i in range(2)]
            bon = [sb("ro_bon%d" % i, [128, 512], F32) for i in range(2)]
            ost = [sb("ro_o%d" % i, [128, 512], BF16) for i in range(2)]
            srcv = {"yf": self.fm(S["yf"]), "yb": self.fm(S["yb"]), "r": self.fm(S["rw_r"]), "k01": self.fm(S["rw_k01"]),
                    "v": self.fm(S["rw_v"])}
            ov = self.fm(S["rwo"])
            lgs = [lg, sb("ro_lg1", [128, 512], F32)]

            def unit(ti, t0, n, c, p):
                lg_ = lgs[ti % 2]
                lgb = B("ro_lg", ti % 2)
                if c == 0:
                    f.dma(f.sp, lg_[:, 0:n], pv[:, 54, t0:t0 + n], writes=[lgb])
                    f.op(f.act, lambda e: e.activation(out=lg_[:, 0:n], in_=lg_[:, 0:n], func=AF.Sigmoid), reads=[lgb], writes=[lgb])
                    yield
                bb = {nm: B("ro_" + nm, p) for nm in tl}
                for nm in tl:
                    f.dma(f.sp, tl[nm][p][:, 0:n], srcv[nm][:, c, t0:t0 + n], writes=[bb[nm]])
                yield
                y = tl["yf"][p]
                f.op(f.pool, lambda e: e.tensor_tensor(out=y[:, 0:n], in0=y[:, 0:n], in1=tl["yb"][p][:, 0:n], op=ALU.add),
                     reads=[bb["yf"], bb["yb"]], writes=[bb["yf"]])
                yield
                f.op(f.act, lambda e: e.activation(out=sq[p][:, 0:n], in_=y[:, 0:n], func=AF.Square), reads=[bb["yf"]], writes=[B("ro_sq", p)])
                yield
                r_ = tl["r"][p]
                f.op(f.dve, lambda e: e.scalar_tensor_tensor(out=r_[:, 0:n], in0=r_[:, 0:n], scalar=self.V(l, "r_k", c),
                                                             in1=tl["k01"][p][:, 0:n], op0=ALU.mult, op1=ALU.mult),
                     reads=[bb["r"], bb["k01"], B("vecs")], writes=[bb["r"]])
                yield
                q = 4 * p
                blk = cf[:, C_BLK:C_BLK + 128]
                f.op(f.pe, lambda e: e.matmul(self.ps[q][:, 0:n], lhsT=blk, rhs=y[:, 0:n], start=True, stop=True),
                     reads=[bb["yf"], B("constf")], writes=[self.psb[q]])
                yield
                f.op(f.pe, lambda e: e.matmul(self.ps[q + 1][:, 0:n], lhsT=blk, rhs=sq[p][:, 0:n], start=True, stop=True),
                     reads=[B("ro_sq", p), B("constf")], writes=[self.psb[q + 1]])
                yield
                f.op(f.pe, lambda e: e.matmul(self.ps[q + 2][:, 0:n], lhsT=blk, rhs=r_[:, 0:n], start=True, stop=True),
                     reads=[bb["r"], B("constf")], writes=[self.psb[q + 2]])
                yield
                f.op(f.pe, lambda e: e.matmul(self.ps[q + 3][:, 0:n], lhsT=gup[:, c * 128:(c + 1) * 128], rhs=lg_[:, 0:n], start=True, stop=True),
                     reads=[lgb, B("ro_gup")], writes=[self.psb[q + 3]])
                yield
                m_, v_, t_ = mean[p], var[p], tt[p]
                f.op(f.act, lambda e: e.activation(out=m_[:, 0:n], in_=self.ps[q][:, 0:n], func=AF.Copy, scale=1.0 / 64),
                     reads=[self.psb[q]], writes=[B("ro_mean", p)])
                yield
                f.op(f.dve, lambda e: e.tensor_tensor(out=v_[:, 0:n], in0=m_[:, 0:n], in1=m_[:, 0:n], op=ALU.mult),
                     reads=[B("ro_mean", p)], writes=[B("ro_var", p)])
                yield
                f.op(f.dve, lambda e: e.scalar_tensor_tensor(out=v_[:, 0:n], in0=self.ps[q + 1][:, 0:n], scalar=1.0 / 64, in1=v_[:, 0:n],
                                                             op0=ALU.mult, op1=ALU.subtract),
                     reads=[self.psb[q + 1], B("ro_var", p)], writes=[B("ro_var", p)])
                yield
                self.rstd_from_ps(v_[:, 0:n], B("ro_var", p), v_[:, 0:n], B("ro_var", p), 1.0, GN_EPS)
                yield
                f.op(f.dve, lambda e: e.tensor_tensor(out=t_[:, 0:n], in0=y[:, 0:n], in1=m_[:, 0:n], op=ALU.subtract),
                     reads=[bb["yf"], B("ro_mean", p)], writes=[B("ro_t", p)])
                yield
                f.op(f.pool, lambda e: e.tensor_tensor(out=t_[:, 0:n], in0=t_[:, 0:n], in1=v_[:, 0:n], op=ALU.mult),
                     reads=[B("ro_t", p), B("ro_var", p)], writes=[B("ro_t", p)])
                yield
                f.op(f.act, lambda e: e.activation(out=t_[:, 0:n], in_=t_[:, 0:n], func=AF.Identity,
                                                   bias=self.V(l, "gn_b", c), scale=self.V(l, "gn_g", c)),
                     reads=[B("ro_t", p), B("vecs")], writes=[B("ro_t", p)])
                yield
                f.op(f.dve, lambda e: e.tensor_tensor(out=bon[p][:, 0:n], in0=self.ps[q + 2][:, 0:n], in1=tl["v"][p][:, 0:n], op=ALU.mult),
                     reads=[self.psb[q + 2], bb["v"]], writes=[B("ro_bon", p)])
                yield
                f.op(f.pool, lambda e: e.tensor_tensor(out=t_[:, 0:n], in0=t_[:, 0:n], in1=bon[p][:, 0:n], op=ALU.add),
                     reads=[B("ro_t", p), B("ro_bon", p)], writes=[B("ro_t", p)])
                yield
                f.op(f.dve, lambda e: e.tensor_tensor(out=ost[p][:, 0:n], in0=self.ps[q + 3][:, 0:n], in1=t_[:, 0:n], op=ALU.mult),
                     reads=[self.psb[q + 3], B("ro_t", p)], writes=[B("ro_o", p)])
                yield
                f.dma(f.sp, ov[:, c, t0:t0 + n], ost[p][:, 0:n], reads=[B("ro_o", p)])
                yield

            units = [(ti, t0, n, c) for ti, (t0, n, s0, sl) in enumerate(self.tiles(self.seqs(last))) for c in range(8)]
            active = []
            nxt = 0
            free = [0]
            rnd = 0
            while nxt < len(units) or active:
                rnd += 1
                if rnd == 11:
                    free.append(1)
                while nxt < len(units) and free:
                    p_ = free.pop()
                    active.append((unit(*units[nxt], p_), p_))
                    nxt += 1
                still = []
                for g_, p_ in active:
                    try:
                        next(g_)
                        still.append((g_, p_))
                    except StopIteration:
                        free.append(p_)
                active = still
            f.barrier()


def _fm(v):
    v = np.asarray(v, np.float32).reshape(-1, 128)
    return np.ascontiguousarray(v.T)


def _consts():
    c = np.zeros((128, NCONST), np.float32)
    c[:, C_ONES:C_ONES + 128] = 1.0
    for h in range(2):
        c[64 * h:64 * h + 64, C_BLK + 64 * h:C_BLK + 64 * h + 64] = 1.0
    c[:, C_ID:C_ID + 128] = np.eye(128, dtype=np.float32)
    P = np.zeros((128, 128), np.float32)
    for m in range(128):
        if m % 64 < 32:
            P[m, m + 32] = -1.0
        else:
            P[m, m - 32] = 1.0
    c[:, C_ROT:C_ROT + 128] = P.T
    s = np.arange(128)[:, None]
    t = np.arange(128)[None, :]
    same = (s // 64) == (t // 64)
    c[:, C_TRIF:C_TRIF + 128] = (same & (s <= t))
    c[:, C_TRIF + 128:C_TRIF + 256] = (same & (s < t))
    c[:, C_TRIB:C_TRIB + 128] = (same & (s >= t))
    c[:, C_TRIB + 128:C_TRIB + 256] = (same & (s > t))
    r = np.arange(64)[:, None]
    q = np.arange(64)[None, :]
    m = np.zeros((128, 5 * 256), np.float32)
    m[:, M_LOS:M_LOS + 256] = np.tile((q < r).astype(np.float32), (2, 4))
    m[:, M_UPS:M_UPS + 256] = np.tile((q > r).astype(np.float32), (2, 4))
    m[:, M_LOI:M_LOI + 256] = np.tile((q <= r).astype(np.float32), (2, 4))
    m[:, M_UPI:M_UPI + 256] = np.tile((q >= r).astype(np.float32), (2, 4))
    m[:, M_ID:M_ID + 256] = np.tile(np.eye(64, dtype=np.float32), (2, 4))
    tt = np.arange(SEQ)
    row = (tt // 64).astype(np.float32)
    col = (tt % 64).astype(np.float32)
    inv = (10000.0 ** (-np.arange(32, dtype=np.float32) / 32)).astype(np.float32)
    cos = np.zeros((128, SEQ), np.float32)
    sin = np.zeros((128, SEQ), np.float32)
    for p in range(128):
        pos = row if p < 64 else col
        ang = (pos * inv[p % 32]).astype(np.float32)
        cos[p] = np.cos(ang)
        sin[p] = np.sin(ang)
    return c, m, cos, sin


def prep_inputs(inp):
    g = lambda k: np.asarray(inp[k], np.float32)
    vecs = np.zeros((DEPTH, 128, NV), np.float32)
    for l in range(DEPTH):
        def put(name, arr):
            a = np.asarray(arr, np.float32)
            vecs[l, :, VCOL[name]:VCOL[name] + a.shape[1]] = a
        put("b_mod", _fm(g("b_mod")[l]))
        for n in ("g_pre_mix", "g_post_mix", "g_pre_ffn", "g_post_ffn", "q_norm", "k_norm", "conv_b",
                  "k_k", "k_a"):
            put(n, _fm(g(n)[l]))
        put("conv_ln_g", _fm(g("conv_ln_g")[l]))
        put("conv_ln_b", _fm(g("conv_ln_b")[l]))
        put("gn_g", _fm(g("wkv_gn_g")[l]))
        put("gn_b", _fm(g("wkv_gn_b")[l]))
        put("r_k", _fm(g("r_k")[l].reshape(-1)))
        cw = g("conv_w")[l].reshape(31, 8, 128).transpose(2, 1, 0).reshape(128, 248)
        put("conv_w", cw)
        sw = g("shift_w")[l].reshape(3, 24, 128).transpose(2, 1, 0).reshape(128, 72)
        put("shift_w", sw)
        fw_ = g("ffn_conv_w")[l].reshape(3, 44, 128).transpose(2, 1, 0).reshape(128, 132)
        put("ffn_conv_w", fw_)
        a0 = g("iclr_a0")[l].reshape(2, 8, 128).transpose(2, 0, 1).reshape(128, 16)
        put("iclr_a0", a0)
    constf, maskf, cos, sin = _consts()
    shared = {
        "vecs": vecs,
        "w0row": np.ascontiguousarray(g("decay_w0").reshape(DEPTH, 1, 2048)),
        "constf": constf, "maskf": maskf, "cos": cos, "sin": sin,
        "w_mod": g("w_mod"), "w_in": g("w_in"),
        "w_attn_o": g("w_attn_o"), "w_conv_o": g("w_conv_o"), "w_rwkv_o": g("w_rwkv_o"), "w_out": g("w_out"),
        "decay_up": np.ascontiguousarray(g("decay_up").reshape(DEPTH, 128, D)),
        "iclr_up": np.ascontiguousarray(g("iclr_up").reshape(DEPTH, 128, D)),
        "gate_up": g("gate_up"),
        "w_ffn_up": g("w_ffn_up"), "w_ffn_down": g("w_ffn_down"),
    }
    x = g("x")
    ctx = g("ctx")
    c = g("c")
    c_ctx = g("c_ctx")
    per_core = []
    for b in range(x.shape[0]):
        xs0 = np.ascontiguousarray(np.concatenate([ctx[b].T, x[b].T], axis=1))
        cc = np.stack([c[b], c_ctx], axis=-1).reshape(8, 128, 2).transpose(1, 0, 2).reshape(128, 16)
        m = dict(shared)
        m["xs0"] = xs0
        m["cc"] = np.ascontiguousarray(cc)
        per_core.append(m)
    return per_core


def kernel(**inputs):
    per_core = prep_inputs(inputs)
    nc = Prog().build()
    res = run_bass_kernel_spmd(nc, per_core, core_ids=list(range(8)))
    outs = [np.ascontiguousarray(np.asarray(r["out"], np.float32).T) for r in res.results]
    return np.stack(outs, axis=0)
```

```python
import math
import contextlib
import numpy as np
import concourse.bass as bass
import concourse.mybir as mybir
from concourse.bass_utils import run_bass_kernel_spmd

F32 = mybir.dt.float32
BF16 = mybir.dt.bfloat16
AF = mybir.ActivationFunctionType
ALU = mybir.AluOpType

D = 1024
SEQ = 4096
CTX = 256
T = SEQ + CTX
DEPTH = 2
NIN = 10112
DFF = 2816
DECAY_SCALE = math.exp(-0.5)
EPS = 1e-6
LN_EPS = 1e-5
GN_EPS = 64 * 1e-5
CH = 64
NCHUNK = T // CH

VCOL = {}
_o = 0
for _n, _w in (("b_mod", 48), ("g_pre_mix", 8), ("g_post_mix", 8), ("g_pre_ffn", 8), ("g_post_ffn", 8),
               ("q_norm", 1), ("k_norm", 1), ("conv_w", 248), ("conv_b", 8), ("conv_ln_g", 8), ("conv_ln_b", 8),
               ("shift_w", 72), ("iclr_a0", 16), ("k_k", 8), ("k_a", 8), ("r_k", 8), ("gn_g", 8), ("gn_b", 8),
               ("ffn_conv_w", 132)):
    VCOL[_n] = _o
    _o += _w
NV = _o
C_ONES, C_BLK, C_ID, C_ROT, C_TRIF, C_TRIB = 0, 128, 256, 384, 512, 768
NCONST = 1024
M_LOS, M_UPS, M_LOI, M_UPI, M_ID = 0, 256, 512, 768, 1024


class Buf:
    __slots__ = ("w", "r")

    def __init__(self):
        self.w = None
        self.r = {}


class Eng:
    def __init__(self, name, e, sem):
        self.name = name
        self.e = e
        self.sem = sem
        self.count = 0
        self.seen = {}


class FW:
    SAME_ENG_DIST = 2

    def __init__(self, nc, es, n_dma_sems=24):
        self.nc = nc
        self.sems = {}

        def mk(name, e):
            s = es.enter_context(nc.semaphore("sem_" + name))
            self.sems[name] = s
            return Eng(name, e, s)
        self.pe = mk("pe", nc.tensor)
        self.act = mk("act", nc.scalar)
        self.dve = mk("dve", nc.vector)
        self.pool = mk("pool", nc.gpsimd)
        self.sp = mk("sp", nc.sync)
        self.engs = [self.pe, self.act, self.dve, self.pool, self.sp]
        self.dma_sems = []
        for i in range(n_dma_sems):
            nm = "dq%d" % i
            self.sems[nm] = es.enter_context(nc.semaphore("sem_" + nm))
            self.dma_sems.append([nm, 0])
        self.dma_rr = 0
        self.bufs = {}
        self.n_ins = 0

    def buf(self, *key):
        b = self.bufs.get(key)
        if b is None:
            b = Buf()
            self.bufs[key] = b
        return b

    def _need(self, eng, tok, raw):
        if tok is None:
            return
        sk, v = tok
        if sk == eng.name:
            if not (raw and (eng.count - v) < self.SAME_ENG_DIST):
                return
        if eng.seen.get(sk, 0) >= v:
            return
        eng.e.wait_ge(self.sems[sk], v)
        eng.seen[sk] = v

    def _deps(self, eng, reads, writes):
        for b in reads:
            self._need(eng, b.w, True)
        for b in writes:
            self._need(eng, b.w, False)
            for sk, v in b.r.items():
                self._need(eng, (sk, v), False)

    def _mark(self, tok, reads, writes):
        for b in reads:
            if b.r.get(tok[0], 0) < tok[1]:
                b.r[tok[0]] = tok[1]
        for b in writes:
            b.w = tok
            b.r = {}

    def op(self, eng, fn, reads=(), writes=(), inc=True):
        self._deps(eng, reads, writes)
        ins = fn(eng.e)
        self.n_ins += 1
        if inc:
            eng.count += 1
            ins.then_inc(eng.sem, 1)
            tok = (eng.name, eng.count)
        else:
            tok = (eng.name, eng.count + 1)
        self._mark(tok, reads, writes)
        return ins

    def dma(self, eng, out, in_, reads=(), writes=()):
        slot = self.dma_sems[self.dma_rr]
        self.dma_rr = (self.dma_rr + 1) % len(self.dma_sems)
        nm, cnt = slot
        self._deps(eng, reads, writes)
        self._need(eng, (nm, cnt), False)
        ins = eng.e.dma_start(out=out, in_=in_)
        slot[1] = cnt + 16
        ins.then_inc(self.sems[nm], 16)
        self.n_ins += 1
        self._mark((nm, cnt + 16), reads, writes)
        return ins

    def barrier(self):
        for e in self.engs:
            for o in self.engs:
                if o is not e and o.count > 0:
                    self._need(e, (o.name, o.count), False)
            for nm, cnt in self.dma_sems:
                if cnt > 0:
                    self._need(e, (nm, cnt), False)


class Prog:
    def __init__(self, debug=None, nlayers=DEPTH, stop_after=None):
        self.debug = debug or []
        self.nlayers = nlayers
        self.stop_after = stop_after
        self.nc = bass.Bass("TRN2", target_bir_lowering=False)
        self.uid = 0

    def din(self, name, shape, dt=F32):
        return self.nc.dram_tensor(name, list(shape), dt, kind="ExternalInput").ap()

    def dscr(self, name, shape, dt=F32):
        kind = "ExternalOutput" if name in self.debug else "Internal"
        return self.nc.dram_tensor(name, list(shape), dt, kind=kind).ap()

    def sbuf(self, name, shape, dt):
        self.uid += 1
        return self.nc.sbuf_tensor("%s_u%d" % (name, self.uid), shape, dt)

    def fm(self, ap):
        return ap.rearrange("(c p) t -> p c t", p=128)

    def build(self):
        nc = self.nc
        L = self.nlayers
        I = {}
        I["xs0"] = self.din("xs0", [D, T])
        I["cc"] = self.din("cc", [128, 16])
        I["vecs"] = self.din("vecs", [DEPTH, 128, NV])
        I["w0row"] = self.din("w0row", [DEPTH, 1, 2048])
        I["constf"] = self.din("constf", [128, NCONST])
        I["maskf"] = self.din("maskf", [128, 5 * 256])
        I["cos"] = self.din("cos", [128, SEQ])
        I["sin"] = self.din("sin", [128, SEQ])
        I["w_mod"] = self.din("w_mod", [DEPTH, D, 6 * D])
        I["w_in"] = self.din("w_in", [DEPTH, D, NIN])
        for n in ("w_attn_o", "w_conv_o", "w_rwkv_o", "w_out"):
            I[n] = self.din(n, [DEPTH, D, D])
        I["decay_up"] = self.din("decay_up", [DEPTH, 128, D])
        I["iclr_up"] = self.din("iclr_up", [DEPTH, 128, D])
        I["gate_up"] = self.din("gate_up", [DEPTH, 128, D])
        I["w_ffn_up"] = self.din("w_ffn_up", [DEPTH, D, 2 * DFF])
        I["w_ffn_down"] = self.din("w_ffn_down", [DEPTH, DFF, D])
        self.I = I
        out = nc.dram_tensor("out", [D, SEQ], F32, kind="ExternalOutput").ap()
        S = {}
        S["proj"] = self.dscr("proj", [NIN, T])
        S["att"] = self.dscr("att", [D, T], BF16)
        S["cnv"] = self.dscr("cnv", [D, T], BF16)
        S["rwo"] = self.dscr("rwo", [D, T], BF16)
        S["ffa"] = self.dscr("ffa", [DFF, T], BF16)
        for n in ("rw_r", "rw_v", "rw_k01", "yf", "yb"):
            S[n] = self.dscr(n, [D, T])
        for d in range(2):
            for n in ("At", "Bt", "Kt", "Rt"):
                S["%s%d" % (n, d)] = self.dscr("%s%d" % (n, d), [D, T])
            S["gC%d" % d] = self.dscr("gC%d" % d, [D, NCHUNK])
        S["xsm0"] = self.dscr("xsm0", [D, T])
        S["xs1"] = self.dscr("xs1", [D, T])
        S["xsm1"] = self.dscr("xsm1", [D, T])
        self.S = S

        with contextlib.ExitStack() as es:
            self.es = es
            f = FW(nc, es)
            self.f = f
            self.constf = es.enter_context(self.sbuf("constf", [128, NCONST], F32))
            self.maskf = es.enter_context(self.sbuf("maskf", [128, 5 * 256], F32))
            self.vecs = es.enter_context(self.sbuf("vecs", [128, DEPTH, NV], F32))
            self.modS = es.enter_context(self.sbuf("modS", [128, 48, 2], F32))
            self.dsc = es.enter_context(self.sbuf("dsc", [128, 64, 2], F32))
            self.onesb = es.enter_context(self.sbuf("onesb", [128, 128], BF16))
            self.ps = [es.enter_context(nc.psum_tensor("ps%d" % i, [128, 512], F32)) for i in range(8)]
            self.psb = [f.buf("ps", i) for i in range(8)]
            B = f.buf
            f.dma(f.sp, self.constf[:], I["constf"][:, :], writes=[B("constf")])
            f.dma(f.sp, self.maskf[:], I["maskf"][:, :], writes=[B("maskf")])
            for l in range(DEPTH):
                f.dma(f.sp, self.vecs[:, l, :], I["vecs"][l, :, :], writes=[B("vecs")])
            f.op(f.dve, lambda e: e.tensor_copy(out=self.onesb[:], in_=self.constf[:, C_ONES:C_ONES + 128]),
                 reads=[B("constf")], writes=[B("onesb")])
            f.barrier()
            xs_in = I["xs0"]
            for l in range(L):
                last = (l == DEPTH - 1)
                xsm = S["xsm%d" % l]
                xs_out = out if last else S["xs1"]
                self.phase_mod(l)
                if self.stop_after == ("mod", l): break
                self.phase_inproj(l, xs_in, last)
                if self.stop_after == ("inproj", l): break
                self.phase_attn(l, last)
                if self.stop_after == ("attn", l): break
                self.phase_conv(l, last)
                if self.stop_after == ("conv", l): break
                self.phase_rwkv_prep(l)
                if self.stop_after == ("rwprep", l): break
                self.phase_rwkv_scan(l, last)
                if self.stop_after == ("rwscan", l): break
                self.phase_rwkv_out(l, last)
                if self.stop_after == ("rwout", l): break
                self.phase_merge(l, xs_in, xsm, last)
                if self.stop_after == ("merge", l): break
                self.phase_ffn_up(l, xsm, last)
                if self.stop_after == ("ffnup", l): break
                self.phase_ffn_down(l, xsm, xs_out, last)
                xs_in = xs_out
            f.barrier()
        return nc

    def V(self, l, name, c=0, n=1):
        o = VCOL[name] + c
        return self.vecs[:, l, o:o + n]

    def seqs(self, last):
        return [(CTX, SEQ)] if last else [(0, CTX), (CTX, SEQ)]

    def tiles(self, seqs, n=512):
        r = []
        for s0, sl in seqs:
            t = s0
            while t < s0 + sl:
                m = min(n, s0 + sl - t)
                r.append((t, m, s0, sl))
                t += m
        return r

    def rstd_from_ps(self, ps_ap, psbuf, out_ap, outbuf, scale, eps):
        f = self.f
        f.op(f.act, lambda e: e.activation(out=out_ap, in_=ps_ap, func=AF.Ln, bias=float(eps), scale=float(scale)),
             reads=[psbuf], writes=[outbuf])
        f.op(f.act, lambda e: e.activation(out=out_ap, in_=out_ap, func=AF.Exp, scale=-0.5),
             reads=[outbuf], writes=[outbuf])

    def phase_mod(self, l):
        nc, f, es0 = self.nc, self.f, self.es
        B = f.buf
        I = self.I
        with contextlib.ExitStack() as es:
            cT = es.enter_context(self.sbuf("cT", [128, 8, 2], F32))
            wt = [es.enter_context(self.sbuf("wmod%d" % i, [128, 8, 1024], F32)) for i in range(2)]
            f.dma(f.sp, cT[:], I["cc"].rearrange("p (k j) -> p k j", j=2), writes=[B("cT")])
            f.op(f.act, lambda e: e.activation(out=cT[:], in_=cT[:], func=AF.Silu), reads=[B("cT")], writes=[B("cT")])
            wv = I["w_mod"][l].rearrange("(k p) n -> p k n", p=128)
            ps = self.ps[0]
            for g in range(6):
                w = wt[g % 2]
                for k in range(8):
                    f.dma(f.sp, w[:, k, :], wv[:, k, g * 1024:(g + 1) * 1024], writes=[B("wmod", g % 2, k)])
                for oc in range(8):
                    col = (g * 8 + oc) * 2
                    for k in range(8):
                        f.op(f.pe, lambda e: e.matmul(ps[:, col:col + 2], lhsT=w[:, k, oc * 128:(oc + 1) * 128],
                                                      rhs=cT[:, k, :], start=(k == 0), stop=(k == 7)),
                             reads=[B("wmod", g % 2, k), B("cT")], writes=[self.psb[0]], inc=(k == 7))
            psv = ps[:, 0:96].rearrange("p (c j) -> p c j", j=2)
            for j in range(2):
                f.op(f.dve, lambda e: e.tensor_tensor(out=self.modS[:, :, j], in0=psv[:, :, j],
                                                      in1=self.V(l, "b_mod", 0, 48), op=ALU.add),
                     reads=[self.psb[0], B("vecs")], writes=[B("modS")])
            for j in range(2):
                def m(i):
                    return self.modS[:, i * 8:(i + 1) * 8, j]
                rd = [B("modS"), B("vecs")]
                wr = [B("dsc")]
                f.op(f.dve, lambda e: e.scalar_tensor_tensor(out=self.dsc[:, 0:8, j], in0=m(1), scalar=1.0,
                                                             in1=self.V(l, "g_pre_mix", 0, 8), op0=ALU.add, op1=ALU.mult),
                     reads=rd, writes=wr)
                f.op(f.dve, lambda e: e.tensor_copy(out=self.dsc[:, 8:16, j], in_=m(0)), reads=rd, writes=wr)
                f.op(f.dve, lambda e: e.tensor_tensor(out=self.dsc[:, 16:24, j], in0=m(2),
                                                      in1=self.V(l, "g_post_mix", 0, 8), op=ALU.mult), reads=rd, writes=wr)
                f.op(f.dve, lambda e: e.scalar_tensor_tensor(out=self.dsc[:, 24:32, j], in0=m(4), scalar=1.0,
                                                             in1=self.V(l, "g_pre_ffn", 0, 8), op0=ALU.add, op1=ALU.mult),
                     reads=rd, writes=wr)
                f.op(f.dve, lambda e: e.tensor_copy(out=self.dsc[:, 32:40, j], in_=m(3)), reads=rd, writes=wr)
                f.op(f.dve, lambda e: e.tensor_tensor(out=self.dsc[:, 40:48, j], in0=m(5),
                                                      in1=self.V(l, "g_post_ffn", 0, 8), op=ALU.mult), reads=rd, writes=wr)
            f.barrier()

    def prenorm(self, es, src, seqs, base, hT, hcol_of):
        nc, f = self.nc, self.f
        B = f.buf
        xt = [es.enter_context(self.sbuf("pn_x%d" % i, [128, 8, 512], F32)) for i in range(2)]
        sq = es.enter_context(self.sbuf("pn_sq", [128, 8, 512], F32))
        rs = es.enter_context(self.sbuf("pn_rs", [128, 512], F32))
        srcv = self.fm(src)
        for i, (t0, n, s0, sl) in enumerate(self.tiles(seqs)):
            j = 1 if s0 == 0 else 0
            x = xt[i % 2]
            xb = B("pn_x", i % 2)
            f.dma(f.sp, x[:, :, 0:n], srcv[:, :, t0:t0 + n], writes=[xb])
            f.op(f.act, lambda e: e.activation(out=sq[:, :, 0:n], in_=x[:, :, 0:n], func=AF.Square),
                 reads=[xb], writes=[B("pn_sq")])
            ps, pb = self.ps[7], self.psb[7]
            for k in range(8):
                f.op(f.pe, lambda e: e.matmul(ps[:, 0:n], lhsT=self.constf[:, C_ONES:C_ONES + 128], rhs=sq[:, k, 0:n],
                                              start=(k == 0), stop=(k == 7)),
                     reads=[B("pn_sq"), B("constf")], writes=[pb], inc=(k == 7))
            self.rstd_from_ps(ps[:, 0:n], pb, rs[:, 0:n], B("pn_rs"), 1.0 / D, EPS)
            c0 = hcol_of(t0)
            for k in range(8):
                f.op(f.dve, lambda e: e.scalar_tensor_tensor(out=x[:, k, 0:n], in0=x[:, k, 0:n],
                                                             scalar=self.dsc[:, base + k, j:j + 1], in1=rs[:, 0:n],
                                                             op0=ALU.mult, op1=ALU.mult),
                     reads=[xb, B("pn_rs"), B("dsc")], writes=[xb])
                f.op(f.act, lambda e: e.activation(out=hT[:, k, c0:c0 + n], in_=x[:, k, 0:n], func=AF.Identity,
                                                   bias=self.dsc[:, base + 8 + k, j:j + 1], scale=1.0),
                     reads=[xb, B("dsc")], writes=[B("hT")])

    def phase_inproj(self, l, xs_in, last):
        nc, f = self.nc, self.f
        B = f.buf
        I, S = self.I, self.S
        seqs = [(0, CTX), (CTX, SEQ)]
        with contextlib.ExitStack() as es:
            hT = es.enter_context(self.sbuf("hT", [128, 8, T], BF16))
            with contextlib.ExitStack() as es2:
                self.prenorm(es2, xs_in, seqs, 0, hT, lambda t: t)
                f.barrier()
            wt = [es.enter_context(self.sbuf("win%d" % i, [128, 8, 1024], BF16)) for i in range(2)]
            st = [es.enter_context(self.sbuf("ipst%d" % i, [128, 512], F32)) for i in range(4)]
            wv = I["w_in"][l].rearrange("(k p) n -> p k n", p=128)
            pv = self.fm(S["proj"])
            ngrp = (NIN + 1023) // 1024
            cnt = 0
            for g in range(ngrp):
                ncol = min(1024, NIN - g * 1024)
                w = wt[g % 2]
                for k in range(8):
                    f.dma(f.pool, w[:, k, 0:ncol], wv[:, k, g * 1024:g * 1024 + ncol], writes=[B("win", g % 2, k)])
                for (t0, n, s0, sl) in self.tiles(seqs):
                    for oc in range(ncol // 128):
                        pi = cnt % 6
                        ps, pb = self.ps[pi], self.psb[pi]
                        for k in range(8):
                            f.op(f.pe, lambda e: e.matmul(ps[:, 0:n], lhsT=w[:, k, oc * 128:(oc + 1) * 128],
                                                          rhs=hT[:, k, t0:t0 + n], start=(k == 0), stop=(k == 7)),
                                 reads=[B("win", g % 2, k), B("hT")], writes=[pb], inc=(k == 7))
                        s = st[cnt % 4]
                        sb = B("ipst", cnt % 4)
                        if cnt % 2 == 0:
                            f.op(f.act, lambda e: e.copy(out=s[:, 0:n], in_=ps[:, 0:n]), reads=[pb], writes=[sb])
                        else:
                            f.op(f.dve, lambda e: e.tensor_copy(out=s[:, 0:n], in_=ps[:, 0:n]), reads=[pb], writes=[sb])
                        f.dma(f.sp, pv[:, g * 8 + oc, t0:t0 + n], s[:, 0:n], reads=[sb])
                        cnt += 1
            f.barrier()


    def phase_attn(self, l, last):
        nc, f = self.nc, self.f
        B = f.buf
        I, S = self.I, self.S
        pv = self.fm(S["proj"])
        av = self.fm(S["att"])
        cf = self.constf
        with contextlib.ExitStack() as es:
            sb = lambda n, s, d: es.enter_context(self.sbuf(n, s, d))
            kT = sb("kT", [128, 2, T], BF16)
            Vt = sb("Vt", [128, T // 128, 2, 128], BF16)
            cos = sb("cos", [128, SEQ], F32)
            sin = sb("sin", [128, SEQ], F32)
            qg = sb("qg", [128, 1], F32)
            raw = [sb("a_raw%d" % i, [128, 512], F32) for i in range(2)]
            sq = sb("a_sq", [128, 512], F32)
            rs = sb("a_rs", [128, 512], F32)
            kn = sb("a_kn", [128, 512], F32)
            t1 = sb("a_t1", [128, 512], F32)
            t2 = sb("a_t2", [128, 512], F32)
            qT = [sb("a_qT%d" % i, [128, 512], BF16) for i in range(2)]
            pT = [sb("a_pT%d" % i, [128, 512], BF16) for i in range(4)]
            rinv = sb("a_rinv", [128, 512], F32)
            ost = [sb("a_ost%d" % i, [128, 512], BF16) for i in range(2)]
            f.dma(f.sp, cos[:], I["cos"][:, :], writes=[B("cos")])
            f.dma(f.sp, sin[:], I["sin"][:, :], writes=[B("sin")])
            f.op(f.dve, lambda e: e.tensor_scalar(out=qg[:], in0=self.V(l, "q_norm"), scalar1=float(128 ** -0.5),
                                                  scalar2=None, op0=ALU.mult), reads=[B("vecs")], writes=[B("qg")])
            self._nr = 0

            def normrope_g(chunk, t0, n, gain, is_x, out_ap, outbuf):
                i = self._nr
                self._nr += 1
                r = raw[i % 2]
                rb = B("a_raw", i % 2)
                f.dma(f.sp, r[:, 0:n], pv[:, chunk, t0:t0 + n], writes=[rb])
                f.op(f.act, lambda e: e.activation(out=sq[:, 0:n], in_=r[:, 0:n], func=AF.Square),
                     reads=[rb], writes=[B("a_sq")])
                f.op(f.pe, lambda e: e.matmul(self.ps[7][:, 0:n], lhsT=cf[:, C_ONES:C_ONES + 128], rhs=sq[:, 0:n],
                                              start=True, stop=True), reads=[B("a_sq"), B("constf")], writes=[self.psb[7]])
                yield
                self.rstd_from_ps(self.ps[7][:, 0:n], self.psb[7], rs[:, 0:n], B("a_rs"), 1.0 / 128, EPS)
                f.op(f.dve, lambda e: e.scalar_tensor_tensor(out=kn[:, 0:n], in0=r[:, 0:n], scalar=gain, in1=rs[:, 0:n],
                                                             op0=ALU.mult, op1=ALU.mult),
                     reads=[rb, B("a_rs"), B("vecs"), B("qg")], writes=[B("a_kn")])
                yield
                if is_x:
                    p0 = t0 - CTX
                    f.op(f.pe, lambda e: e.matmul(self.ps[7][:, 0:n], lhsT=cf[:, C_ROT:C_ROT + 128], rhs=kn[:, 0:n],
                                                  start=True, stop=True), reads=[B("a_kn"), B("constf")], writes=[self.psb[7]])
                    yield
                    f.op(f.pool, lambda e: e.tensor_tensor(out=t1[:, 0:n], in0=kn[:, 0:n], in1=cos[:, p0:p0 + n], op=ALU.mult),
                         reads=[B("a_kn"), B("cos")], writes=[B("a_t1")])
                    f.op(f.dve, lambda e: e.tensor_tensor(out=t2[:, 0:n], in0=self.ps[7][:, 0:n], in1=sin[:, p0:p0 + n], op=ALU.mult),
                         reads=[self.psb[7], B("sin")], writes=[B("a_t2")])
                    f.op(f.pool, lambda e: e.tensor_tensor(out=out_ap, in0=t1[:, 0:n], in1=t2[:, 0:n], op=ALU.add),
                         reads=[B("a_t1"), B("a_t2")], writes=[outbuf])
                else:
                    f.op(f.pool, lambda e: e.tensor_copy(out=out_ap, in_=kn[:, 0:n]), reads=[B("a_kn")], writes=[outbuf])

            def normrope(*a_):
                for _ in normrope_g(*a_):
                    pass

            allseq = [(0, CTX), (CTX, SEQ)]
            for kvh in range(2):
                for (t0, n, s0, sl) in self.tiles(allseq):
                    normrope(8 + kvh, t0, n, self.V(l, "k_norm"), s0 != 0, kT[:, kvh, t0:t0 + n], B("kT"))
            i = 0
            for kvh in range(2):
                for (t0, n, s0, sl) in self.tiles(allseq):
                    r = raw[i % 2]
                    rb = B("a_raw", i % 2)
                    i += 1
                    f.dma(f.sp, r[:, 0:n], pv[:, 10 + kvh, t0:t0 + n], writes=[rb])
                    nb = n // 128
                    for j in range(nb):
                        f.op(f.pe, lambda e: e.transpose(self.ps[7][:, j * 128:(j + 1) * 128], r[:, j * 128:(j + 1) * 128],
                                                         cf[:, C_ID:C_ID + 128]),
                             reads=[rb, B("constf")], writes=[self.psb[7]], inc=(j == nb - 1))
                    b0 = t0 // 128
                    f.op(f.dve, lambda e: e.tensor_copy(out=Vt[:, b0:b0 + nb, kvh, :],
                                                        in_=self.ps[7][:, 0:nb * 128].rearrange("p (b d) -> p b d", d=128)),
                         reads=[self.psb[7]], writes=[B("Vt")])
            units = [(h, t0, n, s0) for h in range(8) for (t0, n, s0, sl) in self.tiles(self.seqs(last))]

            def qprep(ui):
                h, t0, n, s0 = units[ui]
                return normrope_g(h, t0, n, qg[:, 0:1], s0 != 0, qT[ui % 2][:, 0:n], B("a_qT", ui % 2))
            for _ in qprep(0):
                pass
            for qi, (h, t0, n, s0) in enumerate(units):
                kvh = h // 4
                is_x = s0 != 0
                q = qT[qi % 2]
                qb = B("a_qT", qi % 2)
                nxt = qprep(qi + 1) if qi + 1 < len(units) else None
                nblk = (T // 128) if is_x else (CTX // 128)
                po, pob = self.ps[3 + qi % 2], self.psb[3 + qi % 2]
                pr, prb = self.ps[5 + qi % 2], self.psb[5 + qi % 2]

                def smm(jb):
                    f.op(f.pe, lambda e: e.matmul(self.ps[jb % 3][:, 0:n], lhsT=kT[:, kvh, jb * 128:(jb + 1) * 128],
                                                  rhs=q[:, 0:n], start=True, stop=True),
                         reads=[B("kT"), qb], writes=[self.psb[jb % 3]])

                def pvmm(jb):
                    p = pT[jb % 4]
                    pb = B("a_pT", jb % 4)
                    lastb = (jb == nblk - 1)
                    f.op(f.pe, lambda e: e.matmul(po[:, 0:n], lhsT=Vt[:, jb, kvh, :], rhs=p[:, 0:n],
                                                  start=(jb == 0), stop=lastb),
                         reads=[B("Vt"), pb], writes=[pob], inc=False)
                    f.op(f.pe, lambda e: e.matmul(pr[:, 0:n], lhsT=self.onesb[:], rhs=p[:, 0:n],
                                                  start=(jb == 0), stop=lastb),
                         reads=[B("onesb"), pb], writes=[prb], inc=lastb)
                smm(0)
                if nblk > 1:
                    smm(1)
                for jb in range(nblk):
                    p = pT[jb % 4]
                    pb = B("a_pT", jb % 4)
                    f.op(f.act, lambda e: e.activation(out=p[:, 0:n], in_=self.ps[jb % 3][:, 0:n], func=AF.Exp),
                         reads=[self.psb[jb % 3]], writes=[pb])
                    if jb + 2 < nblk:
                        smm(jb + 2)
                    if jb >= 1:
                        pvmm(jb - 1)
                    if nxt is not None and jb in (4, 10, 16, 22):
                        try:
                            next(nxt)
                        except StopIteration:
                            nxt = None
                pvmm(nblk - 1)
                if nxt is not None:
                    for _ in nxt:
                        pass
                f.op(f.dve, lambda e: e.reciprocal(out=rinv[:, 0:n], in_=pr[:, 0:n]), reads=[prb], writes=[B("a_rinv")])
                o = ost[qi % 2]
                ob = B("a_ost", qi % 2)
                f.op(f.dve, lambda e: e.tensor_tensor(out=o[:, 0:n], in0=po[:, 0:n], in1=rinv[:, 0:n], op=ALU.mult),
                     reads=[pob, B("a_rinv")], writes=[ob])
                f.dma(f.sp, av[:, h, t0:t0 + n], o[:, 0:n], reads=[ob])
            f.barrier()

    def phase_conv(self, l, last):
        nc, f = self.nc, self.f
        B = f.buf
        S = self.S
        pv = self.fm(S["proj"])
        cv = self.fm(S["cnv"])
        cf = self.constf
        with contextlib.ExitStack() as es:
            sb = lambda n, s, d: es.enter_context(self.sbuf(n, s, d))
            at = [sb("c_a%d" % i, [128, 544], F32) for i in range(2)]
            bt = [sb("c_b%d" % i, [128, 544], F32) for i in range(2)]
            y = sb("c_y", [128, 8, 512], F32)
            sq = [sb("c_sq%d" % i, [128, 512], F32) for i in range(2)]
            mean = sb("c_mean", [128, 512], F32)
            msq = sb("c_msq", [128, 512], F32)
            rstd = sb("c_rstd", [128, 512], F32)
            tt = [sb("c_t%d" % i, [128, 512], F32) for i in range(2)]
            ost = [sb("c_o%d" % i, [128, 512], BF16) for i in range(2)]
            gbf = [sb("c_gb%d" % i, [128, 544], BF16) for i in range(2)]
            dg = sb("c_dg", [128, 8, 31, 128], BF16)
            k_ = 0
            for c in range(8):
                for j in range(31):
                    wj = self.V(l, "conv_w", c * 31 + j)
                    e3 = k_ % 3
                    k_ += 1
                    if e3 == 0:
                        f.op(f.pool, lambda e: e.tensor_scalar(out=dg[:, c, j, :], in0=cf[:, C_ID:C_ID + 128], scalar1=wj, scalar2=None,
                                                               op0=ALU.mult), reads=[B("constf"), B("vecs")], writes=[B("c_dg", c, 0)])
                    elif e3 == 1:
                        f.op(f.dve, lambda e: e.tensor_scalar(out=dg[:, c, j, :], in0=cf[:, C_ID:C_ID + 128], scalar1=wj, scalar2=None,
                                                              op0=ALU.mult), reads=[B("constf"), B("vecs")], writes=[B("c_dg", c, 1)])
                    else:
                        f.op(f.act, lambda e: e.activation(out=dg[:, c, j, :], in_=cf[:, C_ID:C_ID + 128], func=AF.Copy, scale=wj),
                             reads=[B("constf"), B("vecs")], writes=[B("c_dg", c, 2)])
            it = 0
            for (t0, n, s0, sl) in self.tiles(self.seqs(last)):
                lo = max(t0 - 15, s0)
                hi = min(t0 + n + 15, s0 + sl)
                off = lo - (t0 - 15)
                edge = (lo != t0 - 15) or (hi != t0 + n + 15)
                for c in range(8):
                    a = at[it % 2]
                    b = bt[it % 2]
                    ab = B("c_a", it % 2)
                    bb = B("c_b", it % 2)
                    it += 1
                    if edge:
                        f.op(f.pool, lambda e: e.memset(a[:, 0:n + 30], 0.0), writes=[ab])
                        f.op(f.pool, lambda e: e.memset(b[:, 0:n + 30], 0.0), writes=[bb])
                    f.dma(f.sp, a[:, off:off + hi - lo], pv[:, 12 + c, lo:hi], writes=[ab])
                    f.dma(f.sp, b[:, off:off + hi - lo], pv[:, 20 + c, lo:hi], writes=[bb])
                    f.op(f.act, lambda e: e.activation(out=b[:, 0:n + 30], in_=b[:, 0:n + 30], func=AF.Sigmoid),
                         reads=[bb], writes=[bb])
                    gb_ = gbf[it % 2]
                    gbb = B("c_gb", it % 2)
                    f.op(f.pool, lambda e: e.tensor_tensor(out=gb_[:, 0:n + 30], in0=a[:, 0:n + 30], in1=b[:, 0:n + 30], op=ALU.mult),
                         reads=[ab, bb], writes=[gbb])
                    yb = B("c_y", c)
                    pi = 2 + it % 4
                    for j in range(31):
                        f.op(f.pe, lambda e: e.matmul(self.ps[pi][:, 0:n], lhsT=dg[:, c, j, :], rhs=gb_[:, j:j + n],
                                                      start=(j == 0), stop=(j == 30)),
                             reads=[gbb, B("c_dg", c, 0), B("c_dg", c, 1), B("c_dg", c, 2)], writes=[self.psb[pi]], inc=(j == 30))
                    f.op(f.act, lambda e: e.activation(out=y[:, c, 0:n], in_=self.ps[pi][:, 0:n], func=AF.Identity,
                                                       bias=self.V(l, "conv_b", c), scale=1.0),
                         reads=[self.psb[pi], B("vecs")], writes=[yb])
                    s = sq[c % 2]
                    sqb = B("c_sq", c % 2)
                    f.op(f.act, lambda e: e.activation(out=s[:, 0:n], in_=y[:, c, 0:n], func=AF.Square), reads=[yb], writes=[sqb])
                    f.op(f.pe, lambda e: e.matmul(self.ps[0][:, 0:n], lhsT=cf[:, C_ONES:C_ONES + 128], rhs=y[:, c, 0:n],
                                                  start=(c == 0), stop=(c == 7)), reads=[yb, B("constf")], writes=[self.psb[0]], inc=False)
                    f.op(f.pe, lambda e: e.matmul(self.ps[1][:, 0:n], lhsT=cf[:, C_ONES:C_ONES + 128], rhs=s[:, 0:n],
                                                  start=(c == 0), stop=(c == 7)), reads=[sqb, B("constf")], writes=[self.psb[1]])
                f.op(f.act, lambda e: e.activation(out=mean[:, 0:n], in_=self.ps[0][:, 0:n], func=AF.Copy, scale=1.0 / D),
                     reads=[self.psb[0]], writes=[B("c_mean")])
                f.op(f.dve, lambda e: e.tensor_tensor(out=msq[:, 0:n], in0=mean[:, 0:n], in1=mean[:, 0:n], op=ALU.mult),
                     reads=[B("c_mean")], writes=[B("c_msq")])
                f.op(f.dve, lambda e: e.scalar_tensor_tensor(out=rstd[:, 0:n], in0=self.ps[1][:, 0:n], scalar=1.0 / D,
                                                             in1=msq[:, 0:n], op0=ALU.mult, op1=ALU.subtract),
                     reads=[self.psb[1], B("c_msq")], writes=[B("c_rstd")])
                self.rstd_from_ps(rstd[:, 0:n], B("c_rstd"), rstd[:, 0:n], B("c_rstd"), 1.0, LN_EPS)
                for c in range(8):
                    t = tt[c % 2]
                    tb = B("c_t", c % 2)
                    f.op(f.dve, lambda e: e.tensor_tensor(out=t[:, 0:n], in0=y[:, c, 0:n], in1=mean[:, 0:n], op=ALU.subtract),
                         reads=[B("c_y", c), B("c_mean")], writes=[tb])
                    f.op(f.pool, lambda e: e.tensor_tensor(out=t[:, 0:n], in0=t[:, 0:n], in1=rstd[:, 0:n], op=ALU.mult),
                         reads=[tb, B("c_rstd")], writes=[tb])
                    o = ost[c % 2]
                    ob = B("c_o", c % 2)
                    f.op(f.act, lambda e: e.activation(out=o[:, 0:n], in_=t[:, 0:n], func=AF.Silu,
                                                       bias=self.V(l, "conv_ln_b", c), scale=self.V(l, "conv_ln_g", c)),
                         reads=[tb, B("vecs")], writes=[ob])
                    f.dma(f.sp, cv[:, c, t0:t0 + n], o[:, 0:n], reads=[ob])
            f.barrier()

    def post_residual(self, es, mo, mob, xt, xtb, base, j, dstv, c0, n, sq, rs):
        f = self.f
        B = f.buf
        cf = self.constf
        for k in range(8):
            s = sq[k % 2]
            sqb = B("pr_sq", k % 2)
            f.op(f.act, lambda e: e.activation(out=s[:, 0:n], in_=mo[:, k, 0:n], func=AF.Square), reads=[mob], writes=[sqb])
            f.op(f.pe, lambda e: e.matmul(self.ps[7][:, 0:n], lhsT=cf[:, C_ONES:C_ONES + 128], rhs=s[:, 0:n],
                                          start=(k == 0), stop=(k == 7)), reads=[sqb, B("constf")], writes=[self.psb[7]])
        self.rstd_from_ps(self.ps[7][:, 0:n], self.psb[7], rs[:, 0:n], B("pr_rs"), 1.0 / D, EPS)
        for k in range(8):
            f.op(f.pool, lambda e: e.tensor_tensor(out=mo[:, k, 0:n], in0=mo[:, k, 0:n], in1=rs[:, 0:n], op=ALU.mult),
                 reads=[mob, B("pr_rs")], writes=[mob])
            f.op(f.dve, lambda e: e.scalar_tensor_tensor(out=xt[:, k, 0:n], in0=mo[:, k, 0:n],
                                                         scalar=self.dsc[:, base + k, j:j + 1], in1=xt[:, k, 0:n],
                                                         op0=ALU.mult, op1=ALU.add),
                 reads=[mob, xtb, B("dsc")], writes=[xtb])
        f.dma(f.sp, dstv[:, :, c0:c0 + n], xt[:, :, 0:n], reads=[xtb])

    def phase_merge(self, l, xs_in, xsm, last):
        nc, f = self.nc, self.f
        B = f.buf
        I, S = self.I, self.S
        pv = self.fm(S["proj"])
        with contextlib.ExitStack() as es:
            sb = lambda n, s, d: es.enter_context(self.sbuf(n, s, d))
            W = [sb("m_w%d" % i, [128, 8, 1024], BF16) for i in range(4)]
            for i, nm in enumerate(("w_attn_o", "w_conv_o", "w_rwkv_o", "w_out")):
                wv = I[nm][l].rearrange("(k p) n -> p k n", p=128)
                for k in range(8):
                    f.dma(f.pool, W[i][:, k, :], wv[:, k, :], writes=[B("m_w", i, k)])
            br = [sb("m_br%d" % i, [128, 8, 512], BF16) for i in range(3)]
            gt = [sb("m_g%d" % i, [128, 512], F32) for i in range(6)]
            tt = [sb("m_t%d" % i, [128, 512], F32) for i in range(6)]
            mT = sb("m_mT", [128, 8, 512], BF16)
            mo = sb("m_mo", [128, 8, 512], F32)
            xt = sb("m_xt", [128, 8, 512], F32)
            sq = [sb("m_sq%d" % i, [128, 512], F32) for i in range(2)]
            rs = sb("m_rs", [128, 512], F32)
            srcs = [self.fm(S["att"]), self.fm(S["cnv"]), self.fm(S["rwo"])]
            xv = self.fm(xs_in)
            dv = self.fm(xsm)
            for (t0, n, s0, sl) in self.tiles(self.seqs(last)):
                j = 1 if s0 == 0 else 0
                for b in range(3):
                    f.dma(f.sp, br[b][:, :, 0:n], srcs[b][:, :, t0:t0 + n], writes=[B("m_br", b)])
                f.dma(f.sp, xt[:, :, 0:n], xv[:, :, t0:t0 + n], writes=[B("m_xt")])
                for oc in range(8):
                    par = oc % 2
                    for b in range(3):
                        pi = b + 3 * par
                        for k in range(8):
                            f.op(f.pe, lambda e: e.matmul(self.ps[pi][:, 0:n], lhsT=W[b][:, k, oc * 128:(oc + 1) * 128],
                                                          rhs=br[b][:, k, 0:n], start=(k == 0), stop=(k == 7)),
                                 reads=[B("m_w", b, k), B("m_br", b)], writes=[self.psb[pi]], inc=(k == 7))
                    for b in range(3):
                        pi = b + 3 * par
                        g = gt[pi]
                        gb = B("m_g", pi)
                        f.dma(f.sp, g[:, 0:n], pv[:, 55 + 8 * b + oc, t0:t0 + n], writes=[gb])
                        f.op(f.act, lambda e: e.activation(out=g[:, 0:n], in_=g[:, 0:n], func=AF.Sigmoid), reads=[gb], writes=[gb])
                        f.op(f.dve, lambda e: e.tensor_tensor(out=tt[pi][:, 0:n], in0=self.ps[pi][:, 0:n], in1=g[:, 0:n], op=ALU.mult),
                             reads=[self.psb[pi], gb], writes=[B("m_t", pi)])
                    p0 = 3 * par
                    f.op(f.pool, lambda e: e.tensor_tensor(out=tt[p0][:, 0:n], in0=tt[p0][:, 0:n], in1=tt[p0 + 1][:, 0:n], op=ALU.add),
                         reads=[B("m_t", p0), B("m_t", p0 + 1)], writes=[B("m_t", p0)])
                    f.op(f.pool, lambda e: e.tensor_tensor(out=mT[:, oc, 0:n], in0=tt[p0][:, 0:n], in1=tt[p0 + 2][:, 0:n], op=ALU.add),
                         reads=[B("m_t", p0), B("m_t", p0 + 2)], writes=[B("m_mT")])
                for oc in range(8):
                    pi = 6
                    for k in range(8):
                        f.op(f.pe, lambda e: e.matmul(self.ps[pi][:, 0:n], lhsT=W[3][:, k, oc * 128:(oc + 1) * 128],
                                                      rhs=mT[:, k, 0:n], start=(k == 0), stop=(k == 7)),
                             reads=[B("m_w", 3, k), B("m_mT")], writes=[self.psb[pi]], inc=(k == 7))
                    f.op(f.act, lambda e: e.copy(out=mo[:, oc, 0:n], in_=self.ps[pi][:, 0:n]), reads=[self.psb[pi]], writes=[B("m_mo")])
                self.post_residual(es, mo, B("m_mo"), xt, B("m_xt"), 16, j, dv, t0, n, sq, rs)
            f.barrier()

    def phase_ffn_up(self, l, xsm, last):
        nc, f = self.nc, self.f
        B = f.buf
        I, S = self.I, self.S
        fv = self.fm(S["ffa"])
        TP = T + 4
        hcol = lambda t: (t + 1) if t < CTX else (t + 3)
        with contextlib.ExitStack() as es:
            sb = lambda n, s, d: es.enter_context(self.sbuf(n, s, d))
            hT = sb("hT", [128, 8, TP], BF16)
            for c in (0, CTX + 1, CTX + 2, TP - 1):
                f.op(f.pool, lambda e: e.memset(hT[:, :, c:c + 1], 0.0), writes=[B("hT")])
            with contextlib.ExitStack() as es2:
                self.prenorm(es2, xsm, self.seqs(last), 24, hT, hcol)
                f.barrier()
            GS = 4
            wt = [sb("fu_w%d" % i, [128, 8, 2, GS * 128], BF16) for i in range(2)]
            cg = [sb("fu_cg%d" % i, [128, 512], F32) for i in range(2)]
            cv = [sb("fu_cv%d" % i, [128, 512], F32) for i in range(2)]
            ao = [sb("fu_a%d" % i, [128, 512], BF16) for i in range(2)]
            wv = I["w_ffn_up"][l].rearrange("(k p) n -> p k n", p=128)
            it = 0
            for gi, j0 in enumerate(range(0, 22, GS)):
                gs = min(GS, 22 - j0)
                w = wt[gi % 2]
                for k in range(8):
                    f.dma(f.pool, w[:, k, 0, 0:gs * 128], wv[:, k, j0 * 128:(j0 + gs) * 128], writes=[B("fu_w", gi % 2, k, 0)])
                    f.dma(f.pool, w[:, k, 1, 0:gs * 128], wv[:, k, DFF + j0 * 128:DFF + (j0 + gs) * 128],
                          writes=[B("fu_w", gi % 2, k, 1)])
                for (t0, n, s0, sl) in self.tiles(self.seqs(last), 510):
                    c0 = hcol(t0)
                    for jj in range(gs):
                        jc = j0 + jj
                        par = it % 3
                        for hv in range(2):
                            pi = 2 * par + hv
                            for k in range(8):
                                f.op(f.pe, lambda e: e.matmul(self.ps[pi][:, 0:n + 2], lhsT=w[:, k, hv, jj * 128:(jj + 1) * 128],
                                                              rhs=hT[:, k, c0 - 1:c0 + n + 1], start=(k == 0), stop=(k == 7)),
                                     reads=[B("fu_w", gi % 2, k, hv), B("hT")], writes=[self.psb[pi]], inc=(k == 7))
                        res = []
                        for hv, dst, nm in ((0, cg[it % 2], "fu_cg"), (1, cv[it % 2], "fu_cv")):
                            pi = 2 * par + hv
                            ch = jc + 22 * hv
                            wc = lambda q: self.V(l, "ffn_conv_w", ch * 3 + q)
                            db = B(nm, it % 2)
                            f.op(f.act, lambda e: e.activation(out=dst[:, 0:n], in_=self.ps[pi][:, 0:n], func=AF.Copy, scale=wc(0)),
                                 reads=[self.psb[pi], B("vecs")], writes=[db])
                            for q in (1, 2):
                                f.op(f.dve, lambda e: e.scalar_tensor_tensor(out=dst[:, 0:n], in0=self.ps[pi][:, q:q + n], scalar=wc(q),
                                                                             in1=dst[:, 0:n], op0=ALU.mult, op1=ALU.add),
                                     reads=[self.psb[pi], db, B("vecs")], writes=[db])
                        g_, v_ = cg[it % 2], cv[it % 2]
                        f.op(f.act, lambda e: e.activation(out=g_[:, 0:n], in_=g_[:, 0:n], func=AF.Silu),
                             reads=[B("fu_cg", it % 2)], writes=[B("fu_cg", it % 2)])
                        a = ao[it % 2]
                        f.op(f.pool, lambda e: e.tensor_tensor(out=a[:, 0:n], in0=g_[:, 0:n], in1=v_[:, 0:n], op=ALU.mult),
                             reads=[B("fu_cg", it % 2), B("fu_cv", it % 2)], writes=[B("fu_a", it % 2)])
                        f.dma(f.sp, fv[:, jc, t0:t0 + n], a[:, 0:n], reads=[B("fu_a", it % 2)])
                        it += 1
            f.barrier()

    def phase_ffn_down(self, l, xsm, xs_out, last):
        nc, f = self.nc, self.f
        B = f.buf
        I, S = self.I, self.S
        fv = self.fm(S["ffa"])
        with contextlib.ExitStack() as es:
            sb = lambda n, s, d: es.enter_context(self.sbuf(n, s, d))
            W = sb("fd_w", [128, 22, 1024], BF16)
            wv = I["w_ffn_down"][l].rearrange("(k p) n -> p k n", p=128)
            for k in range(22):
                f.dma(f.pool, W[:, k, :], wv[:, k, :], writes=[B("fd_w", k)])
            at = [sb("fd_a%d" % i, [128, 22, 512], BF16) for i in range(2)]
            mo = sb("fd_mo", [128, 8, 512], F32)
            xt = sb("fd_xt", [128, 8, 512], F32)
            sq = [sb("fd_sq%d" % i, [128, 512], F32) for i in range(2)]
            rs = sb("fd_rs", [128, 512], F32)
            xv = self.fm(xsm)
            dv = self.fm(xs_out)
            for it, (t0, n, s0, sl) in enumerate(self.tiles(self.seqs(last))):
                j = 1 if s0 == 0 else 0
                a = at[it % 2]
                ab = B("fd_a", it % 2)
                f.dma(f.sp, a[:, :, 0:n], fv[:, :, t0:t0 + n], writes=[ab])
                f.dma(f.sp, xt[:, :, 0:n], xv[:, :, t0:t0 + n], writes=[B("fd_xt")])
                for oc in range(8):
                    pi = oc % 4
                    for k in range(22):
                        f.op(f.pe, lambda e: e.matmul(self.ps[pi][:, 0:n], lhsT=W[:, k, oc * 128:(oc + 1) * 128],
                                                      rhs=a[:, k, 0:n], start=(k == 0), stop=(k == 21)),
                             reads=[B("fd_w", k), ab], writes=[self.psb[pi]], inc=(k == 21))
                    f.op(f.act, lambda e: e.copy(out=mo[:, oc, 0:n], in_=self.ps[pi][:, 0:n]), reads=[self.psb[pi]], writes=[B("fd_mo")])
                c0 = (t0 - CTX) if last else t0
                self.post_residual(es, mo, B("fd_mo"), xt, B("fd_xt"), 40, j, dv, c0, n, sq, rs)
            f.barrier()

    def phase_rwkv_prep(self, l):
        nc, f = self.nc, self.f
        B = f.buf
        I, S = self.I, self.S
        pv = self.fm(S["proj"])
        cf = self.constf
        NT = 256
        with contextlib.ExitStack() as es:
            sb = lambda n, s, d: es.enter_context(self.sbuf(n, s, d))
            dup = sb("rp_dup", [128, D], F32)
            iup = sb("rp_iup", [128, D], F32)
            w0r = sb("rp_w0r", [1, 2048], F32)
            omka = sb("rp_omka", [128, 8], F32)
            f.dma(f.sp, dup[:], I["decay_up"][l, :, :], writes=[B("rp_dup")])
            f.dma(f.sp, iup[:], I["iclr_up"][l, :, :], writes=[B("rp_iup")])
            f.dma(f.sp, w0r[:], I["w0row"][l, :, :], writes=[B("rp_w0r")])
            f.op(f.dve, lambda e: e.tensor_scalar(out=omka[:], in0=self.V(l, "k_a", 0, 8), scalar1=-1.0, scalar2=1.0,
                                                  op0=ALU.mult, op1=ALU.add), reads=[B("vecs")], writes=[B("rp_omka")])
            raw = [sb("rp_raw%d" % i, [128, NT + 2], F32) for i in range(3)]
            rkv = [sb("rp_%s" % nm, [128, 8, NT], F32) for nm in ("r", "k", "v")]
            kk = sb("rp_kk", [128, 8, NT], F32)
            sq = [sb("rp_sq%d" % i, [128, NT], F32) for i in range(2)]
            nrm = [sb("rp_nrm%d" % i, [128, NT], F32) for i in range(2)]
            kap = sb("rp_kap", [128, 8, NT], F32)
            lw = sb("rp_lw", [128, NT], F32)
            la = sb("rp_la", [128, NT], F32)
            sg = [sb("rp_sg%d" % i, [128, D], F32) for i in range(2)]
            Ein = sb("rp_Ein", [128, 8, NT], F32)
            Eex = sb("rp_Eex", [128, 8, NT], F32)
            Eng_ = sb("rp_Eneg", [128, 8, NT], F32)
            ag = sb("rp_a", [128, 8, NT], F32)
            kd = sb("rp_kd", [128, 8, NT], F32)
            bd = sb("rp_bd", [128, 8, NT], F32)
            k01 = sb("rp_k01", [128, 8, NT], F32)
            outs = [sb("rp_out%d" % i, [128, 8, NT], F32) for i in range(4)]
            gct = sb("rp_gct", [128, 8, 4], F32)
            ir = 0
            for (t0, n, s0, sl) in self.tiles([(0, CTX), (CTX, SEQ)], NT):
                for c in range(24):
                    r = raw[ir % 3]
                    rb = B("rp_raw", ir % 3)
                    ir += 1
                    lo = max(t0 - 1, s0)
                    hi = min(t0 + n + 1, s0 + sl)
                    off = lo - (t0 - 1)
                    if lo != t0 - 1:
                        f.op(f.pool, lambda e: e.memset(r[:, 0:1], 0.0), writes=[rb])
                    if hi != t0 + n + 1:
                        f.op(f.pool, lambda e: e.memset(r[:, n + 1:n + 2], 0.0), writes=[rb])
                    f.dma(f.sp, r[:, off:off + hi - lo], pv[:, 28 + c, lo:hi], writes=[rb])
                    dst = rkv[c // 8]
                    db = B("rp_rkv", c // 8)
                    cc_ = c % 8
                    wc = lambda q: self.V(l, "shift_w", c * 3 + q)
                    f.op(f.act, lambda e: e.activation(out=dst[:, cc_, 0:n], in_=r[:, 0:n], func=AF.Copy, scale=wc(0)),
                         reads=[rb, B("vecs")], writes=[db])
                    for q in (1, 2):
                        f.op(f.dve, lambda e: e.scalar_tensor_tensor(out=dst[:, cc_, 0:n], in0=r[:, q:q + n], scalar=wc(q),
                                                                     in1=dst[:, cc_, 0:n], op0=ALU.mult, op1=ALU.add),
                             reads=[rb, db, B("vecs")], writes=[db])
                R_, K_, V_ = rkv
                f.dma(f.sp, self.fm(S["rw_r"])[:, :, t0:t0 + n], R_[:, :, 0:n], reads=[B("rp_rkv", 0)])
                f.dma(f.sp, self.fm(S["rw_v"])[:, :, t0:t0 + n], V_[:, :, 0:n], reads=[B("rp_rkv", 2)])
                for c in range(8):
                    f.op(f.pool, lambda e: e.tensor_scalar(out=kk[:, c, 0:n], in0=K_[:, c, 0:n], scalar1=self.V(l, "k_k", c),
                                                           scalar2=None, op0=ALU.mult),
                         reads=[B("rp_rkv", 1), B("vecs")], writes=[B("rp_kk", c)])
                    s = sq[c % 2]
                    sqb = B("rp_sq", c % 2)
                    f.op(f.act, lambda e: e.activation(out=s[:, 0:n], in_=kk[:, c, 0:n], func=AF.Square),
                         reads=[B("rp_kk", c)], writes=[sqb])
                    pi = 6 + c % 2
                    f.op(f.pe, lambda e: e.matmul(self.ps[pi][:, 0:n], lhsT=cf[:, C_BLK:C_BLK + 128], rhs=s[:, 0:n],
                                                  start=True, stop=True), reads=[sqb, B("constf")], writes=[self.psb[pi]])
                    nr = nrm[c % 2]
                    nb = B("rp_nrm", c % 2)
                    f.op(f.dve, lambda e: e.tensor_scalar(out=nr[:, 0:n], in0=self.ps[pi][:, 0:n], scalar1=1e-12, scalar2=None,
                                                          op0=ALU.max), reads=[self.psb[pi]], writes=[nb])
                    self.rstd_from_ps(nr[:, 0:n], nb, nr[:, 0:n], nb, 1.0, 0.0)
                    f.op(f.dve, lambda e: e.tensor_tensor(out=kap[:, c, 0:n], in0=kk[:, c, 0:n], in1=nr[:, 0:n], op=ALU.mult),
                         reads=[B("rp_kk", c), nb], writes=[B("rp_kap")])
                f.dma(f.sp, lw[:, 0:n], pv[:, 52, t0:t0 + n], writes=[B("rp_lw")])
                f.dma(f.sp, la[:, 0:n], pv[:, 53, t0:t0 + n], writes=[B("rp_la")])
                f.op(f.act, lambda e: e.activation(out=lw[:, 0:n], in_=lw[:, 0:n], func=AF.Tanh), reads=[B("rp_lw")], writes=[B("rp_lw")])
                for d in range(2):
                    pr = slice(64 * d, 64 * d + 64)
                    tri = C_TRIF if d == 0 else C_TRIB
                    for jb in range(n // 128):
                        s_ = sg[jb % 2]
                        sgb = B("rp_sg", jb % 2)
                        for fh in range(2):
                            pi = fh
                            f.op(f.pe, lambda e: e.matmul(self.ps[pi][:, 0:512], lhsT=lw[pr, jb * 128:(jb + 1) * 128],
                                                          rhs=dup[pr, fh * 512:(fh + 1) * 512], start=True, stop=False),
                                 reads=[B("rp_lw"), B("rp_dup")], writes=[self.psb[pi]], inc=False)
                            f.op(f.pe, lambda e: e.matmul(self.ps[pi][:, 0:512], lhsT=cf[0:1, C_ONES:C_ONES + 128],
                                                          rhs=w0r[0:1, d * 1024 + fh * 512:d * 1024 + (fh + 1) * 512],
                                                          start=False, stop=True),
                                 reads=[B("rp_w0r"), B("constf")], writes=[self.psb[pi]])
                            f.op(f.act, lambda e: e.activation(out=s_[:, fh * 512:(fh + 1) * 512], in_=self.ps[pi][:, 0:512],
                                                               func=AF.Sigmoid), reads=[self.psb[pi]], writes=[sgb])
                        for c2 in range(4):
                            pi = 2 + c2
                            for h2 in range(2):
                                c = 2 * c2 + h2
                                f.op(f.pe, lambda e: e.matmul(self.ps[pi][:, h2 * 256:(h2 + 1) * 256], lhsT=s_[:, c * 128:(c + 1) * 128],
                                                              rhs=cf[:, tri:tri + 256], start=True, stop=True),
                                     reads=[sgb, B("constf")], writes=[self.psb[pi]], inc=(h2 == 1))
                            pv4 = self.ps[pi][:, 0:512].rearrange("p (c i t) -> p c i t", c=2, i=2)
                            cs = slice(2 * c2, 2 * c2 + 2)
                            ts = slice(jb * 128, (jb + 1) * 128)
                            f.op(f.act, lambda e: e.activation(out=Ein[:, cs, ts], in_=pv4[:, :, 0, :], func=AF.Exp, scale=-DECAY_SCALE),
                                 reads=[self.psb[pi]], writes=[B("rp_Ein")])
                            f.op(f.act, lambda e: e.activation(out=Eex[:, cs, ts], in_=pv4[:, :, 1, :], func=AF.Exp, scale=-DECAY_SCALE),
                                 reads=[self.psb[pi]], writes=[B("rp_Eex")])
                            f.op(f.act, lambda e: e.activation(out=Eng_[:, cs, ts], in_=pv4[:, :, 0, :], func=AF.Exp, scale=DECAY_SCALE),
                                 reads=[self.psb[pi]], writes=[B("rp_Eneg")])
                    for c in range(8):
                        pi = 6 + c % 2
                        f.op(f.pe, lambda e: e.matmul(self.ps[pi][:, 0:n], lhsT=iup[pr, c * 128:(c + 1) * 128], rhs=la[pr, 0:n],
                                                      start=True, stop=True), reads=[B("rp_iup"), B("rp_la")], writes=[self.psb[pi]])
                        f.op(f.act, lambda e: e.activation(out=ag[:, c, 0:n], in_=self.ps[pi][:, 0:n], func=AF.Sigmoid,
                                                           bias=self.V(l, "iclr_a0", d * 8 + c), scale=1.0),
                             reads=[self.psb[pi], B("vecs")], writes=[B("rp_a")])
                        f.op(f.dve, lambda e: e.tensor_scalar(out=kd[:, c, 0:n], in0=ag[:, c, 0:n], scalar1=self.V(l, "k_a", c),
                                                              scalar2=omka[:, c:c + 1], op0=ALU.mult, op1=ALU.add),
                             reads=[B("rp_a"), B("vecs"), B("rp_omka")], writes=[B("rp_kd")])
                    f.op(f.pool, lambda e: e.tensor_tensor(out=kd[:, :, 0:n], in0=kd[:, :, 0:n], in1=K_[:, :, 0:n], op=ALU.mult),
                         reads=[B("rp_kd"), B("rp_rkv", 1)], writes=[B("rp_kd")])
                    f.op(f.pool, lambda e: e.tensor_tensor(out=bd[:, :, 0:n], in0=ag[:, :, 0:n], in1=kap[:, :, 0:n], op=ALU.mult),
                         reads=[B("rp_a"), B("rp_kap")], writes=[B("rp_bd")])
                    if d == 0:
                        f.op(f.pool, lambda e: e.tensor_copy(out=k01[:, :, 0:n], in_=kd[:, :, 0:n]), reads=[B("rp_kd")], writes=[B("rp_k01")])
                    else:
                        f.op(f.pool, lambda e: e.tensor_tensor(out=k01[:, :, 0:n], in0=k01[:, :, 0:n], in1=kd[:, :, 0:n], op=ALU.add),
                             reads=[B("rp_kd"), B("rp_k01")], writes=[B("rp_k01")])
                    o_at, o_bt, o_kt, o_rt = outs
                    f.op(f.dve, lambda e: e.scalar_tensor_tensor(out=o_at[:, :, 0:n], in0=kap[:, :, 0:n], scalar=-1.0, in1=Eex[:, :, 0:n],
                                                                 op0=ALU.mult, op1=ALU.mult),
                         reads=[B("rp_kap"), B("rp_Eex")], writes=[B("rp_out", 0)])
                    f.op(f.pool, lambda e: e.tensor_tensor(out=o_bt[:, :, 0:n], in0=bd[:, :, 0:n], in1=Eng_[:, :, 0:n], op=ALU.mult),
                         reads=[B("rp_bd"), B("rp_Eneg")], writes=[B("rp_out", 1)])
                    f.op(f.dve, lambda e: e.tensor_tensor(out=o_kt[:, :, 0:n], in0=kd[:, :, 0:n], in1=Eng_[:, :, 0:n], op=ALU.mult),
                         reads=[B("rp_kd"), B("rp_Eneg")], writes=[B("rp_out", 2)])
                    f.op(f.pool, lambda e: e.tensor_tensor(out=o_rt[:, :, 0:n], in0=R_[:, :, 0:n], in1=Ein[:, :, 0:n], op=ALU.mult),
                         reads=[B("rp_rkv", 0), B("rp_Ein")], writes=[B("rp_out", 3)])
                    for i_, nm in enumerate(("At", "Bt", "Kt", "Rt")):
                        f.dma(f.sp, self.fm(S["%s%d" % (nm, d)])[:, :, t0:t0 + n], outs[i_][:, :, 0:n], reads=[B("rp_out", i_)])
                    col0 = 63 if d == 0 else 0
                    nch = n // 64
                    f.op(f.act, lambda e: e.copy(out=gct[:, :, 0:nch], in_=Ein[:, :, col0:n:64]), reads=[B("rp_Ein")], writes=[B("rp_gct")])
                    f.dma(f.sp, self.fm(S["gC%d" % d])[:, :, t0 // 64:t0 // 64 + nch], gct[:, :, 0:nch], reads=[B("rp_gct")])
                f.dma(f.sp, self.fm(S["rw_k01"])[:, :, t0:t0 + n], k01[:, :, 0:n], reads=[B("rp_k01")])
            f.barrier()

    def phase_rwkv_scan(self, l, last):
        nc, f = self.nc, self.f
        B = f.buf
        S = self.S
        cf = self.constf
        mk = self.maskf
        with contextlib.ExitStack() as es:
            sb = lambda n, s, d: es.enter_context(self.sbuf(n, s, d))
            ST = [sb("sc_ST%d" % d, [128, 8, 64], F32) for d in range(2)]
            gC = [sb("sc_gC%d" % d, [128, 8, NCHUNK], F32) for d in range(2)]
            names = ("At", "Bt", "Kt", "Rt", "V")
            inp = [[[sb("sc_%s%d_%d" % (nm, d, i), [128, 8, 128], F32) for nm in names] for i in range(2)] for d in range(2)]
            def mk64(nm, k=1):
                return [[[sb("sc_%s%d%d_%d" % (nm, d, h, i), [128, 256], F32) for i in range(k)] for h in range(2)] for d in range(2)]
            Xb = mk64("X", 2)
            XTb = mk64("XT", 2)
            Pb = mk64("P", 6)
            Lb = mk64("L", 3)
            Tk = mk64("Tk", 3)
            Wb = mk64("W", 2)
            Yo = mk64("Yo", 1)
            for d in range(2):
                f.op(f.pool, lambda e: e.memset(ST[d][:], 0.0), writes=[B("ST", d, 0), B("ST", d, 1)])
                f.dma(f.sp, gC[d][:], self.fm(S["gC%d" % d])[:, :, :], writes=[B("sc_gC", d)])
            order = [list(range(NCHUNK)), [3, 2, 1, 0] + list(range(NCHUNK - 1, 3, -1))]
            srcs = [[self.fm(S["%s%d" % (nm, d)]) for nm in ("At", "Bt", "Kt", "Rt")] + [self.fm(S["rw_v"])] for d in range(2)]
            yv = [self.fm(S["yf"]), self.fm(S["yb"])]
            self._psr = 0
            self._cp = 0
            cur_tile = [None, None]
            nload = [0, 0]

            def nps():
                i = self._psr % 8
                self._psr += 1
                return self.ps[i], self.psb[i]

            def evac(out_ap, in_ap, rd, wr):
                self._cp += 1
                if self._cp % 3 == 0:
                    f.op(f.dve, lambda e: e.tensor_copy(out=out_ap, in_=in_ap), reads=rd, writes=wr)
                else:
                    f.op(f.act, lambda e: e.copy(out=out_ap, in_=in_ap), reads=rd, writes=wr)

            def group(d, ch, half):
                tl, cc = ch // 2, ch % 2
                cs = slice(64 * cc, 64 * cc + 64)
                if cur_tile[d] != tl:
                    cur_tile[d] = tl
                    nload[d] += 1
                    bi = nload[d] % 2
                    for i_, nm in enumerate(names):
                        f.dma(f.sp, inp[d][bi][i_][:], srcs[d][i_][:, :, tl * 128:(tl + 1) * 128], writes=[B("sc_in", d, bi, i_)])
                bi = nload[d] % 2
                A_, Bm, Km, R_, Vv = inp[d][bi]
                bA, bB, bK, bR, bV = [B("sc_in", d, bi, i_) for i_ in range(5)]
                heads = [(4 * half + hpi, hh) for hpi in range(4) for hh in range(2)]
                fm_ = lambda T_, hp, hh: T_[64 * hh:64 * hh + 64, hp, cs]
                O = lambda T_, g8: T_[64 * (g8 % 2):64 * (g8 % 2) + 64, (g8 // 2) * 64:(g8 // 2) * 64 + 64]
                stb = B("ST", d, half)
                cst = B("constf")
                mkb = B("maskf")
                if d == 0:
                    mX, mXT, mL = M_LOS, M_UPS, M_UPI
                else:
                    mX, mXT, mL = M_UPS, M_LOS, M_LOI
                Vt_, Bt_, Kt_ = Tk[d][half]
                dh = (d, half)
                for j_, (src, sbuf_, dst) in enumerate(((Vv, bV, Vt_), (Bm, bB, Bt_), (Km, bK, Kt_))):
                    ps, pb = nps()
                    for g8, (hp, hh) in enumerate(heads):
                        f.op(f.pe, lambda e: e.matmul(O(ps, g8), lhsT=fm_(src, hp, hh),
                                                      rhs=cf[64 * hh:64 * hh + 64, C_ID + 64 * hh:C_ID + 64 * hh + 64], start=True, stop=True),
                             reads=[sbuf_, cst], writes=[pb], inc=(g8 == 7))
                    evac(dst[:, :], ps[:, 0:256], [pb], [B("sc_Tk", dh, j_)])
                    yield
                bVt, bBt, bKt = [B("sc_Tk", dh, j_) for j_ in range(3)]

                def mm8(out_rows, fn_l, fn_r, rd):
                    ps, pb = nps()
                    for g8, (hp, hh) in enumerate(heads):
                        f.op(f.pe, lambda e: e.matmul(O(ps, g8), lhsT=fn_l(g8, hp, hh), rhs=fn_r(g8, hp, hh), start=True, stop=True),
                             reads=rd, writes=[pb], inc=(g8 == 7))
                    return ps, pb

                def masked(ps, pb, dst, db, mcol):
                    f.op(f.dve, lambda e: e.tensor_tensor(out=dst[:, :], in0=ps[:, 0:256], in1=mk[:, mcol:mcol + 256], op=ALU.mult),
                         reads=[pb, mkb], writes=[db])
                X = Xb[d][half]
                XT = XTb[d][half]
                P = Pb[d][half]
                bX = [B("sc_X", dh, i_) for i_ in range(2)]
                bXT = [B("sc_XT", dh, i_) for i_ in range(2)]
                bP = [B("sc_P", dh, i_) for i_ in range(6)]
                bL = [B("sc_L", dh, i_) for i_ in range(3)]
                LakT, LrbT, LrkT = Lb[d][half]
                ps, pb = mm8(0, lambda g, hp, hh: fm_(A_, hp, hh), lambda g, hp, hh: fm_(Bm, hp, hh), [bA, bB])
                masked(ps, pb, X[0], bX[0], mX)
                yield
                ps, pb = mm8(0, lambda g, hp, hh: fm_(Bm, hp, hh), lambda g, hp, hh: fm_(A_, hp, hh), [bA, bB])
                masked(ps, pb, XT[0], bXT[0], mXT)
                f.op(f.pool, lambda e: e.tensor_tensor(out=P[0][:, :], in0=XT[0][:, :], in1=mk[:, M_ID:M_ID + 256], op=ALU.add),
                     reads=[bXT[0], mkb], writes=[bP[0]])
                yield
                ps, pb = mm8(0, lambda g, hp, hh: fm_(Km, hp, hh), lambda g, hp, hh: fm_(A_, hp, hh), [bA, bK])
                masked(ps, pb, LakT, bL[0], mXT)
                yield
                ps, pb = mm8(0, lambda g, hp, hh: fm_(Bm, hp, hh), lambda g, hp, hh: fm_(R_, hp, hh), [bR, bB])
                masked(ps, pb, LrbT, bL[1], mL)
                yield
                ps, pb = mm8(0, lambda g, hp, hh: fm_(Km, hp, hh), lambda g, hp, hh: fm_(R_, hp, hh), [bR, bK])
                masked(ps, pb, LrkT, bL[2], mL)
                yield
                for i_ in range(1, 6):
                    p_, c_ = (i_ - 1) % 2, i_ % 2
                    Xp, XTp = X[p_], XT[p_]
                    if i_ <= 4:
                        ps, pb = mm8(0, lambda g, hp, hh: O(XTp, g), lambda g, hp, hh: O(Xp, g), [bX[p_], bXT[p_]])
                        evac(X[c_][:, :], ps[:, 0:256], [pb], [bX[c_]])
                        yield
                    ps, pb = mm8(0, lambda g, hp, hh: O(Xp, g), lambda g, hp, hh: O(XTp, g), [bX[p_], bXT[p_]])
                    evac(XT[c_][:, :], ps[:, 0:256], [pb], [bXT[c_]])
                    f.op(f.pool, lambda e: e.tensor_tensor(out=P[i_][:, :], in0=XT[c_][:, :], in1=mk[:, M_ID:M_ID + 256], op=ALU.add),
                         reads=[bXT[c_], mkb], writes=[bP[i_]])
                    yield
                W = Wb[d][half]
                bW = [B("sc_W", dh, i_) for i_ in range(2)]
                ps, pb = nps()
                for g8, (hp, hh) in enumerate(heads):
                    f.op(f.pe, lambda e: e.matmul(O(ps, g8), lhsT=fm_(A_, hp, hh), rhs=ST[d][64 * hh:64 * hh + 64, hp, :],
                                                  start=True, stop=False), reads=[bA, stb], writes=[pb], inc=False)
                    f.op(f.pe, lambda e: e.matmul(O(ps, g8), lhsT=O(LakT, g8), rhs=O(Vt_, g8),
                                                  start=False, stop=True), reads=[bL[0], bVt], writes=[pb], inc=(g8 == 7))
                evac(W[0][:, :], ps[:, 0:256], [pb], [bW[0]])
                yield
                wi = 0
                for i_ in range(5, -1, -1):
                    Wc = W[wi]
                    ps, pb = mm8(0, lambda g, hp, hh: O(P[i_], g), lambda g, hp, hh: O(Wc, g), [bP[i_], bW[wi]])
                    wi ^= 1
                    evac(W[wi][:, :], ps[:, 0:256], [pb], [bW[wi]])
                    yield
                U = W[wi]
                bU = bW[wi]
                ps, pb = nps()
                for g8, (hp, hh) in enumerate(heads):
                    f.op(f.pe, lambda e: e.matmul(O(ps, g8), lhsT=ST[d][64 * hh:64 * hh + 64, hp, :], rhs=fm_(R_, hp, hh),
                                                  start=True, stop=False), reads=[bR, stb], writes=[pb], inc=False)
                    f.op(f.pe, lambda e: e.matmul(O(ps, g8), lhsT=O(U, g8), rhs=O(LrbT, g8),
                                                  start=False, stop=False), reads=[bU, bL[1]], writes=[pb], inc=False)
                    f.op(f.pe, lambda e: e.matmul(O(ps, g8), lhsT=O(Vt_, g8), rhs=O(LrkT, g8),
                                                  start=False, stop=True), reads=[bVt, bL[2]], writes=[pb], inc=(g8 == 7))
                yo = Yo[d][half][0]
                evac(yo[:, :], ps[:, 0:256], [pb], [B("sc_Yo", dh)])
                f.dma(f.sp, yv[d][:, 4 * half:4 * half + 4, ch * 64:(ch + 1) * 64], yo[:, :].rearrange("p (g t) -> p g t", t=64),
                      reads=[B("sc_Yo", dh)])
                yield
                ps, pb = nps()
                for g8, (hp, hh) in enumerate(heads):
                    o_ = O(ps, g8)
                    f.op(f.pe, lambda e: e.matmul(o_, lhsT=O(Bt_, g8), rhs=O(U, g8), start=True, stop=False),
                         reads=[bBt, bU], writes=[pb], inc=False)
                    f.op(f.pe, lambda e: e.matmul(o_, lhsT=O(Kt_, g8), rhs=O(Vt_, g8), start=False, stop=False),
                         reads=[bKt, bVt], writes=[pb], inc=False)
                    f.op(f.pe, lambda e: e.matmul(o_, lhsT=cf[64 * hh:64 * hh + 64, C_ID + 64 * hh:C_ID + 64 * hh + 64],
                                                  rhs=ST[d][64 * hh:64 * hh + 64, hp, :], start=False, stop=True),
                         reads=[cst, stb], writes=[pb], inc=(g8 == 7))
                for hpi in range(4):
                    hp = 4 * half + hpi
                    f.op(f.act, lambda e: e.activation(out=ST[d][:, hp, :], in_=ps[:, hpi * 64:(hpi + 1) * 64], func=AF.Copy,
                                                       scale=gC[d][:, hp, ch:ch + 1]),
                         reads=[pb, B("sc_gC", d)], writes=[stb])

            for s_ in range(getattr(self, "scan_steps", NCHUNK)):
                gens = [group(d, order[d][s_], half) for half in range(2) for d in range(2)]
                while gens:
                    alive = []
                    for g_ in gens:
                        try:
                            next(g_)
                            alive.append(g_)
                        except StopIteration:
                            pass
                    gens = alive
            f.barrier()

    def phase_rwkv_out(self, l, last):
        nc, f = self.nc, self.f
        B = f.buf
        I, S = self.I, self.S
        pv = self.fm(S["proj"])
        cf = self.constf
        with contextlib.ExitStack() as es:
            sb = lambda n, s, d: es.enter_context(self.sbuf(n, s, d))
            gup = sb("ro_gup", [128, D], F32)
            f.dma(f.sp, gup[:], I["gate_up"][l, :, :], writes=[B("ro_gup")])
            lg = sb("ro_lg", [128, 512], F32)
            tl = {}
            for nm in ("yf", "yb", "r", "k01", "v"):
                tl[nm] = [sb("ro_%s%d" % (nm, i), [128, 512], F32) for i in range(2)]
            sq = [sb("ro_sq%d" % i, [128, 512], F32) for i in range(2)]
            mean = [sb("ro_mean%d" % i, [128, 512], F32) for i in range(2)]
            var = [sb("ro_var%d" % i, [128, 512], F32) for i in range(2)]
            tt = [sb("ro_t%d" % i, [128, 512], F32) for i in range(2)]
            bon = [sb("ro_bon%d" % i, [128, 512], F32) for i in range(2)]
            ost = [sb("ro_o%d" % i, [128, 512], BF16) for i in range(2)]
            srcv = {"yf": self.fm(S["yf"]), "yb": self.fm(S["yb"]), "r": self.fm(S["rw_r"]), "k01": self.fm(S["rw_k01"]),
                    "v": self.fm(S["rw_v"])}
            ov = self.fm(S["rwo"])
            lgs = [lg, sb("ro_lg1", [128, 512], F32)]

            def unit(ti, t0, n, c, p):
                lg_ = lgs[ti % 2]
                lgb = B("ro_lg", ti % 2)
                if c == 0:
                    f.dma(f.sp, lg_[:, 0:n], pv[:, 54, t0:t0 + n], writes=[lgb])
                    f.op(f.act, lambda e: e.activation(out=lg_[:, 0:n], in_=lg_[:, 0:n], func=AF.Sigmoid), reads=[lgb], writes=[lgb])
                    yield
                bb = {nm: B("ro_" + nm, p) for nm in tl}
                for nm in tl:
                    f.dma(f.sp, tl[nm][p][:, 0:n], srcv[nm][:, c, t0:t0 + n], writes=[bb[nm]])
                yield
                y = tl["yf"][p]
                f.op(f.pool, lambda e: e.tensor_tensor(out=y[:, 0:n], in0=y[:, 0:n], in1=tl["yb"][p][:, 0:n], op=ALU.add),
                     reads=[bb["yf"], bb["yb"]], writes=[bb["yf"]])
                yield
                f.op(f.act, lambda e: e.activation(out=sq[p][:, 0:n], in_=y[:, 0:n], func=AF.Square), reads=[bb["yf"]], writes=[B("ro_sq", p)])
                yield
                r_ = tl["r"][p]
                f.op(f.dve, lambda e: e.scalar_tensor_tensor(out=r_[:, 0:n], in0=r_[:, 0:n], scalar=self.V(l, "r_k", c),
                                                             in1=tl["k01"][p][:, 0:n], op0=ALU.mult, op1=ALU.mult),
                     reads=[bb["r"], bb["k01"], B("vecs")], writes=[bb["r"]])
                yield
                q = 4 * p
                blk = cf[:, C_BLK:C_BLK + 128]
                f.op(f.pe, lambda e: e.matmul(self.ps[q][:, 0:n], lhsT=blk, rhs=y[:, 0:n], start=True, stop=True),
                     reads=[bb["yf"], B("constf")], writes=[self.psb[q]])
                yield
                f.op(f.pe, lambda e: e.matmul(self.ps[q + 1][:, 0:n], lhsT=blk, rhs=sq[p][:, 0:n], start=True, stop=True),
                     reads=[B("ro_sq", p), B("constf")], writes=[self.psb[q + 1]])
                yield
                f.op(f.pe, lambda e: e.matmul(self.ps[q + 2][:, 0:n], lhsT=blk, rhs=r_[:, 0:n], start=True, stop=True),
                     reads=[bb["r"], B("constf")], writes=[self.psb[q + 2]])
                yield
                f.op(f.pe, lambda e: e.matmul(self.ps[q + 3][:, 0:n], lhsT=gup[:, c * 128:(c + 1) * 128], rhs=lg_[:, 0:n], start=True, stop=True),
                     reads=[lgb, B("ro_gup")], writes=[self.psb[q + 3]])
                yield
                m_, v_, t_ = mean[p], var[p], tt[p]
                f.op(f.act, lambda e: e.activation(out=m_[:, 0:n], in_=self.ps[q][:, 0:n], func=AF.Copy, scale=1.0 / 64),
                     reads=[self.psb[q]], writes=[B("ro_mean", p)])
                yield
                f.op(f.dve, lambda e: e.tensor_tensor(out=v_[:, 0:n], in0=m_[:, 0:n], in1=m_[:, 0:n], op=ALU.mult),
                     reads=[B("ro_mean", p)], writes=[B("ro_var", p)])
                yield
                f.op(f.dve, lambda e: e.scalar_tensor_tensor(out=v_[:, 0:n], in0=self.ps[q + 1][:, 0:n], scalar=1.0 / 64, in1=v_[:, 0:n],
                                                             op0=ALU.mult, op1=ALU.subtract),
                     reads=[self.psb[q + 1], B("ro_var", p)], writes=[B("ro_var", p)])
                yield
                self.rstd_from_ps(v_[:, 0:n], B("ro_var", p), v_[:, 0:n], B("ro_var", p), 1.0, GN_EPS)
                yield
                f.op(f.dve, lambda e: e.tensor_tensor(out=t_[:, 0:n], in0=y[:, 0:n], in1=m_[:, 0:n], op=ALU.subtract),
                     reads=[bb["yf"], B("ro_mean", p)], writes=[B("ro_t", p)])
                yield
                f.op(f.pool, lambda e: e.tensor_tensor(out=t_[:, 0:n], in0=t_[:, 0:n], in1=v_[:, 0:n], op=ALU.mult),
                     reads=[B("ro_t", p), B("ro_var", p)], writes=[B("ro_t", p)])
                yield
                f.op(f.act, lambda e: e.activation(out=t_[:, 0:n], in_=t_[:, 0:n], func=AF.Identity,
                                                   bias=self.V(l, "gn_b", c), scale=self.V(l, "gn_g", c)),
                     reads=[B("ro_t", p), B("vecs")], writes=[B("ro_t", p)])
                yield
                f.op(f.dve, lambda e: e.tensor_tensor(out=bon[p][:, 0:n], in0=self.ps[q + 2][:, 0:n], in1=tl["v"][p][:, 0:n], op=ALU.mult),
                     reads=[self.psb[q + 2], bb["v"]], writes=[B("ro_bon", p)])
                yield
                f.op(f.pool, lambda e: e.tensor_tensor(out=t_[:, 0:n], in0=t_[:, 0:n], in1=bon[p][:, 0:n], op=ALU.add),
                     reads=[B("ro_t", p), B("ro_bon", p)], writes=[B("ro_t", p)])
                yield
                f.op(f.dve, lambda e: e.tensor_tensor(out=ost[p][:, 0:n], in0=self.ps[q + 3][:, 0:n], in1=t_[:, 0:n], op=ALU.mult),
                     reads=[self.psb[q + 3], B("ro_t", p)], writes=[B("ro_o", p)])
                yield
                f.dma(f.sp, ov[:, c, t0:t0 + n], ost[p][:, 0:n], reads=[B("ro_o", p)])
                yield

            units = [(ti, t0, n, c) for ti, (t0, n, s0, sl) in enumerate(self.tiles(self.seqs(last))) for c in range(8)]
            active = []
            nxt = 0
            free = [0]
            rnd = 0
            while nxt < len(units) or active:
                rnd += 1
                if rnd == 11:
                    free.append(1)
                while nxt < len(units) and free:
                    p_ = free.pop()
                    active.append((unit(*units[nxt], p_), p_))
                    nxt += 1
                still = []
                for g_, p_ in active:
                    try:
                        next(g_)
                        still.append((g_, p_))
                    except StopIteration:
                        free.append(p_)
                active = still
            f.barrier()


def _fm(v):
    v = np.asarray(v, np.float32).reshape(-1, 128)
    return np.ascontiguousarray(v.T)


def _consts():
    c = np.zeros((128, NCONST), np.float32)
    c[:, C_ONES:C_ONES + 128] = 1.0
    for h in range(2):
        c[64 * h:64 * h + 64, C_BLK + 64 * h:C_BLK + 64 * h + 64] = 1.0
    c[:, C_ID:C_ID + 128] = np.eye(128, dtype=np.float32)
    P = np.zeros((128, 128), np.float32)
    for m in range(128):
        if m % 64 < 32:
            P[m, m + 32] = -1.0
        else:
            P[m, m - 32] = 1.0
    c[:, C_ROT:C_ROT + 128] = P.T
    s = np.arange(128)[:, None]
    t = np.arange(128)[None, :]
    same = (s // 64) == (t // 64)
    c[:, C_TRIF:C_TRIF + 128] = (same & (s <= t))
    c[:, C_TRIF + 128:C_TRIF + 256] = (same & (s < t))
    c[:, C_TRIB:C_TRIB + 128] = (same & (s >= t))
    c[:, C_TRIB + 128:C_TRIB + 256] = (same & (s > t))
    r = np.arange(64)[:, None]
    q = np.arange(64)[None, :]
    m = np.zeros((128, 5 * 256), np.float32)
    m[:, M_LOS:M_LOS + 256] = np.tile((q < r).astype(np.float32), (2, 4))
    m[:, M_UPS:M_UPS + 256] = np.tile((q > r).astype(np.float32), (2, 4))
    m[:, M_LOI:M_LOI + 256] = np.tile((q <= r).astype(np.float32), (2, 4))
    m[:, M_UPI:M_UPI + 256] = np.tile((q >= r).astype(np.float32), (2, 4))
    m[:, M_ID:M_ID + 256] = np.tile(np.eye(64, dtype=np.float32), (2, 4))
    tt = np.arange(SEQ)
    row = (tt // 64).astype(np.float32)
    col = (tt % 64).astype(np.float32)
    inv = (10000.0 ** (-np.arange(32, dtype=np.float32) / 32)).astype(np.float32)
    cos = np.zeros((128, SEQ), np.float32)
    sin = np.zeros((128, SEQ), np.float32)
    for p in range(128):
        pos = row if p < 64 else col
        ang = (pos * inv[p % 32]).astype(np.float32)
        cos[p] = np.cos(ang)
        sin[p] = np.sin(ang)
    return c, m, cos, sin


def prep_inputs(inp):
    g = lambda k: np.asarray(inp[k], np.float32)
    vecs = np.zeros((DEPTH, 128, NV), np.float32)
    for l in range(DEPTH):
        def put(name, arr):
            a = np.asarray(arr, np.float32)
            vecs[l, :, VCOL[name]:VCOL[name] + a.shape[1]] = a
        put("b_mod", _fm(g("b_mod")[l]))
        for n in ("g_pre_mix", "g_post_mix", "g_pre_ffn", "g_post_ffn", "q_norm", "k_norm", "conv_b",
                  "k_k", "k_a"):
            put(n, _fm(g(n)[l]))
        put("conv_ln_g", _fm(g("conv_ln_g")[l]))
        put("conv_ln_b", _fm(g("conv_ln_b")[l]))
        put("gn_g", _fm(g("wkv_gn_g")[l]))
        put("gn_b", _fm(g("wkv_gn_b")[l]))
        put("r_k", _fm(g("r_k")[l].reshape(-1)))
        cw = g("conv_w")[l].reshape(31, 8, 128).transpose(2, 1, 0).reshape(128, 248)
        put("conv_w", cw)
        sw = g("shift_w")[l].reshape(3, 24, 128).transpose(2, 1, 0).reshape(128, 72)
        put("shift_w", sw)
        fw_ = g("ffn_conv_w")[l].reshape(3, 44, 128).transpose(2, 1, 0).reshape(128, 132)
        put("ffn_conv_w", fw_)
        a0 = g("iclr_a0")[l].reshape(2, 8, 128).transpose(2, 0, 1).reshape(128, 16)
        put("iclr_a0", a0)
    constf, maskf, cos, sin = _consts()
    shared = {
        "vecs": vecs,
        "w0row": np.ascontiguousarray(g("decay_w0").reshape(DEPTH, 1, 2048)),
        "constf": constf, "maskf": maskf, "cos": cos, "sin": sin,
        "w_mod": g("w_mod"), "w_in": g("w_in"),
        "w_attn_o": g("w_attn_o"), "w_conv_o": g("w_conv_o"), "w_rwkv_o": g("w_rwkv_o"), "w_out": g("w_out"),
        "decay_up": np.ascontiguousarray(g("decay_up").reshape(DEPTH, 128, D)),
        "iclr_up": np.ascontiguousarray(g("iclr_up").reshape(DEPTH, 128, D)),
        "gate_up": g("gate_up"),
        "w_ffn_up": g("w_ffn_up"), "w_ffn_down": g("w_ffn_down"),
    }
    x = g("x")
    ctx = g("ctx")
    c = g("c")
    c_ctx = g("c_ctx")
    per_core = []
    for b in range(x.shape[0]):
        xs0 = np.ascontiguousarray(np.concatenate([ctx[b].T, x[b].T], axis=1))
        cc = np.stack([c[b], c_ctx], axis=-1).reshape(8, 128, 2).transpose(1, 0, 2).reshape(128, 16)
        m = dict(shared)
        m["xs0"] = xs0
        m["cc"] = np.ascontiguousarray(cc)
        per_core.append(m)
    return per_core


def kernel(**inputs):
    per_core = prep_inputs(inputs)
    nc = Prog().build()
    res = run_bass_kernel_spmd(nc, per_core, core_ids=list(range(8)))
    outs = [np.ascontiguousarray(np.asarray(r["out"], np.float32).T) for r in res.results]
    return np.stack(outs, axis=0)
```

```python
import math
import contextlib
import numpy as np
import concourse.bass as bass
import concourse.mybir as mybir
from concourse.bass_utils import run_bass_kernel_spmd

F32 = mybir.dt.float32
BF16 = mybir.dt.bfloat16
AF = mybir.ActivationFunctionType
ALU = mybir.AluOpType

D = 1024
SEQ = 4096
CTX = 256
T = SEQ + CTX
DEPTH = 2
NIN = 10112
DFF = 2816
DECAY_SCALE = math.exp(-0.5)
EPS = 1e-6
LN_EPS = 1e-5
GN_EPS = 64 * 1e-5
CH = 64
NCHUNK = T // CH

VCOL = {}
_o = 0
for _n, _w in (("b_mod", 48), ("g_pre_mix", 8), ("g_post_mix", 8), ("g_pre_ffn", 8), ("g_post_ffn", 8),
               ("q_norm", 1), ("k_norm", 1), ("conv_w", 248), ("conv_b", 8), ("conv_ln_g", 8), ("conv_ln_b", 8),
               ("shift_w", 72), ("iclr_a0", 16), ("k_k", 8), ("k_a", 8), ("r_k", 8), ("gn_g", 8), ("gn_b", 8),
               ("ffn_conv_w", 132)):
    VCOL[_n] = _o
    _o += _w
NV = _o
C_ONES, C_BLK, C_ID, C_ROT, C_TRIF, C_TRIB = 0, 128, 256, 384, 512, 768
NCONST = 1024
M_LOS, M_UPS, M_LOI, M_UPI, M_ID = 0, 256, 512, 768, 1024


class Buf:
    __slots__ = ("w", "r")

    def __init__(self):
        self.w = None
        self.r = {}


class Eng:
    def __init__(self, name, e, sem):
        self.name = name
        self.e = e
        self.sem = sem
        self.count = 0
        self.seen = {}


class FW:
    SAME_ENG_DIST = 2

    def __init__(self, nc, es, n_dma_sems=24):
        self.nc = nc
        self.sems = {}

        def mk(name, e):
            s = es.enter_context(nc.semaphore("sem_" + name))
            self.sems[name] = s
            return Eng(name, e, s)
        self.pe = mk("pe", nc.tensor)
        self.act = mk("act", nc.scalar)
        self.dve = mk("dve", nc.vector)
        self.pool = mk("pool", nc.gpsimd)
        self.sp = mk("sp", nc.sync)
        self.engs = [self.pe, self.act, self.dve, self.pool, self.sp]
        self.dma_sems = []
        for i in range(n_dma_sems):
            nm = "dq%d" % i
            self.sems[nm] = es.enter_context(nc.semaphore("sem_" + nm))
            self.dma_sems.append([nm, 0])
        self.dma_rr = 0
        self.bufs = {}
        self.n_ins = 0

    def buf(self, *key):
        b = self.bufs.get(key)
        if b is None:
            b = Buf()
            self.bufs[key] = b
        return b

    def _need(self, eng, tok, raw):
        if tok is None:
            return
        sk, v = tok
        if sk == eng.name:
            if not (raw and (eng.count - v) < self.SAME_ENG_DIST):
                return
        if eng.seen.get(sk, 0) >= v:
            return
        eng.e.wait_ge(self.sems[sk], v)
        eng.seen[sk] = v

    def _deps(self, eng, reads, writes):
        for b in reads:
            self._need(eng, b.w, True)
        for b in writes:
            self._need(eng, b.w, False)
            for sk, v in b.r.items():
                self._need(eng, (sk, v), False)

    def _mark(self, tok, reads, writes):
        for b in reads:
            if b.r.get(tok[0], 0) < tok[1]:
                b.r[tok[0]] = tok[1]
        for b in writes:
            b.w = tok
            b.r = {}

    def op(self, eng, fn, reads=(), writes=(), inc=True):
        self._deps(eng, reads, writes)
        ins = fn(eng.e)
        self.n_ins += 1
        if inc:
            eng.count += 1
            ins.then_inc(eng.sem, 1)
            tok = (eng.name, eng.count)
        else:
            tok = (eng.name, eng.count + 1)
        self._mark(tok, reads, writes)
        return ins

    def dma(self, eng, out, in_, reads=(), writes=()):
        slot = self.dma_sems[self.dma_rr]
        self.dma_rr = (self.dma_rr + 1) % len(self.dma_sems)
        nm, cnt = slot
        self._deps(eng, reads, writes)
        self._need(eng, (nm, cnt), False)
        ins = eng.e.dma_start(out=out, in_=in_)
        slot[1] = cnt + 16
        ins.then_inc(self.sems[nm], 16)
        self.n_ins += 1
        self._mark((nm, cnt + 16), reads, writes)
        return ins

    def barrier(self):
        for e in self.engs:
            for o in self.engs:
                if o is not e and o.count > 0:
                    self._need(e, (o.name, o.count), False)
            for nm, cnt in self.dma_sems:
                if cnt > 0:
                    self._need(e, (nm, cnt), False)


class Prog:
    def __init__(self, debug=None, nlayers=DEPTH, stop_after=None):
        self.debug = debug or []
        self.nlayers = nlayers
        self.stop_after = stop_after
        self.nc = bass.Bass("TRN2", target_bir_lowering=False)
        self.uid = 0

    def din(self, name, shape, dt=F32):
        return self.nc.dram_tensor(name, list(shape), dt, kind="ExternalInput").ap()

    def dscr(self, name, shape, dt=F32):
        kind = "ExternalOutput" if name in self.debug else "Internal"
        return self.nc.dram_tensor(name, list(shape), dt, kind=kind).ap()

    def sbuf(self, name, shape, dt):
        self.uid += 1
        return self.nc.sbuf_tensor("%s_u%d" % (name, self.uid), shape, dt)

    def fm(self, ap):
        return ap.rearrange("(c p) t -> p c t", p=128)

    def build(self):
        nc = self.nc
        L = self.nlayers
        I = {}
        I["xs0"] = self.din("xs0", [D, T])
        I["cc"] = self.din("cc", [128, 16])
        I["vecs"] = self.din("vecs", [DEPTH, 128, NV])
        I["w0row"] = self.din("w0row", [DEPTH, 1, 2048])
        I["constf"] = self.din("constf", [128, NCONST])
        I["maskf"] = self.din("maskf", [128, 5 * 256])
        I["cos"] = self.din("cos", [128, SEQ])
        I["sin"] = self.din("sin", [128, SEQ])
        I["w_mod"] = self.din("w_mod", [DEPTH, D, 6 * D])
        I["w_in"] = self.din("w_in", [DEPTH, D, NIN])
        for n in ("w_attn_o", "w_conv_o", "w_rwkv_o", "w_out"):
            I[n] = self.din(n, [DEPTH, D, D])
        I["decay_up"] = self.din("decay_up", [DEPTH, 128, D])
        I["iclr_up"] = self.din("iclr_up", [DEPTH, 128, D])
        I["gate_up"] = self.din("gate_up", [DEPTH, 128, D])
        I["w_ffn_up"] = self.din("w_ffn_up", [DEPTH, D, 2 * DFF])
        I["w_ffn_down"] = self.din("w_ffn_down", [DEPTH, DFF, D])
        self.I = I
        out = nc.dram_tensor("out", [D, SEQ], F32, kind="ExternalOutput").ap()
        S = {}
        S["proj"] = self.dscr("proj", [NIN, T])
        S["att"] = self.dscr("att", [D, T], BF16)
        S["cnv"] = self.dscr("cnv", [D, T], BF16)
        S["rwo"] = self.dscr("rwo", [D, T], BF16)
        S["ffa"] = self.dscr("ffa", [DFF, T], BF16)
        for n in ("rw_r", "rw_v", "rw_k01", "yf", "yb"):
            S[n] = self.dscr(n, [D, T])
        for d in range(2):
            for n in ("At", "Bt", "Kt", "Rt"):
                S["%s%d" % (n, d)] = self.dscr("%s%d" % (n, d), [D, T])
            S["gC%d" % d] = self.dscr("gC%d" % d, [D, NCHUNK])
        S["xsm0"] = self.dscr("xsm0", [D, T])
        S["xs1"] = self.dscr("xs1", [D, T])
        S["xsm1"] = self.dscr("xsm1", [D, T])
        self.S = S

        with contextlib.ExitStack() as es:
            self.es = es
            f = FW(nc, es)
            self.f = f
            self.constf = es.enter_context(self.sbuf("constf", [128, NCONST], F32))
            self.maskf = es.enter_context(self.sbuf("maskf", [128, 5 * 256], F32))
            self.vecs = es.enter_context(self.sbuf("vecs", [128, DEPTH, NV], F32))
            self.modS = es.enter_context(self.sbuf("modS", [128, 48, 2], F32))
            self.dsc = es.enter_context(self.sbuf("dsc", [128, 64, 2], F32))
            self.onesb = es.enter_context(self.sbuf("onesb", [128, 128], BF16))
            self.ps = [es.enter_context(nc.psum_tensor("ps%d" % i, [128, 512], F32)) for i in range(8)]
            self.psb = [f.buf("ps", i) for i in range(8)]
            B = f.buf
            f.dma(f.sp, self.constf[:], I["constf"][:, :], writes=[B("constf")])
            f.dma(f.sp, self.maskf[:], I["maskf"][:, :], writes=[B("maskf")])
            for l in range(DEPTH):
                f.dma(f.sp, self.vecs[:, l, :], I["vecs"][l, :, :], writes=[B("vecs")])
            f.op(f.dve, lambda e: e.tensor_copy(out=self.onesb[:], in_=self.constf[:, C_ONES:C_ONES + 128]),
                 reads=[B("constf")], writes=[B("onesb")])
            f.barrier()
            xs_in = I["xs0"]
            for l in range(L):
                last = (l == DEPTH - 1)
                xsm = S["xsm%d" % l]
                xs_out = out if last else S["xs1"]
                self.phase_mod(l)
                if self.stop_after == ("mod", l): break
                self.phase_inproj(l, xs_in, last)
                if self.stop_after == ("inproj", l): break
                self.phase_attn(l, last)
                if self.stop_after == ("attn", l): break
                self.phase_conv(l, last)
                if self.stop_after == ("conv", l): break
                self.phase_rwkv_prep(l)
                if self.stop_after == ("rwprep", l): break
                self.phase_rwkv_scan(l, last)
                if self.stop_after == ("rwscan", l): break
                self.phase_rwkv_out(l, last)
                if self.stop_after == ("rwout", l): break
                self.phase_merge(l, xs_in, xsm, last)
                if self.stop_after == ("merge", l): break
                self.phase_ffn_up(l, xsm, last)
                if self.stop_after == ("ffnup", l): break
                self.phase_ffn_down(l, xsm, xs_out, last)
                xs_in = xs_out
            f.barrier()
        return nc

    def V(self, l, name, c=0, n=1):
        o = VCOL[name] + c
        return self.vecs[:, l, o:o + n]

    def seqs(self, last):
        return [(CTX, SEQ)] if last else [(0, CTX), (CTX, SEQ)]

    def tiles(self, seqs, n=512):
        r = []
        for s0, sl in seqs:
            t = s0
            while t < s0 + sl:
                m = min(n, s0 + sl - t)
                r.append((t, m, s0, sl))
                t += m
        return r

    def rstd_from_ps(self, ps_ap, psbuf, out_ap, outbuf, scale, eps):
        f = self.f
        f.op(f.act, lambda e: e.activation(out=out_ap, in_=ps_ap, func=AF.Ln, bias=float(eps), scale=float(scale)),
             reads=[psbuf], writes=[outbuf])
        f.op(f.act, lambda e: e.activation(out=out_ap, in_=out_ap, func=AF.Exp, scale=-0.5),
             reads=[outbuf], writes=[outbuf])

    def phase_mod(self, l):
        nc, f, es0 = self.nc, self.f, self.es
        B = f.buf
        I = self.I
        with contextlib.ExitStack() as es:
            cT = es.enter_context(self.sbuf("cT", [128, 8, 2], F32))
            wt = [es.enter_context(self.sbuf("wmod%d" % i, [128, 8, 1024], F32)) for i in range(2)]
            f.dma(f.sp, cT[:], I["cc"].rearrange("p (k j) -> p k j", j=2), writes=[B("cT")])
            f.op(f.act, lambda e: e.activation(out=cT[:], in_=cT[:], func=AF.Silu), reads=[B("cT")], writes=[B("cT")])
            wv = I["w_mod"][l].rearrange("(k p) n -> p k n", p=128)
            ps = self.ps[0]
            for g in range(6):
                w = wt[g % 2]
                for k in range(8):
                    f.dma(f.sp, w[:, k, :], wv[:, k, g * 1024:(g + 1) * 1024], writes=[B("wmod", g % 2, k)])
                for oc in range(8):
                    col = (g * 8 + oc) * 2
                    for k in range(8):
                        f.op(f.pe, lambda e: e.matmul(ps[:, col:col + 2], lhsT=w[:, k, oc * 128:(oc + 1) * 128],
                                                      rhs=cT[:, k, :], start=(k == 0), stop=(k == 7)),
                             reads=[B("wmod", g % 2, k), B("cT")], writes=[self.psb[0]], inc=(k == 7))
            psv = ps[:, 0:96].rearrange("p (c j) -> p c j", j=2)
            for j in range(2):
                f.op(f.dve, lambda e: e.tensor_tensor(out=self.modS[:, :, j], in0=psv[:, :, j],
                                                      in1=self.V(l, "b_mod", 0, 48), op=ALU.add),
                     reads=[self.psb[0], B("vecs")], writes=[B("modS")])
            for j in range(2):
                def m(i):
                    return self.modS[:, i * 8:(i + 1) * 8, j]
                rd = [B("modS"), B("vecs")]
                wr = [B("dsc")]
                f.op(f.dve, lambda e: e.scalar_tensor_tensor(out=self.dsc[:, 0:8, j], in0=m(1), scalar=1.0,
                                                             in1=self.V(l, "g_pre_mix", 0, 8), op0=ALU.add, op1=ALU.mult),
                     reads=rd, writes=wr)
                f.op(f.dve, lambda e: e.tensor_copy(out=self.dsc[:, 8:16, j], in_=m(0)), reads=rd, writes=wr)
                f.op(f.dve, lambda e: e.tensor_tensor(out=self.dsc[:, 16:24, j], in0=m(2),
                                                      in1=self.V(l, "g_post_mix", 0, 8), op=ALU.mult), reads=rd, writes=wr)
                f.op(f.dve, lambda e: e.scalar_tensor_tensor(out=self.dsc[:, 24:32, j], in0=m(4), scalar=1.0,
                                                             in1=self.V(l, "g_pre_ffn", 0, 8), op0=ALU.add, op1=ALU.mult),
                     reads=rd, writes=wr)
                f.op(f.dve, lambda e: e.tensor_copy(out=self.dsc[:, 32:40, j], in_=m(3)), reads=rd, writes=wr)
                f.op(f.dve, lambda e: e.tensor_tensor(out=self.dsc[:, 40:48, j], in0=m(5),
                                                      in1=self.V(l, "g_post_ffn", 0, 8), op=ALU.mult), reads=rd, writes=wr)
            f.barrier()

    def prenorm(self, es, src, seqs, base, hT, hcol_of):
        nc, f = self.nc, self.f
        B = f.buf
        xt = [es.enter_context(self.sbuf("pn_x%d" % i, [128, 8, 512], F32)) for i in range(2)]
        sq = es.enter_context(self.sbuf("pn_sq", [128, 8, 512], F32))
        rs = es.enter_context(self.sbuf("pn_rs", [128, 512], F32))
        srcv = self.fm(src)
        for i, (t0, n, s0, sl) in enumerate(self.tiles(seqs)):
            j = 1 if s0 == 0 else 0
            x = xt[i % 2]
            xb = B("pn_x", i % 2)
            f.dma(f.sp, x[:, :, 0:n], srcv[:, :, t0:t0 + n], writes=[xb])
            f.op(f.act, lambda e: e.activation(out=sq[:, :, 0:n], in_=x[:, :, 0:n], func=AF.Square),
                 reads=[xb], writes=[B("pn_sq")])
            ps, pb = self.ps[7], self.psb[7]
            for k in range(8):
                f.op(f.pe, lambda e: e.matmul(ps[:, 0:n], lhsT=self.constf[:, C_ONES:C_ONES + 128], rhs=sq[:, k, 0:n],
                                              start=(k == 0), stop=(k == 7)),
                     reads=[B("pn_sq"), B("constf")], writes=[pb], inc=(k == 7))
            self.rstd_from_ps(ps[:, 0:n], pb, rs[:, 0:n], B("pn_rs"), 1.0 / D, EPS)
            c0 = hcol_of(t0)
            for k in range(8):
                f.op(f.dve, lambda e: e.scalar_tensor_tensor(out=x[:, k, 0:n], in0=x[:, k, 0:n],
                                                             scalar=self.dsc[:, base + k, j:j + 1], in1=rs[:, 0:n],
                                                             op0=ALU.mult, op1=ALU.mult),
                     reads=[xb, B("pn_rs"), B("dsc")], writes=[xb])
                f.op(f.act, lambda e: e.activation(out=hT[:, k, c0:c0 + n], in_=x[:, k, 0:n], func=AF.Identity,
                                                   bias=self.dsc[:, base + 8 + k, j:j + 1], scale=1.0),
                     reads=[xb, B("dsc")], writes=[B("hT")])

    def phase_inproj(self, l, xs_in, last):
        nc, f = self.nc, self.f
        B = f.buf
        I, S = self.I, self.S
        seqs = [(0, CTX), (CTX, SEQ)]
        with contextlib.ExitStack() as es:
            hT = es.enter_context(self.sbuf("hT", [128, 8, T], BF16))
            with contextlib.ExitStack() as es2:
                self.prenorm(es2, xs_in, seqs, 0, hT, lambda t: t)
                f.barrier()
            wt = [es.enter_context(self.sbuf("win%d" % i, [128, 8, 1024], BF16)) for i in range(2)]
            st = [es.enter_context(self.sbuf("ipst%d" % i, [128, 512], F32)) for i in range(4)]
            wv = I["w_in"][l].rearrange("(k p) n -> p k n", p=128)
            pv = self.fm(S["proj"])
            ngrp = (NIN + 1023) // 1024
            cnt = 0
            for g in range(ngrp):
                ncol = min(1024, NIN - g * 1024)
                w = wt[g % 2]
                for k in range(8):
                    f.dma(f.pool, w[:, k, 0:ncol], wv[:, k, g * 1024:g * 1024 + ncol], writes=[B("win", g % 2, k)])
                for (t0, n, s0, sl) in self.tiles(seqs):
                    for oc in range(ncol // 128):
                        pi = cnt % 6
                        ps, pb = self.ps[pi], self.psb[pi]
                        for k in range(8):
                            f.op(f.pe, lambda e: e.matmul(ps[:, 0:n], lhsT=w[:, k, oc * 128:(oc + 1) * 128],
                                                          rhs=hT[:, k, t0:t0 + n], start=(k == 0), stop=(k == 7)),
                                 reads=[B("win", g % 2, k), B("hT")], writes=[pb], inc=(k == 7))
                        s = st[cnt % 4]
                        sb = B("ipst", cnt % 4)
                        if cnt % 2 == 0:
                            f.op(f.act, lambda e: e.copy(out=s[:, 0:n], in_=ps[:, 0:n]), reads=[pb], writes=[sb])
                        else:
                            f.op(f.dve, lambda e: e.tensor_copy(out=s[:, 0:n], in_=ps[:, 0:n]), reads=[pb], writes=[sb])
                        f.dma(f.sp, pv[:, g * 8 + oc, t0:t0 + n], s[:, 0:n], reads=[sb])
                        cnt += 1
            f.barrier()


    def phase_attn(self, l, last):
        nc, f = self.nc, self.f
        B = f.buf
        I, S = self.I, self.S
        pv = self.fm(S["proj"])
        av = self.fm(S["att"])
        cf = self.constf
        with contextlib.ExitStack() as es:
            sb = lambda n, s, d: es.enter_context(self.sbuf(n, s, d))
            kT = sb("kT", [128, 2, T], BF16)
            Vt = sb("Vt", [128, T // 128, 2, 128], BF16)
            cos = sb("cos", [128, SEQ], F32)
            sin = sb("sin", [128, SEQ], F32)
            qg = sb("qg", [128, 1], F32)
            raw = [sb("a_raw%d" % i, [128, 512], F32) for i in range(2)]
            tmps = [[sb("a_%s%d" % (nm_, sl_), [128, 512], F32) for nm_ in ("sq", "rs", "kn", "t1", "t2")] for sl_ in range(2)]
            qT = [sb("a_qT%d" % i, [128, 512], BF16) for i in range(2)]
            pT = [sb("a_pT%d" % i, [128, 512], BF16) for i in range(4)]
            rinv = sb("a_rinv", [128, 512], F32)
            ost = [sb("a_ost%d" % i, [128, 512], BF16) for i in range(2)]
            f.dma(f.sp, cos[:], I["cos"][:, :], writes=[B("cos")])
            f.dma(f.sp, sin[:], I["sin"][:, :], writes=[B("sin")])
            f.op(f.dve, lambda e: e.tensor_scalar(out=qg[:], in0=self.V(l, "q_norm"), scalar1=float(128 ** -0.5),
                                                  scalar2=None, op0=ALU.mult), reads=[B("vecs")], writes=[B("qg")])
            self._nr = 0

            def normrope_g(chunk, t0, n, gain, is_x, out_ap, outbuf, slot=0):
                r = raw[slot]
                rb = B("a_raw", slot)
                sq, rs, kn, t1, t2 = tmps[slot]
                pk = 7 - slot
                f.dma(f.sp, r[:, 0:n], pv[:, chunk, t0:t0 + n], writes=[rb])
                f.op(f.act, lambda e: e.activation(out=sq[:, 0:n], in_=r[:, 0:n], func=AF.Square),
                     reads=[rb], writes=[B("a_sq", slot)])
                f.op(f.pe, lambda e: e.matmul(self.ps[pk][:, 0:n], lhsT=cf[:, C_ONES:C_ONES + 128], rhs=sq[:, 0:n],
                                              start=True, stop=True), reads=[B("a_sq", slot), B("constf")], writes=[self.psb[pk]])
                yield
                self.rstd_from_ps(self.ps[pk][:, 0:n], self.psb[pk], rs[:, 0:n], B("a_rs", slot), 1.0 / 128, EPS)
                f.op(f.dve, lambda e: e.scalar_tensor_tensor(out=kn[:, 0:n], in0=r[:, 0:n], scalar=gain, in1=rs[:, 0:n],
                                                             op0=ALU.mult, op1=ALU.mult),
                     reads=[rb, B("a_rs", slot), B("vecs"), B("qg")], writes=[B("a_kn", slot)])
                yield
                if is_x:
                    p0 = t0 - CTX
                    f.op(f.pe, lambda e: e.matmul(self.ps[pk][:, 0:n], lhsT=cf[:, C_ROT:C_ROT + 128], rhs=kn[:, 0:n],
                                                  start=True, stop=True), reads=[B("a_kn", slot), B("constf")], writes=[self.psb[pk]])
                    yield
                    f.op(f.pool, lambda e: e.tensor_tensor(out=t1[:, 0:n], in0=kn[:, 0:n], in1=cos[:, p0:p0 + n], op=ALU.mult),
                         reads=[B("a_kn", slot), B("cos")], writes=[B("a_t1", slot)])
                    f.op(f.dve, lambda e: e.tensor_tensor(out=t2[:, 0:n], in0=self.ps[pk][:, 0:n], in1=sin[:, p0:p0 + n], op=ALU.mult),
                         reads=[self.psb[pk], B("sin")], writes=[B("a_t2", slot)])
                    f.op(f.pool, lambda e: e.tensor_tensor(out=out_ap, in0=t1[:, 0:n], in1=t2[:, 0:n], op=ALU.add),
                         reads=[B("a_t1", slot), B("a_t2", slot)], writes=[outbuf])
                else:
                    f.op(f.pool, lambda e: e.tensor_copy(out=out_ap, in_=kn[:, 0:n]), reads=[B("a_kn", slot)], writes=[outbuf])

            def normrope(*a_):
                for _ in normrope_g(*a_):
                    pass

            allseq = [(0, CTX), (CTX, SEQ)]
            for (t0, n, s0, sl) in self.tiles(allseq):
                gens = [normrope_g(8 + kvh, t0, n, self.V(l, "k_norm"), s0 != 0, kT[:, kvh, t0:t0 + n], B("kT", kvh), kvh)
                        for kvh in range(2)]
                while gens:
                    alive = []
                    for g_ in gens:
                        try:
                            next(g_)
                            alive.append(g_)
                        except StopIteration:
                            pass
                    gens = alive
            i = 0
            for kvh in range(2):
                for (t0, n, s0, sl) in self.tiles(allseq):
                    r = raw[i % 2]
                    rb = B("a_raw", i % 2)
                    i += 1
                    f.dma(f.sp, r[:, 0:n], pv[:, 10 + kvh, t0:t0 + n], writes=[rb])
                    nb = n // 128
                    for j in range(nb):
                        f.op(f.pe, lambda e: e.transpose(self.ps[7][:, j * 128:(j + 1) * 128], r[:, j * 128:(j + 1) * 128],
                                                         cf[:, C_ID:C_ID + 128]),
                             reads=[rb, B("constf")], writes=[self.psb[7]], inc=(j == nb - 1))
                    b0 = t0 // 128
                    f.op(f.dve, lambda e: e.tensor_copy(out=Vt[:, b0:b0 + nb, kvh, :],
                                                        in_=self.ps[7][:, 0:nb * 128].rearrange("p (b d) -> p b d", d=128)),
                         reads=[self.psb[7]], writes=[B("Vt")])
            units = [(h, t0, n, s0) for h in range(8) for (t0, n, s0, sl) in self.tiles(self.seqs(last))]

            def qprep(ui):
                h, t0, n, s0 = units[ui]
                return normrope_g(h, t0, n, qg[:, 0:1], s0 != 0, qT[ui % 2][:, 0:n], B("a_qT", ui % 2))
            for _ in qprep(0):
                pass
            for qi, (h, t0, n, s0) in enumerate(units):
                kvh = h // 4
                is_x = s0 != 0
                q = qT[qi % 2]
                qb = B("a_qT", qi % 2)
                nxt = qprep(qi + 1) if qi + 1 < len(units) else None
                nblk = (T // 128) if is_x else (CTX // 128)
                po, pob = self.ps[3 + qi % 2], self.psb[3 + qi % 2]
                pr, prb = self.ps[5 + qi % 2], self.psb[5 + qi % 2]

                def smm(jb):
                    f.op(f.pe, lambda e: e.matmul(self.ps[jb % 3][:, 0:n], lhsT=kT[:, kvh, jb * 128:(jb + 1) * 128],
                                                  rhs=q[:, 0:n], start=True, stop=True),
                         reads=[B("kT", kvh), qb], writes=[self.psb[jb % 3]])

                def pvmm(jb):
                    p = pT[jb % 4]
                    pb = B("a_pT", jb % 4)
                    lastb = (jb == nblk - 1)
                    f.op(f.pe, lambda e: e.matmul(po[:, 0:n], lhsT=Vt[:, jb, kvh, :], rhs=p[:, 0:n],
                                                  start=(jb == 0), stop=lastb),
                         reads=[B("Vt"), pb], writes=[pob], inc=False)
                    f.op(f.pe, lambda e: e.matmul(pr[:, 0:n], lhsT=self.onesb[:], rhs=p[:, 0:n],
                                                  start=(jb == 0), stop=lastb),
                         reads=[B("onesb"), pb], writes=[prb], inc=lastb)
                smm(0)
                if nblk > 1:
                    smm(1)
                for jb in range(nblk):
                    p = pT[jb % 4]
                    pb = B("a_pT", jb % 4)
                    f.op(f.act, lambda e: e.activation(out=p[:, 0:n], in_=self.ps[jb % 3][:, 0:n], func=AF.Exp),
                         reads=[self.psb[jb % 3]], writes=[pb])
                    if jb + 2 < nblk:
                        smm(jb + 2)
                    if jb >= 1:
                        pvmm(jb - 1)
                    if nxt is not None and jb in (4, 10, 16, 22):
                        try:
                            next(nxt)
                        except StopIteration:
                            nxt = None
                pvmm(nblk - 1)
                if nxt is not None:
                    for _ in nxt:
                        pass
                f.op(f.dve, lambda e: e.reciprocal(out=rinv[:, 0:n], in_=pr[:, 0:n]), reads=[prb], writes=[B("a_rinv")])
                o = ost[qi % 2]
                ob = B("a_ost", qi % 2)
                f.op(f.dve, lambda e: e.tensor_tensor(out=o[:, 0:n], in0=po[:, 0:n], in1=rinv[:, 0:n], op=ALU.mult),
                     reads=[pob, B("a_rinv")], writes=[ob])
                f.dma(f.sp, av[:, h, t0:t0 + n], o[:, 0:n], reads=[ob])
            f.barrier()

    def phase_conv(self, l, last):
        nc, f = self.nc, self.f
        B = f.buf
        S = self.S
        pv = self.fm(S["proj"])
        cv = self.fm(S["cnv"])
        cf = self.constf
        with contextlib.ExitStack() as es:
            sb = lambda n, s, d: es.enter_context(self.sbuf(n, s, d))
            at = [sb("c_a%d" % i, [128, 544], F32) for i in range(2)]
            bt = [sb("c_b%d" % i, [128, 544], F32) for i in range(2)]
            y = sb("c_y", [128, 8, 512], F32)
            sq = [sb("c_sq%d" % i, [128, 512], F32) for i in range(2)]
            mean = sb("c_mean", [128, 512], F32)
            msq = sb("c_msq", [128, 512], F32)
            rstd = sb("c_rstd", [128, 512], F32)
            tt = [sb("c_t%d" % i, [128, 512], F32) for i in range(2)]
            ost = [sb("c_o%d" % i, [128, 512], BF16) for i in range(2)]
            gbf = [sb("c_gb%d" % i, [128, 544], BF16) for i in range(2)]
            dg = sb("c_dg", [128, 8, 31, 128], BF16)
            k_ = 0
            for c in range(8):
                for j in range(31):
                    wj = self.V(l, "conv_w", c * 31 + j)
                    e3 = k_ % 3
                    k_ += 1
                    if e3 == 0:
                        f.op(f.pool, lambda e: e.tensor_scalar(out=dg[:, c, j, :], in0=cf[:, C_ID:C_ID + 128], scalar1=wj, scalar2=None,
                                                               op0=ALU.mult), reads=[B("constf"), B("vecs")], writes=[B("c_dg", c, 0)])
                    elif e3 == 1:
                        f.op(f.dve, lambda e: e.tensor_scalar(out=dg[:, c, j, :], in0=cf[:, C_ID:C_ID + 128], scalar1=wj, scalar2=None,
                                                              op0=ALU.mult), reads=[B("constf"), B("vecs")], writes=[B("c_dg", c, 1)])
                    else:
                        f.op(f.act, lambda e: e.activation(out=dg[:, c, j, :], in_=cf[:, C_ID:C_ID + 128], func=AF.Copy, scale=wj),
                             reads=[B("constf"), B("vecs")], writes=[B("c_dg", c, 2)])
            it = 0
            for (t0, n, s0, sl) in self.tiles(self.seqs(last)):
                lo = max(t0 - 15, s0)
                hi = min(t0 + n + 15, s0 + sl)
                off = lo - (t0 - 15)
                edge = (lo != t0 - 15) or (hi != t0 + n + 15)
                for c in range(8):
                    a = at[it % 2]
                    b = bt[it % 2]
                    ab = B("c_a", it % 2)
                    bb = B("c_b", it % 2)
                    it += 1
                    if edge:
                        f.op(f.pool, lambda e: e.memset(a[:, 0:n + 30], 0.0), writes=[ab])
                        f.op(f.pool, lambda e: e.memset(b[:, 0:n + 30], 0.0), writes=[bb])
                    f.dma(f.sp, a[:, off:off + hi - lo], pv[:, 12 + c, lo:hi], writes=[ab])
                    f.dma(f.sp, b[:, off:off + hi - lo], pv[:, 20 + c, lo:hi], writes=[bb])
                    f.op(f.act, lambda e: e.activation(out=b[:, 0:n + 30], in_=b[:, 0:n + 30], func=AF.Sigmoid),
                         reads=[bb], writes=[bb])
                    gb_ = gbf[it % 2]
                    gbb = B("c_gb", it % 2)
                    f.op(f.pool, lambda e: e.tensor_tensor(out=gb_[:, 0:n + 30], in0=a[:, 0:n + 30], in1=b[:, 0:n + 30], op=ALU.mult),
                         reads=[ab, bb], writes=[gbb])
                    yb = B("c_y", c)
                    pi = 2 + it % 4
                    for j in range(31):
                        f.op(f.pe, lambda e: e.matmul(self.ps[pi][:, 0:n], lhsT=dg[:, c, j, :], rhs=gb_[:, j:j + n],
                                                      start=(j == 0), stop=(j == 30)),
                             reads=[gbb, B("c_dg", c, 0), B("c_dg", c, 1), B("c_dg", c, 2)], writes=[self.psb[pi]], inc=(j == 30))
                    f.op(f.act, lambda e: e.activation(out=y[:, c, 0:n], in_=self.ps[pi][:, 0:n], func=AF.Identity,
                                                       bias=self.V(l, "conv_b", c), scale=1.0),
                         reads=[self.psb[pi], B("vecs")], writes=[yb])
                    s = sq[c % 2]
                    sqb = B("c_sq", c % 2)
                    f.op(f.act, lambda e: e.activation(out=s[:, 0:n], in_=y[:, c, 0:n], func=AF.Square), reads=[yb], writes=[sqb])
                    f.op(f.pe, lambda e: e.matmul(self.ps[0][:, 0:n], lhsT=cf[:, C_ONES:C_ONES + 128], rhs=y[:, c, 0:n],
                                                  start=(c == 0), stop=(c == 7)), reads=[yb, B("constf")], writes=[self.psb[0]], inc=False)
                    f.op(f.pe, lambda e: e.matmul(self.ps[1][:, 0:n], lhsT=cf[:, C_ONES:C_ONES + 128], rhs=s[:, 0:n],
                                                  start=(c == 0), stop=(c == 7)), reads=[sqb, B("constf")], writes=[self.psb[1]])
                f.op(f.act, lambda e: e.activation(out=mean[:, 0:n], in_=self.ps[0][:, 0:n], func=AF.Copy, scale=1.0 / D),
                     reads=[self.psb[0]], writes=[B("c_mean")])
                f.op(f.dve, lambda e: e.tensor_tensor(out=msq[:, 0:n], in0=mean[:, 0:n], in1=mean[:, 0:n], op=ALU.mult),
                     reads=[B("c_mean")], writes=[B("c_msq")])
                f.op(f.dve, lambda e: e.scalar_tensor_tensor(out=rstd[:, 0:n], in0=self.ps[1][:, 0:n], scalar=1.0 / D,
                                                             in1=msq[:, 0:n], op0=ALU.mult, op1=ALU.subtract),
                     reads=[self.psb[1], B("c_msq")], writes=[B("c_rstd")])
                self.rstd_from_ps(rstd[:, 0:n], B("c_rstd"), rstd[:, 0:n], B("c_rstd"), 1.0, LN_EPS)
                for c in range(8):
                    t = tt[c % 2]
                    tb = B("c_t", c % 2)
                    f.op(f.dve, lambda e: e.tensor_tensor(out=t[:, 0:n], in0=y[:, c, 0:n], in1=mean[:, 0:n], op=ALU.subtract),
                         reads=[B("c_y", c), B("c_mean")], writes=[tb])
                    f.op(f.pool, lambda e: e.tensor_tensor(out=t[:, 0:n], in0=t[:, 0:n], in1=rstd[:, 0:n], op=ALU.mult),
                         reads=[tb, B("c_rstd")], writes=[tb])
                    o = ost[c % 2]
                    ob = B("c_o", c % 2)
                    f.op(f.act, lambda e: e.activation(out=o[:, 0:n], in_=t[:, 0:n], func=AF.Silu,
                                                       bias=self.V(l, "conv_ln_b", c), scale=self.V(l, "conv_ln_g", c)),
                         reads=[tb, B("vecs")], writes=[ob])
                    f.dma(f.sp, cv[:, c, t0:t0 + n], o[:, 0:n], reads=[ob])
            f.barrier()

    def post_residual(self, es, mo, mob, xt, xtb, base, j, dstv, c0, n, sq, rs):
        f = self.f
        B = f.buf
        cf = self.constf
        for k in range(8):
            s = sq[k % 2]
            sqb = B("pr_sq", k % 2)
            f.op(f.act, lambda e: e.activation(out=s[:, 0:n], in_=mo[:, k, 0:n], func=AF.Square), reads=[mob], writes=[sqb])
            f.op(f.pe, lambda e: e.matmul(self.ps[7][:, 0:n], lhsT=cf[:, C_ONES:C_ONES + 128], rhs=s[:, 0:n],
                                          start=(k == 0), stop=(k == 7)), reads=[sqb, B("constf")], writes=[self.psb[7]])
        self.rstd_from_ps(self.ps[7][:, 0:n], self.psb[7], rs[:, 0:n], B("pr_rs"), 1.0 / D, EPS)
        for k in range(8):
            f.op(f.pool, lambda e: e.tensor_tensor(out=mo[:, k, 0:n], in0=mo[:, k, 0:n], in1=rs[:, 0:n], op=ALU.mult),
                 reads=[mob, B("pr_rs")], writes=[mob])
            f.op(f.dve, lambda e: e.scalar_tensor_tensor(out=xt[:, k, 0:n], in0=mo[:, k, 0:n],
                                                         scalar=self.dsc[:, base + k, j:j + 1], in1=xt[:, k, 0:n],
                                                         op0=ALU.mult, op1=ALU.add),
                 reads=[mob, xtb, B("dsc")], writes=[xtb])
        f.dma(f.sp, dstv[:, :, c0:c0 + n], xt[:, :, 0:n], reads=[xtb])

    def phase_merge(self, l, xs_in, xsm, last):
        nc, f = self.nc, self.f
        B = f.buf
        I, S = self.I, self.S
        pv = self.fm(S["proj"])
        with contextlib.ExitStack() as es:
            sb = lambda n, s, d: es.enter_context(self.sbuf(n, s, d))
            W = [sb("m_w%d" % i, [128, 8, 1024], BF16) for i in range(4)]
            for i, nm in enumerate(("w_attn_o", "w_conv_o", "w_rwkv_o", "w_out")):
                wv = I[nm][l].rearrange("(k p) n -> p k n", p=128)
                for k in range(8):
                    f.dma(f.pool, W[i][:, k, :], wv[:, k, :], writes=[B("m_w", i, k)])
            br = [sb("m_br%d" % i, [128, 8, 512], BF16) for i in range(3)]
            gt = [sb("m_g%d" % i, [128, 512], F32) for i in range(6)]
            tt = [sb("m_t%d" % i, [128, 512], F32) for i in range(6)]
            mT = sb("m_mT", [128, 8, 512], BF16)
            mo = sb("m_mo", [128, 8, 512], F32)
            xt = sb("m_xt", [128, 8, 512], F32)
            sq = [sb("m_sq%d" % i, [128, 512], F32) for i in range(2)]
            rs = sb("m_rs", [128, 512], F32)
            srcs = [self.fm(S["att"]), self.fm(S["cnv"]), self.fm(S["rwo"])]
            xv = self.fm(xs_in)
            dv = self.fm(xsm)
            for (t0, n, s0, sl) in self.tiles(self.seqs(last)):
                j = 1 if s0 == 0 else 0
                for b in range(3):
                    f.dma(f.sp, br[b][:, :, 0:n], srcs[b][:, :, t0:t0 + n], writes=[B("m_br", b)])
                f.dma(f.sp, xt[:, :, 0:n], xv[:, :, t0:t0 + n], writes=[B("m_xt")])
                for oc in range(8):
                    par = oc % 2
                    for b in range(3):
                        pi = b + 3 * par
                        for k in range(8):
                            f.op(f.pe, lambda e: e.matmul(self.ps[pi][:, 0:n], lhsT=W[b][:, k, oc * 128:(oc + 1) * 128],
                                                          rhs=br[b][:, k, 0:n], start=(k == 0), stop=(k == 7)),
                                 reads=[B("m_w", b, k), B("m_br", b)], writes=[self.psb[pi]], inc=(k == 7))
                    for b in range(3):
                        pi = b + 3 * par
                        g = gt[pi]
                        gb = B("m_g", pi)
                        f.dma(f.sp, g[:, 0:n], pv[:, 55 + 8 * b + oc, t0:t0 + n], writes=[gb])
                        f.op(f.act, lambda e: e.activation(out=g[:, 0:n], in_=g[:, 0:n], func=AF.Sigmoid), reads=[gb], writes=[gb])
                        f.op(f.dve, lambda e: e.tensor_tensor(out=tt[pi][:, 0:n], in0=self.ps[pi][:, 0:n], in1=g[:, 0:n], op=ALU.mult),
                             reads=[self.psb[pi], gb], writes=[B("m_t", pi)])
                    p0 = 3 * par
                    f.op(f.pool, lambda e: e.tensor_tensor(out=tt[p0][:, 0:n], in0=tt[p0][:, 0:n], in1=tt[p0 + 1][:, 0:n], op=ALU.add),
                         reads=[B("m_t", p0), B("m_t", p0 + 1)], writes=[B("m_t", p0)])
                    f.op(f.pool, lambda e: e.tensor_tensor(out=mT[:, oc, 0:n], in0=tt[p0][:, 0:n], in1=tt[p0 + 2][:, 0:n], op=ALU.add),
                         reads=[B("m_t", p0), B("m_t", p0 + 2)], writes=[B("m_mT")])
                for oc in range(8):
                    pi = 6
                    for k in range(8):
                        f.op(f.pe, lambda e: e.matmul(self.ps[pi][:, 0:n], lhsT=W[3][:, k, oc * 128:(oc + 1) * 128],
                                                      rhs=mT[:, k, 0:n], start=(k == 0), stop=(k == 7)),
                             reads=[B("m_w", 3, k), B("m_mT")], writes=[self.psb[pi]], inc=(k == 7))
                    f.op(f.act, lambda e: e.copy(out=mo[:, oc, 0:n], in_=self.ps[pi][:, 0:n]), reads=[self.psb[pi]], writes=[B("m_mo")])
                self.post_residual(es, mo, B("m_mo"), xt, B("m_xt"), 16, j, dv, t0, n, sq, rs)
            f.barrier()

    def phase_ffn_up(self, l, xsm, last):
        nc, f = self.nc, self.f
        B = f.buf
        I, S = self.I, self.S
        fv = self.fm(S["ffa"])
        TP = T + 4
        hcol = lambda t: (t + 1) if t < CTX else (t + 3)
        with contextlib.ExitStack() as es:
            sb = lambda n, s, d: es.enter_context(self.sbuf(n, s, d))
            hT = sb("hT", [128, 8, TP], BF16)
            for c in (0, CTX + 1, CTX + 2, TP - 1):
                f.op(f.pool, lambda e: e.memset(hT[:, :, c:c + 1], 0.0), writes=[B("hT")])
            with contextlib.ExitStack() as es2:
                self.prenorm(es2, xsm, self.seqs(last), 24, hT, hcol)
                f.barrier()
            GS = 4
            wt = [sb("fu_w%d" % i, [128, 8, 2, GS * 128], BF16) for i in range(2)]
            cg = [sb("fu_cg%d" % i, [128, 512], F32) for i in range(2)]
            cv = [sb("fu_cv%d" % i, [128, 512], F32) for i in range(2)]
            ao = [sb("fu_a%d" % i, [128, 512], BF16) for i in range(2)]
            wv = I["w_ffn_up"][l].rearrange("(k p) n -> p k n", p=128)
            it = 0
            for gi, j0 in enumerate(range(0, 22, GS)):
                gs = min(GS, 22 - j0)
                w = wt[gi % 2]
                for k in range(8):
                    f.dma(f.pool, w[:, k, 0, 0:gs * 128], wv[:, k, j0 * 128:(j0 + gs) * 128], writes=[B("fu_w", gi % 2, k, 0)])
                    f.dma(f.pool, w[:, k, 1, 0:gs * 128], wv[:, k, DFF + j0 * 128:DFF + (j0 + gs) * 128],
                          writes=[B("fu_w", gi % 2, k, 1)])
                for (t0, n, s0, sl) in self.tiles(self.seqs(last), 510):
                    c0 = hcol(t0)
                    for jj in range(gs):
                        jc = j0 + jj
                        par = it % 3
                        for hv in range(2):
                            pi = 2 * par + hv
                            for k in range(8):
                                f.op(f.pe, lambda e: e.matmul(self.ps[pi][:, 0:n + 2], lhsT=w[:, k, hv, jj * 128:(jj + 1) * 128],
                                                              rhs=hT[:, k, c0 - 1:c0 + n + 1], start=(k == 0), stop=(k == 7)),
                                     reads=[B("fu_w", gi % 2, k, hv), B("hT")], writes=[self.psb[pi]], inc=(k == 7))
                        res = []
                        for hv, dst, nm in ((0, cg[it % 2], "fu_cg"), (1, cv[it % 2], "fu_cv")):
                            pi = 2 * par + hv
                            ch = jc + 22 * hv
                            wc = lambda q: self.V(l, "ffn_conv_w", ch * 3 + q)
                            db = B(nm, it % 2)
                            f.op(f.act, lambda e: e.activation(out=dst[:, 0:n], in_=self.ps[pi][:, 0:n], func=AF.Copy, scale=wc(0)),
                                 reads=[self.psb[pi], B("vecs")], writes=[db])
                            for q in (1, 2):
                                f.op(f.dve, lambda e: e.scalar_tensor_tensor(out=dst[:, 0:n], in0=self.ps[pi][:, q:q + n], scalar=wc(q),
                                                                             in1=dst[:, 0:n], op0=ALU.mult, op1=ALU.add),
                                     reads=[self.psb[pi], db, B("vecs")], writes=[db])
                        g_, v_ = cg[it % 2], cv[it % 2]
                        f.op(f.act, lambda e: e.activation(out=g_[:, 0:n], in_=g_[:, 0:n], func=AF.Silu),
                             reads=[B("fu_cg", it % 2)], writes=[B("fu_cg", it % 2)])
                        a = ao[it % 2]
                        f.op(f.pool, lambda e: e.tensor_tensor(out=a[:, 0:n], in0=g_[:, 0:n], in1=v_[:, 0:n], op=ALU.mult),
                             reads=[B("fu_cg", it % 2), B("fu_cv", it % 2)], writes=[B("fu_a", it % 2)])
                        f.dma(f.sp, fv[:, jc, t0:t0 + n], a[:, 0:n], reads=[B("fu_a", it % 2)])
                        it += 1
            f.barrier()

    def phase_ffn_down(self, l, xsm, xs_out, last):
        nc, f = self.nc, self.f
        B = f.buf
        I, S = self.I, self.S
        fv = self.fm(S["ffa"])
        with contextlib.ExitStack() as es:
            sb = lambda n, s, d: es.enter_context(self.sbuf(n, s, d))
            W = sb("fd_w", [128, 22, 1024], BF16)
            wv = I["w_ffn_down"][l].rearrange("(k p) n -> p k n", p=128)
            for k in range(22):
                f.dma(f.pool, W[:, k, :], wv[:, k, :], writes=[B("fd_w", k)])
            at = [sb("fd_a%d" % i, [128, 22, 512], BF16) for i in range(2)]
            mo = sb("fd_mo", [128, 8, 512], F32)
            xt = sb("fd_xt", [128, 8, 512], F32)
            sq = [sb("fd_sq%d" % i, [128, 512], F32) for i in range(2)]
            rs = sb("fd_rs", [128, 512], F32)
            xv = self.fm(xsm)
            dv = self.fm(xs_out)
            for it, (t0, n, s0, sl) in enumerate(self.tiles(self.seqs(last))):
                j = 1 if s0 == 0 else 0
                a = at[it % 2]
                ab = B("fd_a", it % 2)
                f.dma(f.sp, a[:, :, 0:n], fv[:, :, t0:t0 + n], writes=[ab])
                f.dma(f.sp, xt[:, :, 0:n], xv[:, :, t0:t0 + n], writes=[B("fd_xt")])
                for oc in range(8):
                    pi = oc % 4
                    for k in range(22):
                        f.op(f.pe, lambda e: e.matmul(self.ps[pi][:, 0:n], lhsT=W[:, k, oc * 128:(oc + 1) * 128],
                                                      rhs=a[:, k, 0:n], start=(k == 0), stop=(k == 21)),
                             reads=[B("fd_w", k), ab], writes=[self.psb[pi]], inc=(k == 21))
                    f.op(f.act, lambda e: e.copy(out=mo[:, oc, 0:n], in_=self.ps[pi][:, 0:n]), reads=[self.psb[pi]], writes=[B("fd_mo")])
                c0 = (t0 - CTX) if last else t0
                self.post_residual(es, mo, B("fd_mo"), xt, B("fd_xt"), 40, j, dv, c0, n, sq, rs)
            f.barrier()

    def phase_rwkv_prep(self, l):
        nc, f = self.nc, self.f
        B = f.buf
        I, S = self.I, self.S
        pv = self.fm(S["proj"])
        cf = self.constf
        NT = 256
        with contextlib.ExitStack() as es:
            sb = lambda n, s, d: es.enter_context(self.sbuf(n, s, d))
            dup = sb("rp_dup", [128, D], F32)
            iup = sb("rp_iup", [128, D], F32)
            w0r = sb("rp_w0r", [1, 2048], F32)
            omka = sb("rp_omka", [128, 8], F32)
            f.dma(f.sp, dup[:], I["decay_up"][l, :, :], writes=[B("rp_dup")])
            f.dma(f.sp, iup[:], I["iclr_up"][l, :, :], writes=[B("rp_iup")])
            f.dma(f.sp, w0r[:], I["w0row"][l, :, :], writes=[B("rp_w0r")])
            f.op(f.dve, lambda e: e.tensor_scalar(out=omka[:], in0=self.V(l, "k_a", 0, 8), scalar1=-1.0, scalar2=1.0,
                                                  op0=ALU.mult, op1=ALU.add), reads=[B("vecs")], writes=[B("rp_omka")])
            raw = [sb("rp_raw%d" % i, [128, NT + 2], F32) for i in range(3)]
            rkv = [sb("rp_%s" % nm, [128, 8, NT], F32) for nm in ("r", "k", "v")]
            kk = sb("rp_kk", [128, 8, NT], F32)
            sq = [sb("rp_sq%d" % i, [128, NT], F32) for i in range(2)]
            nrm = [sb("rp_nrm%d" % i, [128, NT], F32) for i in range(2)]
            kap = sb("rp_kap", [128, 8, NT], F32)
            kapn = sb("rp_kapn", [128, 8, NT], F32)
            lw = sb("rp_lw", [128, NT], F32)
            la = sb("rp_la", [128, NT], F32)
            sg = [sb("rp_sg%d" % i, [128, D], F32) for i in range(2)]
            Ein = sb("rp_Ein", [128, 8, NT], F32)
            Eex = sb("rp_Eex", [128, 8, NT], F32)
            Eng_ = sb("rp_Eneg", [128, 8, NT], F32)
            ag = sb("rp_a", [128, 8, NT], F32)
            kd = sb("rp_kd", [128, 8, NT], F32)
            bd = sb("rp_bd", [128, 8, NT], F32)
            k01 = sb("rp_k01", [128, 8, NT], F32)
            outs = [sb("rp_out%d" % i, [128, 8, NT], F32) for i in range(4)]
            gct = sb("rp_gct", [128, 8, 4], F32)
            ir = 0
            for (t0, n, s0, sl) in self.tiles([(0, CTX), (CTX, SEQ)], NT):
                for c in range(24):
                    r = raw[ir % 3]
                    rb = B("rp_raw", ir % 3)
                    ir += 1
                    lo = max(t0 - 1, s0)
                    hi = min(t0 + n + 1, s0 + sl)
                    off = lo - (t0 - 1)
                    if lo != t0 - 1:
                        f.op(f.pool, lambda e: e.memset(r[:, 0:1], 0.0), writes=[rb])
                    if hi != t0 + n + 1:
                        f.op(f.pool, lambda e: e.memset(r[:, n + 1:n + 2], 0.0), writes=[rb])
                    f.dma(f.sp, r[:, off:off + hi - lo], pv[:, 28 + c, lo:hi], writes=[rb])
                    dst = rkv[c // 8]
                    db = B("rp_rkv", c // 8)
                    cc_ = c % 8
                    wc = lambda q: self.V(l, "shift_w", c * 3 + q)
                    f.op(f.act, lambda e: e.activation(out=dst[:, cc_, 0:n], in_=r[:, 0:n], func=AF.Copy, scale=wc(0)),
                         reads=[rb, B("vecs")], writes=[db])
                    for q in (1, 2):
                        f.op(f.dve, lambda e: e.scalar_tensor_tensor(out=dst[:, cc_, 0:n], in0=r[:, q:q + n], scalar=wc(q),
                                                                     in1=dst[:, cc_, 0:n], op0=ALU.mult, op1=ALU.add),
                             reads=[rb, db, B("vecs")], writes=[db])
                R_, K_, V_ = rkv
                f.dma(f.sp, self.fm(S["rw_r"])[:, :, t0:t0 + n], R_[:, :, 0:n], reads=[B("rp_rkv", 0)])
                f.dma(f.sp, self.fm(S["rw_v"])[:, :, t0:t0 + n], V_[:, :, 0:n], reads=[B("rp_rkv", 2)])
                for c in range(8):
                    f.op(f.pool, lambda e: e.tensor_scalar(out=kk[:, c, 0:n], in0=K_[:, c, 0:n], scalar1=self.V(l, "k_k", c),
                                                           scalar2=None, op0=ALU.mult),
                         reads=[B("rp_rkv", 1), B("vecs")], writes=[B("rp_kk", c)])
                    s = sq[c % 2]
                    sqb = B("rp_sq", c % 2)
                    f.op(f.act, lambda e: e.activation(out=s[:, 0:n], in_=kk[:, c, 0:n], func=AF.Square),
                         reads=[B("rp_kk", c)], writes=[sqb])
                    pi = 6 + c % 2
                    f.op(f.pe, lambda e: e.matmul(self.ps[pi][:, 0:n], lhsT=cf[:, C_BLK:C_BLK + 128], rhs=s[:, 0:n],
                                                  start=True, stop=True), reads=[sqb, B("constf")], writes=[self.psb[pi]])
                    nr = nrm[c % 2]
                    nb = B("rp_nrm", c % 2)
                    f.op(f.dve, lambda e: e.tensor_scalar(out=nr[:, 0:n], in0=self.ps[pi][:, 0:n], scalar1=1e-12, scalar2=None,
                                                          op0=ALU.max), reads=[self.psb[pi]], writes=[nb])
                    self.rstd_from_ps(nr[:, 0:n], nb, nr[:, 0:n], nb, 1.0, 0.0)
                    f.op(f.dve, lambda e: e.tensor_tensor(out=kap[:, c, 0:n], in0=kk[:, c, 0:n], in1=nr[:, 0:n], op=ALU.mult),
                         reads=[B("rp_kk", c), nb], writes=[B("rp_kap")])
                f.op(f.act, lambda e: e.mul(out=kapn[:, :, 0:n], in_=kap[:, :, 0:n], mul=-1.0), reads=[B("rp_kap")], writes=[B("rp_kapn")])
                f.dma(f.sp, lw[:, 0:n], pv[:, 52, t0:t0 + n], writes=[B("rp_lw")])
                f.dma(f.sp, la[:, 0:n], pv[:, 53, t0:t0 + n], writes=[B("rp_la")])
                f.op(f.act, lambda e: e.activation(out=lw[:, 0:n], in_=lw[:, 0:n], func=AF.Tanh), reads=[B("rp_lw")], writes=[B("rp_lw")])
                for d in range(2):
                    pr = slice(64 * d, 64 * d + 64)
                    tri = C_TRIF if d == 0 else C_TRIB
                    for jb in range(n // 128):
                        s_ = sg[jb % 2]
                        sgb = B("rp_sg", jb % 2)
                        for fh in range(2):
                            pi = fh
                            f.op(f.pe, lambda e: e.matmul(self.ps[pi][:, 0:512], lhsT=lw[pr, jb * 128:(jb + 1) * 128],
                                                          rhs=dup[pr, fh * 512:(fh + 1) * 512], start=True, stop=False),
                                 reads=[B("rp_lw"), B("rp_dup")], writes=[self.psb[pi]], inc=False)
                            f.op(f.pe, lambda e: e.matmul(self.ps[pi][:, 0:512], lhsT=cf[0:1, C_ONES:C_ONES + 128],
                                                          rhs=w0r[0:1, d * 1024 + fh * 512:d * 1024 + (fh + 1) * 512],
                                                          start=False, stop=True),
                                 reads=[B("rp_w0r"), B("constf")], writes=[self.psb[pi]])
                            f.op(f.act, lambda e: e.activation(out=s_[:, fh * 512:(fh + 1) * 512], in_=self.ps[pi][:, 0:512],
                                                               func=AF.Sigmoid), reads=[self.psb[pi]], writes=[sgb])
                        for c2 in range(4):
                            pi = 2 + c2
                            for h2 in range(2):
                                c = 2 * c2 + h2
                                f.op(f.pe, lambda e: e.matmul(self.ps[pi][:, h2 * 256:(h2 + 1) * 256], lhsT=s_[:, c * 128:(c + 1) * 128],
                                                              rhs=cf[:, tri:tri + 256], start=True, stop=True),
                                     reads=[sgb, B("constf")], writes=[self.psb[pi]], inc=(h2 == 1))
                            pv4 = self.ps[pi][:, 0:512].rearrange("p (c i t) -> p c i t", c=2, i=2)
                            cs = slice(2 * c2, 2 * c2 + 2)
                            ts = slice(jb * 128, (jb + 1) * 128)
                            f.op(f.act, lambda e: e.activation(out=Ein[:, cs, ts], in_=pv4[:, :, 0, :], func=AF.Exp, scale=-DECAY_SCALE),
                                 reads=[self.psb[pi]], writes=[B("rp_Ein")])
                            f.op(f.act, lambda e: e.activation(out=Eex[:, cs, ts], in_=pv4[:, :, 1, :], func=AF.Exp, scale=-DECAY_SCALE),
                                 reads=[self.psb[pi]], writes=[B("rp_Eex")])
                            f.op(f.act, lambda e: e.activation(out=Eng_[:, cs, ts], in_=pv4[:, :, 0, :], func=AF.Exp, scale=DECAY_SCALE),
                                 reads=[self.psb[pi]], writes=[B("rp_Eneg")])
                    for c in range(8):
                        pi = 6 + c % 2
                        f.op(f.pe, lambda e: e.matmul(self.ps[pi][:, 0:n], lhsT=iup[pr, c * 128:(c + 1) * 128], rhs=la[pr, 0:n],
                                                      start=True, stop=True), reads=[B("rp_iup"), B("rp_la")], writes=[self.psb[pi]])
                        f.op(f.act, lambda e: e.activation(out=ag[:, c, 0:n], in_=self.ps[pi][:, 0:n], func=AF.Sigmoid,
                                                           bias=self.V(l, "iclr_a0", d * 8 + c), scale=1.0),
                             reads=[self.psb[pi], B("vecs")], writes=[B("rp_a")])
                        f.op(f.dve, lambda e: e.tensor_scalar(out=kd[:, c, 0:n], in0=ag[:, c, 0:n], scalar1=self.V(l, "k_a", c),
                                                              scalar2=omka[:, c:c + 1], op0=ALU.mult, op1=ALU.add),
                             reads=[B("rp_a"), B("vecs"), B("rp_omka")], writes=[B("rp_kd"), B("rp_kd2", 0), B("rp_kd2", 1)])
                    o_at, o_bt, o_kt, o_rt = outs
                    HS = (slice(0, 4), slice(4, 8))

                    def both(fn, rd, wr):
                        for hi_, eng_ in enumerate((f.pool, f.dve)):
                            f.op(eng_, lambda e: fn(e, HS[hi_]), reads=[r_(hi_) if callable(r_) else r_ for r_ in rd],
                                 writes=[w_(hi_) for w_ in wr])
                    hb = lambda nm: (lambda hi_: B(nm, hi_))
                    both(lambda e, cs_: e.tensor_tensor(out=kd[:, cs_, 0:n], in0=kd[:, cs_, 0:n], in1=K_[:, cs_, 0:n], op=ALU.mult),
                         [B("rp_kd"), B("rp_rkv", 1)], [hb("rp_kd2")])
                    both(lambda e, cs_: e.tensor_tensor(out=bd[:, cs_, 0:n], in0=ag[:, cs_, 0:n], in1=kap[:, cs_, 0:n], op=ALU.mult),
                         [B("rp_a"), B("rp_kap")], [hb("rp_bd")])
                    if d == 0:
                        both(lambda e, cs_: e.tensor_copy(out=k01[:, cs_, 0:n], in_=kd[:, cs_, 0:n]), [hb("rp_kd2")], [hb("rp_k01")])
                    else:
                        both(lambda e, cs_: e.tensor_tensor(out=k01[:, cs_, 0:n], in0=k01[:, cs_, 0:n], in1=kd[:, cs_, 0:n], op=ALU.add),
                             [hb("rp_kd2"), hb("rp_k01")], [hb("rp_k01")])
                    both(lambda e, cs_: e.tensor_tensor(out=o_at[:, cs_, 0:n], in0=kapn[:, cs_, 0:n], in1=Eex[:, cs_, 0:n], op=ALU.mult),
                         [B("rp_kapn"), B("rp_Eex")], [hb("rp_out0")])
                    both(lambda e, cs_: e.tensor_tensor(out=o_bt[:, cs_, 0:n], in0=bd[:, cs_, 0:n], in1=Eng_[:, cs_, 0:n], op=ALU.mult),
                         [hb("rp_bd"), B("rp_Eneg")], [hb("rp_out1")])
                    both(lambda e, cs_: e.tensor_tensor(out=o_kt[:, cs_, 0:n], in0=kd[:, cs_, 0:n], in1=Eng_[:, cs_, 0:n], op=ALU.mult),
                         [hb("rp_kd2"), B("rp_Eneg")], [hb("rp_out2")])
                    both(lambda e, cs_: e.tensor_tensor(out=o_rt[:, cs_, 0:n], in0=R_[:, cs_, 0:n], in1=Ein[:, cs_, 0:n], op=ALU.mult),
                         [B("rp_rkv", 0), B("rp_Ein")], [hb("rp_out3")])
                    for i_, nm in enumerate(("At", "Bt", "Kt", "Rt")):
                        f.dma(f.sp, self.fm(S["%s%d" % (nm, d)])[:, :, t0:t0 + n], outs[i_][:, :, 0:n], reads=[B("rp_out%d" % i_, 0), B("rp_out%d" % i_, 1)])
                    col0 = 63 if d == 0 else 0
                    nch = n // 64
                    f.op(f.act, lambda e: e.copy(out=gct[:, :, 0:nch], in_=Ein[:, :, col0:n:64]), reads=[B("rp_Ein")], writes=[B("rp_gct")])
                    f.dma(f.sp, self.fm(S["gC%d" % d])[:, :, t0 // 64:t0 // 64 + nch], gct[:, :, 0:nch], reads=[B("rp_gct")])
                f.dma(f.sp, self.fm(S["rw_k01"])[:, :, t0:t0 + n], k01[:, :, 0:n], reads=[B("rp_k01", 0), B("rp_k01", 1)])
            f.barrier()

    def phase_rwkv_scan(self, l, last):
        nc, f = self.nc, self.f
        B = f.buf
        S = self.S
        cf = self.constf
        mk = self.maskf
        with contextlib.ExitStack() as es:
            sb = lambda n, s, d: es.enter_context(self.sbuf(n, s, d))
            ST = [sb("sc_ST%d" % d, [128, 8, 64], F32) for d in range(2)]
            gC = [sb("sc_gC%d" % d, [128, 8, NCHUNK], F32) for d in range(2)]
            names = ("At", "Bt", "Kt", "Rt", "V")
            inp = [[[sb("sc_%s%d_%d" % (nm, d, i), [128, 8, 128], F32) for nm in names] for i in range(2)] for d in range(2)]
            def mk64(nm, k=1):
                return [[[sb("sc_%s%d%d_%d" % (nm, d, h, i), [128, 256], F32) for i in range(k)] for h in range(2)] for d in range(2)]
            Xb = mk64("X", 2)
            XTb = mk64("XT", 2)
            Pb = mk64("P", 6)
            Lb = mk64("L", 3)
            Tk = mk64("Tk", 3)
            Wb = mk64("W", 2)
            Yo = mk64("Yo", 1)
            for d in range(2):
                f.op(f.pool, lambda e: e.memset(ST[d][:], 0.0), writes=[B("ST", d, 0), B("ST", d, 1)])
                f.dma(f.sp, gC[d][:], self.fm(S["gC%d" % d])[:, :, :], writes=[B("sc_gC", d)])
            order = [list(range(NCHUNK)), [3, 2, 1, 0] + list(range(NCHUNK - 1, 3, -1))]
            srcs = [[self.fm(S["%s%d" % (nm, d)]) for nm in ("At", "Bt", "Kt", "Rt")] + [self.fm(S["rw_v"])] for d in range(2)]
            yv = [self.fm(S["yf"]), self.fm(S["yb"])]
            self._psr = 0
            self._cp = 0
            cur_tile = [None, None]
            nload = [0, 0]

            def nps():
                i = self._psr % 8
                self._psr += 1
                return self.ps[i], self.psb[i]

            def evac(out_ap, in_ap, rd, wr):
                self._cp += 1
                if self._cp % 3 == 0:
                    f.op(f.dve, lambda e: e.tensor_copy(out=out_ap, in_=in_ap), reads=rd, writes=wr)
                else:
                    f.op(f.act, lambda e: e.copy(out=out_ap, in_=in_ap), reads=rd, writes=wr)

            def group(d, ch, half):
                tl, cc = ch // 2, ch % 2
                cs = slice(64 * cc, 64 * cc + 64)
                if cur_tile[d] != tl:
                    cur_tile[d] = tl
                    nload[d] += 1
                    bi = nload[d] % 2
                    for i_, nm in enumerate(names):
                        f.dma(f.sp, inp[d][bi][i_][:], srcs[d][i_][:, :, tl * 128:(tl + 1) * 128], writes=[B("sc_in", d, bi, i_)])
                bi = nload[d] % 2
                A_, Bm, Km, R_, Vv = inp[d][bi]
                bA, bB, bK, bR, bV = [B("sc_in", d, bi, i_) for i_ in range(5)]
                heads = [(4 * half + hpi, hh) for hpi in range(4) for hh in range(2)]
                fm_ = lambda T_, hp, hh: T_[64 * hh:64 * hh + 64, hp, cs]
                O = lambda T_, g8: T_[64 * (g8 % 2):64 * (g8 % 2) + 64, (g8 // 2) * 64:(g8 // 2) * 64 + 64]
                stb = B("ST", d, half)
                cst = B("constf")
                mkb = B("maskf")
                if d == 0:
                    mX, mXT, mL = M_LOS, M_UPS, M_UPI
                else:
                    mX, mXT, mL = M_UPS, M_LOS, M_LOI
                Vt_, Bt_, Kt_ = Tk[d][half]
                dh = (d, half)
                for j_, (src, sbuf_, dst) in enumerate(((Vv, bV, Vt_), (Bm, bB, Bt_), (Km, bK, Kt_))):
                    ps, pb = nps()
                    for g8, (hp, hh) in enumerate(heads):
                        f.op(f.pe, lambda e: e.matmul(O(ps, g8), lhsT=fm_(src, hp, hh),
                                                      rhs=cf[64 * hh:64 * hh + 64, C_ID + 64 * hh:C_ID + 64 * hh + 64], start=True, stop=True),
                             reads=[sbuf_, cst], writes=[pb], inc=(g8 == 7))
                    evac(dst[:, :], ps[:, 0:256], [pb], [B("sc_Tk", dh, j_)])
                    yield
                bVt, bBt, bKt = [B("sc_Tk", dh, j_) for j_ in range(3)]

                def mm8(out_rows, fn_l, fn_r, rd):
                    ps, pb = nps()
                    for g8, (hp, hh) in enumerate(heads):
                        f.op(f.pe, lambda e: e.matmul(O(ps, g8), lhsT=fn_l(g8, hp, hh), rhs=fn_r(g8, hp, hh), start=True, stop=True),
                             reads=rd, writes=[pb], inc=(g8 == 7))
                    return ps, pb

                def masked(ps, pb, dst, db, mcol):
                    f.op(f.dve, lambda e: e.tensor_tensor(out=dst[:, :], in0=ps[:, 0:256], in1=mk[:, mcol:mcol + 256], op=ALU.mult),
                         reads=[pb, mkb], writes=[db])
                X = Xb[d][half]
                XT = XTb[d][half]
                P = Pb[d][half]
                bX = [B("sc_X", dh, i_) for i_ in range(2)]
                bXT = [B("sc_XT", dh, i_) for i_ in range(2)]
                bP = [B("sc_P", dh, i_) for i_ in range(6)]
                bL = [B("sc_L", dh, i_) for i_ in range(3)]
                LakT, LrbT, LrkT = Lb[d][half]
                ps, pb = mm8(0, lambda g, hp, hh: fm_(A_, hp, hh), lambda g, hp, hh: fm_(Bm, hp, hh), [bA, bB])
                masked(ps, pb, X[0], bX[0], mX)
                yield
                ps, pb = mm8(0, lambda g, hp, hh: fm_(Bm, hp, hh), lambda g, hp, hh: fm_(A_, hp, hh), [bA, bB])
                masked(ps, pb, XT[0], bXT[0], mXT)
                f.op(f.pool, lambda e: e.tensor_tensor(out=P[0][:, :], in0=XT[0][:, :], in1=mk[:, M_ID:M_ID + 256], op=ALU.add),
                     reads=[bXT[0], mkb], writes=[bP[0]])
                yield
                ps, pb = mm8(0, lambda g, hp, hh: fm_(Km, hp, hh), lambda g, hp, hh: fm_(A_, hp, hh), [bA, bK])
                masked(ps, pb, LakT, bL[0], mXT)
                yield
                ps, pb = mm8(0, lambda g, hp, hh: fm_(Bm, hp, hh), lambda g, hp, hh: fm_(R_, hp, hh), [bR, bB])
                masked(ps, pb, LrbT, bL[1], mL)
                yield
                ps, pb = mm8(0, lambda g, hp, hh: fm_(Km, hp, hh), lambda g, hp, hh: fm_(R_, hp, hh), [bR, bK])
                masked(ps, pb, LrkT, bL[2], mL)
                yield
                for i_ in range(1, 6):
                    p_, c_ = (i_ - 1) % 2, i_ % 2
                    Xp, XTp = X[p_], XT[p_]
                    if i_ <= 4:
                        ps, pb = mm8(0, lambda g, hp, hh: O(XTp, g), lambda g, hp, hh: O(Xp, g), [bX[p_], bXT[p_]])
                        evac(X[c_][:, :], ps[:, 0:256], [pb], [bX[c_]])
                        yield
                    ps, pb = mm8(0, lambda g, hp, hh: O(Xp, g), lambda g, hp, hh: O(XTp, g), [bX[p_], bXT[p_]])
                    evac(XT[c_][:, :], ps[:, 0:256], [pb], [bXT[c_]])
                    f.op(f.pool, lambda e: e.tensor_tensor(out=P[i_][:, :], in0=XT[c_][:, :], in1=mk[:, M_ID:M_ID + 256], op=ALU.add),
                         reads=[bXT[c_], mkb], writes=[bP[i_]])
                    yield
                W = Wb[d][half]
                bW = [B("sc_W", dh, i_) for i_ in range(2)]
                ps, pb = nps()
                for g8, (hp, hh) in enumerate(heads):
                    f.op(f.pe, lambda e: e.matmul(O(ps, g8), lhsT=fm_(A_, hp, hh), rhs=ST[d][64 * hh:64 * hh + 64, hp, :],
                                                  start=True, stop=False), reads=[bA, stb], writes=[pb], inc=False)
                    f.op(f.pe, lambda e: e.matmul(O(ps, g8), lhsT=O(LakT, g8), rhs=O(Vt_, g8),
                                                  start=False, stop=True), reads=[bL[0], bVt], writes=[pb], inc=(g8 == 7))
                evac(W[0][:, :], ps[:, 0:256], [pb], [bW[0]])
                yield
                wi = 0
                for i_ in range(5, -1, -1):
                    Wc = W[wi]
                    ps, pb = mm8(0, lambda g, hp, hh: O(P[i_], g), lambda g, hp, hh: O(Wc, g), [bP[i_], bW[wi]])
                    wi ^= 1
                    evac(W[wi][:, :], ps[:, 0:256], [pb], [bW[wi]])
                    yield
                U = W[wi]
                bU = bW[wi]
                ps, pb = nps()
                for g8, (hp, hh) in enumerate(heads):
                    f.op(f.pe, lambda e: e.matmul(O(ps, g8), lhsT=ST[d][64 * hh:64 * hh + 64, hp, :], rhs=fm_(R_, hp, hh),
                                                  start=True, stop=False), reads=[bR, stb], writes=[pb], inc=False)
                    f.op(f.pe, lambda e: e.matmul(O(ps, g8), lhsT=O(U, g8), rhs=O(LrbT, g8),
                                                  start=False, stop=False), reads=[bU, bL[1]], writes=[pb], inc=False)
                    f.op(f.pe, lambda e: e.matmul(O(ps, g8), lhsT=O(Vt_, g8), rhs=O(LrkT, g8),
                                                  start=False, stop=True), reads=[bVt, bL[2]], writes=[pb], inc=(g8 == 7))
                yo = Yo[d][half][0]
                evac(yo[:, :], ps[:, 0:256], [pb], [B("sc_Yo", dh)])
                f.dma(f.sp, yv[d][:, 4 * half:4 * half + 4, ch * 64:(ch + 1) * 64], yo[:, :].rearrange("p (g t) -> p g t", t=64),
                      reads=[B("sc_Yo", dh)])
                yield
                ps, pb = nps()
                for g8, (hp, hh) in enumerate(heads):
                    o_ = O(ps, g8)
                    f.op(f.pe, lambda e: e.matmul(o_, lhsT=O(Bt_, g8), rhs=O(U, g8), start=True, stop=False),
                         reads=[bBt, bU], writes=[pb], inc=False)
                    f.op(f.pe, lambda e: e.matmul(o_, lhsT=O(Kt_, g8), rhs=O(Vt_, g8), start=False, stop=False),
                         reads=[bKt, bVt], writes=[pb], inc=False)
                    f.op(f.pe, lambda e: e.matmul(o_, lhsT=cf[64 * hh:64 * hh + 64, C_ID + 64 * hh:C_ID + 64 * hh + 64],
                                                  rhs=ST[d][64 * hh:64 * hh + 64, hp, :], start=False, stop=True),
                         reads=[cst, stb], writes=[pb], inc=(g8 == 7))
                for hpi in range(4):
                    hp = 4 * half + hpi
                    f.op(f.act, lambda e: e.activation(out=ST[d][:, hp, :], in_=ps[:, hpi * 64:(hpi + 1) * 64], func=AF.Copy,
                                                       scale=gC[d][:, hp, ch:ch + 1]),
                         reads=[pb, B("sc_gC", d)], writes=[stb])

            for s_ in range(getattr(self, "scan_steps", NCHUNK)):
                gens = [group(d, order[d][s_], half) for half in range(2) for d in range(2)]
                while gens:
                    alive = []
                    for g_ in gens:
                        try:
                            next(g_)
                            alive.append(g_)
                        except StopIteration:
                            pass
                    gens = alive
            f.barrier()

    def phase_rwkv_out(self, l, last):
        nc, f = self.nc, self.f
        B = f.buf
        I, S = self.I, self.S
        pv = self.fm(S["proj"])
        cf = self.constf
        with contextlib.ExitStack() as es:
            sb = lambda n, s, d: es.enter_context(self.sbuf(n, s, d))
            gup = sb("ro_gup", [128, D], F32)
            f.dma(f.sp, gup[:], I["gate_up"][l, :, :], writes=[B("ro_gup")])
            lg = sb("ro_lg", [128, 512], F32)
            tl = {}
            for nm in ("yf", "yb", "r", "k01", "v"):
                tl[nm] = [sb("ro_%s%d" % (nm, i), [128, 512], F32) for i in range(2)]
            sq = [sb("ro_sq%d" % i, [128, 512], F32) for i in range(2)]
            mean = [sb("ro_mean%d" % i, [128, 512], F32) for i in range(2)]
            var = [sb("ro_var%d" % i, [128, 512], F32) for i in range(2)]
            tt = [sb("ro_t%d" % i, [128, 512], F32) for i in range(2)]
            bon = [sb("ro_bon%d" % i, [128, 512], F32) for i in range(2)]
            ost = [sb("ro_o%d" % i, [128, 512], BF16) for i in range(2)]
            srcv = {"yf": self.fm(S["yf"]), "yb": self.fm(S["yb"]), "r": self.fm(S["rw_r"]), "k01": self.fm(S["rw_k01"]),
                    "v": self.fm(S["rw_v"])}
            ov = self.fm(S["rwo"])
            lgs = [lg, sb("ro_lg1", [128, 512], F32)]

            def unit(ti, t0, n, c, p):
                lg_ = lgs[ti % 2]
                lgb = B("ro_lg", ti % 2)
                if c == 0:
                    f.dma(f.sp, lg_[:, 0:n], pv[:, 54, t0:t0 + n], writes=[lgb])
                    f.op(f.act, lambda e: e.activation(out=lg_[:, 0:n], in_=lg_[:, 0:n], func=AF.Sigmoid), reads=[lgb], writes=[lgb])
                    yield
                bb = {nm: B("ro_" + nm, p) for nm in tl}
                for nm in tl:
                    f.dma(f.sp, tl[nm][p][:, 0:n], srcv[nm][:, c, t0:t0 + n], writes=[bb[nm]])
                yield
                y = tl["yf"][p]
                f.op(f.pool, lambda e: e.tensor_tensor(out=y[:, 0:n], in0=y[:, 0:n], in1=tl["yb"][p][:, 0:n], op=ALU.add),
                     reads=[bb["yf"], bb["yb"]], writes=[bb["yf"]])
                yield
                f.op(f.act, lambda e: e.activation(out=sq[p][:, 0:n], in_=y[:, 0:n], func=AF.Square), reads=[bb["yf"]], writes=[B("ro_sq", p)])
                yield
                r_ = tl["r"][p]
                f.op(f.dve, lambda e: e.scalar_tensor_tensor(out=r_[:, 0:n], in0=r_[:, 0:n], scalar=self.V(l, "r_k", c),
                                                             in1=tl["k01"][p][:, 0:n], op0=ALU.mult, op1=ALU.mult),
                     reads=[bb["r"], bb["k01"], B("vecs")], writes=[bb["r"]])
                yield
                q = 4 * p
                blk = cf[:, C_BLK:C_BLK + 128]
                f.op(f.pe, lambda e: e.matmul(self.ps[q][:, 0:n], lhsT=blk, rhs=y[:, 0:n], start=True, stop=True),
                     reads=[bb["yf"], B("constf")], writes=[self.psb[q]])
                yield
                f.op(f.pe, lambda e: e.matmul(self.ps[q + 1][:, 0:n], lhsT=blk, rhs=sq[p][:, 0:n], start=True, stop=True),
                     reads=[B("ro_sq", p), B("constf")], writes=[self.psb[q + 1]])
                yield
                f.op(f.pe, lambda e: e.matmul(self.ps[q + 2][:, 0:n], lhsT=blk, rhs=r_[:, 0:n], start=True, stop=True),
                     reads=[bb["r"], B("constf")], writes=[self.psb[q + 2]])
                yield
                f.op(f.pe, lambda e: e.matmul(self.ps[q + 3][:, 0:n], lhsT=gup[:, c * 128:(c + 1) * 128], rhs=lg_[:, 0:n], start=True, stop=True),
                     reads=[lgb, B("ro_gup")], writes=[self.psb[q + 3]])
                yield
                m_, v_, t_ = mean[p], var[p], tt[p]
                f.op(f.act, lambda e: e.activation(out=m_[:, 0:n], in_=self.ps[q][:, 0:n], func=AF.Copy, scale=1.0 / 64),
                     reads=[self.psb[q]], writes=[B("ro_mean", p)])
                yield
                f.op(f.dve, lambda e: e.tensor_tensor(out=v_[:, 0:n], in0=m_[:, 0:n], in1=m_[:, 0:n], op=ALU.mult),
                     reads=[B("ro_mean", p)], writes=[B("ro_var", p)])
                yield
                f.op(f.dve, lambda e: e.scalar_tensor_tensor(out=v_[:, 0:n], in0=self.ps[q + 1][:, 0:n], scalar=1.0 / 64, in1=v_[:, 0:n],
                                                             op0=ALU.mult, op1=ALU.subtract),
                     reads=[self.psb[q + 1], B("ro_var", p)], writes=[B("ro_var", p)])
                yield
                self.rstd_from_ps(v_[:, 0:n], B("ro_var", p), v_[:, 0:n], B("ro_var", p), 1.0, GN_EPS)
                yield
                f.op(f.dve, lambda e: e.tensor_tensor(out=t_[:, 0:n], in0=y[:, 0:n], in1=m_[:, 0:n], op=ALU.subtract),
                     reads=[bb["yf"], B("ro_mean", p)], writes=[B("ro_t", p)])
                yield
                f.op(f.pool, lambda e: e.tensor_tensor(out=t_[:, 0:n], in0=t_[:, 0:n], in1=v_[:, 0:n], op=ALU.mult),
                     reads=[B("ro_t", p), B("ro_var", p)], writes=[B("ro_t", p)])
                yield
                f.op(f.act, lambda e: e.activation(out=t_[:, 0:n], in_=t_[:, 0:n], func=AF.Identity,
                                                   bias=self.V(l, "gn_b", c), scale=self.V(l, "gn_g", c)),
                     reads=[B("ro_t", p), B("vecs")], writes=[B("ro_t", p)])
                yield
                f.op(f.dve, lambda e: e.tensor_tensor(out=bon[p][:, 0:n], in0=self.ps[q + 2][:, 0:n], in1=tl["v"][p][:, 0:n], op=ALU.mult),
                     reads=[self.psb[q + 2], bb["v"]], writes=[B("ro_bon", p)])
                yield
                f.op(f.pool, lambda e: e.tensor_tensor(out=t_[:, 0:n], in0=t_[:, 0:n], in1=bon[p][:, 0:n], op=ALU.add),
                     reads=[B("ro_t", p), B("ro_bon", p)], writes=[B("ro_t", p)])
                yield
                f.op(f.dve, lambda e: e.tensor_tensor(out=ost[p][:, 0:n], in0=self.ps[q + 3][:, 0:n], in1=t_[:, 0:n], op=ALU.mult),
                     reads=[self.psb[q + 3], B("ro_t", p)], writes=[B("ro_o", p)])
                yield
                f.dma(f.sp, ov[:, c, t0:t0 + n], ost[p][:, 0:n], reads=[B("ro_o", p)])
                yield

            units = [(ti, t0, n, c) for ti, (t0, n, s0, sl) in enumerate(self.tiles(self.seqs(last))) for c in range(8)]
            active = []
            nxt = 0
            free = [0]
            rnd = 0
            while nxt < len(units) or active:
                rnd += 1
                if rnd == 11:
                    free.append(1)
                while nxt < len(units) and free:
                    p_ = free.pop()
                    active.append((unit(*units[nxt], p_), p_))
                    nxt += 1
                still = []
                for g_, p_ in active:
                    try:
                        next(g_)
                        still.append((g_, p_))
                    except StopIteration:
                        free.append(p_)
                active = still
            f.barrier()


def _fm(v):
    v = np.asarray(v, np.float32).reshape(-1, 128)
    return np.ascontiguousarray(v.T)


def _consts():
    c = np.zeros((128, NCONST), np.float32)
    c[:, C_ONES:C_ONES + 128] = 1.0
    for h in range(2):
        c[64 * h:64 * h + 64, C_BLK + 64 * h:C_BLK + 64 * h + 64] = 1.0
    c[:, C_ID:C_ID + 128] = np.eye(128, dtype=np.float32)
    P = np.zeros((128, 128), np.float32)
    for m in range(128):
        if m % 64 < 32:
            P[m, m + 32] = -1.0
        else:
            P[m, m - 32] = 1.0
    c[:, C_ROT:C_ROT + 128] = P.T
    s = np.arange(128)[:, None]
    t = np.arange(128)[None, :]
    same = (s // 64) == (t // 64)
    c[:, C_TRIF:C_TRIF + 128] = (same & (s <= t))
    c[:, C_TRIF + 128:C_TRIF + 256] = (same & (s < t))
    c[:, C_TRIB:C_TRIB + 128] = (same & (s >= t))
    c[:, C_TRIB + 128:C_TRIB + 256] = (same & (s > t))
    r = np.arange(64)[:, None]
    q = np.arange(64)[None, :]
    m = np.zeros((128, 5 * 256), np.float32)
    m[:, M_LOS:M_LOS + 256] = np.tile((q < r).astype(np.float32), (2, 4))
    m[:, M_UPS:M_UPS + 256] = np.tile((q > r).astype(np.float32), (2, 4))
    m[:, M_LOI:M_LOI + 256] = np.tile((q <= r).astype(np.float32), (2, 4))
    m[:, M_UPI:M_UPI + 256] = np.tile((q >= r).astype(np.float32), (2, 4))
    m[:, M_ID:M_ID + 256] = np.tile(np.eye(64, dtype=np.float32), (2, 4))
    tt = np.arange(SEQ)
    row = (tt // 64).astype(np.float32)
    col = (tt % 64).astype(np.float32)
    inv = (10000.0 ** (-np.arange(32, dtype=np.float32) / 32)).astype(np.float32)
    cos = np.zeros((128, SEQ), np.float32)
    sin = np.zeros((128, SEQ), np.float32)
    for p in range(128):
        pos = row if p < 64 else col
        ang = (pos * inv[p % 32]).astype(np.float32)
        cos[p] = np.cos(ang)
        sin[p] = np.sin(ang)
    return c, m, cos, sin


def prep_inputs(inp):
    g = lambda k: np.asarray(inp[k], np.float32)
    vecs = np.zeros((DEPTH, 128, NV), np.float32)
    for l in range(DEPTH):
        def put(name, arr):
            a = np.asarray(arr, np.float32)
            vecs[l, :, VCOL[name]:VCOL[name] + a.shape[1]] = a
        put("b_mod", _fm(g("b_mod")[l]))
        for n in ("g_pre_mix", "g_post_mix", "g_pre_ffn", "g_post_ffn", "q_norm", "k_norm", "conv_b",
                  "k_k", "k_a"):
            put(n, _fm(g(n)[l]))
        put("conv_ln_g", _fm(g("conv_ln_g")[l]))
        put("conv_ln_b", _fm(g("conv_ln_b")[l]))
        put("gn_g", _fm(g("wkv_gn_g")[l]))
        put("gn_b", _fm(g("wkv_gn_b")[l]))
        put("r_k", _fm(g("r_k")[l].reshape(-1)))
        cw = g("conv_w")[l].reshape(31, 8, 128).transpose(2, 1, 0).reshape(128, 248)
        put("conv_w", cw)
        sw = g("shift_w")[l].reshape(3, 24, 128).transpose(2, 1, 0).reshape(128, 72)
        put("shift_w", sw)
        fw_ = g("ffn_conv_w")[l].reshape(3, 44, 128).transpose(2, 1, 0).reshape(128, 132)
        put("ffn_conv_w", fw_)
        a0 = g("iclr_a0")[l].reshape(2, 8, 128).transpose(2, 0, 1).reshape(128, 16)
        put("iclr_a0", a0)
    constf, maskf, cos, sin = _consts()
    shared = {
        "vecs": vecs,
        "w0row": np.ascontiguousarray(g("decay_w0").reshape(DEPTH, 1, 2048)),
        "constf": constf, "maskf": maskf, "cos": cos, "sin": sin,
        "w_mod": g("w_mod"), "w_in": g("w_in"),
        "w_attn_o": g("w_attn_o"), "w_conv_o": g("w_conv_o"), "w_rwkv_o": g("w_rwkv_o"), "w_out": g("w_out"),
        "decay_up": np.ascontiguousarray(g("decay_up").reshape(DEPTH, 128, D)),
        "iclr_up": np.ascontiguousarray(g("iclr_up").reshape(DEPTH, 128, D)),
        "gate_up": g("gate_up"),
        "w_ffn_up": g("w_ffn_up"), "w_ffn_down": g("w_ffn_down"),
    }
    x = g("x")
    ctx = g("ctx")
    c = g("c")
    c_ctx = g("c_ctx")
    per_core = []
    for b in range(x.shape[0]):
        xs0 = np.ascontiguousarray(np.concatenate([ctx[b].T, x[b].T], axis=1))
        cc = np.stack([c[b], c_ctx], axis=-1).reshape(8, 128, 2).transpose(1, 0, 2).reshape(128, 16)
        m = dict(shared)
        m["xs0"] = xs0
        m["cc"] = np.ascontiguousarray(cc)
        per_core.append(m)
    return per_core


def kernel(**inputs):
    per_core = prep_inputs(inputs)
    nc = Prog().build()
    res = run_bass_kernel_spmd(nc, per_core, core_ids=list(range(8)))
    outs = [np.ascontiguousarray(np.asarray(r["out"], np.float32).T) for r in res.results]
    return np.stack(outs, axis=0)
```

```python
import math
import contextlib
import numpy as np
import concourse.bass as bass
import concourse.mybir as mybir
from concourse.bass_utils import run_bass_kernel_spmd

F32 = mybir.dt.float32
BF16 = mybir.dt.bfloat16
AF = mybir.ActivationFunctionType
ALU = mybir.AluOpType

D = 1024
SEQ = 4096
CTX = 256
T = SEQ + CTX
DEPTH = 2
NIN = 10112
DFF = 2816
DECAY_SCALE = math.exp(-0.5)
EPS = 1e-6
LN_EPS = 1e-5
GN_EPS = 64 * 1e-5
CH = 64
NCHUNK = T // CH

VCOL = {}
_o = 0
for _n, _w in (("b_mod", 48), ("g_pre_mix", 8), ("g_post_mix", 8), ("g_pre_ffn", 8), ("g_post_ffn", 8),
               ("q_norm", 1), ("k_norm", 1), ("conv_w", 248), ("conv_b", 8), ("conv_ln_g", 8), ("conv_ln_b", 8),
               ("shift_w", 72), ("iclr_a0", 16), ("k_k", 8), ("k_a", 8), ("r_k", 8), ("gn_g", 8), ("gn_b", 8),
               ("ffn_conv_w", 132)):
    VCOL[_n] = _o
    _o += _w
NV = _o
C_ONES, C_BLK, C_ID, C_ROT, C_TRIF, C_TRIB = 0, 128, 256, 384, 512, 768
NCONST = 1024
M_LOS, M_UPS, M_LOI, M_UPI, M_ID = 0, 256, 512, 768, 1024


class Buf:
    __slots__ = ("w", "r")

    def __init__(self):
        self.w = None
        self.r = {}


class Eng:
    def __init__(self, name, e, sem):
        self.name = name
        self.e = e
        self.sem = sem
        self.count = 0
        self.seen = {}


class FW:
    SAME_ENG_DIST = 2

    def __init__(self, nc, es, n_dma_sems=24):
        self.nc = nc
        self.sems = {}

        def mk(name, e):
            s = es.enter_context(nc.semaphore("sem_" + name))
            self.sems[name] = s
            return Eng(name, e, s)
        self.pe = mk("pe", nc.tensor)
        self.act = mk("act", nc.scalar)
        self.dve = mk("dve", nc.vector)
        self.pool = mk("pool", nc.gpsimd)
        self.sp = mk("sp", nc.sync)
        self.engs = [self.pe, self.act, self.dve, self.pool, self.sp]
        self.dma_sems = []
        for i in range(n_dma_sems):
            nm = "dq%d" % i
            self.sems[nm] = es.enter_context(nc.semaphore("sem_" + nm))
            self.dma_sems.append([nm, 0])
        self.dma_rr = 0
        self.bufs = {}
        self.n_ins = 0

    def buf(self, *key):
        b = self.bufs.get(key)
        if b is None:
            b = Buf()
            self.bufs[key] = b
        return b

    def _need(self, eng, tok, raw):
        if tok is None:
            return
        sk, v = tok
        if sk == eng.name:
            if not (raw and (eng.count - v) < self.SAME_ENG_DIST):
                return
        if eng.seen.get(sk, 0) >= v:
            return
        eng.e.wait_ge(self.sems[sk], v)
        eng.seen[sk] = v

    def _deps(self, eng, reads, writes):
        for b in reads:
            self._need(eng, b.w, True)
        for b in writes:
            self._need(eng, b.w, False)
            for sk, v in b.r.items():
                self._need(eng, (sk, v), False)

    def _mark(self, tok, reads, writes):
        for b in reads:
            if b.r.get(tok[0], 0) < tok[1]:
                b.r[tok[0]] = tok[1]
        for b in writes:
            b.w = tok
            b.r = {}

    def op(self, eng, fn, reads=(), writes=(), inc=True):
        self._deps(eng, reads, writes)
        ins = fn(eng.e)
        self.n_ins += 1
        if inc:
            eng.count += 1
            ins.then_inc(eng.sem, 1)
            tok = (eng.name, eng.count)
        else:
            tok = (eng.name, eng.count + 1)
        self._mark(tok, reads, writes)
        return ins

    def dma(self, eng, out, in_, reads=(), writes=()):
        slot = self.dma_sems[self.dma_rr]
        self.dma_rr = (self.dma_rr + 1) % len(self.dma_sems)
        nm, cnt = slot
        self._deps(eng, reads, writes)
        self._need(eng, (nm, cnt), False)
        ins = eng.e.dma_start(out=out, in_=in_)
        slot[1] = cnt + 16
        ins.then_inc(self.sems[nm], 16)
        self.n_ins += 1
        self._mark((nm, cnt + 16), reads, writes)
        return ins

    def barrier(self):
        for e in self.engs:
            for o in self.engs:
                if o is not e and o.count > 0:
                    self._need(e, (o.name, o.count), False)
            for nm, cnt in self.dma_sems:
                if cnt > 0:
                    self._need(e, (nm, cnt), False)


class Prog:
    def __init__(self, debug=None, nlayers=DEPTH, stop_after=None):
        self.debug = debug or []
        self.nlayers = nlayers
        self.stop_after = stop_after
        self.nc = bass.Bass("TRN2", target_bir_lowering=False)
        self.uid = 0

    def din(self, name, shape, dt=F32):
        return self.nc.dram_tensor(name, list(shape), dt, kind="ExternalInput").ap()

    def dscr(self, name, shape, dt=F32):
        kind = "ExternalOutput" if name in self.debug else "Internal"
        return self.nc.dram_tensor(name, list(shape), dt, kind=kind).ap()

    def sbuf(self, name, shape, dt):
        self.uid += 1
        return self.nc.sbuf_tensor("%s_u%d" % (name, self.uid), shape, dt)

    def fm(self, ap):
        return ap.rearrange("(c p) t -> p c t", p=128)

    def build(self):
        nc = self.nc
        L = self.nlayers
        I = {}
        I["xs0"] = self.din("xs0", [D, T])
        I["cc"] = self.din("cc", [128, 16])
        I["vecs"] = self.din("vecs", [DEPTH, 128, NV])
        I["w0row"] = self.din("w0row", [DEPTH, 1, 2048])
        I["constf"] = self.din("constf", [128, NCONST])
        I["maskf"] = self.din("maskf", [128, 5 * 256])
        I["cos"] = self.din("cos", [128, SEQ])
        I["sin"] = self.din("sin", [128, SEQ])
        I["w_mod"] = self.din("w_mod", [DEPTH, D, 6 * D])
        I["w_in"] = self.din("w_in", [DEPTH, D, NIN])
        for n in ("w_attn_o", "w_conv_o", "w_rwkv_o", "w_out"):
            I[n] = self.din(n, [DEPTH, D, D])
        I["decay_up"] = self.din("decay_up", [DEPTH, 128, D])
        I["iclr_up"] = self.din("iclr_up", [DEPTH, 128, D])
        I["gate_up"] = self.din("gate_up", [DEPTH, 128, D])
        I["w_ffn_up"] = self.din("w_ffn_up", [DEPTH, D, 2 * DFF])
        I["w_ffn_down"] = self.din("w_ffn_down", [DEPTH, DFF, D])
        self.I = I
        out = nc.dram_tensor("out", [D, SEQ], F32, kind="ExternalOutput").ap()
        S = {}
        S["proj"] = self.dscr("proj", [NIN, T])
        S["att"] = self.dscr("att", [D, T], BF16)
        S["cnv"] = self.dscr("cnv", [D, T], BF16)
        S["rwo"] = self.dscr("rwo", [D, T], BF16)
        S["ffa"] = self.dscr("ffa", [DFF, T], BF16)
        for n in ("rw_r", "rw_v", "rw_k01", "yf", "yb"):
            S[n] = self.dscr(n, [D, T])
        for d in range(2):
            for n in ("At", "Bt", "Kt", "Rt"):
                S["%s%d" % (n, d)] = self.dscr("%s%d" % (n, d), [D, T])
            S["gC%d" % d] = self.dscr("gC%d" % d, [D, NCHUNK])
        S["xsm0"] = self.dscr("xsm0", [D, T])
        S["xs1"] = self.dscr("xs1", [D, T])
        S["xsm1"] = self.dscr("xsm1", [D, T])
        self.S = S

        with contextlib.ExitStack() as es:
            self.es = es
            f = FW(nc, es)
            self.f = f
            self.constf = es.enter_context(self.sbuf("constf", [128, NCONST], F32))
            self.maskf = es.enter_context(self.sbuf("maskf", [128, 5 * 256], F32))
            self.vecs = es.enter_context(self.sbuf("vecs", [128, DEPTH, NV], F32))
            self.modS = es.enter_context(self.sbuf("modS", [128, 48, 2], F32))
            self.dsc = es.enter_context(self.sbuf("dsc", [128, 64, 2], F32))
            self.onesb = es.enter_context(self.sbuf("onesb", [128, 128], BF16))
            self.ps = [es.enter_context(nc.psum_tensor("ps%d" % i, [128, 512], F32)) for i in range(8)]
            self.psb = [f.buf("ps", i) for i in range(8)]
            B = f.buf
            f.dma(f.sp, self.constf[:], I["constf"][:, :], writes=[B("constf")])
            f.dma(f.sp, self.maskf[:], I["maskf"][:, :], writes=[B("maskf")])
            for l in range(DEPTH):
                f.dma(f.sp, self.vecs[:, l, :], I["vecs"][l, :, :], writes=[B("vecs")])
            f.op(f.dve, lambda e: e.tensor_copy(out=self.onesb[:], in_=self.constf[:, C_ONES:C_ONES + 128]),
                 reads=[B("constf")], writes=[B("onesb")])
            f.barrier()
            xs_in = I["xs0"]
            for l in range(L):
                last = (l == DEPTH - 1)
                xsm = S["xsm%d" % l]
                xs_out = out if last else S["xs1"]
                self.phase_mod(l)
                if self.stop_after == ("mod", l): break
                self.phase_inproj(l, xs_in, last)
                if self.stop_after == ("inproj", l): break
                self.phase_attn(l, last)
                if self.stop_after == ("attn", l): break
                self.phase_conv(l, last)
                if self.stop_after == ("conv", l): break
                self.phase_rwkv_prep(l)
                if self.stop_after == ("rwprep", l): break
                self.phase_rwkv_scan(l, last)
                if self.stop_after == ("rwscan", l): break
                self.phase_rwkv_out(l, last)
                if self.stop_after == ("rwout", l): break
                self.phase_merge(l, xs_in, xsm, last)
                if self.stop_after == ("merge", l): break
                self.phase_ffn_up(l, xsm, last)
                if self.stop_after == ("ffnup", l): break
                self.phase_ffn_down(l, xsm, xs_out, last)
                xs_in = xs_out
            f.barrier()
        return nc

    def V(self, l, name, c=0, n=1):
        o = VCOL[name] + c
        return self.vecs[:, l, o:o + n]

    def seqs(self, last):
        return [(CTX, SEQ)] if last else [(0, CTX), (CTX, SEQ)]

    def tiles(self, seqs, n=512):
        r = []
        for s0, sl in seqs:
            t = s0
            while t < s0 + sl:
                m = min(n, s0 + sl - t)
                r.append((t, m, s0, sl))
                t += m
        return r

    def rstd_from_ps(self, ps_ap, psbuf, out_ap, outbuf, scale, eps):
        f = self.f
        f.op(f.act, lambda e: e.activation(out=out_ap, in_=ps_ap, func=AF.Ln, bias=float(eps), scale=float(scale)),
             reads=[psbuf], writes=[outbuf])
        f.op(f.act, lambda e: e.activation(out=out_ap, in_=out_ap, func=AF.Exp, scale=-0.5),
             reads=[outbuf], writes=[outbuf])

    def phase_mod(self, l):
        nc, f, es0 = self.nc, self.f, self.es
        B = f.buf
        I = self.I
        with contextlib.ExitStack() as es:
            cT = es.enter_context(self.sbuf("cT", [128, 8, 2], F32))
            wt = [es.enter_context(self.sbuf("wmod%d" % i, [128, 8, 1024], F32)) for i in range(2)]
            f.dma(f.sp, cT[:], I["cc"].rearrange("p (k j) -> p k j", j=2), writes=[B("cT")])
            f.op(f.act, lambda e: e.activation(out=cT[:], in_=cT[:], func=AF.Silu), reads=[B("cT")], writes=[B("cT")])
            wv = I["w_mod"][l].rearrange("(k p) n -> p k n", p=128)
            ps = self.ps[0]
            for g in range(6):
                w = wt[g % 2]
                for k in range(8):
                    f.dma(f.sp, w[:, k, :], wv[:, k, g * 1024:(g + 1) * 1024], writes=[B("wmod", g % 2, k)])
                for oc in range(8):
                    col = (g * 8 + oc) * 2
                    for k in range(8):
                        f.op(f.pe, lambda e: e.matmul(ps[:, col:col + 2], lhsT=w[:, k, oc * 128:(oc + 1) * 128],
                                                      rhs=cT[:, k, :], start=(k == 0), stop=(k == 7)),
                             reads=[B("wmod", g % 2, k), B("cT")], writes=[self.psb[0]], inc=(k == 7))
            psv = ps[:, 0:96].rearrange("p (c j) -> p c j", j=2)
            for j in range(2):
                f.op(f.dve, lambda e: e.tensor_tensor(out=self.modS[:, :, j], in0=psv[:, :, j],
                                                      in1=self.V(l, "b_mod", 0, 48), op=ALU.add),
                     reads=[self.psb[0], B("vecs")], writes=[B("modS")])
            for j in range(2):
                def m(i):
                    return self.modS[:, i * 8:(i + 1) * 8, j]
                rd = [B("modS"), B("vecs")]
                wr = [B("dsc")]
                f.op(f.dve, lambda e: e.scalar_tensor_tensor(out=self.dsc[:, 0:8, j], in0=m(1), scalar=1.0,
                                                             in1=self.V(l, "g_pre_mix", 0, 8), op0=ALU.add, op1=ALU.mult),
                     reads=rd, writes=wr)
                f.op(f.dve, lambda e: e.tensor_copy(out=self.dsc[:, 8:16, j], in_=m(0)), reads=rd, writes=wr)
                f.op(f.dve, lambda e: e.tensor_tensor(out=self.dsc[:, 16:24, j], in0=m(2),
                                                      in1=self.V(l, "g_post_mix", 0, 8), op=ALU.mult), reads=rd, writes=wr)
                f.op(f.dve, lambda e: e.scalar_tensor_tensor(out=self.dsc[:, 24:32, j], in0=m(4), scalar=1.0,
                                                             in1=self.V(l, "g_pre_ffn", 0, 8), op0=ALU.add, op1=ALU.mult),
                     reads=rd, writes=wr)
                f.op(f.dve, lambda e: e.tensor_copy(out=self.dsc[:, 32:40, j], in_=m(3)), reads=rd, writes=wr)
                f.op(f.dve, lambda e: e.tensor_tensor(out=self.dsc[:, 40:48, j], in0=m(5),
                                                      in1=self.V(l, "g_post_ffn", 0, 8), op=ALU.mult), reads=rd, writes=wr)
            f.barrier()

    def prenorm(self, es, src, seqs, base, hT, hcol_of):
        nc, f = self.nc, self.f
        B = f.buf
        xt = [es.enter_context(self.sbuf("pn_x%d" % i, [128, 8, 512], F32)) for i in range(2)]
        sq = es.enter_context(self.sbuf("pn_sq", [128, 8, 512], F32))
        rs = es.enter_context(self.sbuf("pn_rs", [128, 512], F32))
        srcv = self.fm(src)
        for i, (t0, n, s0, sl) in enumerate(self.tiles(seqs)):
            j = 1 if s0 == 0 else 0
            x = xt[i % 2]
            xb = B("pn_x", i % 2)
            f.dma(f.sp, x[:, :, 0:n], srcv[:, :, t0:t0 + n], writes=[xb])
            f.op(f.act, lambda e: e.activation(out=sq[:, :, 0:n], in_=x[:, :, 0:n], func=AF.Square),
                 reads=[xb], writes=[B("pn_sq")])
            ps, pb = self.ps[7], self.psb[7]
            for k in range(8):
                f.op(f.pe, lambda e: e.matmul(ps[:, 0:n], lhsT=self.constf[:, C_ONES:C_ONES + 128], rhs=sq[:, k, 0:n],
                                              start=(k == 0), stop=(k == 7)),
                     reads=[B("pn_sq"), B("constf")], writes=[pb], inc=(k == 7))
            self.rstd_from_ps(ps[:, 0:n], pb, rs[:, 0:n], B("pn_rs"), 1.0 / D, EPS)
            c0 = hcol_of(t0)
            for k in range(8):
                f.op(f.dve, lambda e: e.scalar_tensor_tensor(out=x[:, k, 0:n], in0=x[:, k, 0:n],
                                                             scalar=self.dsc[:, base + k, j:j + 1], in1=rs[:, 0:n],
                                                             op0=ALU.mult, op1=ALU.mult),
                     reads=[xb, B("pn_rs"), B("dsc")], writes=[xb])
                f.op(f.act, lambda e: e.activation(out=hT[:, k, c0:c0 + n], in_=x[:, k, 0:n], func=AF.Identity,
                                                   bias=self.dsc[:, base + 8 + k, j:j + 1], scale=1.0),
                     reads=[xb, B("dsc")], writes=[B("hT")])

    def phase_inproj(self, l, xs_in, last):
        nc, f = self.nc, self.f
        B = f.buf
        I, S = self.I, self.S
        seqs = [(0, CTX), (CTX, SEQ)]
        with contextlib.ExitStack() as es:
            hT = es.enter_context(self.sbuf("hT", [128, 8, T], BF16))
            with contextlib.ExitStack() as es2:
                self.prenorm(es2, xs_in, seqs, 0, hT, lambda t: t)
                f.barrier()
            wt = [es.enter_context(self.sbuf("win%d" % i, [128, 8, 1024], BF16)) for i in range(2)]
            st = [es.enter_context(self.sbuf("ipst%d" % i, [128, 512], F32)) for i in range(4)]
            wv = I["w_in"][l].rearrange("(k p) n -> p k n", p=128)
            pv = self.fm(S["proj"])
            ngrp = (NIN + 1023) // 1024
            cnt = 0
            for g in range(ngrp):
                ncol = min(1024, NIN - g * 1024)
                w = wt[g % 2]
                for k in range(8):
                    f.dma(f.pool, w[:, k, 0:ncol], wv[:, k, g * 1024:g * 1024 + ncol], writes=[B("win", g % 2, k)])
                for (t0, n, s0, sl) in self.tiles(seqs):
                    for oc in range(ncol // 128):
                        pi = cnt % 6
                        ps, pb = self.ps[pi], self.psb[pi]
                        for k in range(8):
                            f.op(f.pe, lambda e: e.matmul(ps[:, 0:n], lhsT=w[:, k, oc * 128:(oc + 1) * 128],
                                                          rhs=hT[:, k, t0:t0 + n], start=(k == 0), stop=(k == 7)),
                                 reads=[B("win", g % 2, k), B("hT")], writes=[pb], inc=(k == 7))
                        s = st[cnt % 4]
                        sb = B("ipst", cnt % 4)
                        if cnt % 2 == 0:
                            f.op(f.act, lambda e: e.copy(out=s[:, 0:n], in_=ps[:, 0:n]), reads=[pb], writes=[sb])
                        else:
                            f.op(f.dve, lambda e: e.tensor_copy(out=s[:, 0:n], in_=ps[:, 0:n]), reads=[pb], writes=[sb])
                        f.dma(f.sp, pv[:, g * 8 + oc, t0:t0 + n], s[:, 0:n], reads=[sb])
                        cnt += 1
            f.barrier()


    def phase_attn(self, l, last):
        nc, f = self.nc, self.f
        B = f.buf
        I, S = self.I, self.S
        pv = self.fm(S["proj"])
        av = self.fm(S["att"])
        cf = self.constf
        with contextlib.ExitStack() as es:
            sb = lambda n, s, d: es.enter_context(self.sbuf(n, s, d))
            kT = sb("kT", [128, 2, T], BF16)
            Vt = sb("Vt", [128, T // 128, 2, 128], BF16)
            cos = sb("cos", [128, SEQ], F32)
            sin = sb("sin", [128, SEQ], F32)
            qg = sb("qg", [128, 1], F32)
            raw = [sb("a_raw%d" % i, [128, 512], F32) for i in range(2)]
            tmps = [[sb("a_%s%d" % (nm_, sl_), [128, 512], F32) for nm_ in ("sq", "rs", "kn", "t1", "t2")] for sl_ in range(2)]
            qT = [sb("a_qT%d" % i, [128, 512], BF16) for i in range(2)]
            pT = [sb("a_pT%d" % i, [128, 512], BF16) for i in range(4)]
            rinv = sb("a_rinv", [128, 512], F32)
            ost = [sb("a_ost%d" % i, [128, 512], BF16) for i in range(2)]
            f.dma(f.sp, cos[:], I["cos"][:, :], writes=[B("cos")])
            f.dma(f.sp, sin[:], I["sin"][:, :], writes=[B("sin")])
            f.op(f.dve, lambda e: e.tensor_scalar(out=qg[:], in0=self.V(l, "q_norm"), scalar1=float(128 ** -0.5),
                                                  scalar2=None, op0=ALU.mult), reads=[B("vecs")], writes=[B("qg")])
            self._nr = 0

            def normrope_g(chunk, t0, n, gain, is_x, out_ap, outbuf, slot=0):
                r = raw[slot]
                rb = B("a_raw", slot)
                sq, rs, kn, t1, t2 = tmps[slot]
                pk = 7 - slot
                f.dma(f.sp, r[:, 0:n], pv[:, chunk, t0:t0 + n], writes=[rb])
                f.op(f.act, lambda e: e.activation(out=sq[:, 0:n], in_=r[:, 0:n], func=AF.Square),
                     reads=[rb], writes=[B("a_sq", slot)])
                f.op(f.pe, lambda e: e.matmul(self.ps[pk][:, 0:n], lhsT=cf[:, C_ONES:C_ONES + 128], rhs=sq[:, 0:n],
                                              start=True, stop=True), reads=[B("a_sq", slot), B("constf")], writes=[self.psb[pk]])
                yield
                self.rstd_from_ps(self.ps[pk][:, 0:n], self.psb[pk], rs[:, 0:n], B("a_rs", slot), 1.0 / 128, EPS)
                f.op(f.dve, lambda e: e.scalar_tensor_tensor(out=kn[:, 0:n], in0=r[:, 0:n], scalar=gain, in1=rs[:, 0:n],
                                                             op0=ALU.mult, op1=ALU.mult),
                     reads=[rb, B("a_rs", slot), B("vecs"), B("qg")], writes=[B("a_kn", slot)])
                yield
                if is_x:
                    p0 = t0 - CTX
                    f.op(f.pe, lambda e: e.matmul(self.ps[pk][:, 0:n], lhsT=cf[:, C_ROT:C_ROT + 128], rhs=kn[:, 0:n],
                                                  start=True, stop=True), reads=[B("a_kn", slot), B("constf")], writes=[self.psb[pk]])
                    yield
                    f.op(f.pool, lambda e: e.tensor_tensor(out=t1[:, 0:n], in0=kn[:, 0:n], in1=cos[:, p0:p0 + n], op=ALU.mult),
                         reads=[B("a_kn", slot), B("cos")], writes=[B("a_t1", slot)])
                    f.op(f.dve, lambda e: e.tensor_tensor(out=t2[:, 0:n], in0=self.ps[pk][:, 0:n], in1=sin[:, p0:p0 + n], op=ALU.mult),
                         reads=[self.psb[pk], B("sin")], writes=[B("a_t2", slot)])
                    f.op(f.pool, lambda e: e.tensor_tensor(out=out_ap, in0=t1[:, 0:n], in1=t2[:, 0:n], op=ALU.add),
                         reads=[B("a_t1", slot), B("a_t2", slot)], writes=[outbuf])
                else:
                    f.op(f.pool, lambda e: e.tensor_copy(out=out_ap, in_=kn[:, 0:n]), reads=[B("a_kn", slot)], writes=[outbuf])

            def normrope(*a_):
                for _ in normrope_g(*a_):
                    pass

            allseq = [(0, CTX), (CTX, SEQ)]
            for (t0, n, s0, sl) in self.tiles(allseq):
                gens = [normrope_g(8 + kvh, t0, n, self.V(l, "k_norm"), s0 != 0, kT[:, kvh, t0:t0 + n], B("kT", kvh), kvh)
                        for kvh in range(2)]
                while gens:
                    alive = []
                    for g_ in gens:
                        try:
                            next(g_)
                            alive.append(g_)
                        except StopIteration:
                            pass
                    gens = alive
            i = 0
            for kvh in range(2):
                for (t0, n, s0, sl) in self.tiles(allseq):
                    r = raw[i % 2]
                    rb = B("a_raw", i % 2)
                    i += 1
                    f.dma(f.sp, r[:, 0:n], pv[:, 10 + kvh, t0:t0 + n], writes=[rb])
                    nb = n // 128
                    for j in range(nb):
                        f.op(f.pe, lambda e: e.transpose(self.ps[7][:, j * 128:(j + 1) * 128], r[:, j * 128:(j + 1) * 128],
                                                         cf[:, C_ID:C_ID + 128]),
                             reads=[rb, B("constf")], writes=[self.psb[7]], inc=(j == nb - 1))
                    b0 = t0 // 128
                    f.op(f.dve, lambda e: e.tensor_copy(out=Vt[:, b0:b0 + nb, kvh, :],
                                                        in_=self.ps[7][:, 0:nb * 128].rearrange("p (b d) -> p b d", d=128)),
                         reads=[self.psb[7]], writes=[B("Vt")])
            units = [(h, t0, n, s0) for h in range(8) for (t0, n, s0, sl) in self.tiles(self.seqs(last))]

            def qprep(ui):
                h, t0, n, s0 = units[ui]
                return normrope_g(h, t0, n, qg[:, 0:1], s0 != 0, qT[ui % 2][:, 0:n], B("a_qT", ui % 2))
            for _ in qprep(0):
                pass
            for qi, (h, t0, n, s0) in enumerate(units):
                kvh = h // 4
                is_x = s0 != 0
                q = qT[qi % 2]
                qb = B("a_qT", qi % 2)
                nxt = qprep(qi + 1) if qi + 1 < len(units) else None
                nblk = (T // 128) if is_x else (CTX // 128)
                po, pob = self.ps[3 + qi % 2], self.psb[3 + qi % 2]
                pr, prb = self.ps[5 + qi % 2], self.psb[5 + qi % 2]

                def smm(jb):
                    f.op(f.pe, lambda e: e.matmul(self.ps[jb % 3][:, 0:n], lhsT=kT[:, kvh, jb * 128:(jb + 1) * 128],
                                                  rhs=q[:, 0:n], start=True, stop=True),
                         reads=[B("kT", kvh), qb], writes=[self.psb[jb % 3]])

                def pvmm(jb):
                    p = pT[jb % 4]
                    pb = B("a_pT", jb % 4)
                    lastb = (jb == nblk - 1)
                    f.op(f.pe, lambda e: e.matmul(po[:, 0:n], lhsT=Vt[:, jb, kvh, :], rhs=p[:, 0:n],
                                                  start=(jb == 0), stop=lastb),
                         reads=[B("Vt"), pb], writes=[pob], inc=False)
                    f.op(f.pe, lambda e: e.matmul(pr[:, 0:n], lhsT=self.onesb[:], rhs=p[:, 0:n],
                                                  start=(jb == 0), stop=lastb),
                         reads=[B("onesb"), pb], writes=[prb], inc=lastb)
                smm(0)
                if nblk > 1:
                    smm(1)
                for jb in range(nblk):
                    p = pT[jb % 4]
                    pb = B("a_pT", jb % 4)
                    f.op(f.act, lambda e: e.activation(out=p[:, 0:n], in_=self.ps[jb % 3][:, 0:n], func=AF.Exp),
                         reads=[self.psb[jb % 3]], writes=[pb])
                    if jb + 2 < nblk:
                        smm(jb + 2)
                    if jb >= 1:
                        pvmm(jb - 1)
                    if nxt is not None and jb in (4, 10, 16, 22):
                        try:
                            next(nxt)
                        except StopIteration:
                            nxt = None
                pvmm(nblk - 1)
                if nxt is not None:
                    for _ in nxt:
                        pass
                f.op(f.dve, lambda e: e.reciprocal(out=rinv[:, 0:n], in_=pr[:, 0:n]), reads=[prb], writes=[B("a_rinv")])
                o = ost[qi % 2]
                ob = B("a_ost", qi % 2)
                f.op(f.dve, lambda e: e.tensor_tensor(out=o[:, 0:n], in0=po[:, 0:n], in1=rinv[:, 0:n], op=ALU.mult),
                     reads=[pob, B("a_rinv")], writes=[ob])
                f.dma(f.sp, av[:, h, t0:t0 + n], o[:, 0:n], reads=[ob])
            f.barrier()

    def phase_conv(self, l, last):
        nc, f = self.nc, self.f
        B = f.buf
        S = self.S
        pv = self.fm(S["proj"])
        cv = self.fm(S["cnv"])
        cf = self.constf
        with contextlib.ExitStack() as es:
            sb = lambda n, s, d: es.enter_context(self.sbuf(n, s, d))
            at = [sb("c_a%d" % i, [128, 544], F32) for i in range(2)]
            bt = [sb("c_b%d" % i, [128, 544], F32) for i in range(2)]
            y = sb("c_y", [128, 8, 512], F32)
            sq = [sb("c_sq%d" % i, [128, 512], F32) for i in range(2)]
            mean = sb("c_mean", [128, 512], F32)
            msq = sb("c_msq", [128, 512], F32)
            rstd = sb("c_rstd", [128, 512], F32)
            tt = [sb("c_t%d" % i, [128, 512], F32) for i in range(2)]
            ost = [sb("c_o%d" % i, [128, 512], BF16) for i in range(2)]
            gbf = [sb("c_gb%d" % i, [128, 544], BF16) for i in range(2)]
            dg = sb("c_dg", [128, 8, 31, 128], BF16)
            k_ = 0
            for c in range(8):
                for j in range(31):
                    wj = self.V(l, "conv_w", c * 31 + j)
                    e3 = k_ % 3
                    k_ += 1
                    if e3 == 0:
                        f.op(f.pool, lambda e: e.tensor_scalar(out=dg[:, c, j, :], in0=cf[:, C_ID:C_ID + 128], scalar1=wj, scalar2=None,
                                                               op0=ALU.mult), reads=[B("constf"), B("vecs")], writes=[B("c_dg", c, 0)])
                    elif e3 == 1:
                        f.op(f.dve, lambda e: e.tensor_scalar(out=dg[:, c, j, :], in0=cf[:, C_ID:C_ID + 128], scalar1=wj, scalar2=None,
                                                              op0=ALU.mult), reads=[B("constf"), B("vecs")], writes=[B("c_dg", c, 1)])
                    else:
                        f.op(f.act, lambda e: e.activation(out=dg[:, c, j, :], in_=cf[:, C_ID:C_ID + 128], func=AF.Copy, scale=wj),
                             reads=[B("constf"), B("vecs")], writes=[B("c_dg", c, 2)])
            it = 0
            for (t0, n, s0, sl) in self.tiles(self.seqs(last)):
                lo = max(t0 - 15, s0)
                hi = min(t0 + n + 15, s0 + sl)
                off = lo - (t0 - 15)
                edge = (lo != t0 - 15) or (hi != t0 + n + 15)
                for c in range(8):
                    a = at[it % 2]
                    b = bt[it % 2]
                    ab = B("c_a", it % 2)
                    bb = B("c_b", it % 2)
                    it += 1
                    if edge:
                        f.op(f.pool, lambda e: e.memset(a[:, 0:n + 30], 0.0), writes=[ab])
                        f.op(f.pool, lambda e: e.memset(b[:, 0:n + 30], 0.0), writes=[bb])
                    f.dma(f.sp, a[:, off:off + hi - lo], pv[:, 12 + c, lo:hi], writes=[ab])
                    f.dma(f.sp, b[:, off:off + hi - lo], pv[:, 20 + c, lo:hi], writes=[bb])
                    f.op(f.act, lambda e: e.activation(out=b[:, 0:n + 30], in_=b[:, 0:n + 30], func=AF.Sigmoid),
                         reads=[bb], writes=[bb])
                    gb_ = gbf[it % 2]
                    gbb = B("c_gb", it % 2)
                    f.op(f.pool, lambda e: e.tensor_tensor(out=gb_[:, 0:n + 30], in0=a[:, 0:n + 30], in1=b[:, 0:n + 30], op=ALU.mult),
                         reads=[ab, bb], writes=[gbb])
                    yb = B("c_y", c)
                    pi = 2 + it % 4
                    for j in range(31):
                        f.op(f.pe, lambda e: e.matmul(self.ps[pi][:, 0:n], lhsT=dg[:, c, j, :], rhs=gb_[:, j:j + n],
                                                      start=(j == 0), stop=(j == 30)),
                             reads=[gbb, B("c_dg", c, 0), B("c_dg", c, 1), B("c_dg", c, 2)], writes=[self.psb[pi]], inc=(j == 30))
                    f.op(f.act, lambda e: e.activation(out=y[:, c, 0:n], in_=self.ps[pi][:, 0:n], func=AF.Identity,
                                                       bias=self.V(l, "conv_b", c), scale=1.0),
                         reads=[self.psb[pi], B("vecs")], writes=[yb])
                    s = sq[c % 2]
                    sqb = B("c_sq", c % 2)
                    f.op(f.act, lambda e: e.activation(out=s[:, 0:n], in_=y[:, c, 0:n], func=AF.Square), reads=[yb], writes=[sqb])
                    f.op(f.pe, lambda e: e.matmul(self.ps[0][:, 0:n], lhsT=cf[:, C_ONES:C_ONES + 128], rhs=y[:, c, 0:n],
                                                  start=(c == 0), stop=(c == 7)), reads=[yb, B("constf")], writes=[self.psb[0]], inc=False)
                    f.op(f.pe, lambda e: e.matmul(self.ps[1][:, 0:n], lhsT=cf[:, C_ONES:C_ONES + 128], rhs=s[:, 0:n],
                                                  start=(c == 0), stop=(c == 7)), reads=[sqb, B("constf")], writes=[self.psb[1]])
                f.op(f.act, lambda e: e.activation(out=mean[:, 0:n], in_=self.ps[0][:, 0:n], func=AF.Copy, scale=1.0 / D),
                     reads=[self.psb[0]], writes=[B("c_mean")])
                f.op(f.dve, lambda e: e.tensor_tensor(out=msq[:, 0:n], in0=mean[:, 0:n], in1=mean[:, 0:n], op=ALU.mult),
                     reads=[B("c_mean")], writes=[B("c_msq")])
                f.op(f.dve, lambda e: e.scalar_tensor_tensor(out=rstd[:, 0:n], in0=self.ps[1][:, 0:n], scalar=1.0 / D,
                                                             in1=msq[:, 0:n], op0=ALU.mult, op1=ALU.subtract),
                     reads=[self.psb[1], B("c_msq")], writes=[B("c_rstd")])
                self.rstd_from_ps(rstd[:, 0:n], B("c_rstd"), rstd[:, 0:n], B("c_rstd"), 1.0, LN_EPS)
                for c in range(8):
                    t = tt[c % 2]
                    tb = B("c_t", c % 2)
                    f.op(f.dve, lambda e: e.tensor_tensor(out=t[:, 0:n], in0=y[:, c, 0:n], in1=mean[:, 0:n], op=ALU.subtract),
                         reads=[B("c_y", c), B("c_mean")], writes=[tb])
                    f.op(f.pool, lambda e: e.tensor_tensor(out=t[:, 0:n], in0=t[:, 0:n], in1=rstd[:, 0:n], op=ALU.mult),
                         reads=[tb, B("c_rstd")], writes=[tb])
                    o = ost[c % 2]
                    ob = B("c_o", c % 2)
                    f.op(f.act, lambda e: e.activation(out=o[:, 0:n], in_=t[:, 0:n], func=AF.Silu,
                                                       bias=self.V(l, "conv_ln_b", c), scale=self.V(l, "conv_ln_g", c)),
                         reads=[tb, B("vecs")], writes=[ob])
                    f.dma(f.sp, cv[:, c, t0:t0 + n], o[:, 0:n], reads=[ob])
            f.barrier()

    def post_residual(self, es, mo, mob, xt, xtb, base, j, dstv, c0, n, sq, rs):
        f = self.f
        B = f.buf
        cf = self.constf
        for k in range(8):
            s = sq[k % 2]
            sqb = B("pr_sq", k % 2)
            f.op(f.act, lambda e: e.activation(out=s[:, 0:n], in_=mo[:, k, 0:n], func=AF.Square), reads=[mob], writes=[sqb])
            f.op(f.pe, lambda e: e.matmul(self.ps[7][:, 0:n], lhsT=cf[:, C_ONES:C_ONES + 128], rhs=s[:, 0:n],
                                          start=(k == 0), stop=(k == 7)), reads=[sqb, B("constf")], writes=[self.psb[7]])
        self.rstd_from_ps(self.ps[7][:, 0:n], self.psb[7], rs[:, 0:n], B("pr_rs"), 1.0 / D, EPS)
        for k in range(8):
            f.op(f.pool, lambda e: e.tensor_tensor(out=mo[:, k, 0:n], in0=mo[:, k, 0:n], in1=rs[:, 0:n], op=ALU.mult),
                 reads=[mob, B("pr_rs")], writes=[mob])
            f.op(f.dve, lambda e: e.scalar_tensor_tensor(out=xt[:, k, 0:n], in0=mo[:, k, 0:n],
                                                         scalar=self.dsc[:, base + k, j:j + 1], in1=xt[:, k, 0:n],
                                                         op0=ALU.mult, op1=ALU.add),
                 reads=[mob, xtb, B("dsc")], writes=[xtb])
        f.dma(f.sp, dstv[:, :, c0:c0 + n], xt[:, :, 0:n], reads=[xtb])

    def phase_merge(self, l, xs_in, xsm, last):
        nc, f = self.nc, self.f
        B = f.buf
        I, S = self.I, self.S
        pv = self.fm(S["proj"])
        with contextlib.ExitStack() as es:
            sb = lambda n, s, d: es.enter_context(self.sbuf(n, s, d))
            W = [sb("m_w%d" % i, [128, 8, 1024], BF16) for i in range(4)]
            for i, nm in enumerate(("w_attn_o", "w_conv_o", "w_rwkv_o", "w_out")):
                wv = I[nm][l].rearrange("(k p) n -> p k n", p=128)
                for k in range(8):
                    f.dma(f.pool, W[i][:, k, :], wv[:, k, :], writes=[B("m_w", i, k)])
            br = [sb("m_br%d" % i, [128, 8, 512], BF16) for i in range(3)]
            gt = [sb("m_g%d" % i, [128, 512], F32) for i in range(6)]
            tt = [sb("m_t%d" % i, [128, 512], F32) for i in range(6)]
            mT = sb("m_mT", [128, 8, 512], BF16)
            mo = sb("m_mo", [128, 8, 512], F32)
            xt = sb("m_xt", [128, 8, 512], F32)
            sq = [sb("m_sq%d" % i, [128, 512], F32) for i in range(2)]
            rs = sb("m_rs", [128, 512], F32)
            srcs = [self.fm(S["att"]), self.fm(S["cnv"]), self.fm(S["rwo"])]
            xv = self.fm(xs_in)
            dv = self.fm(xsm)
            for (t0, n, s0, sl) in self.tiles(self.seqs(last)):
                j = 1 if s0 == 0 else 0
                for b in range(3):
                    f.dma(f.sp, br[b][:, :, 0:n], srcs[b][:, :, t0:t0 + n], writes=[B("m_br", b)])
                f.dma(f.sp, xt[:, :, 0:n], xv[:, :, t0:t0 + n], writes=[B("m_xt")])
                for oc in range(8):
                    par = oc % 2
                    for b in range(3):
                        pi = b + 3 * par
                        for k in range(8):
                            f.op(f.pe, lambda e: e.matmul(self.ps[pi][:, 0:n], lhsT=W[b][:, k, oc * 128:(oc + 1) * 128],
                                                          rhs=br[b][:, k, 0:n], start=(k == 0), stop=(k == 7)),
                                 reads=[B("m_w", b, k), B("m_br", b)], writes=[self.psb[pi]], inc=(k == 7))
                    for b in range(3):
                        pi = b + 3 * par
                        g = gt[pi]
                        gb = B("m_g", pi)
                        f.dma(f.sp, g[:, 0:n], pv[:, 55 + 8 * b + oc, t0:t0 + n], writes=[gb])
                        f.op(f.act, lambda e: e.activation(out=g[:, 0:n], in_=g[:, 0:n], func=AF.Sigmoid), reads=[gb], writes=[gb])
                        f.op(f.dve, lambda e: e.tensor_tensor(out=tt[pi][:, 0:n], in0=self.ps[pi][:, 0:n], in1=g[:, 0:n], op=ALU.mult),
                             reads=[self.psb[pi], gb], writes=[B("m_t", pi)])
                    p0 = 3 * par
                    f.op(f.pool, lambda e: e.tensor_tensor(out=tt[p0][:, 0:n], in0=tt[p0][:, 0:n], in1=tt[p0 + 1][:, 0:n], op=ALU.add),
                         reads=[B("m_t", p0), B("m_t", p0 + 1)], writes=[B("m_t", p0)])
                    f.op(f.pool, lambda e: e.tensor_tensor(out=mT[:, oc, 0:n], in0=tt[p0][:, 0:n], in1=tt[p0 + 2][:, 0:n], op=ALU.add),
                         reads=[B("m_t", p0), B("m_t", p0 + 2)], writes=[B("m_mT")])
                for oc in range(8):
                    pi = 6
                    for k in range(8):
                        f.op(f.pe, lambda e: e.matmul(self.ps[pi][:, 0:n], lhsT=W[3][:, k, oc * 128:(oc + 1) * 128],
                                                      rhs=mT[:, k, 0:n], start=(k == 0), stop=(k == 7)),
                             reads=[B("m_w", 3, k), B("m_mT")], writes=[self.psb[pi]], inc=(k == 7))
                    f.op(f.act, lambda e: e.copy(out=mo[:, oc, 0:n], in_=self.ps[pi][:, 0:n]), reads=[self.psb[pi]], writes=[B("m_mo")])
                self.post_residual(es, mo, B("m_mo"), xt, B("m_xt"), 16, j, dv, t0, n, sq, rs)
            f.barrier()

    def phase_ffn_up(self, l, xsm, last):
        nc, f = self.nc, self.f
        B = f.buf
        I, S = self.I, self.S
        fv = self.fm(S["ffa"])
        TP = T + 4
        hcol = lambda t: (t + 1) if t < CTX else (t + 3)
        with contextlib.ExitStack() as es:
            sb = lambda n, s, d: es.enter_context(self.sbuf(n, s, d))
            hT = sb("hT", [128, 8, TP], BF16)
            for c in (0, CTX + 1, CTX + 2, TP - 1):
                f.op(f.pool, lambda e: e.memset(hT[:, :, c:c + 1], 0.0), writes=[B("hT")])
            with contextlib.ExitStack() as es2:
                self.prenorm(es2, xsm, self.seqs(last), 24, hT, hcol)
                f.barrier()
            GS = 4
            wt = [sb("fu_w%d" % i, [128, 8, 2, GS * 128], BF16) for i in range(2)]
            cg = [sb("fu_cg%d" % i, [128, 512], F32) for i in range(2)]
            cv = [sb("fu_cv%d" % i, [128, 512], F32) for i in range(2)]
            ao = [sb("fu_a%d" % i, [128, 512], BF16) for i in range(2)]
            wv = I["w_ffn_up"][l].rearrange("(k p) n -> p k n", p=128)
            it = 0
            for gi, j0 in enumerate(range(0, 22, GS)):
                gs = min(GS, 22 - j0)
                w = wt[gi % 2]
                for k in range(8):
                    f.dma(f.pool, w[:, k, 0, 0:gs * 128], wv[:, k, j0 * 128:(j0 + gs) * 128], writes=[B("fu_w", gi % 2, k, 0)])
                    f.dma(f.pool, w[:, k, 1, 0:gs * 128], wv[:, k, DFF + j0 * 128:DFF + (j0 + gs) * 128],
                          writes=[B("fu_w", gi % 2, k, 1)])
                for (t0, n, s0, sl) in self.tiles(self.seqs(last), 510):
                    c0 = hcol(t0)
                    for jj in range(gs):
                        jc = j0 + jj
                        par = it % 3
                        for hv in range(2):
                            pi = 2 * par + hv
                            for k in range(8):
                                f.op(f.pe, lambda e: e.matmul(self.ps[pi][:, 0:n + 2], lhsT=w[:, k, hv, jj * 128:(jj + 1) * 128],
                                                              rhs=hT[:, k, c0 - 1:c0 + n + 1], start=(k == 0), stop=(k == 7)),
                                     reads=[B("fu_w", gi % 2, k, hv), B("hT")], writes=[self.psb[pi]], inc=(k == 7))
                        res = []
                        for hv, dst, nm in ((0, cg[it % 2], "fu_cg"), (1, cv[it % 2], "fu_cv")):
                            pi = 2 * par + hv
                            ch = jc + 22 * hv
                            wc = lambda q: self.V(l, "ffn_conv_w", ch * 3 + q)
                            db = B(nm, it % 2)
                            f.op(f.act, lambda e: e.activation(out=dst[:, 0:n], in_=self.ps[pi][:, 0:n], func=AF.Copy, scale=wc(0)),
                                 reads=[self.psb[pi], B("vecs")], writes=[db])
                            for q in (1, 2):
                                f.op(f.dve, lambda e: e.scalar_tensor_tensor(out=dst[:, 0:n], in0=self.ps[pi][:, q:q + n], scalar=wc(q),
                                                                             in1=dst[:, 0:n], op0=ALU.mult, op1=ALU.add),
                                     reads=[self.psb[pi], db, B("vecs")], writes=[db])
                        g_, v_ = cg[it % 2], cv[it % 2]
                        f.op(f.act, lambda e: e.activation(out=g_[:, 0:n], in_=g_[:, 0:n], func=AF.Silu),
                             reads=[B("fu_cg", it % 2)], writes=[B("fu_cg", it % 2)])
                        a = ao[it % 2]
                        f.op(f.pool, lambda e: e.tensor_tensor(out=a[:, 0:n], in0=g_[:, 0:n], in1=v_[:, 0:n], op=ALU.mult),
                             reads=[B("fu_cg", it % 2), B("fu_cv", it % 2)], writes=[B("fu_a", it % 2)])
                        f.dma(f.sp, fv[:, jc, t0:t0 + n], a[:, 0:n], reads=[B("fu_a", it % 2)])
                        it += 1
            f.barrier()

    def phase_ffn_down(self, l, xsm, xs_out, last):
        nc, f = self.nc, self.f
        B = f.buf
        I, S = self.I, self.S
        fv = self.fm(S["ffa"])
        with contextlib.ExitStack() as es:
            sb = lambda n, s, d: es.enter_context(self.sbuf(n, s, d))
            W = sb("fd_w", [128, 22, 1024], BF16)
            wv = I["w_ffn_down"][l].rearrange("(k p) n -> p k n", p=128)
            for k in range(22):
                f.dma(f.pool, W[:, k, :], wv[:, k, :], writes=[B("fd_w", k)])
            at = [sb("fd_a%d" % i, [128, 22, 512], BF16) for i in range(2)]
            mo = sb("fd_mo", [128, 8, 512], F32)
            xt = sb("fd_xt", [128, 8, 512], F32)
            sq = [sb("fd_sq%d" % i, [128, 512], F32) for i in range(2)]
            rs = sb("fd_rs", [128, 512], F32)
            xv = self.fm(xsm)
            dv = self.fm(xs_out)
            for it, (t0, n, s0, sl) in enumerate(self.tiles(self.seqs(last))):
                j = 1 if s0 == 0 else 0
                a = at[it % 2]
                ab = B("fd_a", it % 2)
                f.dma(f.sp, a[:, :, 0:n], fv[:, :, t0:t0 + n], writes=[ab])
                f.dma(f.sp, xt[:, :, 0:n], xv[:, :, t0:t0 + n], writes=[B("fd_xt")])
                for oc in range(8):
                    pi = oc % 4
                    for k in range(22):
                        f.op(f.pe, lambda e: e.matmul(self.ps[pi][:, 0:n], lhsT=W[:, k, oc * 128:(oc + 1) * 128],
                                                      rhs=a[:, k, 0:n], start=(k == 0), stop=(k == 21)),
                             reads=[B("fd_w", k), ab], writes=[self.psb[pi]], inc=(k == 21))
                    f.op(f.act, lambda e: e.copy(out=mo[:, oc, 0:n], in_=self.ps[pi][:, 0:n]), reads=[self.psb[pi]], writes=[B("fd_mo")])
                c0 = (t0 - CTX) if last else t0
                self.post_residual(es, mo, B("fd_mo"), xt, B("fd_xt"), 40, j, dv, c0, n, sq, rs)
            f.barrier()

    def phase_rwkv_prep(self, l):
        nc, f = self.nc, self.f
        B = f.buf
        I, S = self.I, self.S
        pv = self.fm(S["proj"])
        cf = self.constf
        NT = 256
        with contextlib.ExitStack() as es:
            sb = lambda n, s, d: es.enter_context(self.sbuf(n, s, d))
            dup = sb("rp_dup", [128, D], F32)
            iup = sb("rp_iup", [128, D], F32)
            w0r = sb("rp_w0r", [1, 2048], F32)
            omka = sb("rp_omka", [128, 8], F32)
            f.dma(f.sp, dup[:], I["decay_up"][l, :, :], writes=[B("rp_dup")])
            f.dma(f.sp, iup[:], I["iclr_up"][l, :, :], writes=[B("rp_iup")])
            f.dma(f.sp, w0r[:], I["w0row"][l, :, :], writes=[B("rp_w0r")])
            f.op(f.dve, lambda e: e.tensor_scalar(out=omka[:], in0=self.V(l, "k_a", 0, 8), scalar1=-1.0, scalar2=1.0,
                                                  op0=ALU.mult, op1=ALU.add), reads=[B("vecs")], writes=[B("rp_omka")])
            raw = [sb("rp_raw%d" % i, [128, NT + 2], F32) for i in range(3)]
            rkv = [sb("rp_%s" % nm, [128, 8, NT], F32) for nm in ("r", "k", "v")]
            kk = sb("rp_kk", [128, 8, NT], F32)
            sq = [sb("rp_sq%d" % i, [128, NT], F32) for i in range(2)]
            nrm = [sb("rp_nrm%d" % i, [128, NT], F32) for i in range(2)]
            kap = sb("rp_kap", [128, 8, NT], F32)
            kapn = sb("rp_kapn", [128, 8, NT], F32)
            lw = sb("rp_lw", [128, NT], F32)
            la = sb("rp_la", [128, NT], F32)
            sg = [sb("rp_sg%d" % i, [128, D], F32) for i in range(2)]
            Ein = sb("rp_Ein", [128, 8, NT], F32)
            Eex = sb("rp_Eex", [128, 8, NT], F32)
            Eng_ = sb("rp_Eneg", [128, 8, NT], F32)
            ag = sb("rp_a", [128, 8, NT], F32)
            kd = sb("rp_kd", [128, 8, NT], F32)
            bd = sb("rp_bd", [128, 8, NT], F32)
            k01 = sb("rp_k01", [128, 8, NT], F32)
            outs = [sb("rp_out%d" % i, [128, 8, NT], F32) for i in range(4)]
            gct = sb("rp_gct", [128, 8, 4], F32)
            ir = 0
            for (t0, n, s0, sl) in self.tiles([(0, CTX), (CTX, SEQ)], NT):
                for c in range(24):
                    r = raw[ir % 3]
                    rb = B("rp_raw", ir % 3)
                    ir += 1
                    lo = max(t0 - 1, s0)
                    hi = min(t0 + n + 1, s0 + sl)
                    off = lo - (t0 - 1)
                    if lo != t0 - 1:
                        f.op(f.pool, lambda e: e.memset(r[:, 0:1], 0.0), writes=[rb])
                    if hi != t0 + n + 1:
                        f.op(f.pool, lambda e: e.memset(r[:, n + 1:n + 2], 0.0), writes=[rb])
                    f.dma(f.sp, r[:, off:off + hi - lo], pv[:, 28 + c, lo:hi], writes=[rb])
                    dst = rkv[c // 8]
                    db = B("rp_rkv", c // 8)
                    cc_ = c % 8
                    wc = lambda q: self.V(l, "shift_w", c * 3 + q)
                    f.op(f.act, lambda e: e.activation(out=dst[:, cc_, 0:n], in_=r[:, 0:n], func=AF.Copy, scale=wc(0)),
                         reads=[rb, B("vecs")], writes=[db])
                    for q in (1, 2):
                        f.op(f.dve, lambda e: e.scalar_tensor_tensor(out=dst[:, cc_, 0:n], in0=r[:, q:q + n], scalar=wc(q),
                                                                     in1=dst[:, cc_, 0:n], op0=ALU.mult, op1=ALU.add),
                             reads=[rb, db, B("vecs")], writes=[db])
                R_, K_, V_ = rkv
                f.dma(f.sp, self.fm(S["rw_r"])[:, :, t0:t0 + n], R_[:, :, 0:n], reads=[B("rp_rkv", 0)])
                f.dma(f.sp, self.fm(S["rw_v"])[:, :, t0:t0 + n], V_[:, :, 0:n], reads=[B("rp_rkv", 2)])
                for c in range(8):
                    f.op(f.pool, lambda e: e.tensor_scalar(out=kk[:, c, 0:n], in0=K_[:, c, 0:n], scalar1=self.V(l, "k_k", c),
                                                           scalar2=None, op0=ALU.mult),
                         reads=[B("rp_rkv", 1), B("vecs")], writes=[B("rp_kk", c)])
                    s = sq[c % 2]
                    sqb = B("rp_sq", c % 2)
                    f.op(f.act, lambda e: e.activation(out=s[:, 0:n], in_=kk[:, c, 0:n], func=AF.Square),
                         reads=[B("rp_kk", c)], writes=[sqb])
                    pi = 6 + c % 2
                    f.op(f.pe, lambda e: e.matmul(self.ps[pi][:, 0:n], lhsT=cf[:, C_BLK:C_BLK + 128], rhs=s[:, 0:n],
                                                  start=True, stop=True), reads=[sqb, B("constf")], writes=[self.psb[pi]])
                    nr = nrm[c % 2]
                    nb = B("rp_nrm", c % 2)
                    f.op(f.dve, lambda e: e.tensor_scalar(out=nr[:, 0:n], in0=self.ps[pi][:, 0:n], scalar1=1e-12, scalar2=None,
                                                          op0=ALU.max), reads=[self.psb[pi]], writes=[nb])
                    self.rstd_from_ps(nr[:, 0:n], nb, nr[:, 0:n], nb, 1.0, 0.0)
                    f.op(f.dve, lambda e: e.tensor_tensor(out=kap[:, c, 0:n], in0=kk[:, c, 0:n], in1=nr[:, 0:n], op=ALU.mult),
                         reads=[B("rp_kk", c), nb], writes=[B("rp_kap")])
                f.op(f.act, lambda e: e.mul(out=kapn[:, :, 0:n], in_=kap[:, :, 0:n], mul=-1.0), reads=[B("rp_kap")], writes=[B("rp_kapn")])
                f.dma(f.sp, lw[:, 0:n], pv[:, 52, t0:t0 + n], writes=[B("rp_lw")])
                f.dma(f.sp, la[:, 0:n], pv[:, 53, t0:t0 + n], writes=[B("rp_la")])
                f.op(f.act, lambda e: e.activation(out=lw[:, 0:n], in_=lw[:, 0:n], func=AF.Tanh), reads=[B("rp_lw")], writes=[B("rp_lw")])
                for d in range(2):
                    pr = slice(64 * d, 64 * d + 64)
                    tri = C_TRIF if d == 0 else C_TRIB
                    for jb in range(n // 128):
                        s_ = sg[jb % 2]
                        sgb = B("rp_sg", jb % 2)
                        for fh in range(2):
                            pi = fh
                            f.op(f.pe, lambda e: e.matmul(self.ps[pi][:, 0:512], lhsT=lw[pr, jb * 128:(jb + 1) * 128],
                                                          rhs=dup[pr, fh * 512:(fh + 1) * 512], start=True, stop=False),
                                 reads=[B("rp_lw"), B("rp_dup")], writes=[self.psb[pi]], inc=False)
                            f.op(f.pe, lambda e: e.matmul(self.ps[pi][:, 0:512], lhsT=cf[0:1, C_ONES:C_ONES + 128],
                                                          rhs=w0r[0:1, d * 1024 + fh * 512:d * 1024 + (fh + 1) * 512],
                                                          start=False, stop=True),
                                 reads=[B("rp_w0r"), B("constf")], writes=[self.psb[pi]])
                            f.op(f.act, lambda e: e.activation(out=s_[:, fh * 512:(fh + 1) * 512], in_=self.ps[pi][:, 0:512],
                                                               func=AF.Sigmoid), reads=[self.psb[pi]], writes=[sgb])
                        for c2 in range(4):
                            pi = 2 + c2
                            for h2 in range(2):
                                c = 2 * c2 + h2
                                f.op(f.pe, lambda e: e.matmul(self.ps[pi][:, h2 * 256:(h2 + 1) * 256], lhsT=s_[:, c * 128:(c + 1) * 128],
                                                              rhs=cf[:, tri:tri + 256], start=True, stop=True),
                                     reads=[sgb, B("constf")], writes=[self.psb[pi]], inc=(h2 == 1))
                            pv4 = self.ps[pi][:, 0:512].rearrange("p (c i t) -> p c i t", c=2, i=2)
                            cs = slice(2 * c2, 2 * c2 + 2)
                            ts = slice(jb * 128, (jb + 1) * 128)
                            f.op(f.act, lambda e: e.activation(out=Ein[:, cs, ts], in_=pv4[:, :, 0, :], func=AF.Exp, scale=-DECAY_SCALE),
                                 reads=[self.psb[pi]], writes=[B("rp_Ein")])
                            f.op(f.act, lambda e: e.activation(out=Eex[:, cs, ts], in_=pv4[:, :, 1, :], func=AF.Exp, scale=-DECAY_SCALE),
                                 reads=[self.psb[pi]], writes=[B("rp_Eex")])
                            f.op(f.act, lambda e: e.activation(out=Eng_[:, cs, ts], in_=pv4[:, :, 0, :], func=AF.Exp, scale=DECAY_SCALE),
                                 reads=[self.psb[pi]], writes=[B("rp_Eneg")])
                    for c in range(8):
                        pi = 6 + c % 2
                        f.op(f.pe, lambda e: e.matmul(self.ps[pi][:, 0:n], lhsT=iup[pr, c * 128:(c + 1) * 128], rhs=la[pr, 0:n],
                                                      start=True, stop=True), reads=[B("rp_iup"), B("rp_la")], writes=[self.psb[pi]])
                        f.op(f.act, lambda e: e.activation(out=ag[:, c, 0:n], in_=self.ps[pi][:, 0:n], func=AF.Sigmoid,
                                                           bias=self.V(l, "iclr_a0", d * 8 + c), scale=1.0),
                             reads=[self.psb[pi], B("vecs")], writes=[B("rp_a")])
                        f.op(f.dve, lambda e: e.tensor_scalar(out=kd[:, c, 0:n], in0=ag[:, c, 0:n], scalar1=self.V(l, "k_a", c),
                                                              scalar2=omka[:, c:c + 1], op0=ALU.mult, op1=ALU.add),
                             reads=[B("rp_a"), B("vecs"), B("rp_omka")], writes=[B("rp_kd"), B("rp_kd2", 0), B("rp_kd2", 1)])
                    o_at, o_bt, o_kt, o_rt = outs
                    HS = (slice(0, 4), slice(4, 8))

                    def both(fn, rd, wr):
                        for hi_, eng_ in enumerate((f.pool, f.dve)):
                            f.op(eng_, lambda e: fn(e, HS[hi_]), reads=[r_(hi_) if callable(r_) else r_ for r_ in rd],
                                 writes=[w_(hi_) for w_ in wr])
                    hb = lambda nm: (lambda hi_: B(nm, hi_))
                    both(lambda e, cs_: e.tensor_tensor(out=kd[:, cs_, 0:n], in0=kd[:, cs_, 0:n], in1=K_[:, cs_, 0:n], op=ALU.mult),
                         [B("rp_kd"), B("rp_rkv", 1)], [hb("rp_kd2")])
                    both(lambda e, cs_: e.tensor_tensor(out=bd[:, cs_, 0:n], in0=ag[:, cs_, 0:n], in1=kap[:, cs_, 0:n], op=ALU.mult),
                         [B("rp_a"), B("rp_kap")], [hb("rp_bd")])
                    if d == 0:
                        both(lambda e, cs_: e.tensor_copy(out=k01[:, cs_, 0:n], in_=kd[:, cs_, 0:n]), [hb("rp_kd2")], [hb("rp_k01")])
                    else:
                        both(lambda e, cs_: e.tensor_tensor(out=k01[:, cs_, 0:n], in0=k01[:, cs_, 0:n], in1=kd[:, cs_, 0:n], op=ALU.add),
                             [hb("rp_kd2"), hb("rp_k01")], [hb("rp_k01")])
                    both(lambda e, cs_: e.tensor_tensor(out=o_at[:, cs_, 0:n], in0=kapn[:, cs_, 0:n], in1=Eex[:, cs_, 0:n], op=ALU.mult),
                         [B("rp_kapn"), B("rp_Eex")], [hb("rp_out0")])
                    both(lambda e, cs_: e.tensor_tensor(out=o_bt[:, cs_, 0:n], in0=bd[:, cs_, 0:n], in1=Eng_[:, cs_, 0:n], op=ALU.mult),
                         [hb("rp_bd"), B("rp_Eneg")], [hb("rp_out1")])
                    both(lambda e, cs_: e.tensor_tensor(out=o_kt[:, cs_, 0:n], in0=kd[:, cs_, 0:n], in1=Eng_[:, cs_, 0:n], op=ALU.mult),
                         [hb("rp_kd2"), B("rp_Eneg")], [hb("rp_out2")])
                    both(lambda e, cs_: e.tensor_tensor(out=o_rt[:, cs_, 0:n], in0=R_[:, cs_, 0:n], in1=Ein[:, cs_, 0:n], op=ALU.mult),
                         [B("rp_rkv", 0), B("rp_Ein")], [hb("rp_out3")])
                    for i_, nm in enumerate(("At", "Bt", "Kt", "Rt")):
                        f.dma(f.sp, self.fm(S["%s%d" % (nm, d)])[:, :, t0:t0 + n], outs[i_][:, :, 0:n], reads=[B("rp_out%d" % i_, 0), B("rp_out%d" % i_, 1)])
                    col0 = 63 if d == 0 else 0
                    nch = n // 64
                    f.op(f.act, lambda e: e.copy(out=gct[:, :, 0:nch], in_=Ein[:, :, col0:n:64]), reads=[B("rp_Ein")], writes=[B("rp_gct")])
                    f.dma(f.sp, self.fm(S["gC%d" % d])[:, :, t0 // 64:t0 // 64 + nch], gct[:, :, 0:nch], reads=[B("rp_gct")])
                f.dma(f.sp, self.fm(S["rw_k01"])[:, :, t0:t0 + n], k01[:, :, 0:n], reads=[B("rp_k01", 0), B("rp_k01", 1)])
            f.barrier()

    def phase_rwkv_scan(self, l, last):
        nc, f = self.nc, self.f
        B = f.buf
        S = self.S
        cf = self.constf
        mk = self.maskf
        with contextlib.ExitStack() as es:
            sb = lambda n, s, d: es.enter_context(self.sbuf(n, s, d))
            ST = [sb("sc_ST%d" % d, [128, 8, 64], F32) for d in range(2)]
            gC = [sb("sc_gC%d" % d, [128, 8, NCHUNK], F32) for d in range(2)]
            names = ("At", "Bt", "Kt", "Rt", "V")
            inp = [[[sb("sc_%s%d_%d" % (nm, d, i), [128, 8, 128], F32) for nm in names] for i in range(2)] for d in range(2)]
            def mk64(nm, k=1):
                return [[[sb("sc_%s%d%d_%d" % (nm, d, h, i), [128, 256], F32) for i in range(k)] for h in range(2)] for d in range(2)]
            Xb = mk64("X", 2)
            XTb = mk64("XT", 2)
            Pb = mk64("P", 6)
            Lb = mk64("L", 3)
            Tk = mk64("Tk", 3)
            Wb = mk64("W", 2)
            Yo = mk64("Yo", 1)
            for d in range(2):
                f.op(f.pool, lambda e: e.memset(ST[d][:], 0.0), writes=[B("ST", d, 0), B("ST", d, 1)])
                f.dma(f.sp, gC[d][:], self.fm(S["gC%d" % d])[:, :, :], writes=[B("sc_gC", d)])
            order = [list(range(NCHUNK)), [3, 2, 1, 0] + list(range(NCHUNK - 1, 3, -1))]
            srcs = [[self.fm(S["%s%d" % (nm, d)]) for nm in ("At", "Bt", "Kt", "Rt")] + [self.fm(S["rw_v"])] for d in range(2)]
            yv = [self.fm(S["yf"]), self.fm(S["yb"])]
            self._psr = 0
            self._cp = 0
            cur_tile = [None, None]
            nload = [0, 0]

            def nps():
                i = self._psr % 8
                self._psr += 1
                return self.ps[i], self.psb[i]

            def evac(out_ap, in_ap, rd, wr):
                self._cp += 1
                if self._cp % 3 == 0:
                    f.op(f.dve, lambda e: e.tensor_copy(out=out_ap, in_=in_ap), reads=rd, writes=wr)
                else:
                    f.op(f.act, lambda e: e.copy(out=out_ap, in_=in_ap), reads=rd, writes=wr)

            def group(d, ch, half):
                tl, cc = ch // 2, ch % 2
                cs = slice(64 * cc, 64 * cc + 64)
                if cur_tile[d] != tl:
                    cur_tile[d] = tl
                    nload[d] += 1
                    bi = nload[d] % 2
                    for i_, nm in enumerate(names):
                        f.dma(f.sp, inp[d][bi][i_][:], srcs[d][i_][:, :, tl * 128:(tl + 1) * 128], writes=[B("sc_in", d, bi, i_)])
                bi = nload[d] % 2
                A_, Bm, Km, R_, Vv = inp[d][bi]
                bA, bB, bK, bR, bV = [B("sc_in", d, bi, i_) for i_ in range(5)]
                heads = [(4 * half + hpi, hh) for hpi in range(4) for hh in range(2)]
                fm_ = lambda T_, hp, hh: T_[64 * hh:64 * hh + 64, hp, cs]
                O = lambda T_, g8: T_[64 * (g8 % 2):64 * (g8 % 2) + 64, (g8 // 2) * 64:(g8 // 2) * 64 + 64]
                stb = B("ST", d, half)
                cst = B("constf")
                mkb = B("maskf")
                if d == 0:
                    mX, mXT, mL = M_LOS, M_UPS, M_UPI
                else:
                    mX, mXT, mL = M_UPS, M_LOS, M_LOI
                Vt_, Bt_, Kt_ = Tk[d][half]
                dh = (d, half)
                for j_, (src, sbuf_, dst) in enumerate(((Vv, bV, Vt_), (Bm, bB, Bt_), (Km, bK, Kt_))):
                    ps, pb = nps()
                    for g8, (hp, hh) in enumerate(heads):
                        f.op(f.pe, lambda e: e.matmul(O(ps, g8), lhsT=fm_(src, hp, hh),
                                                      rhs=cf[64 * hh:64 * hh + 64, C_ID + 64 * hh:C_ID + 64 * hh + 64], start=True, stop=True),
                             reads=[sbuf_, cst], writes=[pb], inc=(g8 == 7))
                    evac(dst[:, :], ps[:, 0:256], [pb], [B("sc_Tk", dh, j_)])
                    yield
                bVt, bBt, bKt = [B("sc_Tk", dh, j_) for j_ in range(3)]

                def mm8(out_rows, fn_l, fn_r, rd):
                    ps, pb = nps()
                    for g8, (hp, hh) in enumerate(heads):
                        f.op(f.pe, lambda e: e.matmul(O(ps, g8), lhsT=fn_l(g8, hp, hh), rhs=fn_r(g8, hp, hh), start=True, stop=True),
                             reads=rd, writes=[pb], inc=(g8 == 7))
                    return ps, pb

                def masked(ps, pb, dst, db, mcol):
                    f.op(f.dve, lambda e: e.tensor_tensor(out=dst[:, :], in0=ps[:, 0:256], in1=mk[:, mcol:mcol + 256], op=ALU.mult),
                         reads=[pb, mkb], writes=[db])
                X = Xb[d][half]
                XT = XTb[d][half]
                P = Pb[d][half]
                bX = [B("sc_X", dh, i_) for i_ in range(2)]
                bXT = [B("sc_XT", dh, i_) for i_ in range(2)]
                bP = [B("sc_P", dh, i_) for i_ in range(6)]
                bL = [B("sc_L", dh, i_) for i_ in range(3)]
                LakT, LrbT, LrkT = Lb[d][half]
                ps, pb = mm8(0, lambda g, hp, hh: fm_(A_, hp, hh), lambda g, hp, hh: fm_(Bm, hp, hh), [bA, bB])
                masked(ps, pb, X[0], bX[0], mX)
                yield
                ps, pb = mm8(0, lambda g, hp, hh: fm_(Bm, hp, hh), lambda g, hp, hh: fm_(A_, hp, hh), [bA, bB])
                masked(ps, pb, XT[0], bXT[0], mXT)
                f.op(f.pool, lambda e: e.tensor_tensor(out=P[0][:, :], in0=XT[0][:, :], in1=mk[:, M_ID:M_ID + 256], op=ALU.add),
                     reads=[bXT[0], mkb], writes=[bP[0]])
                yield
                ps, pb = mm8(0, lambda g, hp, hh: fm_(Km, hp, hh), lambda g, hp, hh: fm_(A_, hp, hh), [bA, bK])
                masked(ps, pb, LakT, bL[0], mXT)
                yield
                ps, pb = mm8(0, lambda g, hp, hh: fm_(Bm, hp, hh), lambda g, hp, hh: fm_(R_, hp, hh), [bR, bB])
                masked(ps, pb, LrbT, bL[1], mL)
                yield
                ps, pb = mm8(0, lambda g, hp, hh: fm_(Km, hp, hh), lambda g, hp, hh: fm_(R_, hp, hh), [bR, bK])
                masked(ps, pb, LrkT, bL[2], mL)
                yield
                for i_ in range(1, 6):
                    p_, c_ = (i_ - 1) % 2, i_ % 2
                    Xp, XTp = X[p_], XT[p_]
                    if i_ <= 4:
                        ps, pb = mm8(0, lambda g, hp, hh: O(XTp, g), lambda g, hp, hh: O(Xp, g), [bX[p_], bXT[p_]])
                        evac(X[c_][:, :], ps[:, 0:256], [pb], [bX[c_]])
                        yield
                    ps, pb = mm8(0, lambda g, hp, hh: O(Xp, g), lambda g, hp, hh: O(XTp, g), [bX[p_], bXT[p_]])
                    evac(XT[c_][:, :], ps[:, 0:256], [pb], [bXT[c_]])
                    f.op(f.pool, lambda e: e.tensor_tensor(out=P[i_][:, :], in0=XT[c_][:, :], in1=mk[:, M_ID:M_ID + 256], op=ALU.add),
                         reads=[bXT[c_], mkb], writes=[bP[i_]])
                    yield
                W = Wb[d][half]
                bW = [B("sc_W", dh, i_) for i_ in range(2)]
                ps, pb = nps()
                for g8, (hp, hh) in enumerate(heads):
                    f.op(f.pe, lambda e: e.matmul(O(ps, g8), lhsT=fm_(A_, hp, hh), rhs=ST[d][64 * hh:64 * hh + 64, hp, :],
                                                  start=True, stop=False), reads=[bA, stb], writes=[pb], inc=False)
                    f.op(f.pe, lambda e: e.matmul(O(ps, g8), lhsT=O(LakT, g8), rhs=O(Vt_, g8),
                                                  start=False, stop=True), reads=[bL[0], bVt], writes=[pb], inc=(g8 == 7))
                evac(W[0][:, :], ps[:, 0:256], [pb], [bW[0]])
                yield
                wi = 0
                for i_ in range(5, -1, -1):
                    Wc = W[wi]
                    ps, pb = mm8(0, lambda g, hp, hh: O(P[i_], g), lambda g, hp, hh: O(Wc, g), [bP[i_], bW[wi]])
                    wi ^= 1
                    evac(W[wi][:, :], ps[:, 0:256], [pb], [bW[wi]])
                    yield
                U = W[wi]
                bU = bW[wi]
                ps, pb = nps()
                for g8, (hp, hh) in enumerate(heads):
                    f.op(f.pe, lambda e: e.matmul(O(ps, g8), lhsT=ST[d][64 * hh:64 * hh + 64, hp, :], rhs=fm_(R_, hp, hh),
                                                  start=True, stop=False), reads=[bR, stb], writes=[pb], inc=False)
                    f.op(f.pe, lambda e: e.matmul(O(ps, g8), lhsT=O(U, g8), rhs=O(LrbT, g8),
                                                  start=False, stop=False), reads=[bU, bL[1]], writes=[pb], inc=False)
                    f.op(f.pe, lambda e: e.matmul(O(ps, g8), lhsT=O(Vt_, g8), rhs=O(LrkT, g8),
                                                  start=False, stop=True), reads=[bVt, bL[2]], writes=[pb], inc=(g8 == 7))
                yo = Yo[d][half][0]
                evac(yo[:, :], ps[:, 0:256], [pb], [B("sc_Yo", dh)])
                f.dma(f.sp, yv[d][:, 4 * half:4 * half + 4, ch * 64:(ch + 1) * 64], yo[:, :].rearrange("p (g t) -> p g t", t=64),
                      reads=[B("sc_Yo", dh)])
                yield
                ps, pb = nps()
                for g8, (hp, hh) in enumerate(heads):
                    o_ = O(ps, g8)
                    f.op(f.pe, lambda e: e.matmul(o_, lhsT=O(Bt_, g8), rhs=O(U, g8), start=True, stop=False),
                         reads=[bBt, bU], writes=[pb], inc=False)
                    f.op(f.pe, lambda e: e.matmul(o_, lhsT=O(Kt_, g8), rhs=O(Vt_, g8), start=False, stop=False),
                         reads=[bKt, bVt], writes=[pb], inc=False)
                    f.op(f.pe, lambda e: e.matmul(o_, lhsT=cf[64 * hh:64 * hh + 64, C_ID + 64 * hh:C_ID + 64 * hh + 64],
                                                  rhs=ST[d][64 * hh:64 * hh + 64, hp, :], start=False, stop=True),
                         reads=[cst, stb], writes=[pb], inc=(g8 == 7))
                for hpi in range(4):
                    hp = 4 * half + hpi
                    f.op(f.act, lambda e: e.activation(out=ST[d][:, hp, :], in_=ps[:, hpi * 64:(hpi + 1) * 64], func=AF.Copy,
                                                       scale=gC[d][:, hp, ch:ch + 1]),
                         reads=[pb, B("sc_gC", d)], writes=[stb])

            for s_ in range(getattr(self, "scan_steps", NCHUNK)):
                gens = [group(d, order[d][s_], half) for half in range(2) for d in range(2)]
                while gens:
                    alive = []
                    for g_ in gens:
                        try:
                            next(g_)
                            alive.append(g_)
                        except StopIteration:
                            pass
                    gens = alive
            f.barrier()

    def phase_rwkv_out(self, l, last):
        nc, f = self.nc, self.f
        B = f.buf
        I, S = self.I, self.S
        pv = self.fm(S["proj"])
        cf = self.constf
        with contextlib.ExitStack() as es:
            sb = lambda n, s, d: es.enter_context(self.sbuf(n, s, d))
            gup = sb("ro_gup", [128, D], F32)
            f.dma(f.sp, gup[:], I["gate_up"][l, :, :], writes=[B("ro_gup")])
            lg = sb("ro_lg", [128, 512], F32)
            tl = {}
            for nm in ("yf", "yb", "r", "k01", "v"):
                tl[nm] = [sb("ro_%s%d" % (nm, i), [128, 512], F32) for i in range(4)]
            sq = [sb("ro_sq%d" % i, [128, 512], F32) for i in range(4)]
            mean = [sb("ro_mean%d" % i, [128, 512], F32) for i in range(4)]
            var = [sb("ro_var%d" % i, [128, 512], F32) for i in range(4)]
            tt = [sb("ro_t%d" % i, [128, 512], F32) for i in range(4)]
            bon = [sb("ro_bon%d" % i, [128, 512], F32) for i in range(4)]
            ost = [sb("ro_o%d" % i, [128, 512], BF16) for i in range(4)]
            srcv = {"yf": self.fm(S["yf"]), "yb": self.fm(S["yb"]), "r": self.fm(S["rw_r"]), "k01": self.fm(S["rw_k01"]),
                    "v": self.fm(S["rw_v"])}
            ov = self.fm(S["rwo"])
            lgs = [lg, sb("ro_lg1", [128, 512], F32)]

            def unit(ti, t0, n, c, p):
                lg_ = lgs[ti % 2]
                lgb = B("ro_lg", ti % 2)
                if c == 0:
                    f.dma(f.sp, lg_[:, 0:n], pv[:, 54, t0:t0 + n], writes=[lgb])
                    f.op(f.act, lambda e: e.activation(out=lg_[:, 0:n], in_=lg_[:, 0:n], func=AF.Sigmoid), reads=[lgb], writes=[lgb])
                    yield
                bb = {nm: B("ro_" + nm, p) for nm in tl}
                for nm in tl:
                    f.dma(f.sp, tl[nm][p][:, 0:n], srcv[nm][:, c, t0:t0 + n], writes=[bb[nm]])
                yield
                y = tl["yf"][p]
                f.op(f.pool, lambda e: e.tensor_tensor(out=y[:, 0:n], in0=y[:, 0:n], in1=tl["yb"][p][:, 0:n], op=ALU.add),
                     reads=[bb["yf"], bb["yb"]], writes=[bb["yf"]])
                yield
                f.op(f.act, lambda e: e.activation(out=sq[p][:, 0:n], in_=y[:, 0:n], func=AF.Square), reads=[bb["yf"]], writes=[B("ro_sq", p)])
                yield
                r_ = tl["r"][p]
                f.op(f.dve, lambda e: e.scalar_tensor_tensor(out=r_[:, 0:n], in0=r_[:, 0:n], scalar=self.V(l, "r_k", c),
                                                             in1=tl["k01"][p][:, 0:n], op0=ALU.mult, op1=ALU.mult),
                     reads=[bb["r"], bb["k01"], B("vecs")], writes=[bb["r"]])
                yield
                blk = cf[:, C_BLK:C_BLK + 128]
                PQ = [self.ps[2 * p][:, 0:256], self.ps[2 * p][:, 256:512], self.ps[2 * p + 1][:, 0:256], self.ps[2 * p + 1][:, 256:512]]
                PB = [self.psb[2 * p], self.psb[2 * p], self.psb[2 * p + 1], self.psb[2 * p + 1]]
                f.op(f.pe, lambda e: e.matmul(PQ[0][:, 0:n], lhsT=blk, rhs=y[:, 0:n], start=True, stop=True),
                     reads=[bb["yf"], B("constf")], writes=[PB[0]])
                yield
                f.op(f.pe, lambda e: e.matmul(PQ[1][:, 0:n], lhsT=blk, rhs=sq[p][:, 0:n], start=True, stop=True),
                     reads=[B("ro_sq", p), B("constf")], writes=[PB[1]])
                yield
                f.op(f.pe, lambda e: e.matmul(PQ[2][:, 0:n], lhsT=blk, rhs=r_[:, 0:n], start=True, stop=True),
                     reads=[bb["r"], B("constf")], writes=[PB[2]])
                yield
                f.op(f.pe, lambda e: e.matmul(PQ[3][:, 0:n], lhsT=gup[:, c * 128:(c + 1) * 128], rhs=lg_[:, 0:n], start=True, stop=True),
                     reads=[lgb, B("ro_gup")], writes=[PB[3]])
                yield
                m_, v_, t_ = mean[p], var[p], tt[p]
                f.op(f.act, lambda e: e.activation(out=m_[:, 0:n], in_=PQ[0][:, 0:n], func=AF.Copy, scale=1.0 / 64),
                     reads=[PB[0]], writes=[B("ro_mean", p)])
                yield
                f.op(f.dve, lambda e: e.tensor_tensor(out=v_[:, 0:n], in0=m_[:, 0:n], in1=m_[:, 0:n], op=ALU.mult),
                     reads=[B("ro_mean", p)], writes=[B("ro_var", p)])
                yield
                f.op(f.dve, lambda e: e.scalar_tensor_tensor(out=v_[:, 0:n], in0=PQ[1][:, 0:n], scalar=1.0 / 64, in1=v_[:, 0:n],
                                                             op0=ALU.mult, op1=ALU.subtract),
                     reads=[PB[1], B("ro_var", p)], writes=[B("ro_var", p)])
                yield
                self.rstd_from_ps(v_[:, 0:n], B("ro_var", p), v_[:, 0:n], B("ro_var", p), 1.0, GN_EPS)
                yield
                f.op(f.dve, lambda e: e.tensor_tensor(out=t_[:, 0:n], in0=y[:, 0:n], in1=m_[:, 0:n], op=ALU.subtract),
                     reads=[bb["yf"], B("ro_mean", p)], writes=[B("ro_t", p)])
                yield
                f.op(f.pool, lambda e: e.tensor_tensor(out=t_[:, 0:n], in0=t_[:, 0:n], in1=v_[:, 0:n], op=ALU.mult),
                     reads=[B("ro_t", p), B("ro_var", p)], writes=[B("ro_t", p)])
                yield
                f.op(f.act, lambda e: e.activation(out=t_[:, 0:n], in_=t_[:, 0:n], func=AF.Identity,
                                                   bias=self.V(l, "gn_b", c), scale=self.V(l, "gn_g", c)),
                     reads=[B("ro_t", p), B("vecs")], writes=[B("ro_t", p)])
                yield
                f.op(f.dve, lambda e: e.tensor_tensor(out=bon[p][:, 0:n], in0=PQ[2][:, 0:n], in1=tl["v"][p][:, 0:n], op=ALU.mult),
                     reads=[PB[2], bb["v"]], writes=[B("ro_bon", p)])
                yield
                f.op(f.pool, lambda e: e.tensor_tensor(out=t_[:, 0:n], in0=t_[:, 0:n], in1=bon[p][:, 0:n], op=ALU.add),
                     reads=[B("ro_t", p), B("ro_bon", p)], writes=[B("ro_t", p)])
                yield
                f.op(f.dve, lambda e: e.tensor_tensor(out=ost[p][:, 0:n], in0=PQ[3][:, 0:n], in1=t_[:, 0:n], op=ALU.mult),
                     reads=[PB[3], B("ro_t", p)], writes=[B("ro_o", p)])
                yield
                f.dma(f.sp, ov[:, c, t0:t0 + n], ost[p][:, 0:n], reads=[B("ro_o", p)])
                yield

            units = [(ti, t0, n, c) for ti, (t0, n, s0, sl) in enumerate(self.tiles(self.seqs(last), 256)) for c in range(8)]
            active = []
            nxt = 0
            free = [0]
            rnd = 0
            while nxt < len(units) or active:
                rnd += 1
                if rnd in (6, 11, 16):
                    free.append(rnd // 5)
                while nxt < len(units) and free:
                    p_ = free.pop()
                    active.append((unit(*units[nxt], p_), p_))
                    nxt += 1
                still = []
                for g_, p_ in active:
                    try:
                        next(g_)
                        still.append((g_, p_))
                    except StopIteration:
                        free.append(p_)
                active = still
            f.barrier()


def _fm(v):
    v = np.asarray(v, np.float32).reshape(-1, 128)
    return np.ascontiguousarray(v.T)


def _consts():
    c = np.zeros((128, NCONST), np.float32)
    c[:, C_ONES:C_ONES + 128] = 1.0
    for h in range(2):
        c[64 * h:64 * h + 64, C_BLK + 64 * h:C_BLK + 64 * h + 64] = 1.0
    c[:, C_ID:C_ID + 128] = np.eye(128, dtype=np.float32)
    P = np.zeros((128, 128), np.float32)
    for m in range(128):
        if m % 64 < 32:
            P[m, m + 32] = -1.0
        else:
            P[m, m - 32] = 1.0
    c[:, C_ROT:C_ROT + 128] = P.T
    s = np.arange(128)[:, None]
    t = np.arange(128)[None, :]
    same = (s // 64) == (t // 64)
    c[:, C_TRIF:C_TRIF + 128] = (same & (s <= t))
    c[:, C_TRIF + 128:C_TRIF + 256] = (same & (s < t))
    c[:, C_TRIB:C_TRIB + 128] = (same & (s >= t))
    c[:, C_TRIB + 128:C_TRIB + 256] = (same & (s > t))
    r = np.arange(64)[:, None]
    q = np.arange(64)[None, :]
    m = np.zeros((128, 5 * 256), np.float32)
    m[:, M_LOS:M_LOS + 256] = np.tile((q < r).astype(np.float32), (2, 4))
    m[:, M_UPS:M_UPS + 256] = np.tile((q > r).astype(np.float32), (2, 4))
    m[:, M_LOI:M_LOI + 256] = np.tile((q <= r).astype(np.float32), (2, 4))
    m[:, M_UPI:M_UPI + 256] = np.tile((q >= r).astype(np.float32), (2, 4))
    m[:, M_ID:M_ID + 256] = np.tile(np.eye(64, dtype=np.float32), (2, 4))
    tt = np.arange(SEQ)
    row = (tt // 64).astype(np.float32)
    col = (tt % 64).astype(np.float32)
    inv = (10000.0 ** (-np.arange(32, dtype=np.float32) / 32)).astype(np.float32)
    cos = np.zeros((128, SEQ), np.float32)
    sin = np.zeros((128, SEQ), np.float32)
    for p in range(128):
        pos = row if p < 64 else col
        ang = (pos * inv[p % 32]).astype(np.float32)
        cos[p] = np.cos(ang)
        sin[p] = np.sin(ang)
    return c, m, cos, sin


def prep_inputs(inp):
    g = lambda k: np.asarray(inp[k], np.float32)
    vecs = np.zeros((DEPTH, 128, NV), np.float32)
    for l in range(DEPTH):
        def put(name, arr):
            a = np.asarray(arr, np.float32)
            vecs[l, :, VCOL[name]:VCOL[name] + a.shape[1]] = a
        put("b_mod", _fm(g("b_mod")[l]))
        for n in ("g_pre_mix", "g_post_mix", "g_pre_ffn", "g_post_ffn", "q_norm", "k_norm", "conv_b",
                  "k_k", "k_a"):
            put(n, _fm(g(n)[l]))
        put("conv_ln_g", _fm(g("conv_ln_g")[l]))
        put("conv_ln_b", _fm(g("conv_ln_b")[l]))
        put("gn_g", _fm(g("wkv_gn_g")[l]))
        put("gn_b", _fm(g("wkv_gn_b")[l]))
        put("r_k", _fm(g("r_k")[l].reshape(-1)))
        cw = g("conv_w")[l].reshape(31, 8, 128).transpose(2, 1, 0).reshape(128, 248)
        put("conv_w", cw)
        sw = g("shift_w")[l].reshape(3, 24, 128).transpose(2, 1, 0).reshape(128, 72)
        put("shift_w", sw)
        fw_ = g("ffn_conv_w")[l].reshape(3, 44, 128).transpose(2, 1, 0).reshape(128, 132)
        put("ffn_conv_w", fw_)
        a0 = g("iclr_a0")[l].reshape(2, 8, 128).transpose(2, 0, 1).reshape(128, 16)
        put("iclr_a0", a0)
    constf, maskf, cos, sin = _consts()
    shared = {
        "vecs": vecs,
        "w0row": np.ascontiguousarray(g("decay_w0").reshape(DEPTH, 1, 2048)),
        "constf": constf, "maskf": maskf, "cos": cos, "sin": sin,
        "w_mod": g("w_mod"), "w_in": g("w_in"),
        "w_attn_o": g("w_attn_o"), "w_conv_o": g("w_conv_o"), "w_rwkv_o": g("w_rwkv_o"), "w_out": g("w_out"),
        "decay_up": np.ascontiguousarray(g("decay_up").reshape(DEPTH, 128, D)),
        "iclr_up": np.ascontiguousarray(g("iclr_up").reshape(DEPTH, 128, D)),
        "gate_up": g("gate_up"),
        "w_ffn_up": g("w_ffn_up"), "w_ffn_down": g("w_ffn_down"),
    }
    x = g("x")
    ctx = g("ctx")
    c = g("c")
    c_ctx = g("c_ctx")
    per_core = []
    for b in range(x.shape[0]):
        xs0 = np.ascontiguousarray(np.concatenate([ctx[b].T, x[b].T], axis=1))
        cc = np.stack([c[b], c_ctx], axis=-1).reshape(8, 128, 2).transpose(1, 0, 2).reshape(128, 16)
        m = dict(shared)
        m["xs0"] = xs0
        m["cc"] = np.ascontiguousarray(cc)
        per_core.append(m)
    return per_core


def kernel(**inputs):
    per_core = prep_inputs(inputs)
    nc = Prog().build()
    res = run_bass_kernel_spmd(nc, per_core, core_ids=list(range(8)))
    outs = [np.ascontiguousarray(np.asarray(r["out"], np.float32).T) for r in res.results]
    return np.stack(outs, axis=0)
```

```python
import math
import contextlib
import numpy as np
import concourse.bass as bass
import concourse.mybir as mybir
from concourse.bass_utils import run_bass_kernel_spmd

F32 = mybir.dt.float32
BF16 = mybir.dt.bfloat16
AF = mybir.ActivationFunctionType
ALU = mybir.AluOpType

D = 1024
SEQ = 4096
CTX = 256
T = SEQ + CTX
DEPTH = 2
NIN = 10112
DFF = 2816
DECAY_SCALE = math.exp(-0.5)
EPS = 1e-6
LN_EPS = 1e-5
GN_EPS = 64 * 1e-5
CH = 64
NCHUNK = T // CH

VCOL = {}
_o = 0
for _n, _w in (("b_mod", 48), ("g_pre_mix", 8), ("g_post_mix", 8), ("g_pre_ffn", 8), ("g_post_ffn", 8),
               ("q_norm", 1), ("k_norm", 1), ("conv_w", 248), ("conv_b", 8), ("conv_ln_g", 8), ("conv_ln_b", 8),
               ("shift_w", 72), ("iclr_a0", 16), ("k_k", 8), ("k_a", 8), ("r_k", 8), ("gn_g", 8), ("gn_b", 8),
               ("ffn_conv_w", 132)):
    VCOL[_n] = _o
    _o += _w
NV = _o
C_ONES, C_BLK, C_ID, C_ROT, C_TRIF, C_TRIB = 0, 128, 256, 384, 512, 768
NCONST = 1024
M_LOS, M_UPS, M_LOI, M_UPI, M_ID = 0, 256, 512, 768, 1024


class Buf:
    __slots__ = ("w", "r")

    def __init__(self):
        self.w = None
        self.r = {}


class Eng:
    def __init__(self, name, e, sem):
        self.name = name
        self.e = e
        self.sem = sem
        self.count = 0
        self.seen = {}


class FW:
    SAME_ENG_DIST = 2

    def __init__(self, nc, es, n_dma_sems=24):
        self.nc = nc
        self.sems = {}

        def mk(name, e):
            s = es.enter_context(nc.semaphore("sem_" + name))
            self.sems[name] = s
            return Eng(name, e, s)
        self.pe = mk("pe", nc.tensor)
        self.act = mk("act", nc.scalar)
        self.dve = mk("dve", nc.vector)
        self.pool = mk("pool", nc.gpsimd)
        self.sp = mk("sp", nc.sync)
        self.engs = [self.pe, self.act, self.dve, self.pool, self.sp]
        self.dma_sems = []
        for i in range(n_dma_sems):
            nm = "dq%d" % i
            self.sems[nm] = es.enter_context(nc.semaphore("sem_" + nm))
            self.dma_sems.append([nm, 0])
        self.dma_rr = 0
        self.bufs = {}
        self.n_ins = 0

    def buf(self, *key):
        b = self.bufs.get(key)
        if b is None:
            b = Buf()
            self.bufs[key] = b
        return b

    def _need(self, eng, tok, raw):
        if tok is None:
            return
        sk, v = tok
        if sk == eng.name:
            if not (raw and (eng.count - v) < self.SAME_ENG_DIST):
                return
        if eng.seen.get(sk, 0) >= v:
            return
        eng.e.wait_ge(self.sems[sk], v)
        eng.seen[sk] = v

    def _deps(self, eng, reads, writes):
        for b in reads:
            self._need(eng, b.w, True)
        for b in writes:
            self._need(eng, b.w, False)
            for sk, v in b.r.items():
                self._need(eng, (sk, v), False)

    def _mark(self, tok, reads, writes):
        for b in reads:
            if b.r.get(tok[0], 0) < tok[1]:
                b.r[tok[0]] = tok[1]
        for b in writes:
            b.w = tok
            b.r = {}

    def op(self, eng, fn, reads=(), writes=(), inc=True):
        self._deps(eng, reads, writes)
        ins = fn(eng.e)
        self.n_ins += 1
        if inc:
            eng.count += 1
            ins.then_inc(eng.sem, 1)
            tok = (eng.name, eng.count)
        else:
            tok = (eng.name, eng.count + 1)
        self._mark(tok, reads, writes)
        return ins

    def dma(self, eng, out, in_, reads=(), writes=()):
        slot = self.dma_sems[self.dma_rr]
        self.dma_rr = (self.dma_rr + 1) % len(self.dma_sems)
        nm, cnt = slot
        self._deps(eng, reads, writes)
        self._need(eng, (nm, cnt), False)
        ins = eng.e.dma_start(out=out, in_=in_)
        slot[1] = cnt + 16
        ins.then_inc(self.sems[nm], 16)
        self.n_ins += 1
        self._mark((nm, cnt + 16), reads, writes)
        return ins

    def barrier(self):
        for e in self.engs:
            for o in self.engs:
                if o is not e and o.count > 0:
                    self._need(e, (o.name, o.count), False)
            for nm, cnt in self.dma_sems:
                if cnt > 0:
                    self._need(e, (nm, cnt), False)


class Prog:
    def __init__(self, debug=None, nlayers=DEPTH, stop_after=None):
        self.debug = debug or []
        self.nlayers = nlayers
        self.stop_after = stop_after
        self.nc = bass.Bass("TRN2", target_bir_lowering=False)
        self.uid = 0

    def din(self, name, shape, dt=F32):
        return self.nc.dram_tensor(name, list(shape), dt, kind="ExternalInput").ap()

    def dscr(self, name, shape, dt=F32):
        kind = "ExternalOutput" if name in self.debug else "Internal"
        return self.nc.dram_tensor(name, list(shape), dt, kind=kind).ap()

    def sbuf(self, name, shape, dt):
        self.uid += 1
        return self.nc.sbuf_tensor("%s_u%d" % (name, self.uid), shape, dt)

    def fm(self, ap):
        return ap.rearrange("(c p) t -> p c t", p=128)

    def build(self):
        nc = self.nc
        L = self.nlayers
        I = {}
        I["xs0"] = self.din("xs0", [D, T])
        I["cc"] = self.din("cc", [128, 16])
        I["vecs"] = self.din("vecs", [DEPTH, 128, NV])
        I["w0row"] = self.din("w0row", [DEPTH, 1, 2048])
        I["constf"] = self.din("constf", [128, NCONST])
        I["maskf"] = self.din("maskf", [128, 5 * 256])
        I["cos"] = self.din("cos", [128, SEQ])
        I["sin"] = self.din("sin", [128, SEQ])
        I["w_mod"] = self.din("w_mod", [DEPTH, D, 6 * D])
        I["w_in"] = self.din("w_in", [DEPTH, D, NIN])
        for n in ("w_attn_o", "w_conv_o", "w_rwkv_o", "w_out"):
            I[n] = self.din(n, [DEPTH, D, D])
        I["decay_up"] = self.din("decay_up", [DEPTH, 128, D])
        I["iclr_up"] = self.din("iclr_up", [DEPTH, 128, D])
        I["gate_up"] = self.din("gate_up", [DEPTH, 128, D])
        I["w_ffn_up"] = self.din("w_ffn_up", [DEPTH, D, 2 * DFF])
        I["w_ffn_down"] = self.din("w_ffn_down", [DEPTH, DFF, D])
        self.I = I
        out = nc.dram_tensor("out", [D, SEQ], F32, kind="ExternalOutput").ap()
        S = {}
        S["proj"] = self.dscr("proj", [NIN, T])
        S["att"] = self.dscr("att", [D, T], BF16)
        S["cnv"] = self.dscr("cnv", [D, T], BF16)
        S["rwo"] = self.dscr("rwo", [D, T], BF16)
        S["ffa"] = self.dscr("ffa", [DFF, T], BF16)
        for n in ("rw_r", "rw_v", "rw_k01", "yf", "yb"):
            S[n] = self.dscr(n, [D, T])
        for d in range(2):
            for n in ("At", "Bt", "Kt", "Rt"):
                S["%s%d" % (n, d)] = self.dscr("%s%d" % (n, d), [D, T])
            S["gC%d" % d] = self.dscr("gC%d" % d, [D, NCHUNK])
        S["xsm0"] = self.dscr("xsm0", [D, T])
        S["xs1"] = self.dscr("xs1", [D, T])
        S["xsm1"] = self.dscr("xsm1", [D, T])
        self.S = S

        with contextlib.ExitStack() as es:
            self.es = es
            f = FW(nc, es)
            self.f = f
            self.constf = es.enter_context(self.sbuf("constf", [128, NCONST], F32))
            self.maskf = es.enter_context(self.sbuf("maskf", [128, 5 * 256], F32))
            self.vecs = es.enter_context(self.sbuf("vecs", [128, DEPTH, NV], F32))
            self.modS = es.enter_context(self.sbuf("modS", [128, 48, 2], F32))
            self.dsc = es.enter_context(self.sbuf("dsc", [128, 64, 2], F32))
            self.onesb = es.enter_context(self.sbuf("onesb", [128, 128], BF16))
            self.ps = [es.enter_context(nc.psum_tensor("ps%d" % i, [128, 512], F32)) for i in range(8)]
            self.psb = [f.buf("ps", i) for i in range(8)]
            B = f.buf
            f.dma(f.sp, self.constf[:], I["constf"][:, :], writes=[B("constf")])
            f.dma(f.sp, self.maskf[:], I["maskf"][:, :], writes=[B("maskf")])
            for l in range(DEPTH):
                f.dma(f.sp, self.vecs[:, l, :], I["vecs"][l, :, :], writes=[B("vecs")])
            f.op(f.dve, lambda e: e.tensor_copy(out=self.onesb[:], in_=self.constf[:, C_ONES:C_ONES + 128]),
                 reads=[B("constf")], writes=[B("onesb")])
            f.barrier()
            xs_in = I["xs0"]
            for l in range(L):
                last = (l == DEPTH - 1)
                xsm = S["xsm%d" % l]
                xs_out = out if last else S["xs1"]
                self.phase_mod(l)
                if self.stop_after == ("mod", l): break
                self.phase_inproj(l, xs_in, last)
                if self.stop_after == ("inproj", l): break
                self.phase_attn(l, last)
                if self.stop_after == ("attn", l): break
                self.phase_conv(l, last)
                if self.stop_after == ("conv", l): break
                self.phase_rwkv_prep(l)
                if self.stop_after == ("rwprep", l): break
                self.phase_rwkv_scan(l, last)
                if self.stop_after == ("rwscan", l): break
                self.phase_rwkv_out(l, last)
                if self.stop_after == ("rwout", l): break
                self.phase_merge(l, xs_in, xsm, last)
                if self.stop_after == ("merge", l): break
                self.phase_ffn_up(l, xsm, last)
                if self.stop_after == ("ffnup", l): break
                self.phase_ffn_down(l, xsm, xs_out, last)
                xs_in = xs_out
            f.barrier()
        return nc

    def V(self, l, name, c=0, n=1):
        o = VCOL[name] + c
        return self.vecs[:, l, o:o + n]

    def seqs(self, last):
        return [(CTX, SEQ)] if last else [(0, CTX), (CTX, SEQ)]

    def tiles(self, seqs, n=512):
        r = []
        for s0, sl in seqs:
            t = s0
            while t < s0 + sl:
                m = min(n, s0 + sl - t)
                r.append((t, m, s0, sl))
                t += m
        return r

    def rstd_from_ps(self, ps_ap, psbuf, out_ap, outbuf, scale, eps):
        f = self.f
        f.op(f.act, lambda e: e.activation(out=out_ap, in_=ps_ap, func=AF.Ln, bias=float(eps), scale=float(scale)),
             reads=[psbuf], writes=[outbuf])
        f.op(f.act, lambda e: e.activation(out=out_ap, in_=out_ap, func=AF.Exp, scale=-0.5),
             reads=[outbuf], writes=[outbuf])

    def phase_mod(self, l):
        nc, f, es0 = self.nc, self.f, self.es
        B = f.buf
        I = self.I
        with contextlib.ExitStack() as es:
            cT = es.enter_context(self.sbuf("cT", [128, 8, 2], F32))
            wt = [es.enter_context(self.sbuf("wmod%d" % i, [128, 8, 1024], F32)) for i in range(2)]
            f.dma(f.sp, cT[:], I["cc"].rearrange("p (k j) -> p k j", j=2), writes=[B("cT")])
            f.op(f.act, lambda e: e.activation(out=cT[:], in_=cT[:], func=AF.Silu), reads=[B("cT")], writes=[B("cT")])
            wv = I["w_mod"][l].rearrange("(k p) n -> p k n", p=128)
            ps = self.ps[0]
            for g in range(6):
                w = wt[g % 2]
                for k in range(8):
                    f.dma(f.sp, w[:, k, :], wv[:, k, g * 1024:(g + 1) * 1024], writes=[B("wmod", g % 2, k)])
                for oc in range(8):
                    col = (g * 8 + oc) * 2
                    for k in range(8):
                        f.op(f.pe, lambda e: e.matmul(ps[:, col:col + 2], lhsT=w[:, k, oc * 128:(oc + 1) * 128],
                                                      rhs=cT[:, k, :], start=(k == 0), stop=(k == 7)),
                             reads=[B("wmod", g % 2, k), B("cT")], writes=[self.psb[0]], inc=(k == 7))
            psv = ps[:, 0:96].rearrange("p (c j) -> p c j", j=2)
            for j in range(2):
                f.op(f.dve, lambda e: e.tensor_tensor(out=self.modS[:, :, j], in0=psv[:, :, j],
                                                      in1=self.V(l, "b_mod", 0, 48), op=ALU.add),
                     reads=[self.psb[0], B("vecs")], writes=[B("modS")])
            for j in range(2):
                def m(i):
                    return self.modS[:, i * 8:(i + 1) * 8, j]
                rd = [B("modS"), B("vecs")]
                wr = [B("dsc")]
                f.op(f.dve, lambda e: e.scalar_tensor_tensor(out=self.dsc[:, 0:8, j], in0=m(1), scalar=1.0,
                                                             in1=self.V(l, "g_pre_mix", 0, 8), op0=ALU.add, op1=ALU.mult),
                     reads=rd, writes=wr)
                f.op(f.dve, lambda e: e.tensor_copy(out=self.dsc[:, 8:16, j], in_=m(0)), reads=rd, writes=wr)
                f.op(f.dve, lambda e: e.tensor_tensor(out=self.dsc[:, 16:24, j], in0=m(2),
                                                      in1=self.V(l, "g_post_mix", 0, 8), op=ALU.mult), reads=rd, writes=wr)
                f.op(f.dve, lambda e: e.scalar_tensor_tensor(out=self.dsc[:, 24:32, j], in0=m(4), scalar=1.0,
                                                             in1=self.V(l, "g_pre_ffn", 0, 8), op0=ALU.add, op1=ALU.mult),
                     reads=rd, writes=wr)
                f.op(f.dve, lambda e: e.tensor_copy(out=self.dsc[:, 32:40, j], in_=m(3)), reads=rd, writes=wr)
                f.op(f.dve, lambda e: e.tensor_tensor(out=self.dsc[:, 40:48, j], in0=m(5),
                                                      in1=self.V(l, "g_post_ffn", 0, 8), op=ALU.mult), reads=rd, writes=wr)
            f.barrier()

    def prenorm(self, es, src, seqs, base, hT, hcol_of):
        nc, f = self.nc, self.f
        B = f.buf
        xt = [es.enter_context(self.sbuf("pn_x%d" % i, [128, 8, 512], F32)) for i in range(2)]
        sq = es.enter_context(self.sbuf("pn_sq", [128, 8, 512], F32))
        rs = es.enter_context(self.sbuf("pn_rs", [128, 512], F32))
        srcv = self.fm(src)
        for i, (t0, n, s0, sl) in enumerate(self.tiles(seqs)):
            j = 1 if s0 == 0 else 0
            x = xt[i % 2]
            xb = B("pn_x", i % 2)
            f.dma(f.sp, x[:, :, 0:n], srcv[:, :, t0:t0 + n], writes=[xb])
            f.op(f.act, lambda e: e.activation(out=sq[:, :, 0:n], in_=x[:, :, 0:n], func=AF.Square),
                 reads=[xb], writes=[B("pn_sq")])
            ps, pb = self.ps[7], self.psb[7]
            for k in range(8):
                f.op(f.pe, lambda e: e.matmul(ps[:, 0:n], lhsT=self.constf[:, C_ONES:C_ONES + 128], rhs=sq[:, k, 0:n],
                                              start=(k == 0), stop=(k == 7)),
                     reads=[B("pn_sq"), B("constf")], writes=[pb], inc=(k == 7))
            self.rstd_from_ps(ps[:, 0:n], pb, rs[:, 0:n], B("pn_rs"), 1.0 / D, EPS)
            c0 = hcol_of(t0)
            for k in range(8):
                f.op(f.dve, lambda e: e.scalar_tensor_tensor(out=x[:, k, 0:n], in0=x[:, k, 0:n],
                                                             scalar=self.dsc[:, base + k, j:j + 1], in1=rs[:, 0:n],
                                                             op0=ALU.mult, op1=ALU.mult),
                     reads=[xb, B("pn_rs"), B("dsc")], writes=[xb])
                f.op(f.act, lambda e: e.activation(out=hT[:, k, c0:c0 + n], in_=x[:, k, 0:n], func=AF.Identity,
                                                   bias=self.dsc[:, base + 8 + k, j:j + 1], scale=1.0),
                     reads=[xb, B("dsc")], writes=[B("hT")])

    def phase_inproj(self, l, xs_in, last):
        nc, f = self.nc, self.f
        B = f.buf
        I, S = self.I, self.S
        seqs = [(0, CTX), (CTX, SEQ)]
        with contextlib.ExitStack() as es:
            hT = es.enter_context(self.sbuf("hT", [128, 8, T], BF16))
            with contextlib.ExitStack() as es2:
                self.prenorm(es2, xs_in, seqs, 0, hT, lambda t: t)
                f.barrier()
            wt = [es.enter_context(self.sbuf("win%d" % i, [128, 8, 1024], BF16)) for i in range(2)]
            st = [es.enter_context(self.sbuf("ipst%d" % i, [128, 512], F32)) for i in range(4)]
            wv = I["w_in"][l].rearrange("(k p) n -> p k n", p=128)
            pv = self.fm(S["proj"])
            ngrp = (NIN + 1023) // 1024
            cnt = 0
            for g in range(ngrp):
                ncol = min(1024, NIN - g * 1024)
                w = wt[g % 2]
                for k in range(8):
                    f.dma(f.pool, w[:, k, 0:ncol], wv[:, k, g * 1024:g * 1024 + ncol], writes=[B("win", g % 2, k)])
                for (t0, n, s0, sl) in self.tiles(seqs):
                    for oc in range(ncol // 128):
                        pi = cnt % 6
                        ps, pb = self.ps[pi], self.psb[pi]
                        for k in range(8):
                            f.op(f.pe, lambda e: e.matmul(ps[:, 0:n], lhsT=w[:, k, oc * 128:(oc + 1) * 128],
                                                          rhs=hT[:, k, t0:t0 + n], start=(k == 0), stop=(k == 7)),
                                 reads=[B("win", g % 2, k), B("hT")], writes=[pb], inc=(k == 7))
                        s = st[cnt % 4]
                        sb = B("ipst", cnt % 4)
                        if cnt % 2 == 0:
                            f.op(f.act, lambda e: e.copy(out=s[:, 0:n], in_=ps[:, 0:n]), reads=[pb], writes=[sb])
                        else:
                            f.op(f.dve, lambda e: e.tensor_copy(out=s[:, 0:n], in_=ps[:, 0:n]), reads=[pb], writes=[sb])
                        f.dma(f.sp, pv[:, g * 8 + oc, t0:t0 + n], s[:, 0:n], reads=[sb])
                        cnt += 1
            f.barrier()


    def phase_attn(self, l, last):
        nc, f = self.nc, self.f
        B = f.buf
        I, S = self.I, self.S
        pv = self.fm(S["proj"])
        av = self.fm(S["att"])
        cf = self.constf
        with contextlib.ExitStack() as es:
            sb = lambda n, s, d: es.enter_context(self.sbuf(n, s, d))
            kT = sb("kT", [128, 2, T], BF16)
            Vt = sb("Vt", [128, T // 128, 2, 128], BF16)
            cos = sb("cos", [128, SEQ], F32)
            sin = sb("sin", [128, SEQ], F32)
            qg = sb("qg", [128, 1], F32)
            raw = [sb("a_raw%d" % i, [128, 512], F32) for i in range(2)]
            tmps = [[sb("a_%s%d" % (nm_, sl_), [128, 512], F32) for nm_ in ("sq", "rs", "kn", "t1", "t2")] for sl_ in range(2)]
            qT = [sb("a_qT%d" % i, [128, 512], BF16) for i in range(2)]
            pT = [sb("a_pT%d" % i, [128, 512], BF16) for i in range(4)]
            rinv = sb("a_rinv", [128, 512], F32)
            ost = [sb("a_ost%d" % i, [128, 512], BF16) for i in range(2)]
            f.dma(f.sp, cos[:], I["cos"][:, :], writes=[B("cos")])
            f.dma(f.sp, sin[:], I["sin"][:, :], writes=[B("sin")])
            f.op(f.dve, lambda e: e.tensor_scalar(out=qg[:], in0=self.V(l, "q_norm"), scalar1=float(128 ** -0.5),
                                                  scalar2=None, op0=ALU.mult), reads=[B("vecs")], writes=[B("qg")])
            self._nr = 0

            def normrope_g(chunk, t0, n, gain, is_x, out_ap, outbuf, slot=0):
                r = raw[slot]
                rb = B("a_raw", slot)
                sq, rs, kn, t1, t2 = tmps[slot]
                pk = 7 - slot
                f.dma(f.sp, r[:, 0:n], pv[:, chunk, t0:t0 + n], writes=[rb])
                f.op(f.act, lambda e: e.activation(out=sq[:, 0:n], in_=r[:, 0:n], func=AF.Square),
                     reads=[rb], writes=[B("a_sq", slot)])
                f.op(f.pe, lambda e: e.matmul(self.ps[pk][:, 0:n], lhsT=cf[:, C_ONES:C_ONES + 128], rhs=sq[:, 0:n],
                                              start=True, stop=True), reads=[B("a_sq", slot), B("constf")], writes=[self.psb[pk]])
                yield
                self.rstd_from_ps(self.ps[pk][:, 0:n], self.psb[pk], rs[:, 0:n], B("a_rs", slot), 1.0 / 128, EPS)
                f.op(f.dve, lambda e: e.scalar_tensor_tensor(out=kn[:, 0:n], in0=r[:, 0:n], scalar=gain, in1=rs[:, 0:n],
                                                             op0=ALU.mult, op1=ALU.mult),
                     reads=[rb, B("a_rs", slot), B("vecs"), B("qg")], writes=[B("a_kn", slot)])
                yield
                if is_x:
                    p0 = t0 - CTX
                    f.op(f.pe, lambda e: e.matmul(self.ps[pk][:, 0:n], lhsT=cf[:, C_ROT:C_ROT + 128], rhs=kn[:, 0:n],
                                                  start=True, stop=True), reads=[B("a_kn", slot), B("constf")], writes=[self.psb[pk]])
                    yield
                    f.op(f.pool, lambda e: e.tensor_tensor(out=t1[:, 0:n], in0=kn[:, 0:n], in1=cos[:, p0:p0 + n], op=ALU.mult),
                         reads=[B("a_kn", slot), B("cos")], writes=[B("a_t1", slot)])
                    f.op(f.dve, lambda e: e.tensor_tensor(out=t2[:, 0:n], in0=self.ps[pk][:, 0:n], in1=sin[:, p0:p0 + n], op=ALU.mult),
                         reads=[self.psb[pk], B("sin")], writes=[B("a_t2", slot)])
                    f.op(f.pool, lambda e: e.tensor_tensor(out=out_ap, in0=t1[:, 0:n], in1=t2[:, 0:n], op=ALU.add),
                         reads=[B("a_t1", slot), B("a_t2", slot)], writes=[outbuf])
                else:
                    f.op(f.pool, lambda e: e.tensor_copy(out=out_ap, in_=kn[:, 0:n]), reads=[B("a_kn", slot)], writes=[outbuf])

            def normrope(*a_):
                for _ in normrope_g(*a_):
                    pass

            allseq = [(0, CTX), (CTX, SEQ)]
            for (t0, n, s0, sl) in self.tiles(allseq):
                gens = [normrope_g(8 + kvh, t0, n, self.V(l, "k_norm"), s0 != 0, kT[:, kvh, t0:t0 + n], B("kT", kvh), kvh)
                        for kvh in range(2)]
                while gens:
                    alive = []
                    for g_ in gens:
                        try:
                            next(g_)
                            alive.append(g_)
                        except StopIteration:
                            pass
                    gens = alive
            i = 0
            for kvh in range(2):
                for (t0, n, s0, sl) in self.tiles(allseq):
                    r = raw[i % 2]
                    rb = B("a_raw", i % 2)
                    i += 1
                    f.dma(f.sp, r[:, 0:n], pv[:, 10 + kvh, t0:t0 + n], writes=[rb])
                    nb = n // 128
                    for j in range(nb):
                        f.op(f.pe, lambda e: e.transpose(self.ps[7][:, j * 128:(j + 1) * 128], r[:, j * 128:(j + 1) * 128],
                                                         cf[:, C_ID:C_ID + 128]),
                             reads=[rb, B("constf")], writes=[self.psb[7]], inc=(j == nb - 1))
                    b0 = t0 // 128
                    f.op(f.dve, lambda e: e.tensor_copy(out=Vt[:, b0:b0 + nb, kvh, :],
                                                        in_=self.ps[7][:, 0:nb * 128].rearrange("p (b d) -> p b d", d=128)),
                         reads=[self.psb[7]], writes=[B("Vt")])
            units = [(h, t0, n, s0) for h in range(8) for (t0, n, s0, sl) in self.tiles(self.seqs(last))]

            def qprep(ui):
                h, t0, n, s0 = units[ui]
                return normrope_g(h, t0, n, qg[:, 0:1], s0 != 0, qT[ui % 2][:, 0:n], B("a_qT", ui % 2))
            for _ in qprep(0):
                pass
            for qi, (h, t0, n, s0) in enumerate(units):
                kvh = h // 4
                is_x = s0 != 0
                q = qT[qi % 2]
                qb = B("a_qT", qi % 2)
                nxt = qprep(qi + 1) if qi + 1 < len(units) else None
                nblk = (T // 128) if is_x else (CTX // 128)
                po, pob = self.ps[3 + qi % 2], self.psb[3 + qi % 2]
                pr, prb = self.ps[5 + qi % 2], self.psb[5 + qi % 2]

                def smm(jb):
                    f.op(f.pe, lambda e: e.matmul(self.ps[jb % 3][:, 0:n], lhsT=kT[:, kvh, jb * 128:(jb + 1) * 128],
                                                  rhs=q[:, 0:n], start=True, stop=True),
                         reads=[B("kT", kvh), qb], writes=[self.psb[jb % 3]])

                def pvmm(jb):
                    p = pT[jb % 4]
                    pb = B("a_pT", jb % 4)
                    lastb = (jb == nblk - 1)
                    f.op(f.pe, lambda e: e.matmul(po[:, 0:n], lhsT=Vt[:, jb, kvh, :], rhs=p[:, 0:n],
                                                  start=(jb == 0), stop=lastb),
                         reads=[B("Vt"), pb], writes=[pob], inc=False)
                    f.op(f.pe, lambda e: e.matmul(pr[:, 0:n], lhsT=self.onesb[:], rhs=p[:, 0:n],
                                                  start=(jb == 0), stop=lastb),
                         reads=[B("onesb"), pb], writes=[prb], inc=lastb)
                smm(0)
                if nblk > 1:
                    smm(1)
                for jb in range(nblk):
                    p = pT[jb % 4]
                    pb = B("a_pT", jb % 4)
                    f.op(f.act, lambda e: e.activation(out=p[:, 0:n], in_=self.ps[jb % 3][:, 0:n], func=AF.Exp),
                         reads=[self.psb[jb % 3]], writes=[pb])
                    if jb + 2 < nblk:
                        smm(jb + 2)
                    if jb >= 1:
                        pvmm(jb - 1)
                    if nxt is not None and jb in (4, 10, 16, 22):
                        try:
                            next(nxt)
                        except StopIteration:
                            nxt = None
                pvmm(nblk - 1)
                if nxt is not None:
                    for _ in nxt:
                        pass
                f.op(f.dve, lambda e: e.reciprocal(out=rinv[:, 0:n], in_=pr[:, 0:n]), reads=[prb], writes=[B("a_rinv")])
                o = ost[qi % 2]
                ob = B("a_ost", qi % 2)
                f.op(f.dve, lambda e: e.tensor_tensor(out=o[:, 0:n], in0=po[:, 0:n], in1=rinv[:, 0:n], op=ALU.mult),
                     reads=[pob, B("a_rinv")], writes=[ob])
                f.dma(f.sp, av[:, h, t0:t0 + n], o[:, 0:n], reads=[ob])
            f.barrier()

    def phase_conv(self, l, last):
        nc, f = self.nc, self.f
        B = f.buf
        S = self.S
        pv = self.fm(S["proj"])
        cv = self.fm(S["cnv"])
        cf = self.constf
        with contextlib.ExitStack() as es:
            sb = lambda n, s, d: es.enter_context(self.sbuf(n, s, d))
            at = [sb("c_a%d" % i, [128, 544], F32) for i in range(2)]
            bt = [sb("c_b%d" % i, [128, 544], F32) for i in range(2)]
            y = sb("c_y", [128, 8, 512], F32)
            sq = [sb("c_sq%d" % i, [128, 512], F32) for i in range(2)]
            mean = sb("c_mean", [128, 512], F32)
            msq = sb("c_msq", [128, 512], F32)
            rstd = sb("c_rstd", [128, 512], F32)
            tt = [sb("c_t%d" % i, [128, 512], F32) for i in range(2)]
            ost = [sb("c_o%d" % i, [128, 512], BF16) for i in range(2)]
            gbf = [sb("c_gb%d" % i, [128, 544], BF16) for i in range(2)]
            dg = sb("c_dg", [128, 8, 31, 128], BF16)
            k_ = 0
            for c in range(8):
                for j in range(31):
                    wj = self.V(l, "conv_w", c * 31 + j)
                    e3 = k_ % 3
                    k_ += 1
                    if e3 == 0:
                        f.op(f.pool, lambda e: e.tensor_scalar(out=dg[:, c, j, :], in0=cf[:, C_ID:C_ID + 128], scalar1=wj, scalar2=None,
                                                               op0=ALU.mult), reads=[B("constf"), B("vecs")], writes=[B("c_dg", c, 0)])
                    elif e3 == 1:
                        f.op(f.dve, lambda e: e.tensor_scalar(out=dg[:, c, j, :], in0=cf[:, C_ID:C_ID + 128], scalar1=wj, scalar2=None,
                                                              op0=ALU.mult), reads=[B("constf"), B("vecs")], writes=[B("c_dg", c, 1)])
                    else:
                        f.op(f.act, lambda e: e.activation(out=dg[:, c, j, :], in_=cf[:, C_ID:C_ID + 128], func=AF.Copy, scale=wj),
                             reads=[B("constf"), B("vecs")], writes=[B("c_dg", c, 2)])
            it = 0
            for (t0, n, s0, sl) in self.tiles(self.seqs(last)):
                lo = max(t0 - 15, s0)
                hi = min(t0 + n + 15, s0 + sl)
                off = lo - (t0 - 15)
                edge = (lo != t0 - 15) or (hi != t0 + n + 15)
                for c in range(8):
                    a = at[it % 2]
                    b = bt[it % 2]
                    ab = B("c_a", it % 2)
                    bb = B("c_b", it % 2)
                    it += 1
                    if edge:
                        f.op(f.pool, lambda e: e.memset(a[:, 0:n + 30], 0.0), writes=[ab])
                        f.op(f.pool, lambda e: e.memset(b[:, 0:n + 30], 0.0), writes=[bb])
                    f.dma(f.sp, a[:, off:off + hi - lo], pv[:, 12 + c, lo:hi], writes=[ab])
                    f.dma(f.sp, b[:, off:off + hi - lo], pv[:, 20 + c, lo:hi], writes=[bb])
                    f.op(f.act, lambda e: e.activation(out=b[:, 0:n + 30], in_=b[:, 0:n + 30], func=AF.Sigmoid),
                         reads=[bb], writes=[bb])
                    gb_ = gbf[it % 2]
                    gbb = B("c_gb", it % 2)
                    f.op(f.pool, lambda e: e.tensor_tensor(out=gb_[:, 0:n + 30], in0=a[:, 0:n + 30], in1=b[:, 0:n + 30], op=ALU.mult),
                         reads=[ab, bb], writes=[gbb])
                    yb = B("c_y", c)
                    pi = 2 + it % 4
                    for j in range(31):
                        f.op(f.pe, lambda e: e.matmul(self.ps[pi][:, 0:n], lhsT=dg[:, c, j, :], rhs=gb_[:, j:j + n],
                                                      start=(j == 0), stop=(j == 30)),
                             reads=[gbb, B("c_dg", c, 0), B("c_dg", c, 1), B("c_dg", c, 2)], writes=[self.psb[pi]], inc=(j == 30))
                    f.op(f.act, lambda e: e.activation(out=y[:, c, 0:n], in_=self.ps[pi][:, 0:n], func=AF.Identity,
                                                       bias=self.V(l, "conv_b", c), scale=1.0),
                         reads=[self.psb[pi], B("vecs")], writes=[yb])
                    s = sq[c % 2]
                    sqb = B("c_sq", c % 2)
                    f.op(f.act, lambda e: e.activation(out=s[:, 0:n], in_=y[:, c, 0:n], func=AF.Square), reads=[yb], writes=[sqb])
                    f.op(f.pe, lambda e: e.matmul(self.ps[0][:, 0:n], lhsT=cf[:, C_ONES:C_ONES + 128], rhs=y[:, c, 0:n],
                                                  start=(c == 0), stop=(c == 7)), reads=[yb, B("constf")], writes=[self.psb[0]], inc=False)
                    f.op(f.pe, lambda e: e.matmul(self.ps[1][:, 0:n], lhsT=cf[:, C_ONES:C_ONES + 128], rhs=s[:, 0:n],
                                                  start=(c == 0), stop=(c == 7)), reads=[sqb, B("constf")], writes=[self.psb[1]])
                f.op(f.act, lambda e: e.activation(out=mean[:, 0:n], in_=self.ps[0][:, 0:n], func=AF.Copy, scale=1.0 / D),
                     reads=[self.psb[0]], writes=[B("c_mean")])
                f.op(f.dve, lambda e: e.tensor_tensor(out=msq[:, 0:n], in0=mean[:, 0:n], in1=mean[:, 0:n], op=ALU.mult),
                     reads=[B("c_mean")], writes=[B("c_msq")])
                f.op(f.dve, lambda e: e.scalar_tensor_tensor(out=rstd[:, 0:n], in0=self.ps[1][:, 0:n], scalar=1.0 / D,
                                                             in1=msq[:, 0:n], op0=ALU.mult, op1=ALU.subtract),
                     reads=[self.psb[1], B("c_msq")], writes=[B("c_rstd")])
                self.rstd_from_ps(rstd[:, 0:n], B("c_rstd"), rstd[:, 0:n], B("c_rstd"), 1.0, LN_EPS)
                for c in range(8):
                    t = tt[c % 2]
                    tb = B("c_t", c % 2)
                    f.op(f.dve, lambda e: e.tensor_tensor(out=t[:, 0:n], in0=y[:, c, 0:n], in1=mean[:, 0:n], op=ALU.subtract),
                         reads=[B("c_y", c), B("c_mean")], writes=[tb])
                    f.op(f.pool, lambda e: e.tensor_tensor(out=t[:, 0:n], in0=t[:, 0:n], in1=rstd[:, 0:n], op=ALU.mult),
                         reads=[tb, B("c_rstd")], writes=[tb])
                    o = ost[c % 2]
                    ob = B("c_o", c % 2)
                    f.op(f.act, lambda e: e.activation(out=o[:, 0:n], in_=t[:, 0:n], func=AF.Silu,
                                                       bias=self.V(l, "conv_ln_b", c), scale=self.V(l, "conv_ln_g", c)),
                         reads=[tb, B("vecs")], writes=[ob])
                    f.dma(f.sp, cv[:, c, t0:t0 + n], o[:, 0:n], reads=[ob])
            f.barrier()

    def post_residual(self, es, mo, mob, xt, xtb, base, j, dstv, c0, n, sq, rs):
        f = self.f
        B = f.buf
        cf = self.constf
        for k in range(8):
            s = sq[k % 2]
            sqb = B("pr_sq", k % 2)
            f.op(f.act, lambda e: e.activation(out=s[:, 0:n], in_=mo[:, k, 0:n], func=AF.Square), reads=[mob], writes=[sqb])
            f.op(f.pe, lambda e: e.matmul(self.ps[7][:, 0:n], lhsT=cf[:, C_ONES:C_ONES + 128], rhs=s[:, 0:n],
                                          start=(k == 0), stop=(k == 7)), reads=[sqb, B("constf")], writes=[self.psb[7]])
        self.rstd_from_ps(self.ps[7][:, 0:n], self.psb[7], rs[:, 0:n], B("pr_rs"), 1.0 / D, EPS)
        for k in range(8):
            f.op(f.pool, lambda e: e.tensor_tensor(out=mo[:, k, 0:n], in0=mo[:, k, 0:n], in1=rs[:, 0:n], op=ALU.mult),
                 reads=[mob, B("pr_rs")], writes=[mob])
            f.op(f.dve, lambda e: e.scalar_tensor_tensor(out=xt[:, k, 0:n], in0=mo[:, k, 0:n],
                                                         scalar=self.dsc[:, base + k, j:j + 1], in1=xt[:, k, 0:n],
                                                         op0=ALU.mult, op1=ALU.add),
                 reads=[mob, xtb, B("dsc")], writes=[xtb])
        f.dma(f.sp, dstv[:, :, c0:c0 + n], xt[:, :, 0:n], reads=[xtb])

    def phase_merge(self, l, xs_in, xsm, last):
        nc, f = self.nc, self.f
        B = f.buf
        I, S = self.I, self.S
        pv = self.fm(S["proj"])
        with contextlib.ExitStack() as es:
            sb = lambda n, s, d: es.enter_context(self.sbuf(n, s, d))
            W = [sb("m_w%d" % i, [128, 8, 1024], BF16) for i in range(4)]
            for i, nm in enumerate(("w_attn_o", "w_conv_o", "w_rwkv_o", "w_out")):
                wv = I[nm][l].rearrange("(k p) n -> p k n", p=128)
                for k in range(8):
                    f.dma(f.pool, W[i][:, k, :], wv[:, k, :], writes=[B("m_w", i, k)])
            br = [sb("m_br%d" % i, [128, 8, 512], BF16) for i in range(3)]
            gt = [sb("m_g%d" % i, [128, 512], F32) for i in range(6)]
            tt = [sb("m_t%d" % i, [128, 512], F32) for i in range(6)]
            mT = sb("m_mT", [128, 8, 512], BF16)
            mo = sb("m_mo", [128, 8, 512], F32)
            xt = sb("m_xt", [128, 8, 512], F32)
            sq = [sb("m_sq%d" % i, [128, 512], F32) for i in range(2)]
            rs = sb("m_rs", [128, 512], F32)
            srcs = [self.fm(S["att"]), self.fm(S["cnv"]), self.fm(S["rwo"])]
            xv = self.fm(xs_in)
            dv = self.fm(xsm)
            for (t0, n, s0, sl) in self.tiles(self.seqs(last)):
                j = 1 if s0 == 0 else 0
                for b in range(3):
                    f.dma(f.sp, br[b][:, :, 0:n], srcs[b][:, :, t0:t0 + n], writes=[B("m_br", b)])
                f.dma(f.sp, xt[:, :, 0:n], xv[:, :, t0:t0 + n], writes=[B("m_xt")])
                for oc in range(8):
                    par = oc % 2
                    for b in range(3):
                        pi = b + 3 * par
                        for k in range(8):
                            f.op(f.pe, lambda e: e.matmul(self.ps[pi][:, 0:n], lhsT=W[b][:, k, oc * 128:(oc + 1) * 128],
                                                          rhs=br[b][:, k, 0:n], start=(k == 0), stop=(k == 7)),
                                 reads=[B("m_w", b, k), B("m_br", b)], writes=[self.psb[pi]], inc=(k == 7))
                    for b in range(3):
                        pi = b + 3 * par
                        g = gt[pi]
                        gb = B("m_g", pi)
                        f.dma(f.sp, g[:, 0:n], pv[:, 55 + 8 * b + oc, t0:t0 + n], writes=[gb])
                        f.op(f.act, lambda e: e.activation(out=g[:, 0:n], in_=g[:, 0:n], func=AF.Sigmoid), reads=[gb], writes=[gb])
                        f.op(f.dve, lambda e: e.tensor_tensor(out=tt[pi][:, 0:n], in0=self.ps[pi][:, 0:n], in1=g[:, 0:n], op=ALU.mult),
                             reads=[self.psb[pi], gb], writes=[B("m_t", pi)])
                    p0 = 3 * par
                    f.op(f.pool, lambda e: e.tensor_tensor(out=tt[p0][:, 0:n], in0=tt[p0][:, 0:n], in1=tt[p0 + 1][:, 0:n], op=ALU.add),
                         reads=[B("m_t", p0), B("m_t", p0 + 1)], writes=[B("m_t", p0)])
                    f.op(f.pool, lambda e: e.tensor_tensor(out=mT[:, oc, 0:n], in0=tt[p0][:, 0:n], in1=tt[p0 + 2][:, 0:n], op=ALU.add),
                         reads=[B("m_t", p0), B("m_t", p0 + 2)], writes=[B("m_mT")])
                for oc in range(8):
                    pi = 6
                    for k in range(8):
                        f.op(f.pe, lambda e: e.matmul(self.ps[pi][:, 0:n], lhsT=W[3][:, k, oc * 128:(oc + 1) * 128],
                                                      rhs=mT[:, k, 0:n], start=(k == 0), stop=(k == 7)),
                             reads=[B("m_w", 3, k), B("m_mT")], writes=[self.psb[pi]], inc=(k == 7))
                    f.op(f.act, lambda e: e.copy(out=mo[:, oc, 0:n], in_=self.ps[pi][:, 0:n]), reads=[self.psb[pi]], writes=[B("m_mo")])
                self.post_residual(es, mo, B("m_mo"), xt, B("m_xt"), 16, j, dv, t0, n, sq, rs)
            f.barrier()

    def phase_ffn_up(self, l, xsm, last):
        nc, f = self.nc, self.f
        B = f.buf
        I, S = self.I, self.S
        fv = self.fm(S["ffa"])
        TP = T + 4
        hcol = lambda t: (t + 1) if t < CTX else (t + 3)
        with contextlib.ExitStack() as es:
            sb = lambda n, s, d: es.enter_context(self.sbuf(n, s, d))
            hT = sb("hT", [128, 8, TP], BF16)
            for c in (0, CTX + 1, CTX + 2, TP - 1):
                f.op(f.pool, lambda e: e.memset(hT[:, :, c:c + 1], 0.0), writes=[B("hT")])
            with contextlib.ExitStack() as es2:
                self.prenorm(es2, xsm, self.seqs(last), 24, hT, hcol)
                f.barrier()
            GS = 4
            wt = [sb("fu_w%d" % i, [128, 8, 2, GS * 128], BF16) for i in range(2)]
            cg = [sb("fu_cg%d" % i, [128, 512], F32) for i in range(2)]
            cv = [sb("fu_cv%d" % i, [128, 512], F32) for i in range(2)]
            ao = [sb("fu_a%d" % i, [128, 512], BF16) for i in range(2)]
            wv = I["w_ffn_up"][l].rearrange("(k p) n -> p k n", p=128)
            it = 0
            for gi, j0 in enumerate(range(0, 22, GS)):
                gs = min(GS, 22 - j0)
                w = wt[gi % 2]
                for k in range(8):
                    f.dma(f.pool, w[:, k, 0, 0:gs * 128], wv[:, k, j0 * 128:(j0 + gs) * 128], writes=[B("fu_w", gi % 2, k, 0)])
                    f.dma(f.pool, w[:, k, 1, 0:gs * 128], wv[:, k, DFF + j0 * 128:DFF + (j0 + gs) * 128],
                          writes=[B("fu_w", gi % 2, k, 1)])
                for (t0, n, s0, sl) in self.tiles(self.seqs(last), 510):
                    c0 = hcol(t0)
                    for jj in range(gs):
                        jc = j0 + jj
                        par = it % 3
                        for hv in range(2):
                            pi = 2 * par + hv
                            for k in range(8):
                                f.op(f.pe, lambda e: e.matmul(self.ps[pi][:, 0:n + 2], lhsT=w[:, k, hv, jj * 128:(jj + 1) * 128],
                                                              rhs=hT[:, k, c0 - 1:c0 + n + 1], start=(k == 0), stop=(k == 7)),
                                     reads=[B("fu_w", gi % 2, k, hv), B("hT")], writes=[self.psb[pi]], inc=(k == 7))
                        res = []
                        for hv, dst, nm in ((0, cg[it % 2], "fu_cg"), (1, cv[it % 2], "fu_cv")):
                            pi = 2 * par + hv
                            ch = jc + 22 * hv
                            wc = lambda q: self.V(l, "ffn_conv_w", ch * 3 + q)
                            db = B(nm, it % 2)
                            f.op(f.act, lambda e: e.activation(out=dst[:, 0:n], in_=self.ps[pi][:, 0:n], func=AF.Copy, scale=wc(0)),
                                 reads=[self.psb[pi], B("vecs")], writes=[db])
                            for q in (1, 2):
                                f.op(f.dve, lambda e: e.scalar_tensor_tensor(out=dst[:, 0:n], in0=self.ps[pi][:, q:q + n], scalar=wc(q),
                                                                             in1=dst[:, 0:n], op0=ALU.mult, op1=ALU.add),
                                     reads=[self.psb[pi], db, B("vecs")], writes=[db])
                        g_, v_ = cg[it % 2], cv[it % 2]
                        f.op(f.act, lambda e: e.activation(out=g_[:, 0:n], in_=g_[:, 0:n], func=AF.Silu),
                             reads=[B("fu_cg", it % 2)], writes=[B("fu_cg", it % 2)])
                        a = ao[it % 2]
                        f.op(f.pool, lambda e: e.tensor_tensor(out=a[:, 0:n], in0=g_[:, 0:n], in1=v_[:, 0:n], op=ALU.mult),
                             reads=[B("fu_cg", it % 2), B("fu_cv", it % 2)], writes=[B("fu_a", it % 2)])
                        f.dma(f.sp, fv[:, jc, t0:t0 + n], a[:, 0:n], reads=[B("fu_a", it % 2)])
                        it += 1
            f.barrier()

    def phase_ffn_down(self, l, xsm, xs_out, last):
        nc, f = self.nc, self.f
        B = f.buf
        I, S = self.I, self.S
        fv = self.fm(S["ffa"])
        with contextlib.ExitStack() as es:
            sb = lambda n, s, d: es.enter_context(self.sbuf(n, s, d))
            W = sb("fd_w", [128, 22, 1024], BF16)
            wv = I["w_ffn_down"][l].rearrange("(k p) n -> p k n", p=128)
            for k in range(22):
                f.dma(f.pool, W[:, k, :], wv[:, k, :], writes=[B("fd_w", k)])
            at = [sb("fd_a%d" % i, [128, 22, 512], BF16) for i in range(2)]
            mo = sb("fd_mo", [128, 8, 512], F32)
            xt = sb("fd_xt", [128, 8, 512], F32)
            sq = [sb("fd_sq%d" % i, [128, 512], F32) for i in range(2)]
            rs = sb("fd_rs", [128, 512], F32)
            xv = self.fm(xsm)
            dv = self.fm(xs_out)
            for it, (t0, n, s0, sl) in enumerate(self.tiles(self.seqs(last))):
                j = 1 if s0 == 0 else 0
                a = at[it % 2]
                ab = B("fd_a", it % 2)
                f.dma(f.sp, a[:, :, 0:n], fv[:, :, t0:t0 + n], writes=[ab])
                f.dma(f.sp, xt[:, :, 0:n], xv[:, :, t0:t0 + n], writes=[B("fd_xt")])
                for oc in range(8):
                    pi = oc % 4
                    for k in range(22):
                        f.op(f.pe, lambda e: e.matmul(self.ps[pi][:, 0:n], lhsT=W[:, k, oc * 128:(oc + 1) * 128],
                                                      rhs=a[:, k, 0:n], start=(k == 0), stop=(k == 21)),
                             reads=[B("fd_w", k), ab], writes=[self.psb[pi]], inc=(k == 21))
                    f.op(f.act, lambda e: e.copy(out=mo[:, oc, 0:n], in_=self.ps[pi][:, 0:n]), reads=[self.psb[pi]], writes=[B("fd_mo")])
                c0 = (t0 - CTX) if last else t0
                self.post_residual(es, mo, B("fd_mo"), xt, B("fd_xt"), 40, j, dv, c0, n, sq, rs)
            f.barrier()

    def phase_rwkv_prep(self, l):
        nc, f = self.nc, self.f
        B = f.buf
        I, S = self.I, self.S
        pv = self.fm(S["proj"])
        cf = self.constf
        NT = 256
        with contextlib.ExitStack() as es:
            sb = lambda n, s, d: es.enter_context(self.sbuf(n, s, d))
            dup = sb("rp_dup", [128, D], F32)
            iup = sb("rp_iup", [128, D], F32)
            w0r = sb("rp_w0r", [1, 2048], F32)
            omka = sb("rp_omka", [128, 8], F32)
            f.dma(f.sp, dup[:], I["decay_up"][l, :, :], writes=[B("rp_dup")])
            f.dma(f.sp, iup[:], I["iclr_up"][l, :, :], writes=[B("rp_iup")])
            f.dma(f.sp, w0r[:], I["w0row"][l, :, :], writes=[B("rp_w0r")])
            f.op(f.dve, lambda e: e.tensor_scalar(out=omka[:], in0=self.V(l, "k_a", 0, 8), scalar1=-1.0, scalar2=1.0,
                                                  op0=ALU.mult, op1=ALU.add), reads=[B("vecs")], writes=[B("rp_omka")])
            raw = [sb("rp_raw%d" % i, [128, NT + 2], F32) for i in range(3)]
            rkv = [sb("rp_%s" % nm, [128, 8, NT], F32) for nm in ("r", "k", "v")]
            kk = sb("rp_kk", [128, 8, NT], F32)
            sq = [sb("rp_sq%d" % i, [128, NT], F32) for i in range(2)]
            nrm = [sb("rp_nrm%d" % i, [128, NT], F32) for i in range(2)]
            kap = sb("rp_kap", [128, 8, NT], F32)
            kapn = sb("rp_kapn", [128, 8, NT], F32)
            lw = sb("rp_lw", [128, NT], F32)
            la = sb("rp_la", [128, NT], F32)
            sg = [sb("rp_sg%d" % i, [128, D], F32) for i in range(2)]
            Ein = sb("rp_Ein", [128, 8, NT], F32)
            Eex = sb("rp_Eex", [128, 8, NT], F32)
            Eng_ = sb("rp_Eneg", [128, 8, NT], F32)
            ag = sb("rp_a", [128, 8, NT], F32)
            kd = sb("rp_kd", [128, 8, NT], F32)
            bd = sb("rp_bd", [128, 8, NT], F32)
            k01 = sb("rp_k01", [128, 8, NT], F32)
            outs = [sb("rp_out%d" % i, [128, 8, NT], F32) for i in range(4)]
            gct = sb("rp_gct", [128, 8, 4], F32)
            ir = 0
            for (t0, n, s0, sl) in self.tiles([(0, CTX), (CTX, SEQ)], NT):
                for c in range(24):
                    r = raw[ir % 3]
                    rb = B("rp_raw", ir % 3)
                    ir += 1
                    lo = max(t0 - 1, s0)
                    hi = min(t0 + n + 1, s0 + sl)
                    off = lo - (t0 - 1)
                    if lo != t0 - 1:
                        f.op(f.pool, lambda e: e.memset(r[:, 0:1], 0.0), writes=[rb])
                    if hi != t0 + n + 1:
                        f.op(f.pool, lambda e: e.memset(r[:, n + 1:n + 2], 0.0), writes=[rb])
                    f.dma(f.sp, r[:, off:off + hi - lo], pv[:, 28 + c, lo:hi], writes=[rb])
                    dst = rkv[c // 8]
                    db = B("rp_rkv", c // 8)
                    cc_ = c % 8
                    wc = lambda q: self.V(l, "shift_w", c * 3 + q)
                    f.op(f.act, lambda e: e.activation(out=dst[:, cc_, 0:n], in_=r[:, 0:n], func=AF.Copy, scale=wc(0)),
                         reads=[rb, B("vecs")], writes=[db])
                    for q in (1, 2):
                        f.op(f.dve, lambda e: e.scalar_tensor_tensor(out=dst[:, cc_, 0:n], in0=r[:, q:q + n], scalar=wc(q),
                                                                     in1=dst[:, cc_, 0:n], op0=ALU.mult, op1=ALU.add),
                             reads=[rb, db, B("vecs")], writes=[db])
                R_, K_, V_ = rkv
                f.dma(f.sp, self.fm(S["rw_r"])[:, :, t0:t0 + n], R_[:, :, 0:n], reads=[B("rp_rkv", 0)])
                f.dma(f.sp, self.fm(S["rw_v"])[:, :, t0:t0 + n], V_[:, :, 0:n], reads=[B("rp_rkv", 2)])
                for c in range(8):
                    f.op(f.pool, lambda e: e.tensor_scalar(out=kk[:, c, 0:n], in0=K_[:, c, 0:n], scalar1=self.V(l, "k_k", c),
                                                           scalar2=None, op0=ALU.mult),
                         reads=[B("rp_rkv", 1), B("vecs")], writes=[B("rp_kk", c)])
                    s = sq[c % 2]
                    sqb = B("rp_sq", c % 2)
                    f.op(f.act, lambda e: e.activation(out=s[:, 0:n], in_=kk[:, c, 0:n], func=AF.Square),
                         reads=[B("rp_kk", c)], writes=[sqb])
                    pi = 6 + c % 2
                    f.op(f.pe, lambda e: e.matmul(self.ps[pi][:, 0:n], lhsT=cf[:, C_BLK:C_BLK + 128], rhs=s[:, 0:n],
                                                  start=True, stop=True), reads=[sqb, B("constf")], writes=[self.psb[pi]])
                    nr = nrm[c % 2]
                    nb = B("rp_nrm", c % 2)
                    f.op(f.dve, lambda e: e.tensor_scalar(out=nr[:, 0:n], in0=self.ps[pi][:, 0:n], scalar1=1e-12, scalar2=None,
                                                          op0=ALU.max), reads=[self.psb[pi]], writes=[nb])
                    self.rstd_from_ps(nr[:, 0:n], nb, nr[:, 0:n], nb, 1.0, 0.0)
                    f.op(f.dve, lambda e: e.tensor_tensor(out=kap[:, c, 0:n], in0=kk[:, c, 0:n], in1=nr[:, 0:n], op=ALU.mult),
                         reads=[B("rp_kk", c), nb], writes=[B("rp_kap")])
                f.op(f.act, lambda e: e.mul(out=kapn[:, :, 0:n], in_=kap[:, :, 0:n], mul=-1.0), reads=[B("rp_kap")], writes=[B("rp_kapn")])
                f.dma(f.sp, lw[:, 0:n], pv[:, 52, t0:t0 + n], writes=[B("rp_lw")])
                f.dma(f.sp, la[:, 0:n], pv[:, 53, t0:t0 + n], writes=[B("rp_la")])
                f.op(f.act, lambda e: e.activation(out=lw[:, 0:n], in_=lw[:, 0:n], func=AF.Tanh), reads=[B("rp_lw")], writes=[B("rp_lw")])
                for d in range(2):
                    pr = slice(64 * d, 64 * d + 64)
                    tri = C_TRIF if d == 0 else C_TRIB
                    for jb in range(n // 128):
                        s_ = sg[jb % 2]
                        sgb = B("rp_sg", jb % 2)
                        for fh in range(2):
                            pi = fh
                            f.op(f.pe, lambda e: e.matmul(self.ps[pi][:, 0:512], lhsT=lw[pr, jb * 128:(jb + 1) * 128],
                                                          rhs=dup[pr, fh * 512:(fh + 1) * 512], start=True, stop=False),
                                 reads=[B("rp_lw"), B("rp_dup")], writes=[self.psb[pi]], inc=False)
                            f.op(f.pe, lambda e: e.matmul(self.ps[pi][:, 0:512], lhsT=cf[0:1, C_ONES:C_ONES + 128],
                                                          rhs=w0r[0:1, d * 1024 + fh * 512:d * 1024 + (fh + 1) * 512],
                                                          start=False, stop=True),
                                 reads=[B("rp_w0r"), B("constf")], writes=[self.psb[pi]])
                            f.op(f.act, lambda e: e.activation(out=s_[:, fh * 512:(fh + 1) * 512], in_=self.ps[pi][:, 0:512],
                                                               func=AF.Sigmoid), reads=[self.psb[pi]], writes=[sgb])
                        for c2 in range(4):
                            pi = 2 + c2
                            for h2 in range(2):
                                c = 2 * c2 + h2
                                f.op(f.pe, lambda e: e.matmul(self.ps[pi][:, h2 * 256:(h2 + 1) * 256], lhsT=s_[:, c * 128:(c + 1) * 128],
                                                              rhs=cf[:, tri:tri + 256], start=True, stop=True),
                                     reads=[sgb, B("constf")], writes=[self.psb[pi]], inc=(h2 == 1))
                            pv4 = self.ps[pi][:, 0:512].rearrange("p (c i t) -> p c i t", c=2, i=2)
                            cs = slice(2 * c2, 2 * c2 + 2)
                            ts = slice(jb * 128, (jb + 1) * 128)
                            f.op(f.act, lambda e: e.activation(out=Ein[:, cs, ts], in_=pv4[:, :, 0, :], func=AF.Exp, scale=-DECAY_SCALE),
                                 reads=[self.psb[pi]], writes=[B("rp_Ein")])
                            f.op(f.act, lambda e: e.activation(out=Eex[:, cs, ts], in_=pv4[:, :, 1, :], func=AF.Exp, scale=-DECAY_SCALE),
                                 reads=[self.psb[pi]], writes=[B("rp_Eex")])
                            f.op(f.act, lambda e: e.activation(out=Eng_[:, cs, ts], in_=pv4[:, :, 0, :], func=AF.Exp, scale=DECAY_SCALE),
                                 reads=[self.psb[pi]], writes=[B("rp_Eneg")])
                    for c in range(8):
                        pi = 6 + c % 2
                        f.op(f.pe, lambda e: e.matmul(self.ps[pi][:, 0:n], lhsT=iup[pr, c * 128:(c + 1) * 128], rhs=la[pr, 0:n],
                                                      start=True, stop=True), reads=[B("rp_iup"), B("rp_la")], writes=[self.psb[pi]])
                        f.op(f.act, lambda e: e.activation(out=ag[:, c, 0:n], in_=self.ps[pi][:, 0:n], func=AF.Sigmoid,
                                                           bias=self.V(l, "iclr_a0", d * 8 + c), scale=1.0),
                             reads=[self.psb[pi], B("vecs")], writes=[B("rp_a")])
                        f.op(f.dve, lambda e: e.tensor_scalar(out=kd[:, c, 0:n], in0=ag[:, c, 0:n], scalar1=self.V(l, "k_a", c),
                                                              scalar2=omka[:, c:c + 1], op0=ALU.mult, op1=ALU.add),
                             reads=[B("rp_a"), B("vecs"), B("rp_omka")], writes=[B("rp_kd"), B("rp_kd2", 0), B("rp_kd2", 1)])
                    o_at, o_bt, o_kt, o_rt = outs
                    HS = (slice(0, 4), slice(4, 8))

                    def both(fn, rd, wr):
                        for hi_, eng_ in enumerate((f.pool, f.dve)):
                            f.op(eng_, lambda e: fn(e, HS[hi_]), reads=[r_(hi_) if callable(r_) else r_ for r_ in rd],
                                 writes=[w_(hi_) for w_ in wr])
                    hb = lambda nm: (lambda hi_: B(nm, hi_))
                    both(lambda e, cs_: e.tensor_tensor(out=kd[:, cs_, 0:n], in0=kd[:, cs_, 0:n], in1=K_[:, cs_, 0:n], op=ALU.mult),
                         [B("rp_kd"), B("rp_rkv", 1)], [hb("rp_kd2")])
                    both(lambda e, cs_: e.tensor_tensor(out=bd[:, cs_, 0:n], in0=ag[:, cs_, 0:n], in1=kap[:, cs_, 0:n], op=ALU.mult),
                         [B("rp_a"), B("rp_kap")], [hb("rp_bd")])
                    if d == 0:
                        both(lambda e, cs_: e.tensor_copy(out=k01[:, cs_, 0:n], in_=kd[:, cs_, 0:n]), [hb("rp_kd2")], [hb("rp_k01")])
                    else:
                        both(lambda e, cs_: e.tensor_tensor(out=k01[:, cs_, 0:n], in0=k01[:, cs_, 0:n], in1=kd[:, cs_, 0:n], op=ALU.add),
                             [hb("rp_kd2"), hb("rp_k01")], [hb("rp_k01")])
                    both(lambda e, cs_: e.tensor_tensor(out=o_at[:, cs_, 0:n], in0=kapn[:, cs_, 0:n], in1=Eex[:, cs_, 0:n], op=ALU.mult),
                         [B("rp_kapn"), B("rp_Eex")], [hb("rp_out0")])
                    both(lambda e, cs_: e.tensor_tensor(out=o_bt[:, cs_, 0:n], in0=bd[:, cs_, 0:n], in1=Eng_[:, cs_, 0:n], op=ALU.mult),
                         [hb("rp_bd"), B("rp_Eneg")], [hb("rp_out1")])
                    both(lambda e, cs_: e.tensor_tensor(out=o_kt[:, cs_, 0:n], in0=kd[:, cs_, 0:n], in1=Eng_[:, cs_, 0:n], op=ALU.mult),
                         [hb("rp_kd2"), B("rp_Eneg")], [hb("rp_out2")])
                    both(lambda e, cs_: e.tensor_tensor(out=o_rt[:, cs_, 0:n], in0=R_[:, cs_, 0:n], in1=Ein[:, cs_, 0:n], op=ALU.mult),
                         [B("rp_rkv", 0), B("rp_Ein")], [hb("rp_out3")])
                    for i_, nm in enumerate(("At", "Bt", "Kt", "Rt")):
                        f.dma(f.sp, self.fm(S["%s%d" % (nm, d)])[:, :, t0:t0 + n], outs[i_][:, :, 0:n], reads=[B("rp_out%d" % i_, 0), B("rp_out%d" % i_, 1)])
                    col0 = 63 if d == 0 else 0
                    nch = n // 64
                    f.op(f.act, lambda e: e.copy(out=gct[:, :, 0:nch], in_=Ein[:, :, col0:n:64]), reads=[B("rp_Ein")], writes=[B("rp_gct")])
                    f.dma(f.sp, self.fm(S["gC%d" % d])[:, :, t0 // 64:t0 // 64 + nch], gct[:, :, 0:nch], reads=[B("rp_gct")])
                f.dma(f.sp, self.fm(S["rw_k01"])[:, :, t0:t0 + n], k01[:, :, 0:n], reads=[B("rp_k01", 0), B("rp_k01", 1)])
            f.barrier()

    def phase_rwkv_scan(self, l, last):
        nc, f = self.nc, self.f
        B = f.buf
        S = self.S
        cf = self.constf
        mk = self.maskf
        with contextlib.ExitStack() as es:
            sb = lambda n, s, d: es.enter_context(self.sbuf(n, s, d))
            ST = [sb("sc_ST%d" % d, [128, 8, 64], F32) for d in range(2)]
            gC = [sb("sc_gC%d" % d, [128, 8, NCHUNK], F32) for d in range(2)]
            names = ("At", "Bt", "Kt", "Rt", "V")
            inp = [[[sb("sc_%s%d_%d" % (nm, d, i), [128, 8, 128], F32) for nm in names] for i in range(2)] for d in range(2)]
            def mk64(nm, k=1):
                return [[[sb("sc_%s%d%d_%d" % (nm, d, h, i), [128, 256], F32) for i in range(k)] for h in range(2)] for d in range(2)]
            Xb = mk64("X", 2)
            XTb = mk64("XT", 2)
            Pb = mk64("P", 6)
            Lb = mk64("L", 3)
            Tk = mk64("Tk", 3)
            Wb = mk64("W", 2)
            Yo = mk64("Yo", 1)
            for d in range(2):
                f.op(f.pool, lambda e: e.memset(ST[d][:], 0.0), writes=[B("ST", d, 0), B("ST", d, 1)])
                f.dma(f.sp, gC[d][:], self.fm(S["gC%d" % d])[:, :, :], writes=[B("sc_gC", d)])
            order = [list(range(NCHUNK)), [3, 2, 1, 0] + list(range(NCHUNK - 1, 3, -1))]
            srcs = [[self.fm(S["%s%d" % (nm, d)]) for nm in ("At", "Bt", "Kt", "Rt")] + [self.fm(S["rw_v"])] for d in range(2)]
            yv = [self.fm(S["yf"]), self.fm(S["yb"])]
            self._psr = 0
            self._cp = 0
            cur_tile = [None, None]
            nload = [0, 0]

            tile_buf = [dict(), dict()]

            def ensure(d, tl):
                if tl in tile_buf[d]:
                    return
                nload[d] += 1
                bi = nload[d] % 2
                tile_buf[d][tl] = bi
                for i_, nm in enumerate(names):
                    f.dma(f.sp, inp[d][bi][i_][:], srcs[d][i_][:, :, tl * 128:(tl + 1) * 128], writes=[B("sc_in", d, bi, i_)])

            def nps():
                i = self._psr % 8
                self._psr += 1
                return self.ps[i], self.psb[i]

            def evac(out_ap, in_ap, rd, wr):
                self._cp += 1
                if self._cp % 3 == 0:
                    f.op(f.dve, lambda e: e.tensor_copy(out=out_ap, in_=in_ap), reads=rd, writes=wr)
                else:
                    f.op(f.act, lambda e: e.copy(out=out_ap, in_=in_ap), reads=rd, writes=wr)

            def group(d, ch, half):
                tl, cc = ch // 2, ch % 2
                cs = slice(64 * cc, 64 * cc + 64)
                ensure(d, tl)
                bi = tile_buf[d][tl]
                A_, Bm, Km, R_, Vv = inp[d][bi]
                bA, bB, bK, bR, bV = [B("sc_in", d, bi, i_) for i_ in range(5)]
                heads = [(4 * half + hpi, hh) for hpi in range(4) for hh in range(2)]
                fm_ = lambda T_, hp, hh: T_[64 * hh:64 * hh + 64, hp, cs]
                O = lambda T_, g8: T_[64 * (g8 % 2):64 * (g8 % 2) + 64, (g8 // 2) * 64:(g8 // 2) * 64 + 64]
                stb = B("ST", d, half)
                cst = B("constf")
                mkb = B("maskf")
                if d == 0:
                    mX, mXT, mL = M_LOS, M_UPS, M_UPI
                else:
                    mX, mXT, mL = M_UPS, M_LOS, M_LOI
                Vt_, Bt_, Kt_ = Tk[d][half]
                dh = (d, half)
                for j_, (src, sbuf_, dst) in enumerate(((Vv, bV, Vt_), (Bm, bB, Bt_), (Km, bK, Kt_))):
                    ps, pb = nps()
                    for g8, (hp, hh) in enumerate(heads):
                        f.op(f.pe, lambda e: e.matmul(O(ps, g8), lhsT=fm_(src, hp, hh),
                                                      rhs=cf[64 * hh:64 * hh + 64, C_ID + 64 * hh:C_ID + 64 * hh + 64], start=True, stop=True),
                             reads=[sbuf_, cst], writes=[pb], inc=(g8 == 7))
                    evac(dst[:, :], ps[:, 0:256], [pb], [B("sc_Tk", dh, j_)])
                    yield
                bVt, bBt, bKt = [B("sc_Tk", dh, j_) for j_ in range(3)]

                def mm8(out_rows, fn_l, fn_r, rd):
                    ps, pb = nps()
                    for g8, (hp, hh) in enumerate(heads):
                        f.op(f.pe, lambda e: e.matmul(O(ps, g8), lhsT=fn_l(g8, hp, hh), rhs=fn_r(g8, hp, hh), start=True, stop=True),
                             reads=rd, writes=[pb], inc=(g8 == 7))
                    return ps, pb

                def masked(ps, pb, dst, db, mcol):
                    f.op(f.dve, lambda e: e.tensor_tensor(out=dst[:, :], in0=ps[:, 0:256], in1=mk[:, mcol:mcol + 256], op=ALU.mult),
                         reads=[pb, mkb], writes=[db])
                X = Xb[d][half]
                XT = XTb[d][half]
                P = Pb[d][half]
                bX = [B("sc_X", dh, i_) for i_ in range(2)]
                bXT = [B("sc_XT", dh, i_) for i_ in range(2)]
                bP = [B("sc_P", dh, i_) for i_ in range(6)]
                bL = [B("sc_L", dh, i_) for i_ in range(3)]
                LakT, LrbT, LrkT = Lb[d][half]
                ps, pb = mm8(0, lambda g, hp, hh: fm_(A_, hp, hh), lambda g, hp, hh: fm_(Bm, hp, hh), [bA, bB])
                masked(ps, pb, X[0], bX[0], mX)
                yield
                ps, pb = mm8(0, lambda g, hp, hh: fm_(Bm, hp, hh), lambda g, hp, hh: fm_(A_, hp, hh), [bA, bB])
                masked(ps, pb, XT[0], bXT[0], mXT)
                f.op(f.pool, lambda e: e.tensor_tensor(out=P[0][:, :], in0=XT[0][:, :], in1=mk[:, M_ID:M_ID + 256], op=ALU.add),
                     reads=[bXT[0], mkb], writes=[bP[0]])
                yield
                ps, pb = mm8(0, lambda g, hp, hh: fm_(Km, hp, hh), lambda g, hp, hh: fm_(A_, hp, hh), [bA, bK])
                masked(ps, pb, LakT, bL[0], mXT)
                yield
                ps, pb = mm8(0, lambda g, hp, hh: fm_(Bm, hp, hh), lambda g, hp, hh: fm_(R_, hp, hh), [bR, bB])
                masked(ps, pb, LrbT, bL[1], mL)
                yield
                ps, pb = mm8(0, lambda g, hp, hh: fm_(Km, hp, hh), lambda g, hp, hh: fm_(R_, hp, hh), [bR, bK])
                masked(ps, pb, LrkT, bL[2], mL)
                yield
                for i_ in range(1, 6):
                    p_, c_ = (i_ - 1) % 2, i_ % 2
                    Xp, XTp = X[p_], XT[p_]
                    if i_ <= 4:
                        ps, pb = mm8(0, lambda g, hp, hh: O(XTp, g), lambda g, hp, hh: O(Xp, g), [bX[p_], bXT[p_]])
                        evac(X[c_][:, :], ps[:, 0:256], [pb], [bX[c_]])
                        yield
                    ps, pb = mm8(0, lambda g, hp, hh: O(Xp, g), lambda g, hp, hh: O(XTp, g), [bX[p_], bXT[p_]])
                    evac(XT[c_][:, :], ps[:, 0:256], [pb], [bXT[c_]])
                    f.op(f.pool, lambda e: e.tensor_tensor(out=P[i_][:, :], in0=XT[c_][:, :], in1=mk[:, M_ID:M_ID + 256], op=ALU.add),
                         reads=[bXT[c_], mkb], writes=[bP[i_]])
                    yield
                W = Wb[d][half]
                bW = [B("sc_W", dh, i_) for i_ in range(2)]
                ps, pb = nps()
                for g8, (hp, hh) in enumerate(heads):
                    f.op(f.pe, lambda e: e.matmul(O(ps, g8), lhsT=fm_(A_, hp, hh), rhs=ST[d][64 * hh:64 * hh + 64, hp, :],
                                                  start=True, stop=False), reads=[bA, stb], writes=[pb], inc=False)
                    f.op(f.pe, lambda e: e.matmul(O(ps, g8), lhsT=O(LakT, g8), rhs=O(Vt_, g8),
                                                  start=False, stop=True), reads=[bL[0], bVt], writes=[pb], inc=(g8 == 7))
                evac(W[0][:, :], ps[:, 0:256], [pb], [bW[0]])
                yield
                wi = 0
                for i_ in range(5, -1, -1):
                    Wc = W[wi]
                    ps, pb = mm8(0, lambda g, hp, hh: O(P[i_], g), lambda g, hp, hh: O(Wc, g), [bP[i_], bW[wi]])
                    wi ^= 1
                    evac(W[wi][:, :], ps[:, 0:256], [pb], [bW[wi]])
                    yield
                U = W[wi]
                bU = bW[wi]
                ps, pb = nps()
                for g8, (hp, hh) in enumerate(heads):
                    f.op(f.pe, lambda e: e.matmul(O(ps, g8), lhsT=ST[d][64 * hh:64 * hh + 64, hp, :], rhs=fm_(R_, hp, hh),
                                                  start=True, stop=False), reads=[bR, stb], writes=[pb], inc=False)
                    f.op(f.pe, lambda e: e.matmul(O(ps, g8), lhsT=O(U, g8), rhs=O(LrbT, g8),
                                                  start=False, stop=False), reads=[bU, bL[1]], writes=[pb], inc=False)
                    f.op(f.pe, lambda e: e.matmul(O(ps, g8), lhsT=O(Vt_, g8), rhs=O(LrkT, g8),
                                                  start=False, stop=True), reads=[bVt, bL[2]], writes=[pb], inc=(g8 == 7))
                yo = Yo[d][half][0]
                evac(yo[:, :], ps[:, 0:256], [pb], [B("sc_Yo", dh)])
                f.dma(f.sp, yv[d][:, 4 * half:4 * half + 4, ch * 64:(ch + 1) * 64], yo[:, :].rearrange("p (g t) -> p g t", t=64),
                      reads=[B("sc_Yo", dh)])
                yield
                ps, pb = nps()
                for g8, (hp, hh) in enumerate(heads):
                    o_ = O(ps, g8)
                    f.op(f.pe, lambda e: e.matmul(o_, lhsT=O(Bt_, g8), rhs=O(U, g8), start=True, stop=False),
                         reads=[bBt, bU], writes=[pb], inc=False)
                    f.op(f.pe, lambda e: e.matmul(o_, lhsT=O(Kt_, g8), rhs=O(Vt_, g8), start=False, stop=False),
                         reads=[bKt, bVt], writes=[pb], inc=False)
                    f.op(f.pe, lambda e: e.matmul(o_, lhsT=cf[64 * hh:64 * hh + 64, C_ID + 64 * hh:C_ID + 64 * hh + 64],
                                                  rhs=ST[d][64 * hh:64 * hh + 64, hp, :], start=False, stop=True),
                         reads=[cst, stb], writes=[pb], inc=(g8 == 7))
                for hpi in range(4):
                    hp = 4 * half + hpi
                    f.op(f.act, lambda e: e.activation(out=ST[d][:, hp, :], in_=ps[:, hpi * 64:(hpi + 1) * 64], func=AF.Copy,
                                                       scale=gC[d][:, hp, ch:ch + 1]),
                         reads=[pb, B("sc_gC", d)], writes=[stb])

            for s_ in range(getattr(self, "scan_steps", NCHUNK)):
                gens = [group(d, order[d][s_], half) for half in range(2) for d in range(2)]
                rnd_ = 0
                while gens:
                    rnd_ += 1
                    if rnd_ == 2 and s_ + 1 < NCHUNK:
                        for d in range(2):
                            ensure(d, order[d][s_ + 1] // 2)
                    alive = []
                    for g_ in gens:
                        try:
                            next(g_)
                            alive.append(g_)
                        except StopIteration:
                            pass
                    gens = alive
            f.barrier()

    def phase_rwkv_out(self, l, last):
        nc, f = self.nc, self.f
        B = f.buf
        I, S = self.I, self.S
        pv = self.fm(S["proj"])
        cf = self.constf
        with contextlib.ExitStack() as es:
            sb = lambda n, s, d: es.enter_context(self.sbuf(n, s, d))
            gup = sb("ro_gup", [128, D], F32)
            f.dma(f.sp, gup[:], I["gate_up"][l, :, :], writes=[B("ro_gup")])
            lg = sb("ro_lg", [128, 512], F32)
            tl = {}
            for nm in ("yf", "yb", "r", "k01", "v"):
                tl[nm] = [sb("ro_%s%d" % (nm, i), [128, 512], F32) for i in range(4)]
            sq = [sb("ro_sq%d" % i, [128, 512], F32) for i in range(4)]
            mean = [sb("ro_mean%d" % i, [128, 512], F32) for i in range(4)]
            var = [sb("ro_var%d" % i, [128, 512], F32) for i in range(4)]
            tt = [sb("ro_t%d" % i, [128, 512], F32) for i in range(4)]
            bon = [sb("ro_bon%d" % i, [128, 512], F32) for i in range(4)]
            ost = [sb("ro_o%d" % i, [128, 512], BF16) for i in range(4)]
            srcv = {"yf": self.fm(S["yf"]), "yb": self.fm(S["yb"]), "r": self.fm(S["rw_r"]), "k01": self.fm(S["rw_k01"]),
                    "v": self.fm(S["rw_v"])}
            ov = self.fm(S["rwo"])
            lgs = [lg, sb("ro_lg1", [128, 512], F32)]

            def unit(ti, t0, n, c, p):
                lg_ = lgs[ti % 2]
                lgb = B("ro_lg", ti % 2)
                if c == 0:
                    f.dma(f.sp, lg_[:, 0:n], pv[:, 54, t0:t0 + n], writes=[lgb])
                    f.op(f.act, lambda e: e.activation(out=lg_[:, 0:n], in_=lg_[:, 0:n], func=AF.Sigmoid), reads=[lgb], writes=[lgb])
                    yield
                bb = {nm: B("ro_" + nm, p) for nm in tl}
                for nm in tl:
                    f.dma(f.sp, tl[nm][p][:, 0:n], srcv[nm][:, c, t0:t0 + n], writes=[bb[nm]])
                yield
                y = tl["yf"][p]
                f.op(f.pool, lambda e: e.tensor_tensor(out=y[:, 0:n], in0=y[:, 0:n], in1=tl["yb"][p][:, 0:n], op=ALU.add),
                     reads=[bb["yf"], bb["yb"]], writes=[bb["yf"]])
                yield
                f.op(f.act, lambda e: e.activation(out=sq[p][:, 0:n], in_=y[:, 0:n], func=AF.Square), reads=[bb["yf"]], writes=[B("ro_sq", p)])
                yield
                r_ = tl["r"][p]
                f.op(f.dve, lambda e: e.scalar_tensor_tensor(out=r_[:, 0:n], in0=r_[:, 0:n], scalar=self.V(l, "r_k", c),
                                                             in1=tl["k01"][p][:, 0:n], op0=ALU.mult, op1=ALU.mult),
                     reads=[bb["r"], bb["k01"], B("vecs")], writes=[bb["r"]])
                yield
                blk = cf[:, C_BLK:C_BLK + 128]
                PQ = [self.ps[2 * p][:, 0:256], self.ps[2 * p][:, 256:512], self.ps[2 * p + 1][:, 0:256], self.ps[2 * p + 1][:, 256:512]]
                PB = [self.psb[2 * p], self.psb[2 * p], self.psb[2 * p + 1], self.psb[2 * p + 1]]
                f.op(f.pe, lambda e: e.matmul(PQ[0][:, 0:n], lhsT=blk, rhs=y[:, 0:n], start=True, stop=True),
                     reads=[bb["yf"], B("constf")], writes=[PB[0]])
                yield
                f.op(f.pe, lambda e: e.matmul(PQ[1][:, 0:n], lhsT=blk, rhs=sq[p][:, 0:n], start=True, stop=True),
                     reads=[B("ro_sq", p), B("constf")], writes=[PB[1]])
                yield
                f.op(f.pe, lambda e: e.matmul(PQ[2][:, 0:n], lhsT=blk, rhs=r_[:, 0:n], start=True, stop=True),
                     reads=[bb["r"], B("constf")], writes=[PB[2]])
                yield
                f.op(f.pe, lambda e: e.matmul(PQ[3][:, 0:n], lhsT=gup[:, c * 128:(c + 1) * 128], rhs=lg_[:, 0:n], start=True, stop=True),
                     reads=[lgb, B("ro_gup")], writes=[PB[3]])
                yield
                m_, v_, t_ = mean[p], var[p], tt[p]
                f.op(f.act, lambda e: e.activation(out=m_[:, 0:n], in_=PQ[0][:, 0:n], func=AF.Copy, scale=1.0 / 64),
                     reads=[PB[0]], writes=[B("ro_mean", p)])
                yield
                f.op(f.dve, lambda e: e.tensor_tensor(out=v_[:, 0:n], in0=m_[:, 0:n], in1=m_[:, 0:n], op=ALU.mult),
                     reads=[B("ro_mean", p)], writes=[B("ro_var", p)])
                yield
                f.op(f.dve, lambda e: e.scalar_tensor_tensor(out=v_[:, 0:n], in0=PQ[1][:, 0:n], scalar=1.0 / 64, in1=v_[:, 0:n],
                                                             op0=ALU.mult, op1=ALU.subtract),
                     reads=[PB[1], B("ro_var", p)], writes=[B("ro_var", p)])
                yield
                self.rstd_from_ps(v_[:, 0:n], B("ro_var", p), v_[:, 0:n], B("ro_var", p), 1.0, GN_EPS)
                yield
                f.op(f.dve, lambda e: e.tensor_tensor(out=t_[:, 0:n], in0=y[:, 0:n], in1=m_[:, 0:n], op=ALU.subtract),
                     reads=[bb["yf"], B("ro_mean", p)], writes=[B("ro_t", p)])
                yield
                f.op(f.pool, lambda e: e.tensor_tensor(out=t_[:, 0:n], in0=t_[:, 0:n], in1=v_[:, 0:n], op=ALU.mult),
                     reads=[B("ro_t", p), B("ro_var", p)], writes=[B("ro_t", p)])
                yield
                f.op(f.act, lambda e: e.activation(out=t_[:, 0:n], in_=t_[:, 0:n], func=AF.Identity,
                                                   bias=self.V(l, "gn_b", c), scale=self.V(l, "gn_g", c)),
                     reads=[B("ro_t", p), B("vecs")], writes=[B("ro_t", p)])
                yield
                f.op(f.dve, lambda e: e.tensor_tensor(out=bon[p][:, 0:n], in0=PQ[2][:, 0:n], in1=tl["v"][p][:, 0:n], op=ALU.mult),
                     reads=[PB[2], bb["v"]], writes=[B("ro_bon", p)])
                yield
                f.op(f.pool, lambda e: e.tensor_tensor(out=t_[:, 0:n], in0=t_[:, 0:n], in1=bon[p][:, 0:n], op=ALU.add),
                     reads=[B("ro_t", p), B("ro_bon", p)], writes=[B("ro_t", p)])
                yield
                f.op(f.dve, lambda e: e.tensor_tensor(out=ost[p][:, 0:n], in0=PQ[3][:, 0:n], in1=t_[:, 0:n], op=ALU.mult),
                     reads=[PB[3], B("ro_t", p)], writes=[B("ro_o", p)])
                yield
                f.dma(f.sp, ov[:, c, t0:t0 + n], ost[p][:, 0:n], reads=[B("ro_o", p)])
                yield

            units = [(ti, t0, n, c) for ti, (t0, n, s0, sl) in enumerate(self.tiles(self.seqs(last), 256)) for c in range(8)]
            active = []
            nxt = 0
            free = [0]
            rnd = 0
            while nxt < len(units) or active:
                rnd += 1
                if rnd in (6, 11, 16):
                    free.append(rnd // 5)
                while nxt < len(units) and free:
                    p_ = free.pop()
                    active.append((unit(*units[nxt], p_), p_))
                    nxt += 1
                still = []
                for g_, p_ in active:
                    try:
                        next(g_)
                        still.append((g_, p_))
                    except StopIteration:
                        free.append(p_)
                active = still
            f.barrier()


def _fm(v):
    v = np.asarray(v, np.float32).reshape(-1, 128)
    return np.ascontiguousarray(v.T)


def _consts():
    c = np.zeros((128, NCONST), np.float32)
    c[:, C_ONES:C_ONES + 128] = 1.0
    for h in range(2):
        c[64 * h:64 * h + 64, C_BLK + 64 * h:C_BLK + 64 * h + 64] = 1.0
    c[:, C_ID:C_ID + 128] = np.eye(128, dtype=np.float32)
    P = np.zeros((128, 128), np.float32)
    for m in range(128):
        if m % 64 < 32:
            P[m, m + 32] = -1.0
        else:
            P[m, m - 32] = 1.0
    c[:, C_ROT:C_ROT + 128] = P.T
    s = np.arange(128)[:, None]
    t = np.arange(128)[None, :]
    same = (s // 64) == (t // 64)
    c[:, C_TRIF:C_TRIF + 128] = (same & (s <= t))
    c[:, C_TRIF + 128:C_TRIF + 256] = (same & (s < t))
    c[:, C_TRIB:C_TRIB + 128] = (same & (s >= t))
    c[:, C_TRIB + 128:C_TRIB + 256] = (same & (s > t))
    r = np.arange(64)[:, None]
    q = np.arange(64)[None, :]
    m = np.zeros((128, 5 * 256), np.float32)
    m[:, M_LOS:M_LOS + 256] = np.tile((q < r).astype(np.float32), (2, 4))
    m[:, M_UPS:M_UPS + 256] = np.tile((q > r).astype(np.float32), (2, 4))
    m[:, M_LOI:M_LOI + 256] = np.tile((q <= r).astype(np.float32), (2, 4))
    m[:, M_UPI:M_UPI + 256] = np.tile((q >= r).astype(np.float32), (2, 4))
    m[:, M_ID:M_ID + 256] = np.tile(np.eye(64, dtype=np.float32), (2, 4))
    tt = np.arange(SEQ)
    row = (tt // 64).astype(np.float32)
    col = (tt % 64).astype(np.float32)
    inv = (10000.0 ** (-np.arange(32, dtype=np.float32) / 32)).astype(np.float32)
    cos = np.zeros((128, SEQ), np.float32)
    sin = np.zeros((128, SEQ), np.float32)
    for p in range(128):
        pos = row if p < 64 else col
        ang = (pos * inv[p % 32]).astype(np.float32)
        cos[p] = np.cos(ang)
        sin[p] = np.sin(ang)
    return c, m, cos, sin


def prep_inputs(inp):
    g = lambda k: np.asarray(inp[k], np.float32)
    vecs = np.zeros((DEPTH, 128, NV), np.float32)
    for l in range(DEPTH):
        def put(name, arr):
            a = np.asarray(arr, np.float32)
            vecs[l, :, VCOL[name]:VCOL[name] + a.shape[1]] = a
        put("b_mod", _fm(g("b_mod")[l]))
        for n in ("g_pre_mix", "g_post_mix", "g_pre_ffn", "g_post_ffn", "q_norm", "k_norm", "conv_b",
                  "k_k", "k_a"):
            put(n, _fm(g(n)[l]))
        put("conv_ln_g", _fm(g("conv_ln_g")[l]))
        put("conv_ln_b", _fm(g("conv_ln_b")[l]))
        put("gn_g", _fm(g("wkv_gn_g")[l]))
        put("gn_b", _fm(g("wkv_gn_b")[l]))
        put("r_k", _fm(g("r_k")[l].reshape(-1)))
        cw = g("conv_w")[l].reshape(31, 8, 128).transpose(2, 1, 0).reshape(128, 248)
        put("conv_w", cw)
        sw = g("shift_w")[l].reshape(3, 24, 128).transpose(2, 1, 0).reshape(128, 72)
        put("shift_w", sw)
        fw_ = g("ffn_conv_w")[l].reshape(3, 44, 128).transpose(2, 1, 0).reshape(128, 132)
        put("ffn_conv_w", fw_)
        a0 = g("iclr_a0")[l].reshape(2, 8, 128).transpose(2, 0, 1).reshape(128, 16)
        put("iclr_a0", a0)
    constf, maskf, cos, sin = _consts()
    shared = {
        "vecs": vecs,
        "w0row": np.ascontiguousarray(g("decay_w0").reshape(DEPTH, 1, 2048)),
        "constf": constf, "maskf": maskf, "cos": cos, "sin": sin,
        "w_mod": g("w_mod"), "w_in": g("w_in"),
        "w_attn_o": g("w_attn_o"), "w_conv_o": g("w_conv_o"), "w_rwkv_o": g("w_rwkv_o"), "w_out": g("w_out"),
        "decay_up": np.ascontiguousarray(g("decay_up").reshape(DEPTH, 128, D)),
        "iclr_up": np.ascontiguousarray(g("iclr_up").reshape(DEPTH, 128, D)),
        "gate_up": g("gate_up"),
        "w_ffn_up": g("w_ffn_up"), "w_ffn_down": g("w_ffn_down"),
    }
    x = g("x")
    ctx = g("ctx")
    c = g("c")
    c_ctx = g("c_ctx")
    per_core = []
    for b in range(x.shape[0]):
        xs0 = np.ascontiguousarray(np.concatenate([ctx[b].T, x[b].T], axis=1))
        cc = np.stack([c[b], c_ctx], axis=-1).reshape(8, 128, 2).transpose(1, 0, 2).reshape(128, 16)
        m = dict(shared)
        m["xs0"] = xs0
        m["cc"] = np.ascontiguousarray(cc)
        per_core.append(m)
    return per_core


def kernel(**inputs):
    per_core = prep_inputs(inputs)
    nc = Prog().build()
    res = run_bass_kernel_spmd(nc, per_core, core_ids=list(range(8)))
    outs = [np.ascontiguousarray(np.asarray(r["out"], np.float32).T) for r in res.results]
    return np.stack(outs, axis=0)
```
